# Optimizing a Trainium2 kernel written in Bass

```python
import math
import jax, jax.numpy as jnp
from jax import lax
import numpy as np

D_MODEL = 1024
BATCH = 8
SEQ = 2048
DEPTH = 4
DEC_BATCH = 128
DEC_SEQ = 8
PAST_LEN = 8192
PAGE_SIZE = 128

HEAD_DIM = 64
RET_HEADS = D_MODEL // (2 * HEAD_DIM)
RET_DK = HEAD_DIM
RET_DV = HEAD_DIM
RET_QK = RET_HEADS * RET_DK
RET_WIDTH = RET_HEADS * RET_DV
RET_CHUNK = 128
SWA_HEADS = D_MODEL // (2 * HEAD_DIM)
SWA_KV_HEADS = 2
SWA_GROUP = SWA_HEADS // SWA_KV_HEADS
SWA_WIDTH = SWA_HEADS * HEAD_DIM
SWA_KV = SWA_KV_HEADS * HEAD_DIM
WINDOW = 128
SWA_BLOCK = 128
NUM_BUCKETS = 32
MAX_DISTANCE = WINDOW
MIX_WIDTH = RET_WIDTH + SWA_WIDTH
IN_COLS = 2 * RET_QK + 2 * RET_WIDTH + SWA_WIDTH + 2 * SWA_KV
SPLITS = (RET_QK, 2 * RET_QK, 2 * RET_QK + RET_WIDTH, 2 * RET_QK + 2 * RET_WIDTH,
          2 * RET_QK + 2 * RET_WIDTH + SWA_WIDTH, 2 * RET_QK + 2 * RET_WIDTH + SWA_WIDTH + SWA_KV)
D_FF = ((8 * D_MODEL // 3 + 127) // 128) * 128
CONV_W = 3
ROPE_BASE = 10000.0
DEEPNORM_ALPHA = (2.0 * DEPTH) ** 0.25
DEEPNORM_BETA = (8.0 * DEPTH) ** -0.25
LN_EPS = 1e-5
NEG_INF = -1e30

kernel_name = "hybrid_retention_swa_convffn_step"


def layer_norm(x, g, b):
    xf = x.astype(jnp.float32)
    mu = jnp.mean(xf, axis=-1, keepdims=True)
    var = jnp.mean(jnp.square(xf - mu), axis=-1, keepdims=True)
    return ((xf - mu) * lax.rsqrt(var + LN_EPS) * g.astype(jnp.float32) + b.astype(jnp.float32)).astype(x.dtype)


def head_norm(o):
    mu = jnp.mean(o, axis=-1, keepdims=True)
    var = jnp.mean(jnp.square(o - mu), axis=-1, keepdims=True)
    return (o - mu) * lax.rsqrt(var + LN_EPS)


def rotary(x, pos):
    half = x.shape[-1] // 2
    inv = 1.0 / (ROPE_BASE ** (jnp.arange(half, dtype=jnp.float32) / half))
    ang = pos.astype(jnp.float32)[:, None] * inv[None, :]
    cos = jnp.cos(ang)[None, :, None, :]
    sin = jnp.sin(ang)[None, :, None, :]
    x1, x2 = x[..., :half], x[..., half:]
    return jnp.concatenate([x1 * cos - x2 * sin, x1 * sin + x2 * cos], axis=-1)


def retention_chunkwise(q, k, v, s0):
    B, L, H, dk = q.shape
    dv = v.shape[-1]
    chunk = L if L <= RET_CHUNK else RET_CHUNK
    nc = L // chunk
    log_g = jnp.log(1.0 - 2.0 ** (-5.0 - jnp.arange(H, dtype=jnp.float32)))
    idx = jnp.arange(chunk)
    diff = idx[:, None] - idx[None, :]
    decay_mat = jnp.where(diff[None] >= 0,
                          jnp.exp(log_g[:, None, None] * jnp.maximum(diff, 0)[None].astype(jnp.float32)),
                          0.0)
    q_dec = jnp.exp(log_g[None, :] * (idx[:, None] + 1).astype(jnp.float32))
    k_dec = jnp.exp(log_g[None, :] * (chunk - 1 - idx[:, None]).astype(jnp.float32))
    chunk_dec = jnp.exp(log_g * chunk)

    qc = q.reshape(B, nc, chunk, H, dk).swapaxes(0, 1)
    kc = k.reshape(B, nc, chunk, H, dk).swapaxes(0, 1)
    vc = v.reshape(B, nc, chunk, H, dv).swapaxes(0, 1)

    def step(S, inp):
        qi, ki, vi = inp
        scores = jnp.einsum('bihd,bjhd->bhij', qi, ki) * decay_mat[None]
        o = (jnp.einsum('bhij,bjhe->bihe', scores, vi)
             + jnp.einsum('bihd,bhde->bihe', qi, S) * q_dec[None, :, :, None])
        S = (S * chunk_dec[None, :, None, None]
             + jnp.einsum('bjhd,bjhe->bhde', ki * k_dec[None, :, :, None], vi))
        return S, o

    S, o = lax.scan(step, s0, (qc, kc, vc))
    o = o.swapaxes(0, 1).reshape(B, L, H, dv)
    return o, S


def rel_bucket(dist):
    n = jnp.maximum(dist, 0)
    max_exact = NUM_BUCKETS // 2
    nf = jnp.maximum(n, max_exact).astype(jnp.float32)
    large = max_exact + (jnp.log(nf / max_exact) / math.log(MAX_DISTANCE / max_exact)
                         * (NUM_BUCKETS - max_exact)).astype(jnp.int32)
    large = jnp.minimum(large, NUM_BUCKETS - 1)
    return jnp.where(n < max_exact, n, large)


def swa_band(q, k_all, v_all, sinks, rel_bias, pos0):
    B, L = q.shape[0], q.shape[1]
    qb = L if L <= SWA_BLOCK else SWA_BLOCK
    nb = L // qb
    kw = WINDOW + qb
    kidx = jnp.arange(nb)[:, None] * qb + jnp.arange(kw)[None, :]
    kb = k_all[:, kidx].astype(jnp.float32)
    vb = v_all[:, kidx].astype(jnp.float32)
    qg = q.astype(jnp.float32).reshape(B, nb, qb, SWA_KV_HEADS, SWA_GROUP, HEAD_DIM)
    q_pos = pos0 + jnp.arange(L).reshape(nb, qb)
    k_pos = pos0 - WINDOW + kidx
    dist = q_pos[:, :, None] - k_pos[:, None, :]
    valid = (k_pos[:, None, :] >= 0) & (dist >= 0) & (dist < WINDOW)
    bias = rel_bias.astype(jnp.float32)[rel_bucket(dist)]
    bias = bias.transpose(0, 3, 1, 2).reshape(nb, SWA_KV_HEADS, SWA_GROUP, qb, kw)
    logits = jnp.einsum('bnqkgd,bnjkd->bnkgqj', qg, kb) * (HEAD_DIM ** -0.5) + bias[None]
    logits = jnp.where(valid[None, :, None, None], logits, NEG_INF)
    sink = sinks.astype(jnp.float32).reshape(SWA_KV_HEADS, SWA_GROUP)[None, None, :, :, None, None]
    m = jnp.maximum(jnp.max(logits, axis=-1, keepdims=True), sink)
    p = jnp.exp(logits - m)
    w = p / (jnp.sum(p, axis=-1, keepdims=True) + jnp.exp(sink - m))
    out = jnp.einsum('bnkgqj,bnjkd->bnqkgd', w, vb)
    return out.reshape(B, L, SWA_WIDTH)


def hybrid_layer(x, c, ret_s0, swa_k_prev, swa_v_prev, conv_prev, pos0, rel_bias,
                 w_ada, b_ada, w_in, swa_sinks, w_out, ln1_g, ln1_b,
                 w_up, conv_w, conv_b, w_down, ln2_g, ln2_b):
    B, L, _ = x.shape
    f32 = jnp.float32
    mod = jax.nn.silu(c) @ w_ada + b_ada
    sh1, sc1, g1, sh2, sc2, g2 = [m[:, None, :] for m in jnp.split(mod, 6, axis=-1)]

    h = x * (1.0 + sc1) + sh1
    proj = h @ w_in
    rq, rk, rv, rg, sq, sk, sv = jnp.split(proj, SPLITS, axis=-1)
    pos = pos0 + jnp.arange(L, dtype=jnp.int32)

    rq = rotary(rq.reshape(B, L, RET_HEADS, RET_DK).astype(f32), pos)
    rk = rotary(rk.reshape(B, L, RET_HEADS, RET_DK).astype(f32), pos) * (RET_DK ** -0.5)
    rv = rv.reshape(B, L, RET_HEADS, RET_DV).astype(f32)
    ret_o, ret_s = retention_chunkwise(rq, rk, rv, ret_s0.astype(f32))
    ret_o = head_norm(ret_o).reshape(B, L, RET_WIDTH) * jax.nn.silu(rg.astype(f32))

    q = sq.reshape(B, L, SWA_HEADS, HEAD_DIM)
    k_all = jnp.concatenate([swa_k_prev, sk.reshape(B, L, SWA_KV_HEADS, HEAD_DIM).astype(swa_k_prev.dtype)], axis=1)
    v_all = jnp.concatenate([swa_v_prev, sv.reshape(B, L, SWA_KV_HEADS, HEAD_DIM).astype(swa_v_prev.dtype)], axis=1)
    swa_o = swa_band(q, k_all, v_all, swa_sinks, rel_bias, pos0)

    mix = jnp.concatenate([ret_o.astype(x.dtype), swa_o.astype(x.dtype)], axis=-1) @ w_out
    x = layer_norm(DEEPNORM_ALPHA * x + g1 * mix, ln1_g, ln1_b)

    h2 = x * (1.0 + sc2) + sh2
    up = h2 @ w_up
    up_all = jnp.concatenate([conv_prev.astype(up.dtype), up], axis=1)
    conv = conv_b + up_all[:, 0:L] * conv_w[0]
    for tap in range(1, CONV_W):
        conv = conv + up_all[:, tap:tap + L] * conv_w[tap]
    ua, ub = jnp.split(conv, 2, axis=-1)
    f = (jax.nn.silu(ua) * ub) @ w_down
    x = layer_norm(DEEPNORM_ALPHA * x + g2 * f, ln2_g, ln2_b)

    return (x, ret_s.astype(ret_s0.dtype), k_all[:, -WINDOW:], v_all[:, -WINDOW:],
            up_all[:, -(CONV_W - 1):])


def setup_inputs(seed: int = 0) -> dict:
    key = jax.random.key(seed)
    ks = jax.random.split(key, 24)
    f32 = jnp.float32

    def nrm(k, shape, scale):
        return jax.random.normal(k, shape, f32) * scale

    return {
        "x_prompt": nrm(ks[0], (BATCH, SEQ, D_MODEL), 1.0),
        "x_sample": nrm(ks[1], (DEC_BATCH, DEC_SEQ, D_MODEL), 1.0),
        "c_prompt": nrm(ks[2], (BATCH, D_MODEL), 1.0),
        "c_sample": nrm(ks[3], (DEC_BATCH, D_MODEL), 1.0),
        "state_ret": nrm(ks[4], (DEPTH, DEC_BATCH, RET_HEADS, RET_DK, RET_DV), 1.0),
        "cache_swa_k": nrm(ks[5], (DEPTH, DEC_BATCH, WINDOW, SWA_KV_HEADS, HEAD_DIM), 1.0),
        "cache_swa_v": nrm(ks[6], (DEPTH, DEC_BATCH, WINDOW, SWA_KV_HEADS, HEAD_DIM), 1.0),
        "state_conv": nrm(ks[7], (DEPTH, DEC_BATCH, CONV_W - 1, 2 * D_FF), 1.0),
        "rel_bias": nrm(ks[8], (NUM_BUCKETS, SWA_HEADS), 0.5),
        "w_ada": nrm(ks[9], (DEPTH, D_MODEL, 6 * D_MODEL), 0.5 * D_MODEL ** -0.5),
        "b_ada": nrm(ks[10], (DEPTH, 6 * D_MODEL), 0.01),
        "w_in": nrm(ks[11], (DEPTH, D_MODEL, IN_COLS), D_MODEL ** -0.5),
        "swa_sinks": nrm(ks[12], (DEPTH, SWA_HEADS), 1.0),
        "w_out": nrm(ks[13], (DEPTH, MIX_WIDTH, D_MODEL), DEEPNORM_BETA * MIX_WIDTH ** -0.5),
        "ln1_g": 1.0 + nrm(ks[14], (DEPTH, D_MODEL), 0.05),
        "ln1_b": nrm(ks[15], (DEPTH, D_MODEL), 0.02),
        "w_up": nrm(ks[16], (DEPTH, D_MODEL, 2 * D_FF), D_MODEL ** -0.5),
        "conv_w": nrm(ks[17], (DEPTH, CONV_W, 2 * D_FF), CONV_W ** -0.5),
        "conv_b": nrm(ks[18], (DEPTH, 2 * D_FF), 0.01),
        "w_down": nrm(ks[19], (DEPTH, D_FF, D_MODEL), DEEPNORM_BETA * D_FF ** -0.5),
        "ln2_g": 1.0 + nrm(ks[20], (DEPTH, D_MODEL), 0.05),
        "ln2_b": nrm(ks[21], (DEPTH, D_MODEL), 0.02),
    }


def reference(x_prompt, x_sample, c_prompt, c_sample, state_ret, cache_swa_k, cache_swa_v, state_conv,
              rel_bias, w_ada, b_ada, w_in, swa_sinks, w_out, ln1_g, ln1_b,
              w_up, conv_w, conv_b, w_down, ln2_g, ln2_b):
    B = x_prompt.shape[0]
    xp = x_prompt
    xs = x_sample
    p_ret, p_k, p_v, p_conv = [], [], [], []
    s_ret, s_k, s_v, s_conv = [], [], [], []
    for l in range(DEPTH):
        params = (w_ada[l], b_ada[l], w_in[l], swa_sinks[l], w_out[l], ln1_g[l], ln1_b[l],
                  w_up[l], conv_w[l], conv_b[l], w_down[l], ln2_g[l], ln2_b[l])
        ret0 = jnp.zeros((B, RET_HEADS, RET_DK, RET_DV), state_ret.dtype)
        kv0 = jnp.zeros((B, WINDOW, SWA_KV_HEADS, HEAD_DIM), cache_swa_k.dtype)
        conv0 = jnp.zeros((B, CONV_W - 1, 2 * D_FF), state_conv.dtype)
        xp, r_p, k_p, v_p, c_p = hybrid_layer(xp, c_prompt, ret0, kv0, kv0, conv0, 0, rel_bias, *params)
        p_ret.append(r_p); p_k.append(k_p); p_v.append(v_p); p_conv.append(c_p)
        xs, r_s, k_s, v_s, c_s = hybrid_layer(xs, c_sample, state_ret[l], cache_swa_k[l], cache_swa_v[l],
                                              state_conv[l], PAST_LEN, rel_bias, *params)
        s_ret.append(r_s); s_k.append(k_s); s_v.append(v_s); s_conv.append(c_s)
    return (xp, xs,
            jnp.stack(p_ret), jnp.stack(p_k), jnp.stack(p_v), jnp.stack(p_conv),
            jnp.stack(s_ret), jnp.stack(s_k), jnp.stack(s_v), jnp.stack(s_conv))
```

```python
import numpy as np
import ml_dtypes
import concourse.bass as bass
import concourse.mybir as mybir
from concourse.ap import AP
from concourse.bass_utils import run_bass_kernel_spmd

F32 = mybir.dt.float32
BF16 = mybir.dt.bfloat16
ALU = mybir.AluOpType
AF = mybir.ActivationFunctionType
AX = mybir.AxisListType

D = 1024
L = 4
NT = 2176
NP = 2048
DFF = 2816
NU = 22
ALPHA = (2.0 * L) ** 0.25
EPS = 1e-5
PAST = 8192
BLOCKS = [(0, 512), (512, 512), (1024, 512), (1536, 512), (2048, 128)]
FGROUPS = [(0, 3), (3, 3), (6, 3), (9, 3), (12, 3), (15, 3), (18, 2), (20, 2)]
SLOT_ELEMS = 10240
ARENA_F32 = 12050
import os
DBG = int(os.environ.get('KDBG', '0'))


class Ring:
    def __init__(self, sems):
        self.sems = sems
        self.vals = [0] * len(sems)
        self.i = 0


class Prog:
    ENG = ("pe", "act", "dve", "pool", "sp")

    def __init__(self, nc, esem):
        self.nc = nc
        self.esem = esem
        self.q = {e: [] for e in self.ENG}
        self.cnt = {e: 0 for e in self.ENG}
        self.seen = {e: {} for e in self.ENG}
        self.res = {}
        self.out_toks = []
        self.recent = []

    def _need(self, eng, tok):
        sem, val, teng, kind = tok
        sid = id(sem)
        if self.seen[eng].get(sid, 0) >= val:
            return
        self.seen[eng][sid] = val
        self.q[eng].append(("wait", sem, val))

    def _deps(self, eng, r, w, is_dma):
        toks = []
        for k in r:
            st = self.res.get(k)
            if st and st["w"] is not None:
                toks.append(st["w"])
        for k in w:
            st = self.res.get(k)
            if not st:
                continue
            if st["w"] is not None:
                toks.append(st["w"])
            toks.extend(st["r"])
        best = {}
        for t in toks:
            sid = id(t[0])
            if sid not in best or best[sid][1] < t[1]:
                best[sid] = t
        for t in best.values():
            self._need(eng, t)

    def _record(self, tok, r, w):
        for k in r:
            st = self.res.setdefault(k, {"w": None, "r": []})
            st["r"].append(tok)
        for k in w:
            self.res[k] = {"w": tok, "r": []}

    def op(self, eng, fn, r=(), w=()):
        self._deps(eng, r, w, False)
        self.cnt[eng] += 1
        tok = (self.esem[eng], self.cnt[eng], eng, "op")
        self.q[eng].append(("op", fn))
        self._record(tok, r, w)

    def dma(self, q, ring, out, in_, r=(), w=(), is_out=False):
        self._deps(q, r, w, True)
        s = ring.i
        ring.i = (ring.i + 1) % len(ring.sems)
        sem = ring.sems[s]
        if ring.vals[s] > 0:
            self._need(q, (sem, ring.vals[s], None, "dma"))
        ring.vals[s] += 16
        tok = (sem, ring.vals[s], None, "dma")
        self.q[q].append(("dma", out, in_, sem))
        self.recent.append(tok)
        self._record(tok, r, w)
        if is_out:
            self.out_toks.append(tok)

    def barrier(self):
        for e in self.ENG:
            for t in self.recent:
                self._need(e, t)
            for o in self.ENG:
                if o != e and self.cnt[o] > 0:
                    self._need(e, (self.esem[o], self.cnt[o], o, "op"))
        self.recent = []

    def emit(self, block):
        nc = self.nc
        engs = {"pe": nc.tensor, "act": nc.scalar, "dve": nc.vector, "pool": nc.gpsimd, "sp": nc.sync}
        for t in self.out_toks:
            self._need("sp", t)

        def run(name):
            e = engs[name]
            for it in self.q[name]:
                if it[0] == "wait":
                    e.wait_ge(it[1], it[2])
                elif it[0] == "op":
                    it[1](e).then_inc(self.esem[name], 1)
                else:
                    e.dma_start(out=it[1], in_=it[2]).then_inc(it[3], 16)

        @block.tensor
        def _(t):
            run("pe")

        @block.scalar
        def _(t):
            run("act")

        @block.vector
        def _(t):
            run("dve")

        @block.gpsimd
        def _(t):
            run("pool")

        @block.sync
        def _(t):
            run("sp")


def _ins(ap, pos, cnt):
    l = [list(x) for x in ap.ap]
    l.insert(pos, [0, cnt])
    return AP(ap.tensor, ap.offset, l)


def build_program():
    nc = bass.Bass("TRN2", target_bir_lowering=False)

    def din(name, shape, dt=F32):
        return nc.dram_tensor(name, list(shape), dt, kind="ExternalInput").ap()

    def dout(name, shape):
        return nc.dram_tensor(name, list(shape), F32, kind="ExternalOutput").ap()

    xT = din("xT", [D, NT])
    cT = din("cT", [128, 8, 17])
    w_ada = din("w_ada", [L, D, 6 * D])
    bT_ada = din("bT_ada", [128, L, 48])
    w_r = din("w_r", [L, 4, 128, 8, 768])
    w_s = din("w_s", [L, 128, 8, 768])
    w_or = din("w_or", [L, 4, 128, D])
    w_os = din("w_os", [L, 128, 4, D])
    w_up = din("w_up", [L, NU, 128, 8, 256])
    w_down = din("w_down", [L, 128, NU, D])
    convT = din("convT", [128, L, 44, 4])
    lnT = din("lnT", [128, L, 4, 8])
    sinkP = din("sinkP", [128, L, 8])
    sinkS = din("sinkS", [32, L, 2])
    rel_bias = din("rel_bias", [32, 8])
    ohr = din("ohr", [32, 128])
    rot = din("rot", [5, 128, 4, 512])
    decm = din("decm", [128, 8, 128])
    blkind = din("blkind", [128, 128])
    qdec = din("qdec", [128, 4, 2, 512])
    kdec = din("kdec", [128, 2, 8])
    cdec = din("cdec", [128, 2, 4, 3])
    hmask = din("hmask", [128, 2])
    seqm = din("seqm", [128, 16])
    identf = din("identf", [128, 128])
    hsel = din("hsel", [128, 128])
    state_ret = din("state_ret", [L, 16, 8, 64, 64])
    cache_k = din("cache_k", [L, 16, 128, 128])
    cache_v = din("cache_v", [L, 16, 128, 128])
    state_conv = din("state_conv", [L, 32, 2 * DFF])

    yT = dout("yT", [D, NT])
    o_ret_p = dout("o_ret_p", [L, 4, 128, 64])
    o_ret_s = dout("o_ret_s", [L, 4, 128, 16, 64])
    o_k_p = dout("o_k_p", [L, 128, 128])
    o_v_p = dout("o_v_p", [L, 128, 128])
    o_k_s = dout("o_k_s", [L, 16, 128, 128])
    o_v_s = dout("o_v_s", [L, 16, 128, 128])
    o_conv_p = dout("o_conv_p", [L, 128, 44, 2])
    o_conv_s = dout("o_conv_s", [L, 128, 44, 16, 2])
    ext_d = nc.dram_tensor("ext_d", [8, 128, 512], F32, kind="Internal").ap()

    from contextlib import ExitStack
    es = ExitStack()

    def sb(name, shape, dt=F32):
        return es.enter_context(nc.sbuf_tensor(name, list(shape), dt))

    with es:
        X = sb("X", [128, 8, NT])
        HT = sb("HT", [128, 8, NT], BF16)
        WS = [sb("WS0", [128, SLOT_ELEMS], BF16), sb("WS1", [128, SLOT_ELEMS], BF16)]
        ARF = sb("ARENA", [128, ARENA_F32])
        ARB = ARF.bitcast(BF16)
        modT = sb("modT", [128, 48, 17])
        scT = sb("scT", [128, 8, 17], BF16)
        c_sb = sb("c_sb", [128, 8, 17])
        bada = sb("bada", [128, L, 48])
        convp = sb("convp", [128, L, 44, 4])
        lnp = sb("lnp", [128, L, 4, 8])
        sinkp = sb("sinkp", [128, L, 8])
        sinks = sb("sinks", [32, L, 2])
        identb = sb("identb", [128, 128], BF16)
        identF = sb("identF", [128, 128])
        hselb = sb("hselb", [128, 128], BF16)
        onesb = sb("onesb", [128, 128], BF16)
        blk = sb("blk", [128, 128])
        kdecS = sb("kdecS", [128, 2, 8])
        cdecS = sb("cdecS", [128, 2, 4, 3])
        hmS = sb("hmS", [128, 2])
        seqS = sb("seqS", [128, 16])
        epsT = sb("epsT", [128, 1])
        biasS = sb("biasS", [32, 2, 136], BF16)
        tabA = sb("tabA", [128, 4, 8, 17])
        gate = sb("gate", [128, 2, 8, 17])
        Sp = sb("Sp", [128, 4, 64])
        Sblk = sb("Sblk", [128, 4, 2, 64], BF16)
        kvf = sb("kvf", [128, 2, 128])
        stat = sb("stat", [128, 4, 32])
        stat2 = sb("stat2", [128, 2, 32])
        halo = sb("halo", [128, 44, 2])

        PSB = [es.enter_context(nc.psum_tensor("ps%d" % i, [128, 512], F32)) for i in range(8)]

        def sem(name):
            return es.enter_context(nc.semaphore(name))

        esem = {e: sem("e_" + e) for e in Prog.ENG}
        P = Prog(nc, esem)
        rW = Ring([sem("rw%d" % i) for i in range(6)])
        rC = Ring([sem("rc%d" % i) for i in range(6)])
        rO = Ring([sem("ro%d" % i) for i in range(8)])
        rT = Ring([sem("rt%d" % i) for i in range(6)])
        rP = Ring([sem("rp%d" % i) for i in range(6)])

        psi = [0]

        reserved = set()

        def psum():
            while True:
                i = psi[0]
                psi[0] = (i + 1) % 8
                if i not in reserved:
                    return i

        def PS(i):
            return PSB[i]

        def PSb(i):
            return PSB[i].bitcast(BF16)

        class Arena:
            def __init__(self):
                self.off = 0

            def f32(self, shape):
                n = int(np.prod(shape[1:]))
                v = ARF[0:shape[0], self.off:self.off + n]
                self.off += n
                assert self.off <= ARENA_F32, ("arena overflow", self.off)
                return _View(v, shape)

            def bf16(self, shape):
                n = int(np.prod(shape[1:]))
                nw = (n + 1) // 2
                v = ARB[0:shape[0], 2 * self.off:2 * self.off + n]
                self.off += nw
                assert self.off <= ARENA_F32, ("arena overflow", self.off)
                return _View(v, shape)

        class _View:
            def __init__(self, ap, shape):
                self.ap = ap
                self.shape = list(shape)
                if len(shape) == 2:
                    self.v = ap
                else:
                    names = "abcdefg"[:len(shape) - 1]
                    pat = "p (" + " ".join(names) + ") -> p " + " ".join(names)
                    kw = {names[i]: shape[i + 1] for i in range(len(names))}
                    self.v = ap.rearrange(pat, **kw)

            def __getitem__(self, k):
                return self.v[k]

        def mm(out, lhsT, rhs, start, stop, r, w):
            P.op("pe", lambda e: e.matmul(out, lhsT=lhsT, rhs=rhs, start=start, stop=stop), r=r, w=w)

        def tr(out, in_, r, w):
            k = in_.shape[0]
            P.op("pe", lambda e: e.transpose(out, in_, identb[0:k, 0:k]), r=list(r) + ["identb"], w=w)

        def act(out, in_, func, r, w, bias=None, scale=None, accum=None):
            kw = {}
            if bias is not None:
                kw["bias"] = bias
            if scale is not None:
                kw["scale"] = scale
            if accum is not None:
                kw["accum_out"] = accum
            P.op("act", lambda e: e.activation(out=out, in_=in_, func=func, **kw), r=r, w=w)

        def tt(eng, out, in0, in1, op, r, w):
            P.op(eng, lambda e: e.tensor_tensor(out=out, in0=in0, in1=in1, op=op), r=r, w=w)

        def ts(eng, out, in0, s1, s2, op0, op1, r, w):
            if op1 is None:
                P.op(eng, lambda e: e.tensor_scalar(out=out, in0=in0, scalar1=s1, scalar2=None, op0=op0), r=r, w=w)
            else:
                P.op(eng, lambda e: e.tensor_scalar(out=out, in0=in0, scalar1=s1, scalar2=s2, op0=op0, op1=op1), r=r, w=w)

        def stt(eng, out, in0, scalar, in1, op0, op1, r, w):
            P.op(eng, lambda e: e.scalar_tensor_tensor(out=out, in0=in0, scalar=scalar, in1=in1, op0=op0, op1=op1), r=r, w=w)

        def cp(eng, out, in_, r, w):
            if eng == "act":
                P.op("act", lambda e: e.copy(out=out, in_=in_), r=r, w=w)
            else:
                P.op(eng, lambda e: e.tensor_copy(out=out, in_=in_), r=r, w=w)

        def ms(eng, ap, val, w):
            P.op(eng, lambda e: e.memset(ap, val), r=(), w=w)

        def ld(q, ring, out, in_, w, r=()):
            P.dma(q, ring, out, in_, r=r, w=w)

        arS = Arena()
        arS.off = 9000
        ext = arS.f32([8, 512])
        rb = arS.f32([32, 8])
        ohrS = arS.f32([32, 128])
        ld("sp", rC, c_sb[:], cT, ["c_sb"])
        ld("sp", rC, bada[:], bT_ada, ["bada"])
        ld("sp", rC, convp[:], convT, ["convp"])
        ld("sp", rC, lnp[:], lnT, ["lnp"])
        ld("sp", rC, sinkp[:], sinkP, ["sinkp"])
        ld("sp", rC, sinks[:], sinkS, ["sinks"])
        ld("sp", rC, blk[:], blkind, ["blk"])
        ld("sp", rC, kdecS[:], kdec, ["kdec"])
        ld("sp", rC, cdecS[:], cdec, ["cdec"])
        ld("sp", rC, hmS[:], hmask, ["hm"])
        ld("sp", rC, seqS[:], seqm, ["seqm"])
        ld("sp", rC, identF[:], identf, ["identF"])
        ld("sp", rC, rb[:, :], rel_bias, ["rb"])
        ld("sp", rC, ohrS[:, :], ohr, ["ohrS"])
        ld("pool", rP, identb[:], identf, ["identb"])
        ld("pool", rP, hselb[:], hsel, ["hselb"])
        ms("dve", epsT[:], EPS, ["epsT"])
        ms("dve", onesb[:], 1.0 / 1024.0, ["ones"])
        ms("dve", ext[:, :], -1e30, ["ext"])
        for fc in range(8):
            ld("sp", rC, X[:, fc, :], xT[fc * 128:(fc + 1) * 128, :], [("X", fc, b) for b in range(5)])

        b0 = psum()
        mm(PS(b0)[0:8, 0:128], rb[:, :], ohrS[:, :], True, True, ["rb", "ohrS"], [("ps", b0)])
        cp("dve", ext[:, 128:256], PS(b0)[0:8, 0:128], [("ps", b0), "ext"], ["ext"])
        P.dma("sp", rC, ext_d, _ins(ext[:, :], 1, 128), r=["ext"], w=["ext_d"])
        for cbi in range(8):
            c_, g_ = cbi // 2, cbi % 2
            h_ = g_ * 4 + c_
            src2 = AP(ext_d.tensor, h_ * 65536 + 127, [[511, 8], [1, 136]])
            P.dma("pool", rP, biasS[c_ * 8:(c_ + 1) * 8, g_, :], src2, r=["ext_d"], w=["biasS"])
        act(scT[:], c_sb[:], AF.Silu, ["c_sb"], ["scT"])

        deferred = []

        def drain(k=None):
            n_ = len(deferred) if k is None else min(k, len(deferred))
            for _ in range(n_):
                deferred.pop(0)()

        def load_pass_weights(kind, l, idx, s):
            W = WS[s]
            key = ("W", s)

            def q(out, in_):
                deferred.append(lambda: P.dma("pool", rW, out, in_, w=[key]))
            if kind == "R":
                v1 = W[:, 0:8 * 768].rearrange("p (k n) -> p k n", k=8)
                q(v1[:, 0:4, :], w_r[l, idx, :, 0:4, :])
                q(v1[:, 4:8, :], w_r[l, idx, :, 4:8, :])
                v2 = W[:, 6144:6144 + 1024]
                q(v2, w_or[l, idx])
                return v1, v2
            if kind == "S":
                v1 = W[:, 0:8 * 768].rearrange("p (k n) -> p k n", k=8)
                q(v1[:, 0:4, :], w_s[l, :, 0:4, :])
                q(v1[:, 4:8, :], w_s[l, :, 4:8, :])
                v2 = W[:, 6144:6144 + 4096].rearrange("p (k n) -> p k n", k=4)
                q(v2, w_os[l])
                return v1, v2
            u0, nu = FGROUPS[idx]
            vu = W[:, 0:nu * 2048].rearrange("p (u k n) -> p u k n", u=nu, k=8)
            vd = W[:, 6144:6144 + nu * 1024].rearrange("p (u n) -> p u n", u=nu)
            for ui in range(nu):
                q(vu[:, ui, :, :], w_up[l, u0 + ui])
            q(vd, w_down[l, :, u0:u0 + nu, :])
            return vu, vd

        def compute_mod(l):
            ar = Arena()
            WA = [ar.bf16([128, 8, 256]), ar.bf16([128, 8, 256])]
            mk = "modT"
            bks = [psum(), psum()]
            for jj in range(24):
                wa = WA[jj % 2]
                wk = ("WA", jj % 2)
                P.dma("pool", rW, wa[:], w_ada[l, :, jj * 256:(jj + 1) * 256].rearrange("(k p) n -> p k n", p=128), w=[wk])
                for sub in range(2):
                    j = jj * 2 + sub
                    bk = bks[j // 24]
                    for kc in range(8):
                        mm(PS(bk)[:, (j % 24) * 17:(j % 24) * 17 + 17], wa[:, kc, sub * 128:(sub + 1) * 128], scT[:, kc, :],
                           kc == 0, kc == 7, [wk, "scT"], [("ps", bk)])
            for hf in range(2):
                tt("dve", modT[:, hf * 24:(hf + 1) * 24, :],
                   PS(bks[hf])[:, 0:408].rearrange("p (j r) -> p j r", r=17),
                   _ins(bada[:, l, hf * 24:(hf + 1) * 24], 2, 17), ALU.add,
                   [("ps", bks[hf]), "bada"], [mk])

        def make_tables(use_mod, which_mod, g_ap, b_ap, last):
            mk = "modT"
            if use_mod:
                sh = modT[:, which_mod * 24 + 0:which_mod * 24 + 8, :]
                sc = modT[:, which_mod * 24 + 8:which_mod * 24 + 16, :]
                onep = tabA[:, 1, :, :]
                ts("pool", onep, sc, 1.0, None, ALU.add, None, [mk, "tabA"], ["tabA"])
                if g_ap is None:
                    cp("pool", tabA[:, 0, :, :], onep, ["tabA"], ["tabA"])
                    cp("pool", tabA[:, 1, :, :], sh, [mk, "tabA"], ["tabA"])
                else:
                    tt("pool", tabA[:, 0, :, :], onep, _ins(g_ap, 2, 17), ALU.mult, ["tabA", "lnp"], ["tabA"])
                    tt("pool", tabA[:, 1, :, :], onep, _ins(b_ap, 2, 17), ALU.mult, ["tabA", "lnp"], ["tabA"])
                    tt("pool", tabA[:, 1, :, :], tabA[:, 1, :, :], sh, ALU.add, ["tabA", mk], ["tabA"])
            a = 1.0 if last else ALPHA
            if g_ap is None:
                ms("pool", tabA[:, 2, :, 0:1], a, ["tabA"])
                ms("pool", tabA[:, 3, :, 0:1], 0.0, ["tabA"])
            else:
                ts("pool", tabA[:, 2, :, 0:1], _ins(g_ap, 2, 1), a, None, ALU.mult, None, ["lnp", "tabA"], ["tabA"])
                ts("pool", tabA[:, 3, :, 0:1], _ins(b_ap, 2, 1), a, None, ALU.mult, None, ["lnp", "tabA"], ["tabA"])

        def make_gates():
            cp("pool", gate[:, 0, :, :], modT[:, 16:24, :], ["modT", "gate"], ["gate"])
            cp("pool", gate[:, 1, :, :], modT[:, 40:48, :], ["modT", "gate"], ["gate"])

        def finish_fc(bi, n0, n, fc, src, src_keys, do_h, xh1):
            smp = bi == 4
            src_keys = list(src_keys) + [("X", fc, bi)]
            if do_h:
                if not smp:
                    act(HT[:, fc, n0:n0 + n], src, AF.Identity, list(src_keys) + ["tabA"], [("HT", fc, bi)],
                        bias=tabA[:, 1, fc, 0:1], scale=tabA[:, 0, fc, 0:1])
                else:
                    s3 = src.rearrange("p (s i) -> p s i", i=8)
                    x3 = xh1[:, 0:128].rearrange("p (s i) -> p s i", i=8)
                    tt("dve", x3, s3, _ins(tabA[:, 0, fc, 1:17], 2, 8), ALU.mult, list(src_keys) + ["tabA", "xh1"], ["xh1"])
                    tt("dve", HT[:, fc, n0:n0 + n].rearrange("p (s i) -> p s i", i=8), x3,
                       _ins(tabA[:, 1, fc, 1:17], 2, 8), ALU.add, ["xh1", "tabA"], [("HT", fc, bi)])
            ts("dve" if fc % 2 == 0 else "pool", X[:, fc, n0:n0 + n], src, tabA[:, 2, fc, 0:1], tabA[:, 3, fc, 0:1],
               ALU.mult, ALU.add, list(src_keys) + ["tabA", ("X", fc, bi)], [("X", fc, bi)])

        def finish(bi, n0, n, src_fn, src_keys, do_h, xh1):
            for fc in range(8):
                finish_fc(bi, n0, n, fc, src_fn(fc), src_keys, do_h, xh1)

        def accum(bi, fc, bk, gi, tmp):
            n0, n = BLOCKS[bi]
            if bi < 4 and fc % 2 == 1 and len(tmp) > 1:
                tb = tmp[1 + (fc // 2) % 2]
                tkk = ("acct2", (fc // 2) % 2)
                act(tb[:, 0:n], PS(bk)[:, 0:n], AF.Identity, [("ps", bk), "gate", tkk], [tkk], scale=gate[:, gi, fc, 0:1])
                tt("pool", X[:, fc, n0:n0 + n], X[:, fc, n0:n0 + n], tb[:, 0:n], ALU.add, [tkk, ("X", fc, bi)], [("X", fc, bi)])
            elif bi < 4:
                stt("dve", X[:, fc, n0:n0 + n], PS(bk)[:, 0:n], gate[:, gi, fc, 0:1], X[:, fc, n0:n0 + n], ALU.mult, ALU.add,
                    [("ps", bk), "gate", ("X", fc, bi)], [("X", fc, bi)])
            else:
                tt("dve", tmp[0][:, 0:128].rearrange("p (s i) -> p s i", i=8), PS(bk)[:, 0:128].rearrange("p (s i) -> p s i", i=8),
                   _ins(gate[:, gi, fc, 1:17], 2, 8), ALU.mult, [("ps", bk), "gate", "acctmp"], ["acctmp"])
                tt("pool", X[:, fc, n0:n0 + n], X[:, fc, n0:n0 + n], tmp[0][:, 0:128], ALU.add, ["acctmp", ("X", fc, bi)], [("X", fc, bi)])

        class LNBuf:
            def __init__(self, ar):
                self.vb = [ar.bf16([128, 512]), ar.bf16([128, 512])]
                self.vsq = [ar.bf16([128, 512]), ar.bf16([128, 512])]
                self.tmp = [ar.f32([128, 512]) for _ in range(3)]
                self.mean = ar.f32([128, 512])
                self.rstd = ar.f32([128, 512])
                self.xh1 = ar.f32([128, 128])

        def ln_finish_block(bi, lb, do_h):
            n0, n = BLOCKS[bi]
            bm, bq = psum(), psum()
            reserved.add(bm)
            reserved.add(bq)
            for fc in range(8):
                vbk, vsk = ("vb", fc % 2), ("vsq", fc % 2)
                cp("dve", lb.vb[fc % 2][:, 0:n], X[:, fc, n0:n0 + n], [("X", fc, bi), vbk], [vbk])
                act(lb.vsq[fc % 2][:, 0:n], X[:, fc, n0:n0 + n], AF.Square, [("X", fc, bi), vsk], [vsk])
                mm(PS(bm)[:, 0:n], onesb[:, :], lb.vb[fc % 2][:, 0:n], fc == 0, fc == 7, ["ones", vbk], [("ps", bm)])
                mm(PS(bq)[:, 0:n], onesb[:, :], lb.vsq[fc % 2][:, 0:n], fc == 0, fc == 7, ["ones", vsk], [("ps", bq)])
            cp("act", lb.mean[:, 0:n], PS(bm)[:, 0:n], [("ps", bm), "mean"], ["mean"])
            tt("dve", lb.rstd[:, 0:n], lb.mean[:, 0:n], lb.mean[:, 0:n], ALU.mult, ["mean", "rstd"], ["rstd"])
            tt("dve", lb.rstd[:, 0:n], PS(bq)[:, 0:n], lb.rstd[:, 0:n], ALU.subtract, [("ps", bq), "rstd"], ["rstd"])
            reserved.discard(bm)
            reserved.discard(bq)
            act(lb.rstd[:, 0:n], lb.rstd[:, 0:n], AF.Ln, ["rstd", "epsT"], ["rstd"], bias=epsT[:, 0:1])
            act(lb.rstd[:, 0:n], lb.rstd[:, 0:n], AF.Exp, ["rstd"], ["rstd"], scale=-0.5)
            for fc in range(8):
                tk = ("lntmp", fc % 3)
                t_ = lb.tmp[fc % 3]
                tt("pool", t_[:, 0:n], X[:, fc, n0:n0 + n], lb.mean[:, 0:n], ALU.subtract, [("X", fc, bi), "mean", tk], [tk])
                tt("dve", t_[:, 0:n], t_[:, 0:n], lb.rstd[:, 0:n], ALU.mult, [tk, "rstd"], [tk])
                finish_fc(bi, n0, n, fc, t_[:, 0:n], [tk], do_h, lb.xh1)

        def ret_pass(l, gp, WR_, WO_, wk):
            ar = Arena()
            rotS = ar.f32([128, 4, 512])
            decmS = ar.f32([128, 2, 128])
            qdecS = ar.f32([128, 2, 512])
            t1 = ar.f32([128, 512])
            t2 = ar.f32([128, 512])
            t3 = t1
            t4 = t2
            qp = ar.bf16([128, 512])
            qpp = ar.bf16([128, 512])
            kf = ar.bf16([128, 512])
            ka = ar.bf16([128, 512])
            kb = ar.bf16([128, 512])
            gs = ar.bf16([128, 512])
            retT = ar.bf16([128, 512])
            vpad = [ar.bf16([128, 2, 128]), ar.bf16([128, 2, 128])]
            sT = [ar.bf16([128, 2, 128]), ar.bf16([128, 2, 128])]
            ktok = [ar.bf16([128, 128]), ar.bf16([128, 128])]
            of32 = [ar.f32([128, 128]), ar.f32([128, 128])]
            ob = [ar.bf16([128, 2, 128]), ar.bf16([128, 2, 128])]
            hn1 = [ar.f32([128, 128]), ar.f32([128, 128])]
            hn2 = [ar.f32([128, 128]), ar.f32([128, 128])]
            Ss = ar.f32([128, 16, 64])
            Ssblk = ar.bf16([128, 16, 2, 64])
            st1 = ar.f32([128, 4, 64])
            vblk = ar.bf16([128, 4, 128])
            acct = [ar.f32([128, 128]), ar.f32([128, 512]), ar.f32([128, 512])]
            ld("sp", rT, decmS[:, :, :], decm[:, gp * 2:gp * 2 + 2, :], ["decm"])
            ld("sp", rT, qdecS[:, :, :], qdec[:, gp, :, :], ["qdec"])
            for hb in range(2):
                h = gp * 2 + hb
                P.dma("sp", rT, Ss[hb * 64:(hb + 1) * 64, :, :], state_ret[l, :, h, :, :].rearrange("s d e -> d s e"), w=["Ss"])
            ms("pool", ka[:, :], 0.0, ["ka"])
            ms("pool", kb[:, :], 0.0, ["kb"])
            ms("pool", vpad[0][:, :, :], 0.0, [("vpad", 0)])
            ms("pool", vpad[1][:, :, :], 0.0, [("vpad", 1)])
            ms("dve", Sp[:, gp, :], 0.0, [("Sp", gp)])
            ms("dve", Sblk[:, gp, :, :], 0.0, [("Sblk", gp)])
            tt("dve", Ssblk[:, :, :, :], _ins(Ss[:, :, :], 2, 2),
               _ins(_ins(hmS[:, :], 1, 16), 3, 64), ALU.mult, ["Ss", "hm", "Ssblk"], ["Ssblk"])
            for bi, (n0, n) in enumerate(BLOCKS):
                drain(1)
                smp = bi == 4
                di = 1 if smp else 0
                hk = [("HT", kc, bi) for kc in range(8)]
                nch = n // 128
                P.dma("sp", rT, rotS[:, :, :], rot[bi], w=["rot"])

                def proj(col0):
                    bk = psum()
                    for kc in range(8):
                        mm(PS(bk)[:, 0:n], WR_[:, kc, col0:col0 + 128], HT[:, kc, n0:n0 + n], kc == 0, kc == 7,
                           [wk, hk[kc]], [("ps", bk)])
                    return bk
                bq = proj(0)
                bqs = proj(128)
                tt("dve", t1[:, 0:n], PS(bq)[:, 0:n], rotS[:, 0, 0:n], ALU.mult, [("ps", bq), "rot", "t1"], ["t1"])
                tt("dve", t2[:, 0:n], PS(bqs)[:, 0:n], rotS[:, 1, 0:n], ALU.mult, [("ps", bqs), "rot", "t2"], ["t2"])
                tt("pool", t1[:, 0:n], t1[:, 0:n], t2[:, 0:n], ALU.add, ["t1", "t2"], ["t1"])
                cp("act", qp[:, 0:n], t1[:, 0:n], ["t1", "qp"], ["qp"])
                tt("pool", qpp[:, 0:n], t1[:, 0:n], qdecS[:, di, 0:n], ALU.mult, ["t1", "qdec", "qpp"], ["qpp"])
                bk_ = proj(256)
                bks_ = proj(384)
                tt("dve", t3[:, 0:n], PS(bk_)[:, 0:n], rotS[:, 2, 0:n], ALU.mult, [("ps", bk_), "rot", "t1"], ["t1"])
                tt("dve", t4[:, 0:n], PS(bks_)[:, 0:n], rotS[:, 3, 0:n], ALU.mult, [("ps", bks_), "rot", "t2"], ["t2"])
                tt("pool", kf[:, 0:n], t3[:, 0:n], t4[:, 0:n], ALU.add, ["t1", "t2", "kf"], ["kf"])
                cp("act", ka[0:64, 0:n], kf[0:64, 0:n], ["kf", "ka"], ["ka"])
                cp("act", kb[64:128, 0:n], kf[64:128, 0:n], ["kf", "kb"], ["kb"])
                bg = proj(640)
                act(gs[:, 0:n], PS(bg)[:, 0:n], AF.Silu, [("ps", bg), "gs"], ["gs"])

                def stageA(ci):
                    cidx = bi * 4 + ci
                    par = cidx % 2
                    c0 = n0 + ci * 128
                    sl = slice(ci * 128, (ci + 1) * 128)
                    bv = psum()
                    if smp:
                        reserved.add(bv)
                    for kc in range(8):
                        mm(PS(bv)[:, 0:128], HT[:, kc, c0:c0 + 128], WR_[:, kc, 512:640], kc == 0, kc == 7, [wk, hk[kc]], [("ps", bv)])
                    vp = vpad[par]
                    vk = ("vpad", par)
                    vo = AP(vp.ap.tensor, vp.ap.offset, [list(vp.ap.ap[0]), [192, 2], [1, 64]])
                    cp("act", vo, PS(bv)[:, 0:128].rearrange("p (b e) -> p b e", b=2), [("ps", bv), vk], [vk])
                    bs = psum()
                    mm(PS(bs)[:, 0:128], ka[:, sl], qp[:, sl], True, True, ["ka", "qp"], [("ps", bs)])
                    mm(PS(bs)[:, 128:256], kb[:, sl], qp[:, sl], True, True, ["kb", "qp"], [("ps", bs)])
                    st_ = sT[par]
                    sk_ = ("sT", par)
                    tt("dve", st_[:, :, :], PS(bs)[:, 0:256].rearrange("p (a i) -> p a i", a=2),
                       decmS[:, :, :], ALU.mult, [("ps", bs), "decm", sk_], [sk_])
                    if smp:
                        tt("pool", st_[:, :, :], st_[:, :, :], _ins(blk[:, :], 1, 2), ALU.mult, [sk_, "blk"], [sk_])
                    bo = psum()
                    reserved.add(bo)
                    mm(PS(bo)[:, 0:128], vp[:, 0, :], st_[:, 0, :], True, False, [vk, sk_], [("ps", bo)])
                    mm(PS(bo)[:, 0:128], vp[:, 1, :], st_[:, 1, :], False, False, [vk, sk_], [("ps", bo)])
                    if not smp:
                        mm(PS(bo)[:, 0:128], Sblk[:, gp, :, :].rearrange("p a e -> p (a e)"), qpp[:, sl], False, True,
                           [("Sblk", gp), "qpp"], [("ps", bo)])
                    else:
                        for s_ in range(16):
                            mm(PS(bo)[:, s_ * 8:(s_ + 1) * 8], Ssblk[:, s_, :, :].rearrange("p a e -> p (a e)"),
                               qpp[:, ci * 128 + s_ * 8:ci * 128 + (s_ + 1) * 8], False, s_ == 15,
                               ["Ssblk", "qpp"], [("ps", bo)])
                    bt = psum()
                    tr(PSb(bt)[:, 0:128], kf[:, sl], ["kf"], [("ps", bt)])
                    kt = ktok[par]
                    kk = ("ktok", par)
                    tt("dve", kt[:, :].rearrange("p (a d) -> p a d", a=2), PSb(bt)[:, 0:128].rearrange("p (a d) -> p a d", a=2),
                       _ins(kdecS[:, di, gp * 2:gp * 2 + 2], 2, 64), ALU.mult, [("ps", bt), "kdec", kk], [kk])
                    if not smp:
                        bd = psum()
                        mm(PS(bd)[:, 0:128], kt[:, :], vp[:, 0, :], True, False, [kk, vk], [("ps", bd)])
                        mm(PS(bd)[:, 0:128], kt[:, :], vp[:, 1, :], False, True, [kk, vk], [("ps", bd)])
                        ts("dve", st1[:, 0, :], PS(bd)[:, 0:64], cdecS[:, 0, gp, 1:2], None, ALU.mult, None,
                           [("ps", bd), "cdec", "st1"], ["st1"])
                        stt("dve", st1[:, 0, :], PS(bd)[:, 64:128], cdecS[:, 0, gp, 2:3], st1[:, 0, :], ALU.mult, ALU.add,
                            [("ps", bd), "cdec", "st1"], ["st1"])
                        stt("dve", Sp[:, gp, :], Sp[:, gp, :], cdecS[:, 0, gp, 0:1], st1[:, 0, :], ALU.mult, ALU.add,
                            ["st1", "cdec", ("Sp", gp)], [("Sp", gp)])
                        tt("pool", Sblk[:, gp, :, :], _ins(Sp[:, gp, :], 1, 2), _ins(hmS[:, :], 2, 64),
                           ALU.mult, [("Sp", gp), "hm", ("Sblk", gp)], [("Sblk", gp)])
                        if cidx == 15:
                            P.dma("sp", rO, o_ret_p[l, gp], Sp[:, gp, :], r=[("Sp", gp)], w=[("o_ret_p", l, gp)], is_out=True)
                    else:
                        for q4 in range(4):
                            tt("dve", vblk[:, :, :], _ins(PS(bv)[:, 0:128], 1, 4),
                               _ins(seqS[:, q4 * 4:(q4 + 1) * 4], 2, 128), ALU.mult, [("ps", bv), "seqm", "vblk"], ["vblk"])
                            bd = psum()
                            mm(PS(bd)[:, 0:512], kt[:, :], vblk[:, :, :].rearrange("p s e -> p (s e)"), True, True,
                               [kk, "vblk"], [("ps", bd)])
                            pv = PS(bd)[:, 0:512].rearrange("p (s a e) -> p s a e", s=4, a=2)
                            ts("dve", st1[:, :, :], pv[:, :, 0, :], cdecS[:, 1, gp, 1:2], None, ALU.mult, None,
                               [("ps", bd), "cdec", "st1"], ["st1"])
                            stt("dve", st1[:, :, :], pv[:, :, 1, :], cdecS[:, 1, gp, 2:3], st1[:, :, :],
                                ALU.mult, ALU.add, [("ps", bd), "cdec", "st1"], ["st1"])
                            stt("dve", Ss[:, q4 * 4:(q4 + 1) * 4, :], Ss[:, q4 * 4:(q4 + 1) * 4, :], cdecS[:, 1, gp, 0:1], st1[:, :, :],
                                ALU.mult, ALU.add, ["st1", "cdec", "Ss", "Ssblk"], ["Ss"])
                        reserved.discard(bv)
                        P.dma("sp", rO, o_ret_s[l, gp], Ss[:, :, :], r=["Ss"], w=[("o_ret_s", l, gp)], is_out=True)
                    return bo

                def stageB(ci, bo):
                    cidx = bi * 4 + ci
                    par = cidx % 2
                    sl = slice(ci * 128, (ci + 1) * 128)
                    o_, ob_, h1, h2 = of32[par], ob[par], hn1[par], hn2[par]
                    ko, kob, k1, k2 = ("of32", par), ("ob", par), ("hn1", par), ("hn2", par)
                    cp("act", o_[:, :], PS(bo)[:, 0:128], [("ps", bo), ko], [ko])
                    cp("act", ob_[:, 0, :], PS(bo)[:, 0:128], [("ps", bo), kob], [kob])
                    act(ob_[:, 1, :], PS(bo)[:, 0:128], AF.Square, [("ps", bo), kob], [kob])
                    reserved.discard(bo)
                    bh = psum()
                    mm(PS(bh)[:, 0:256], hselb[:, :], ob_[:, :, :].rearrange("p a i -> p (a i)"), True, True, ["hselb", kob], [("ps", bh)])
                    cp("act", h2[:, :], PS(bh)[:, 0:128], [("ps", bh), k2], [k2])
                    tt("dve", h1[:, :], h2[:, :], h2[:, :], ALU.mult, [k2, k1], [k1])
                    tt("dve", h1[:, :], PS(bh)[:, 128:256], h1[:, :], ALU.subtract, [("ps", bh), k1], [k1])
                    act(h1[:, :], h1[:, :], AF.Ln, [k1, "epsT"], [k1], bias=epsT[:, 0:1])
                    act(h1[:, :], h1[:, :], AF.Exp, [k1], [k1], scale=-0.5)
                    tt("pool", o_[:, :], o_[:, :], h2[:, :], ALU.subtract, [ko, k2], [ko])
                    tt("pool", o_[:, :], o_[:, :], h1[:, :], ALU.mult, [ko, k1], [ko])
                    tt("dve", retT[:, sl], o_[:, :], gs[:, sl], ALU.mult, [ko, "gs", "retT"], ["retT"])

                prev = None
                for ci in range(nch):
                    bo = stageA(ci)
                    if prev is not None:
                        stageB(*prev)
                    prev = (ci, bo)
                stageB(*prev)
                for fo in range(8):
                    bk = psum()
                    mm(PS(bk)[:, 0:n], WO_[:, fo * 128:(fo + 1) * 128], retT[:, 0:n], True, True, [wk, "retT"], [("ps", bk)])
                    accum(bi, fo, bk, 0, acct)

        def swa_pass(l, WQ_, WO_, wk):
            ar = Arena()
            biasP = ar.bf16([128, 8, 256])
            qa = ar.bf16([128, 4, 512])
            qb = ar.bf16([128, 4, 512])
            skT = ar.bf16([128, NT])
            svt = ar.bf16([128, 17, 128])
            pP = [ar.bf16([128, 256]) for _ in range(8)]
            pT = [ar.bf16([128, 4, 2, 128]) for _ in range(2)]
            attn = ar.bf16([128, 512])
            swaT = ar.bf16([128, 4, 512])
            kcr = ar.bf16([128, 4, 128])
            kTc = ar.bf16([128, 8, 128])
            vc = ar.bf16([128, 8, 128])
            pS = [ar.bf16([32, 256]) for _ in range(4)]
            pTs = [ar.bf16([128, 2, 32]) for _ in range(4)]
            attS = ar.bf16([32, 4, 128])
            acct = [ar.f32([128, 128]), ar.f32([128, 512]), ar.f32([128, 512])]
            for cbi in range(8):
                c_, g_ = cbi // 2, cbi % 2
                h_ = g_ * 4 + c_
                src = AP(ext_d.tensor, h_ * 65536 + 127, [[511, 128], [1, 256]])
                P.dma("pool", rP, biasP[:, cbi, :], src, r=["ext_d"], w=["biasP"])
            ms("pool", qa[:, :, :], 0.0, [("qa", c) for c in range(4)])
            ms("pool", qb[:, :, :], 0.0, [("qb", c) for c in range(4)])
            for _k in range(4):
                ms("pool", pS[_k][:, :], 0.0, [("pS", _k)])
            def load_cache_half(hf):
                P.dma("pool", rP, vc[:, :, :], cache_v[l, hf * 8:(hf + 1) * 8].rearrange("s k e -> k s e"), w=["vc"])
                for q4 in range(2):
                    P.dma("pool", rP, kcr[:, :, :], cache_k[l, hf * 8 + q4 * 4:hf * 8 + (q4 + 1) * 4].rearrange("s k e -> k s e"), w=["kcr"])
                    bt = psum()
                    for j in range(4):
                        tr(PSb(bt)[:, j * 128:(j + 1) * 128], kcr[:, j, :], ["kcr"], [("ps", bt)])
                    cp("act", kTc[:, q4 * 4:q4 * 4 + 4, :].rearrange("p s k -> p (s k)"), PSb(bt)[:, 0:512],
                       [("ps", bt), "kTc"], ["kTc"])
            P.dma("sp", rO, o_k_s[l, :, 0:120, :], cache_k[l, :, 8:128, :], w=[("o_k_s", l, 0)], is_out=True)
            P.dma("sp", rO, o_v_s[l, :, 0:120, :], cache_v[l, :, 8:128, :], w=[("o_v_s", l, 0)], is_out=True)
            for bi, (n0, n) in enumerate(BLOCKS):
                drain(1)
                smp = bi == 4
                hk = [("HT", kc, bi) for kc in range(8)]
                for c in range(4):
                    bk = psum()
                    for kc in range(8):
                        mm(PS(bk)[:, 0:n], WQ_[:, kc, c * 128:(c + 1) * 128], HT[:, kc, n0:n0 + n], kc == 0, kc == 7, [wk, hk[kc]], [("ps", bk)])
                    act(qa[0:64, c, 0:n], PS(bk)[0:64, 0:n], AF.Identity, [("ps", bk), ("qa", c)], [("qa", c)], scale=0.125)
                    act(qb[64:128, c, 0:n], PS(bk)[64:128, 0:n], AF.Identity, [("ps", bk), ("qb", c)], [("qb", c)], scale=0.125)
                bk = psum()
                for kc in range(8):
                    mm(PS(bk)[:, 0:n], WQ_[:, kc, 512:640], HT[:, kc, n0:n0 + n], kc == 0, kc == 7, [wk, hk[kc]], [("ps", bk)])
                cp("act", skT[:, n0:n0 + n], PS(bk)[:, 0:n], [("ps", bk), ("skT", bi)], [("skT", bi)])
                if smp:
                    pa = pP[0].ap
                    qS = [attn.ap.rearrange("p (s c i) -> p s c i", s=16, c=4),
                          AP(pa.tensor, pa.offset, [list(pa.ap[0]), [1, 512]]).rearrange("p (s c i) -> p s c i", s=16, c=4)]
                    qSk = ["attn", ("pP", 0), ("pP", 1)]
                    cp("dve", qS[0], qa[:, :, 0:128].rearrange("p c (s i) -> p s c i", i=8), [("qa", c) for c in range(4)] + qSk, qSk)
                    cp("dve", qS[1], qb[:, :, 0:128].rearrange("p c (s i) -> p s c i", i=8), [("qb", c) for c in range(4)] + qSk, qSk)
                for ci in range(n // 128):
                    cidx = bi * 4 + ci
                    c0 = n0 + ci * 128
                    sl = slice(ci * 128, (ci + 1) * 128)
                    bv = psum()
                    for kc in range(8):
                        mm(PS(bv)[:, 0:128], HT[:, kc, c0:c0 + 128], WQ_[:, kc, 640:768], kc == 0, kc == 7, [wk, hk[kc]], [("ps", bv)])
                    cp("act", svt[:, cidx, :], PS(bv)[:, 0:128], [("ps", bv), ("svt", cidx)], [("svt", cidx)])
                    if cidx >= 15:
                        bk2 = psum()
                        for kc in range(8):
                            mm(PS(bk2)[:, 0:128], HT[:, kc, c0:c0 + 128], WQ_[:, kc, 512:640], kc == 0, kc == 7, [wk, hk[kc]], [("ps", bk2)])
                        cp("dve", kvf[:, 0, :], PS(bk2)[:, 0:128], [("ps", bk2), "kvf"], ["kvf"])
                        cp("dve", kvf[:, 1, :], PS(bv)[:, 0:128], [("ps", bv), "kvf"], ["kvf"])
                        if cidx == 15:
                            P.dma("sp", rO, o_k_p[l], kvf[:, 0, :], r=["kvf"], w=[("o_k_p", l)], is_out=True)
                            P.dma("sp", rO, o_v_p[l], kvf[:, 1, :], r=["kvf"], w=[("o_v_p", l)], is_out=True)
                        else:
                            for s_ in range(16):
                                P.dma("sp", rO, o_k_s[l, s_, 120:128, :], kvf[s_ * 8:(s_ + 1) * 8, 0, :], r=["kvf"], w=[("o_k_s", l, 1 + s_)], is_out=True)
                                P.dma("sp", rO, o_v_s[l, s_, 120:128, :], kvf[s_ * 8:(s_ + 1) * 8, 1, :], r=["kvf"], w=[("o_v_s", l, 1 + s_)], is_out=True)
                    if not smp:
                        ba = psum()
                        reserved.add(ba)
                        if cidx == 0:
                            k0, nk, boff = c0, 128, 128
                        else:
                            k0, nk, boff = c0 - 128, 256, 0
                        nj = nk // 128
                        lbk = []
                        for hp in range(4):
                            bl = psum()
                            reserved.add(bl)
                            lbk.append(bl)
                            for hh in range(2):
                                cbi = hp * 2 + hh
                                c_, g_ = cbi // 2, cbi % 2
                                qsrc = qa if g_ == 0 else qb
                                qk = ("qa", c_) if g_ == 0 else ("qb", c_)
                                mm(PS(bl)[:, hh * 256:hh * 256 + nk], qsrc[:, c_, sl], skT[:, k0:k0 + nk], True, False,
                                   [qk, ("skT", bi), ("skT", max(bi - 1, 0))], [("ps", bl)])
                                mm(PS(bl)[:, hh * 256:hh * 256 + nk], identb[:, :], biasP[:, cbi, boff:boff + nk], False, True,
                                   ["identb", "biasP"], [("ps", bl)])
                        for hp in range(4):
                            bl = lbk[hp]
                            P.op("dve", lambda e, bl=bl, nk=nk, hp=hp: e.reduce_max(
                                out=stat[:, 0, hp * 2:hp * 2 + 2], in_=PS(bl)[:, 0:512].rearrange("p (a k) -> p a k", a=2)[:, :, 0:nk], axis=AX.X),
                                r=[("ps", bl), "stat"], w=["stat"])
                        tt("dve", stat[:, 1, 0:8], stat[:, 0, 0:8], sinkp[:, l, 0:8], ALU.max, ["stat", "sinkp"], ["stat"])
                        ts("dve", stat[:, 1, 0:8], stat[:, 1, 0:8], -1.0, None, ALU.mult, None, ["stat"], ["stat"])
                        tt("dve", stat2[:, 0, 8:16], sinkp[:, l, 0:8], stat[:, 1, 0:8], ALU.add, ["stat", "sinkp", "stat2"], ["stat2"])
                        act(stat[:, 3, 0:8], stat2[:, 0, 8:16], AF.Exp, ["stat2", "stat"], ["stat"])
                        ms("dve", stat[:, 2, 0:8], 0.0, ["stat"])
                        for cbi in range(8):
                            bl = lbk[cbi // 2]
                            pk = ("pP", cbi)
                            act(pP[cbi][:, 0:nk], PS(bl)[:, (cbi % 2) * 256:(cbi % 2) * 256 + nk], AF.Exp, [("ps", bl), "stat", pk], [pk, "stat"],
                                bias=stat[:, 1, cbi:cbi + 1], accum=stat[:, 2, cbi:cbi + 1])
                        for bl in lbk:
                            reserved.discard(bl)
                        bts = [psum(), psum()]
                        for q_ in range(2):
                            reserved.add(bts[q_])
                        for cbi in range(8):
                            for j in range(nj):
                                tr(PSb(bts[cbi // 4])[:, ((cbi % 4) * 2 + j) * 128:((cbi % 4) * 2 + j + 1) * 128], pP[cbi][:, j * 128:(j + 1) * 128],
                                   [("pP", cbi)], [("ps", bts[cbi // 4])])
                        for q_ in range(2):
                            tk = ("pT", q_)
                            srcv = PSb(bts[q_])[:, 0:1024].rearrange("p (h j i) -> p h j i", h=4, j=2)[:, :, 0:nj, :]
                            dstv = pT[q_][:, :, 0:nj, :]
                            cp("act" if q_ == 0 else "dve", dstv, srcv, [("ps", bts[q_]), tk], [tk])
                            reserved.discard(bts[q_])
                        for cbi in range(8):
                            c_, g_ = cbi // 2, cbi % 2
                            col = c_ * 128 + g_ * 64
                            for j in range(nj):
                                kci = cidx - (nj - 1) + j
                                mm(PS(ba)[:, col:col + 64], pT[cbi // 4][:, cbi % 4, j, :], svt[:, kci, g_ * 64:(g_ + 1) * 64], j == 0, j == nj - 1,
                                   [("pT", cbi // 4), ("svt", kci)], [("ps", ba)])
                        tt("dve", stat2[:, 0, 0:8], stat[:, 2, 0:8], stat[:, 3, 0:8], ALU.add, ["stat", "stat2"], ["stat2"])
                        P.op("dve", lambda e: e.reciprocal(out=stat2[:, 1, 0:8], in_=stat2[:, 0, 0:8]), r=["stat2"], w=["stat2"])
                        tt("dve", attn[:, :].rearrange("p (h e) -> p h e", e=64), PS(ba)[:, 0:512].rearrange("p (h e) -> p h e", e=64),
                           _ins(stat2[:, 1, 0:8], 2, 64), ALU.mult, [("ps", ba), "stat2", "attn"], ["attn"])
                        reserved.discard(ba)
                        bt = psum()
                        for c in range(4):
                            tr(PSb(bt)[:, c * 128:(c + 1) * 128], attn[:, c * 128:(c + 1) * 128], ["attn"], [("ps", bt)])
                        cp("act", swaT[:, :, sl], PSb(bt)[:, 0:512].rearrange("p (c i) -> p c i", c=4), [("ps", bt)] + [("swaT", c) for c in range(4)],
                           [("swaT", c) for c in range(4)])
                    else:
                        for s4 in range(4):
                            if s4 % 2 == 0:
                                load_cache_half(s4 // 2)
                            ba = psum()
                            reserved.add(ba)
                            ms("dve", stat[:, 2:4, :], 0.0, ["stat"])
                            for sg in range(8):
                                s_ = s4 * 4 + sg // 2
                                g_ = sg % 2
                                qks = qSk
                                bl = psum()
                                lq = qS[g_][:, s_, :, :].rearrange("p c i -> p (c i)")
                                mm(PS(bl)[0:32, 0:128], lq, kTc[:, s_ % 8, :], True, False, qks + ["kTc"], [("ps", bl)])
                                mm(PS(bl)[0:32, 0:128], identb[0:32, 0:32], biasS[:, g_, 0:128], False, True, ["identb", "biasS"], [("ps", bl)])
                                mm(PS(bl)[0:32, 128:136], lq, skT[:, n0 + s_ * 8:n0 + (s_ + 1) * 8], True, False, qks + [("skT", bi)], [("ps", bl)])
                                mm(PS(bl)[0:32, 128:136], identb[0:32, 0:32], biasS[:, g_, 128:136], False, True, ["identb", "biasS"], [("ps", bl)])
                                col = sg
                                P.op("dve", lambda e, bl=bl, col=col: e.reduce_max(out=stat[0:32, 0, col:col + 1], in_=PS(bl)[0:32, 0:136], axis=AX.X),
                                     r=[("ps", bl), "stat"], w=["stat"])
                                ts("dve", stat[0:32, 1, col:col + 1], stat[0:32, 0, col:col + 1], sinks[:, l, g_:g_ + 1], -1.0, ALU.max, ALU.mult,
                                   ["stat", "sinks"], ["stat"])
                                pp = pS[sg % 4]
                                pk = ("pS", sg % 4)
                                ms("pool", pp[:, 128:256], 0.0, [pk])
                                act(pp[:, 0:128], PS(bl)[0:32, 0:128], AF.Exp, [("ps", bl), "stat", pk], [pk, "stat"],
                                    bias=stat[0:32, 1, col:col + 1], accum=stat[0:32, 2, col:col + 1])
                                act(pp[:, 128 + s_ * 8:128 + (s_ + 1) * 8], PS(bl)[0:32, 128:136], AF.Exp, [("ps", bl), "stat", pk], [pk, "stat"],
                                    bias=stat[0:32, 1, col:col + 1], accum=stat[0:32, 3, col:col + 1])
                                act(stat2[0:32, 0, col:col + 1], sinks[:, l, g_:g_ + 1], AF.Exp, ["sinks", "stat", "stat2"], ["stat2"],
                                    bias=stat[0:32, 1, col:col + 1])
                                bt = psum()
                                tr(PSb(bt)[:, 0:32], pp[:, 0:128], [pk], [("ps", bt)])
                                tr(PSb(bt)[:, 32:64], pp[:, 128:256], [pk], [("ps", bt)])
                                pt = pTs[sg % 4]
                                tk = ("pTs", sg % 4)
                                cp("act", pt[:, :, :].rearrange("p a i -> p (a i)"), PSb(bt)[:, 0:64], [("ps", bt), tk], [tk])
                                oc = (sg // 2) * 128 + g_ * 64
                                mm(PS(ba)[0:32, oc:oc + 64], pt[:, 0, :], vc[:, s_ % 8, g_ * 64:(g_ + 1) * 64], True, False, [tk, "vc"], [("ps", ba)])
                                mm(PS(ba)[0:32, oc:oc + 64], pt[:, 1, :], svt[:, 16, g_ * 64:(g_ + 1) * 64], False, True, [tk, ("svt", 16)], [("ps", ba)])
                            tt("dve", stat2[0:32, 1, 0:8], stat[0:32, 2, 0:8], stat[0:32, 3, 0:8], ALU.add, ["stat", "stat2"], ["stat2"])
                            tt("dve", stat2[0:32, 1, 0:8], stat2[0:32, 1, 0:8], stat2[0:32, 0, 0:8], ALU.add, ["stat2"], ["stat2"])
                            P.op("dve", lambda e: e.reciprocal(out=stat2[0:32, 1, 8:16], in_=stat2[0:32, 1, 0:8]), r=["stat2"], w=["stat2"])
                            tt("dve", attS[:, :, :].rearrange("p s (g e) -> p (s g) e", g=2), PS(ba)[0:32, 0:512].rearrange("p (h e) -> p h e", e=64),
                               _ins(stat2[0:32, 1, 8:16], 2, 64), ALU.mult, [("ps", ba), "stat2", "attS"], ["attS"])
                            reserved.discard(ba)
                            bt = psum()
                            for j in range(4):
                                tr(PSb(bt)[:, j * 32:(j + 1) * 32], attS[:, j, :], ["attS"], [("ps", bt)])
                            sw_ap = swaT.ap
                            dst = AP(sw_ap.tensor, sw_ap.offset + s4 * 32, [list(sw_ap.ap[0]), [512, 4], [8, 4], [1, 8]])
                            src = PSb(bt)[:, 0:128].rearrange("p (s c i) -> p c s i", s=4, c=4)
                            cp("act", dst, src, [("ps", bt)] + [("swaT", c) for c in range(4)], [("swaT", c) for c in range(4)])
                if DBG and l == 0 and bi == 3:
                    for c in range(4):
                        P.dma("pool", rO, yT[512 + c * 128:512 + (c + 1) * 128, 0:512], swaT[:, c, 0:512], r=[("swaT", c)], w=[("dbg", 4 + c)], is_out=True)
                if DBG and l == 0 and bi == 4:
                    for c in range(4):
                        P.dma("pool", rO, yT[512 + c * 128:512 + (c + 1) * 128, 512:640], swaT[:, c, 0:128], r=[("swaT", c)], w=[("dbgs", 4 + c)], is_out=True)
                for fo in range(8):
                    bk = psum()
                    for c in range(4):
                        mm(PS(bk)[:, 0:n], WO_[:, c, fo * 128:(fo + 1) * 128], swaT[:, c, 0:n], c == 0, c == 3,
                           [wk, ("swaT", c)], [("ps", bk)])
                    accum(bi, fo, bk, 0, acct)
            make_tables(True, 1, lnp[:, l, 0, :], lnp[:, l, 1, :], False)

        def ffn_pass(l, gi, last_layer, WU_, WD_, wk):
            ar = Arena()
            ubuf = [ar.f32([128, 514]) for _ in range(4)]
            ca = [ar.f32([128, 512]), ar.f32([128, 512])]
            cb = [ar.f32([128, 512]), ar.f32([128, 512])]
            gT = [ar.bf16([128, 3, 512]), ar.bf16([128, 3, 512])]
            halos = ar.f32([128, 6, 16, 2])
            scv = [ar.f32([32, 128]) for _ in range(6)]
            acct = [ar.f32([128, 128]), ar.f32([128, 512]), ar.f32([128, 512])]
            lb = LNBuf(ar)
            u0, nu = FGROUPS[gi]
            firstg = gi == 0
            lastg = gi == len(FGROUPS) - 1
            if firstg:
                ms("pool", halo[:], 0.0, ["halo"])
            for ab in range(2):
                for ui in range(nu):
                    j = ab * 22 + u0 + ui
                    P.dma("sp", rT, scv[ab * 3 + ui][:, :], state_conv[l, :, j * 128:(j + 1) * 128], w=[("scv", ab * 3 + ui)])

            def load_halos():
                for ab in range(2):
                    for ui in range(nu):
                        k = ab * 3 + ui
                        bt = psum()
                        P.op("pe", lambda e, bt=bt, k=k: e.transpose(PS(bt)[:, 0:32], scv[k][:, :], identF[0:32, 0:32]),
                             r=[("scv", k), "identF"], w=[("ps", bt)])
                        cp("act", halos[:, k, :, :].rearrange("p s r -> p (s r)"), PS(bt)[:, 0:32], [("ps", bt), "halos"], ["halos"])
            if lastg:
                if last_layer:
                    make_tables(False, 0, lnp[:, l, 2, :], lnp[:, l, 3, :], True)
                else:
                    make_tables(True, 0, lnp[:, l, 2, :], lnp[:, l, 3, :], False)

            def stage1(bi):
                n0, n = BLOCKS[bi]
                smp = bi == 4
                g_ = gT[bi % 2]
                if firstg:
                    ln_finish_block(bi, lb, True)
                hk = [("HT", kc, bi) for kc in range(8)]

                def front(ui):
                    u = u0 + ui
                    cres = []
                    for ab in range(2):
                        j = ab * 22 + u
                        bk = psum()
                        for kc in range(8):
                            mm(PS(bk)[:, 0:n], WU_[:, ui, kc, ab * 128:(ab + 1) * 128], HT[:, kc, n0:n0 + n], kc == 0, kc == 7, [wk, hk[kc]], [("ps", bk)])
                        ub = ubuf[(ui % 2) * 2 + ab]
                        uk = ("ubuf", (ui % 2) * 2 + ab)
                        cc = (ca if ab == 0 else cb)[ui % 2]
                        ck = ("c", ab, ui % 2)
                        w0 = convp[:, l, j, 0:1]
                        w1 = convp[:, l, j, 1:2]
                        w2 = convp[:, l, j, 2:3]
                        bb = convp[:, l, j, 3:4]
                        if not smp:
                            cp("act", ub[:, 2:2 + n], PS(bk)[:, 0:n], [("ps", bk), uk], [uk])
                            cp("pool", ub[:, 0:2], halo[:, j, :], ["halo", uk], [uk])
                            cp("pool", halo[:, j, :], ub[:, n:n + 2], [uk, "halo"], ["halo"])
                            ts("pool", cc[:, 0:n], ub[:, 2:2 + n], w2, bb, ALU.mult, ALU.add, [uk, "convp", ck], [ck])
                            stt("dve", cc[:, 0:n], ub[:, 1:1 + n], w1, cc[:, 0:n], ALU.mult, ALU.add, [uk, "convp", ck], [ck])
                            stt("dve", cc[:, 0:n], ub[:, 0:n], w0, cc[:, 0:n], ALU.mult, ALU.add, [uk, "convp", ck], [ck])
                        else:
                            u3 = ub[:, 0:160].rearrange("p (s t) -> p s t", t=10)
                            hs = halos[:, ab * 3 + ui, :, :]
                            cp("act", u3[:, :, 2:10], PS(bk)[:, 0:128].rearrange("p (s i) -> p s i", i=8), [("ps", bk), uk], [uk])
                            cp("pool", u3[:, :, 0:2], hs, ["halos", uk], [uk])
                            cp("pool", hs, u3[:, :, 8:10], [uk, "halos"], ["halos"])
                            P.dma("sp", rO, o_conv_s[l, :, j, :, :], hs, r=["halos"], w=[("o_conv_s", l, j)], is_out=True)
                            c3 = cc[:, 0:128].rearrange("p (s i) -> p s i", i=8)
                            ts("pool", c3, u3[:, :, 2:10], w2, bb, ALU.mult, ALU.add, [uk, "convp", ck], [ck])
                            stt("dve", c3, u3[:, :, 1:9], w1, c3, ALU.mult, ALU.add, [uk, "convp", ck], [ck])
                            stt("dve", c3, u3[:, :, 0:8], w0, c3, ALU.mult, ALU.add, [uk, "convp", ck], [ck])
                        cres.append((cc, ck))
                    return cres

                def back(ui, cres):
                    (cca, cka), (ccb, ckb) = cres
                    act(cca[:, 0:n], cca[:, 0:n], AF.Silu, [cka], [cka])
                    tt("dve", g_[:, ui, 0:n], cca[:, 0:n], ccb[:, 0:n], ALU.mult, [cka, ckb, ("gT", bi % 2, ui)], [("gT", bi % 2, ui)])

                prev = None
                for ui in range(nu):
                    cr = front(ui)
                    if prev is not None:
                        back(*prev)
                    prev = (ui, cr)
                back(*prev)

            def stage2(bi):
                drain(1)
                n0, n = BLOCKS[bi]
                g_ = gT[bi % 2]
                for fo in range(8):
                    bk = psum()
                    for ui in range(nu):
                        mm(PS(bk)[:, 0:n], WD_[:, ui, fo * 128:(fo + 1) * 128], g_[:, ui, 0:n], ui == 0, ui == nu - 1,
                           [wk, ("gT", bi % 2, ui)], [("ps", bk)])
                    accum(bi, fo, bk, 1, acct)
                if lastg:
                    ln_finish_block(bi, lb, not last_layer)
                    if last_layer and not DBG:
                        for fc in range(8):
                            P.dma("sp", rO, yT[fc * 128:(fc + 1) * 128, n0:n0 + n], X[:, fc, n0:n0 + n], r=[("X", fc, bi)], w=[("yT", fc, bi)], is_out=True)

            nb = len(BLOCKS)
            stage1(0)
            for bi in range(1, nb):
                if bi == nb - 1:
                    load_halos()
                stage1(bi)
                stage2(bi - 1)
            stage2(nb - 1)
            if lastg:
                P.dma("sp", rO, o_conv_p[l], halo[:], r=["halo"], w=[("o_conv_p", l)], is_out=True)

        passes = []
        for l in range(L):
            for gp in range(4):
                passes.append(("R", l, gp))
            passes.append(("S", l, 0))
            for gi in range(len(FGROUPS)):
                passes.append(("F", l, gi))

        compute_mod(0)
        make_tables(True, 0, None, None, False)
        make_gates()
        ar0 = Arena()
        xh1_0 = ar0.f32([128, 128])
        for bi, (n0b, nb) in enumerate(BLOCKS):
            for sub in range(nb // 128):
                n0 = n0b + sub * 128
                finish(bi, n0, 128, lambda fc, n0=n0: X[:, fc, n0:n0 + 128], [], True, xh1_0)
        views = {}
        views[0] = load_pass_weights(passes[0][0], passes[0][1], passes[0][2], 0)
        drain()
        marks = []
        for pi, (kind, l, idx) in enumerate(passes):
            marks.append((kind, l, idx, P.cnt["pe"]))
            if pi == 0 or passes[pi - 1][0] != kind or not os.environ.get('KNOBAR'):
                P.barrier()
            s = pi % 2
            if pi + 1 < len(passes):
                nk, nl, nidx = passes[pi + 1]
                views[pi + 1] = load_pass_weights(nk, nl, nidx, (pi + 1) % 2)
                drain(1)
            v1, v2 = views.pop(pi)
            wk = ("W", s)
            if kind == "R":
                ret_pass(l, idx, v1, v2, wk)
                drain()
            elif kind == "S":
                swa_pass(l, v1, v2, wk)
                drain()
                if l + 1 < L:
                    P.barrier()
                    compute_mod(l + 1)
                    make_gates_next = True
            else:
                if idx == 0 and l + 1 <= L:
                    pass
                ffn_pass(l, idx, l == L - 1, v1, v2, wk)
                drain()
                if idx == len(FGROUPS) - 1 and l + 1 < L:
                    make_gates()

        marks.append(("END", 0, 0, P.cnt["pe"]))
        if os.environ.get("KMARKS"):
            import json
            json.dump(marks, open(os.environ["KMARKS"], "w"))
        with nc.Block() as block:
            P.emit(block)
    return nc


def _consts():
    f32 = np.float32
    h = np.arange(8)
    gam = 1.0 - 2.0 ** (-5.0 - h.astype(np.float64))
    inv = (1.0 / (np.float32(10000.0) ** (np.arange(32, dtype=f32) / np.float32(32)))).astype(f32)
    rot = np.zeros((17, 128, 4, 128), f32)
    for c in range(17):
        if c < 16:
            pos = (c * 128 + np.arange(128)).astype(f32)
        else:
            pos = (PAST + (np.arange(128) % 8)).astype(f32)
        ang = pos[:, None] * inv[None, :]
        cs, sn = np.cos(ang).astype(f32), np.sin(ang).astype(f32)
        for p in range(128):
            d = p % 64
            f = d % 32
            sg = -1.0 if d < 32 else 1.0
            rot[c, p, 0] = cs[:, f]
            rot[c, p, 1] = sg * sn[:, f]
            rot[c, p, 2] = cs[:, f] * 0.125
            rot[c, p, 3] = sg * sn[:, f] * 0.125
    i = np.arange(128)
    decm = np.zeros((128, 8, 128), f32)
    for hh in range(8):
        diff = i[None, :] - i[:, None]
        decm[:, hh, :] = np.where(diff >= 0, gam[hh] ** np.maximum(diff, 0), 0.0)
    blkind = (i[:, None] // 8 == i[None, :] // 8).astype(f32)
    qdec = np.zeros((128, 4, 2, 128), f32)
    cdec = np.zeros((128, 2, 4, 3), f32)
    for p in range(128):
        hb = p // 64
        for pair in range(4):
            hh = pair * 2 + hb
            qdec[p, pair, 0] = gam[hh] ** (i + 1)
            qdec[p, pair, 1] = gam[hh] ** ((i % 8) + 1)
            for k, C in enumerate((128, 8)):
                dec = gam[hh] ** C
                cdec[p, k, pair] = (dec, dec if hb == 0 else 0.0, dec if hb == 1 else 0.0)
    cdec[:, :, :, 1] = (np.arange(128)[:, None, None] // 64 == 0)
    cdec[:, :, :, 2] = (np.arange(128)[:, None, None] // 64 == 1)
    kdec = np.zeros((128, 2, 8), f32)
    for hh in range(8):
        kdec[:, 0, hh] = gam[hh] ** (127 - i)
        kdec[:, 1, hh] = gam[hh] ** (7 - (i % 8))
    hmask = np.zeros((128, 2), f32)
    hmask[:64, 0] = 1
    hmask[64:, 1] = 1
    seqm = (i[:, None] // 8 == np.arange(16)[None, :]).astype(f32)
    ident = np.eye(128, dtype=f32)
    hsel = np.zeros((128, 128), f32)
    hsel[:64, :64] = 1.0 / 64
    hsel[64:, 64:] = 1.0 / 64
    dist = 127 - np.arange(128)
    n = np.maximum(dist, 0)
    nf = np.maximum(n, 16).astype(f32)
    large = 16 + (np.log(nf / np.float32(16)) / np.float32(np.log(128 / 16)) * np.float32(16)).astype(np.int32)
    large = np.minimum(large, 31)
    bucket = np.where(n < 16, n, large)
    ohr = (np.arange(32)[:, None] == bucket[None, :]).astype(f32)
    rot_b = np.zeros((5, 128, 4, 512), f32)
    for c in range(17):
        rot_b[c // 4, :, :, (c % 4) * 128:(c % 4 + 1) * 128] = rot[c]
    rot = rot_b
    qdec = np.ascontiguousarray(np.tile(qdec, (1, 1, 1, 4)))
    return dict(rot=rot, decm=decm, blkind=blkind, qdec=qdec, kdec=kdec, cdec=cdec, hmask=hmask,
                seqm=seqm, identf=ident, hsel=hsel, ohr=ohr)


_NC_CACHE = {}


def kernel(x_prompt, x_sample, c_prompt, c_sample, state_ret, cache_swa_k, cache_swa_v, state_conv,
           rel_bias, w_ada, b_ada, w_in, swa_sinks, w_out, ln1_g, ln1_b,
           w_up, conv_w, conv_b, w_down, ln2_g, ln2_b):
    f32 = np.float32
    A = lambda a: np.ascontiguousarray(np.asarray(a, dtype=f32))
    x_prompt, x_sample, c_prompt, c_sample = A(x_prompt), A(x_sample), A(c_prompt), A(c_sample)
    state_ret, cache_swa_k, cache_swa_v, state_conv = A(state_ret), A(cache_swa_k), A(cache_swa_v), A(state_conv)
    rel_bias, w_ada, b_ada, w_in, swa_sinks, w_out = A(rel_bias), A(w_ada), A(b_ada), A(w_in), A(swa_sinks), A(w_out)
    ln1_g, ln1_b, w_up, conv_w, conv_b, w_down, ln2_g, ln2_b = A(ln1_g), A(ln1_b), A(w_up), A(conv_w), A(conv_b), A(w_down), A(ln2_g), A(ln2_b)

    sw = np.concatenate([np.arange(32, 64), np.arange(0, 32)])
    w_r = np.zeros((L, 4, D, 768), f32)
    w_or = np.zeros((L, 4, 128, D), f32)
    for gp in range(4):
        hs = np.arange(gp * 2, gp * 2 + 2)
        cols = (hs[:, None] * 64 + np.arange(64)[None, :]).reshape(-1)
        cols_sw = (hs[:, None] * 64 + sw[None, :]).reshape(-1)
        w_r[:, gp, :, 0:128] = w_in[:, :, 0 + cols]
        w_r[:, gp, :, 128:256] = w_in[:, :, 0 + cols_sw]
        w_r[:, gp, :, 256:384] = w_in[:, :, 512 + cols]
        w_r[:, gp, :, 384:512] = w_in[:, :, 512 + cols_sw]
        w_r[:, gp, :, 512:640] = w_in[:, :, 1024 + cols]
        w_r[:, gp, :, 640:768] = w_in[:, :, 1536 + cols]
        w_or[:, gp] = w_out[:, cols, :]
    perm = np.concatenate([np.concatenate([np.arange(c * 64, c * 64 + 64), np.arange((4 + c) * 64, (4 + c) * 64 + 64)]) for c in range(4)])
    w_s = np.concatenate([w_in[:, :, 2048 + perm], w_in[:, :, 2560:2816]], axis=2)
    w_os = w_out[:, 512 + perm, :]
    w_r = np.ascontiguousarray(w_r.reshape(L, 4, 8, 128, 768).transpose(0, 1, 3, 2, 4))
    w_s = np.ascontiguousarray(w_s.reshape(L, 8, 128, 768).transpose(0, 2, 1, 3))
    w_os = np.ascontiguousarray(w_os.reshape(L, 4, 128, D).transpose(0, 2, 1, 3))
    wu = np.concatenate([w_up[:, :, :DFF].reshape(L, 8, 128, NU, 128), w_up[:, :, DFF:].reshape(L, 8, 128, NU, 128)], axis=4)
    w_up = np.ascontiguousarray(wu.transpose(0, 3, 2, 1, 4))
    w_down = np.ascontiguousarray(w_down.reshape(L, NU, 128, D).transpose(0, 2, 1, 3))
    bT_ada = np.ascontiguousarray(b_ada.reshape(L, 48, 128).transpose(2, 0, 1))
    convT = np.ascontiguousarray(np.concatenate([conv_w, conv_b[:, None, :]], axis=1).reshape(L, 4, 44, 128).transpose(3, 0, 2, 1))
    lnT = np.ascontiguousarray(np.stack([ln1_g, ln1_b, ln2_g, ln2_b], axis=1).reshape(L, 4, 8, 128).transpose(3, 0, 1, 2))
    cb_heads = np.array([(cbi % 2) * 4 + cbi // 2 for cbi in range(8)])
    sinkP = np.ascontiguousarray(np.broadcast_to(swa_sinks[:, cb_heads][None], (128, L, 8)))
    sinkS = np.zeros((32, L, 2), f32)
    for c in range(4):
        for g in range(2):
            sinkS[c * 8:(c + 1) * 8, :, g] = swa_sinks[None, :, g * 4 + c]
    cst = _consts()

    if "nc" not in _NC_CACHE:
        _NC_CACHE["nc"] = build_program()
    nc = _NC_CACHE["nc"]

    in_maps = []
    for b in range(8):
        xall = np.concatenate([x_prompt[b], x_sample[16 * b:16 * b + 16].reshape(128, D)], axis=0)
        call = np.concatenate([c_prompt[b:b + 1], c_sample[16 * b:16 * b + 16]], axis=0)
        m = dict(
            xT=np.ascontiguousarray(xall.T),
            cT=np.ascontiguousarray(call.T.reshape(8, 128, 17).transpose(1, 0, 2)),
            w_ada=w_ada, bT_ada=bT_ada, w_r=w_r, w_s=w_s, w_or=w_or, w_os=w_os, w_up=w_up, w_down=w_down,
            convT=convT, lnT=lnT, sinkP=sinkP, sinkS=sinkS, rel_bias=rel_bias,
            state_ret=np.ascontiguousarray(state_ret[:, 16 * b:16 * b + 16]),
            cache_k=np.ascontiguousarray(cache_swa_k[:, 16 * b:16 * b + 16].reshape(L, 16, 128, 128)),
            cache_v=np.ascontiguousarray(cache_swa_v[:, 16 * b:16 * b + 16].reshape(L, 16, 128, 128)),
            state_conv=np.ascontiguousarray(state_conv[:, 16 * b:16 * b + 16].reshape(L, 32, 2 * DFF)),
        )
        m.update(cst)
        in_maps.append(m)
    res = run_bass_kernel_spmd(nc, in_maps, core_ids=list(range(8)))
    R = res.results

    y_p = np.zeros((8, NP, D), f32)
    y_s = np.zeros((128, 8, D), f32)
    ret_p = np.zeros((L, 8, 8, 64, 64), f32)
    ret_s = np.zeros((L, 128, 8, 64, 64), f32)
    k_p = np.zeros((L, 8, 128, 2, 64), f32)
    v_p = np.zeros((L, 8, 128, 2, 64), f32)
    k_s = np.zeros((L, 128, 128, 2, 64), f32)
    v_s = np.zeros((L, 128, 128, 2, 64), f32)
    conv_p = np.zeros((L, 8, 2, 2 * DFF), f32)
    conv_s = np.zeros((L, 128, 2, 2 * DFF), f32)
    for b in range(8):
        r = R[b]
        yt = r["yT"].T
        y_p[b] = yt[:NP]
        y_s[16 * b:16 * b + 16] = yt[NP:].reshape(16, 8, D)
        ret_p[:, b] = r["o_ret_p"].reshape(L, 4, 2, 64, 64).reshape(L, 8, 64, 64)
        ret_s[:, 16 * b:16 * b + 16] = r["o_ret_s"].reshape(L, 4, 2, 64, 16, 64).transpose(0, 4, 1, 2, 3, 5).reshape(L, 16, 8, 64, 64)
        k_p[:, b] = r["o_k_p"].reshape(L, 128, 2, 64)
        v_p[:, b] = r["o_v_p"].reshape(L, 128, 2, 64)
        k_s[:, 16 * b:16 * b + 16] = r["o_k_s"].reshape(L, 16, 128, 2, 64)
        v_s[:, 16 * b:16 * b + 16] = r["o_v_s"].reshape(L, 16, 128, 2, 64)
        conv_p[:, b] = r["o_conv_p"].transpose(0, 3, 2, 1).reshape(L, 2, 2 * DFF)
        conv_s[:, 16 * b:16 * b + 16] = r["o_conv_s"].transpose(0, 3, 4, 2, 1).reshape(L, 16, 2, 2 * DFF)
    return (y_p, y_s, ret_p, k_p, v_p, conv_p, ret_s, k_s, v_s, conv_s)
```

```python
import numpy as np
import ml_dtypes
import concourse.bass as bass
import concourse.mybir as mybir
from concourse.ap import AP
from concourse.bass_utils import run_bass_kernel_spmd

F32 = mybir.dt.float32
BF16 = mybir.dt.bfloat16
ALU = mybir.AluOpType
AF = mybir.ActivationFunctionType
AX = mybir.AxisListType

D = 1024
L = 4
NT = 2176
NP = 2048
DFF = 2816
NU = 22
ALPHA = (2.0 * L) ** 0.25
EPS = 1e-5
PAST = 8192
BLOCKS = [(0, 512), (512, 512), (1024, 512), (1536, 512), (2048, 128)]
FGROUPS = [(0, 3), (3, 3), (6, 3), (9, 3), (12, 3), (15, 3), (18, 2), (20, 2)]
SLOT_ELEMS = 10240
ARENA_F32 = 12050
import os
DBG = int(os.environ.get('KDBG', '0'))


class Ring:
    def __init__(self, sems):
        self.sems = sems
        self.vals = [0] * len(sems)
        self.i = 0


class Prog:
    ENG = ("pe", "act", "dve", "pool", "sp")

    def __init__(self, nc, esem):
        self.nc = nc
        self.esem = esem
        self.q = {e: [] for e in self.ENG}
        self.cnt = {e: 0 for e in self.ENG}
        self.seen = {e: {} for e in self.ENG}
        self.res = {}
        self.out_toks = []
        self.recent = []

    def _need(self, eng, tok):
        sem, val, teng, kind = tok
        sid = id(sem)
        if self.seen[eng].get(sid, 0) >= val:
            return
        self.seen[eng][sid] = val
        self.q[eng].append(("wait", sem, val))

    def _deps(self, eng, r, w, is_dma):
        toks = []
        for k in r:
            st = self.res.get(k)
            if st and st["w"] is not None:
                toks.append(st["w"])
        for k in w:
            st = self.res.get(k)
            if not st:
                continue
            if st["w"] is not None:
                t = st["w"]
                if not (eng == "pe" and t[3] == "op" and t[2] == "pe"):
                    toks.append(t)
            for t in st["r"]:
                if not (eng == "pe" and t[3] == "op" and t[2] == "pe"):
                    toks.append(t)
        best = {}
        for t in toks:
            sid = id(t[0])
            if sid not in best or best[sid][1] < t[1]:
                best[sid] = t
        for t in best.values():
            self._need(eng, t)

    def _record(self, tok, r, w):
        for k in r:
            st = self.res.setdefault(k, {"w": None, "r": []})
            st["r"].append(tok)
        for k in w:
            self.res[k] = {"w": tok, "r": []}

    def op(self, eng, fn, r=(), w=()):
        self._deps(eng, r, w, False)
        self.cnt[eng] += 1
        tok = (self.esem[eng], self.cnt[eng], eng, "op")
        self.q[eng].append(("op", fn))
        self._record(tok, r, w)

    def dma(self, q, ring, out, in_, r=(), w=(), is_out=False):
        self._deps(q, r, w, True)
        s = ring.i
        ring.i = (ring.i + 1) % len(ring.sems)
        sem = ring.sems[s]
        if ring.vals[s] > 0:
            self._need(q, (sem, ring.vals[s], None, "dma"))
        ring.vals[s] += 16
        tok = (sem, ring.vals[s], None, "dma")
        self.q[q].append(("dma", out, in_, sem))
        self.recent.append(tok)
        self._record(tok, r, w)
        if is_out:
            self.out_toks.append(tok)

    def barrier(self):
        for e in self.ENG:
            for t in self.recent:
                self._need(e, t)
            for o in self.ENG:
                if o != e and self.cnt[o] > 0:
                    self._need(e, (self.esem[o], self.cnt[o], o, "op"))
        self.recent = []

    def emit(self, block):
        nc = self.nc
        engs = {"pe": nc.tensor, "act": nc.scalar, "dve": nc.vector, "pool": nc.gpsimd, "sp": nc.sync}
        for t in self.out_toks:
            self._need("sp", t)

        def run(name):
            e = engs[name]
            for it in self.q[name]:
                if it[0] == "wait":
                    e.wait_ge(it[1], it[2])
                elif it[0] == "op":
                    it[1](e).then_inc(self.esem[name], 1)
                else:
                    e.dma_start(out=it[1], in_=it[2]).then_inc(it[3], 16)

        @block.tensor
        def _(t):
            run("pe")

        @block.scalar
        def _(t):
            run("act")

        @block.vector
        def _(t):
            run("dve")

        @block.gpsimd
        def _(t):
            run("pool")

        @block.sync
        def _(t):
            run("sp")


def _ins(ap, pos, cnt):
    l = [list(x) for x in ap.ap]
    l.insert(pos, [0, cnt])
    return AP(ap.tensor, ap.offset, l)


def build_program():
    nc = bass.Bass("TRN2", target_bir_lowering=False)

    def din(name, shape, dt=F32):
        return nc.dram_tensor(name, list(shape), dt, kind="ExternalInput").ap()

    def dout(name, shape):
        return nc.dram_tensor(name, list(shape), F32, kind="ExternalOutput").ap()

    xT = din("xT", [D, NT])
    cT = din("cT", [128, 8, 17])
    w_ada = din("w_ada", [L, D, 6 * D])
    bT_ada = din("bT_ada", [128, L, 48])
    w_r = din("w_r", [L, 4, 128, 8, 768])
    w_s = din("w_s", [L, 128, 8, 768])
    w_or = din("w_or", [L, 4, 128, D])
    w_os = din("w_os", [L, 128, 4, D])
    w_up = din("w_up", [L, NU, 128, 8, 256])
    w_down = din("w_down", [L, 128, NU, D])
    convT = din("convT", [128, L, 44, 4])
    lnT = din("lnT", [128, L, 4, 8])
    sinkP = din("sinkP", [128, L, 8])
    sinkS = din("sinkS", [32, L, 2])
    rel_bias = din("rel_bias", [32, 8])
    ohr = din("ohr", [32, 128])
    rot = din("rot", [5, 128, 4, 512])
    decm = din("decm", [128, 8, 128])
    blkind = din("blkind", [128, 128])
    qdec = din("qdec", [128, 4, 2, 512])
    kdec = din("kdec", [128, 2, 8])
    cdec = din("cdec", [128, 2, 4, 3])
    hmask = din("hmask", [128, 2])
    seqm = din("seqm", [128, 16])
    identf = din("identf", [128, 128])
    hsel = din("hsel", [128, 128])
    state_ret = din("state_ret", [L, 16, 8, 64, 64])
    cache_k = din("cache_k", [L, 16, 128, 128])
    cache_v = din("cache_v", [L, 16, 128, 128])
    state_conv = din("state_conv", [L, 32, 2 * DFF])

    yT = dout("yT", [D, NT])
    o_ret_p = dout("o_ret_p", [L, 4, 128, 64])
    o_ret_s = dout("o_ret_s", [L, 4, 128, 16, 64])
    o_k_p = dout("o_k_p", [L, 128, 128])
    o_v_p = dout("o_v_p", [L, 128, 128])
    o_k_s = dout("o_k_s", [L, 16, 128, 128])
    o_v_s = dout("o_v_s", [L, 16, 128, 128])
    o_conv_p = dout("o_conv_p", [L, 128, 44, 2])
    o_conv_s = dout("o_conv_s", [L, 128, 44, 16, 2])
    ext_d = nc.dram_tensor("ext_d", [8, 128, 512], F32, kind="Internal").ap()

    from contextlib import ExitStack
    es = ExitStack()

    def sb(name, shape, dt=F32):
        return es.enter_context(nc.sbuf_tensor(name, list(shape), dt))

    with es:
        X = sb("X", [128, 8, NT])
        HT = sb("HT", [128, 8, NT], BF16)
        WS = [sb("WS0", [128, SLOT_ELEMS], BF16), sb("WS1", [128, SLOT_ELEMS], BF16)]
        ARF = sb("ARENA", [128, ARENA_F32])
        ARB = ARF.bitcast(BF16)
        modT = sb("modT", [128, 48, 17])
        scT = sb("scT", [128, 8, 17], BF16)
        c_sb = sb("c_sb", [128, 8, 17])
        bada = sb("bada", [128, L, 48])
        convp = sb("convp", [128, L, 44, 4])
        lnp = sb("lnp", [128, L, 4, 8])
        sinkp = sb("sinkp", [128, L, 8])
        sinks = sb("sinks", [32, L, 2])
        identb = sb("identb", [128, 128], BF16)
        identF = sb("identF", [128, 128])
        hselb = sb("hselb", [128, 128], BF16)
        onesb = sb("onesb", [128, 128], BF16)
        blk = sb("blk", [128, 128])
        kdecS = sb("kdecS", [128, 2, 8])
        cdecS = sb("cdecS", [128, 2, 4, 3])
        hmS = sb("hmS", [128, 2])
        seqS = sb("seqS", [128, 16])
        epsT = sb("epsT", [128, 1])
        biasS = sb("biasS", [32, 2, 136], BF16)
        tabA = sb("tabA", [128, 4, 8, 17])
        gate = sb("gate", [128, 2, 8, 17])
        Sp = sb("Sp", [128, 4, 64])
        Sblk = sb("Sblk", [128, 4, 2, 64], BF16)
        kvf = sb("kvf", [128, 2, 128])
        stat = sb("stat", [128, 4, 32])
        stat2 = sb("stat2", [128, 2, 32])
        halo = sb("halo", [128, 44, 2])

        PSB = [es.enter_context(nc.psum_tensor("ps%d" % i, [128, 512], F32)) for i in range(8)]

        def sem(name):
            return es.enter_context(nc.semaphore(name))

        esem = {e: sem("e_" + e) for e in Prog.ENG}
        P = Prog(nc, esem)
        rW = Ring([sem("rw%d" % i) for i in range(6)])
        rC = Ring([sem("rc%d" % i) for i in range(6)])
        rO = Ring([sem("ro%d" % i) for i in range(8)])
        rT = Ring([sem("rt%d" % i) for i in range(6)])
        rP = Ring([sem("rp%d" % i) for i in range(6)])

        psi = [0]

        reserved = set()

        def psum():
            while True:
                i = psi[0]
                psi[0] = (i + 1) % 8
                if i not in reserved:
                    return i

        def PS(i):
            return PSB[i]

        def PSb(i):
            return PSB[i].bitcast(BF16)

        class Arena:
            def __init__(self):
                self.off = 0

            def f32(self, shape):
                n = int(np.prod(shape[1:]))
                v = ARF[0:shape[0], self.off:self.off + n]
                self.off += n
                assert self.off <= ARENA_F32, ("arena overflow", self.off)
                return _View(v, shape)

            def bf16(self, shape):
                n = int(np.prod(shape[1:]))
                nw = (n + 1) // 2
                v = ARB[0:shape[0], 2 * self.off:2 * self.off + n]
                self.off += nw
                assert self.off <= ARENA_F32, ("arena overflow", self.off)
                return _View(v, shape)

        class _View:
            def __init__(self, ap, shape):
                self.ap = ap
                self.shape = list(shape)
                if len(shape) == 2:
                    self.v = ap
                else:
                    names = "abcdefg"[:len(shape) - 1]
                    pat = "p (" + " ".join(names) + ") -> p " + " ".join(names)
                    kw = {names[i]: shape[i + 1] for i in range(len(names))}
                    self.v = ap.rearrange(pat, **kw)

            def __getitem__(self, k):
                return self.v[k]

        def mm(out, lhsT, rhs, start, stop, r, w):
            P.op("pe", lambda e: e.matmul(out, lhsT=lhsT, rhs=rhs, start=start, stop=stop), r=r, w=w)

        def tr(out, in_, r, w):
            k = in_.shape[0]
            P.op("pe", lambda e: e.transpose(out, in_, identb[0:k, 0:k]), r=list(r) + ["identb"], w=w)

        def act(out, in_, func, r, w, bias=None, scale=None, accum=None):
            kw = {}
            if bias is not None:
                kw["bias"] = bias
            if scale is not None:
                kw["scale"] = scale
            if accum is not None:
                kw["accum_out"] = accum
            P.op("act", lambda e: e.activation(out=out, in_=in_, func=func, **kw), r=r, w=w)

        def tt(eng, out, in0, in1, op, r, w):
            P.op(eng, lambda e: e.tensor_tensor(out=out, in0=in0, in1=in1, op=op), r=r, w=w)

        def ts(eng, out, in0, s1, s2, op0, op1, r, w):
            if op1 is None:
                P.op(eng, lambda e: e.tensor_scalar(out=out, in0=in0, scalar1=s1, scalar2=None, op0=op0), r=r, w=w)
            else:
                P.op(eng, lambda e: e.tensor_scalar(out=out, in0=in0, scalar1=s1, scalar2=s2, op0=op0, op1=op1), r=r, w=w)

        def stt(eng, out, in0, scalar, in1, op0, op1, r, w):
            P.op(eng, lambda e: e.scalar_tensor_tensor(out=out, in0=in0, scalar=scalar, in1=in1, op0=op0, op1=op1), r=r, w=w)

        def cp(eng, out, in_, r, w):
            if eng == "act":
                P.op("act", lambda e: e.copy(out=out, in_=in_), r=r, w=w)
            else:
                P.op(eng, lambda e: e.tensor_copy(out=out, in_=in_), r=r, w=w)

        def ms(eng, ap, val, w):
            P.op(eng, lambda e: e.memset(ap, val), r=(), w=w)

        def ld(q, ring, out, in_, w, r=()):
            P.dma(q, ring, out, in_, r=r, w=w)

        arS = Arena()
        arS.off = 9000
        ext = arS.f32([8, 512])
        rb = arS.f32([32, 8])
        ohrS = arS.f32([32, 128])
        ld("sp", rC, c_sb[:], cT, ["c_sb"])
        ld("sp", rC, bada[:], bT_ada, ["bada"])
        ld("sp", rC, convp[:], convT, ["convp"])
        ld("sp", rC, lnp[:], lnT, ["lnp"])
        ld("sp", rC, sinkp[:], sinkP, ["sinkp"])
        ld("sp", rC, sinks[:], sinkS, ["sinks"])
        ld("sp", rC, blk[:], blkind, ["blk"])
        ld("sp", rC, kdecS[:], kdec, ["kdec"])
        ld("sp", rC, cdecS[:], cdec, ["cdec"])
        ld("sp", rC, hmS[:], hmask, ["hm"])
        ld("sp", rC, seqS[:], seqm, ["seqm"])
        ld("sp", rC, identF[:], identf, ["identF"])
        ld("sp", rC, rb[:, :], rel_bias, ["rb"])
        ld("sp", rC, ohrS[:, :], ohr, ["ohrS"])
        ld("pool", rP, identb[:], identf, ["identb"])
        ld("pool", rP, hselb[:], hsel, ["hselb"])
        ms("dve", epsT[:], EPS, ["epsT"])
        ms("dve", onesb[:], 1.0 / 1024.0, ["ones"])
        ms("dve", ext[:, :], -1e30, ["ext"])
        for fc in range(8):
            ld("sp", rC, X[:, fc, :], xT[fc * 128:(fc + 1) * 128, :], [("X", fc, b) for b in range(5)])

        b0 = psum()
        mm(PS(b0)[0:8, 0:128], rb[:, :], ohrS[:, :], True, True, ["rb", "ohrS"], [("ps", b0)])
        cp("dve", ext[:, 128:256], PS(b0)[0:8, 0:128], [("ps", b0), "ext"], ["ext"])
        P.dma("sp", rC, ext_d, _ins(ext[:, :], 1, 128), r=["ext"], w=["ext_d"])
        for cbi in range(8):
            c_, g_ = cbi // 2, cbi % 2
            h_ = g_ * 4 + c_
            src2 = AP(ext_d.tensor, h_ * 65536 + 127, [[511, 8], [1, 136]])
            P.dma("pool", rP, biasS[c_ * 8:(c_ + 1) * 8, g_, :], src2, r=["ext_d"], w=["biasS"])
        act(scT[:], c_sb[:], AF.Silu, ["c_sb"], ["scT"])

        deferred = []

        def drain(k=None):
            n_ = len(deferred) if k is None else min(k, len(deferred))
            for _ in range(n_):
                deferred.pop(0)()

        def load_pass_weights(kind, l, idx, s):
            W = WS[s]
            key = ("W", s)

            def q(out, in_):
                deferred.append(lambda: P.dma("pool", rW, out, in_, w=[key]))
            if kind == "R":
                v1 = W[:, 0:8 * 768].rearrange("p (k n) -> p k n", k=8)
                q(v1[:, 0:4, :], w_r[l, idx, :, 0:4, :])
                q(v1[:, 4:8, :], w_r[l, idx, :, 4:8, :])
                v2 = W[:, 6144:6144 + 1024]
                q(v2, w_or[l, idx])
                return v1, v2
            if kind == "S":
                v1 = W[:, 0:8 * 768].rearrange("p (k n) -> p k n", k=8)
                q(v1[:, 0:4, :], w_s[l, :, 0:4, :])
                q(v1[:, 4:8, :], w_s[l, :, 4:8, :])
                v2 = W[:, 6144:6144 + 4096].rearrange("p (k n) -> p k n", k=4)
                q(v2, w_os[l])
                return v1, v2
            u0, nu = FGROUPS[idx]
            vu = W[:, 0:nu * 2048].rearrange("p (u k n) -> p u k n", u=nu, k=8)
            vd = W[:, 6144:6144 + nu * 1024].rearrange("p (u n) -> p u n", u=nu)
            for ui in range(nu):
                q(vu[:, ui, :, :], w_up[l, u0 + ui])
            q(vd, w_down[l, :, u0:u0 + nu, :])
            return vu, vd

        def compute_mod(l):
            ar = Arena()
            WA = [ar.bf16([128, 8, 256]), ar.bf16([128, 8, 256])]
            mk = "modT"
            bks = [psum(), psum()]
            for jj in range(24):
                wa = WA[jj % 2]
                wk = ("WA", jj % 2)
                P.dma("pool", rW, wa[:], w_ada[l, :, jj * 256:(jj + 1) * 256].rearrange("(k p) n -> p k n", p=128), w=[wk])
                for sub in range(2):
                    j = jj * 2 + sub
                    bk = bks[j // 24]
                    for kc in range(8):
                        mm(PS(bk)[:, (j % 24) * 17:(j % 24) * 17 + 17], wa[:, kc, sub * 128:(sub + 1) * 128], scT[:, kc, :],
                           kc == 0, kc == 7, [wk, "scT"], [("ps", bk)])
            for hf in range(2):
                tt("dve", modT[:, hf * 24:(hf + 1) * 24, :],
                   PS(bks[hf])[:, 0:408].rearrange("p (j r) -> p j r", r=17),
                   _ins(bada[:, l, hf * 24:(hf + 1) * 24], 2, 17), ALU.add,
                   [("ps", bks[hf]), "bada"], [mk])

        def make_tables(use_mod, which_mod, g_ap, b_ap, last):
            mk = "modT"
            if use_mod:
                sh = modT[:, which_mod * 24 + 0:which_mod * 24 + 8, :]
                sc = modT[:, which_mod * 24 + 8:which_mod * 24 + 16, :]
                onep = tabA[:, 1, :, :]
                ts("pool", onep, sc, 1.0, None, ALU.add, None, [mk, "tabA"], ["tabA"])
                if g_ap is None:
                    cp("pool", tabA[:, 0, :, :], onep, ["tabA"], ["tabA"])
                    cp("pool", tabA[:, 1, :, :], sh, [mk, "tabA"], ["tabA"])
                else:
                    tt("pool", tabA[:, 0, :, :], onep, _ins(g_ap, 2, 17), ALU.mult, ["tabA", "lnp"], ["tabA"])
                    tt("pool", tabA[:, 1, :, :], onep, _ins(b_ap, 2, 17), ALU.mult, ["tabA", "lnp"], ["tabA"])
                    tt("pool", tabA[:, 1, :, :], tabA[:, 1, :, :], sh, ALU.add, ["tabA", mk], ["tabA"])
            a = 1.0 if last else ALPHA
            if g_ap is None:
                ms("pool", tabA[:, 2, :, 0:1], a, ["tabA"])
                ms("pool", tabA[:, 3, :, 0:1], 0.0, ["tabA"])
            else:
                ts("pool", tabA[:, 2, :, 0:1], _ins(g_ap, 2, 1), a, None, ALU.mult, None, ["lnp", "tabA"], ["tabA"])
                ts("pool", tabA[:, 3, :, 0:1], _ins(b_ap, 2, 1), a, None, ALU.mult, None, ["lnp", "tabA"], ["tabA"])

        def make_gates():
            cp("pool", gate[:, 0, :, :], modT[:, 16:24, :], ["modT", "gate"], ["gate"])
            cp("pool", gate[:, 1, :, :], modT[:, 40:48, :], ["modT", "gate"], ["gate"])

        def finish_fc(bi, n0, n, fc, src, src_keys, do_h, xh1):
            smp = bi == 4
            src_keys = list(src_keys) + [("X", fc, bi)]
            if do_h:
                if not smp:
                    act(HT[:, fc, n0:n0 + n], src, AF.Identity, list(src_keys) + ["tabA"], [("HT", fc, bi)],
                        bias=tabA[:, 1, fc, 0:1], scale=tabA[:, 0, fc, 0:1])
                else:
                    s3 = src.rearrange("p (s i) -> p s i", i=8)
                    x3 = xh1[:, 0:128].rearrange("p (s i) -> p s i", i=8)
                    tt("dve", x3, s3, _ins(tabA[:, 0, fc, 1:17], 2, 8), ALU.mult, list(src_keys) + ["tabA", "xh1"], ["xh1"])
                    tt("dve", HT[:, fc, n0:n0 + n].rearrange("p (s i) -> p s i", i=8), x3,
                       _ins(tabA[:, 1, fc, 1:17], 2, 8), ALU.add, ["xh1", "tabA"], [("HT", fc, bi)])
            ts("dve" if fc % 2 == 0 else "pool", X[:, fc, n0:n0 + n], src, tabA[:, 2, fc, 0:1], tabA[:, 3, fc, 0:1],
               ALU.mult, ALU.add, list(src_keys) + ["tabA", ("X", fc, bi)], [("X", fc, bi)])

        def finish(bi, n0, n, src_fn, src_keys, do_h, xh1):
            for fc in range(8):
                finish_fc(bi, n0, n, fc, src_fn(fc), src_keys, do_h, xh1)

        def accum(bi, fc, bk, gi, tmp):
            n0, n = BLOCKS[bi]
            if bi < 4 and fc % 2 == 1 and len(tmp) > 1:
                tb = tmp[1 + (fc // 2) % 2]
                tkk = ("acct2", (fc // 2) % 2)
                act(tb[:, 0:n], PS(bk)[:, 0:n], AF.Identity, [("ps", bk), "gate", tkk], [tkk], scale=gate[:, gi, fc, 0:1])
                tt("pool", X[:, fc, n0:n0 + n], X[:, fc, n0:n0 + n], tb[:, 0:n], ALU.add, [tkk, ("X", fc, bi)], [("X", fc, bi)])
            elif bi < 4:
                stt("dve", X[:, fc, n0:n0 + n], PS(bk)[:, 0:n], gate[:, gi, fc, 0:1], X[:, fc, n0:n0 + n], ALU.mult, ALU.add,
                    [("ps", bk), "gate", ("X", fc, bi)], [("X", fc, bi)])
            else:
                tt("dve", tmp[0][:, 0:128].rearrange("p (s i) -> p s i", i=8), PS(bk)[:, 0:128].rearrange("p (s i) -> p s i", i=8),
                   _ins(gate[:, gi, fc, 1:17], 2, 8), ALU.mult, [("ps", bk), "gate", "acctmp"], ["acctmp"])
                tt("pool", X[:, fc, n0:n0 + n], X[:, fc, n0:n0 + n], tmp[0][:, 0:128], ALU.add, ["acctmp", ("X", fc, bi)], [("X", fc, bi)])

        class LNBuf:
            def __init__(self, ar):
                self.vb = [ar.bf16([128, 512]), ar.bf16([128, 512])]
                self.vsq = [ar.bf16([128, 512]), ar.bf16([128, 512])]
                self.tmp = [ar.f32([128, 512]) for _ in range(3)]
                self.mean = ar.f32([128, 512])
                self.rstd = ar.f32([128, 512])
                self.xh1 = ar.f32([128, 128])

        def ln_finish_block(bi, lb, do_h):
            n0, n = BLOCKS[bi]
            bm, bq = psum(), psum()
            reserved.add(bm)
            reserved.add(bq)
            for fc in range(8):
                vbk, vsk = ("vb", fc % 2), ("vsq", fc % 2)
                cp("dve", lb.vb[fc % 2][:, 0:n], X[:, fc, n0:n0 + n], [("X", fc, bi), vbk], [vbk])
                act(lb.vsq[fc % 2][:, 0:n], X[:, fc, n0:n0 + n], AF.Square, [("X", fc, bi), vsk], [vsk])
                mm(PS(bm)[:, 0:n], onesb[:, :], lb.vb[fc % 2][:, 0:n], fc == 0, fc == 7, ["ones", vbk], [("ps", bm)])
                mm(PS(bq)[:, 0:n], onesb[:, :], lb.vsq[fc % 2][:, 0:n], fc == 0, fc == 7, ["ones", vsk], [("ps", bq)])
            cp("act", lb.mean[:, 0:n], PS(bm)[:, 0:n], [("ps", bm), "mean"], ["mean"])
            tt("dve", lb.rstd[:, 0:n], lb.mean[:, 0:n], lb.mean[:, 0:n], ALU.mult, ["mean", "rstd"], ["rstd"])
            tt("dve", lb.rstd[:, 0:n], PS(bq)[:, 0:n], lb.rstd[:, 0:n], ALU.subtract, [("ps", bq), "rstd"], ["rstd"])
            reserved.discard(bm)
            reserved.discard(bq)
            act(lb.rstd[:, 0:n], lb.rstd[:, 0:n], AF.Ln, ["rstd", "epsT"], ["rstd"], bias=epsT[:, 0:1])
            act(lb.rstd[:, 0:n], lb.rstd[:, 0:n], AF.Exp, ["rstd"], ["rstd"], scale=-0.5)
            for fc in range(8):
                tk = ("lntmp", fc % 3)
                t_ = lb.tmp[fc % 3]
                tt("pool", t_[:, 0:n], X[:, fc, n0:n0 + n], lb.mean[:, 0:n], ALU.subtract, [("X", fc, bi), "mean", tk], [tk])
                tt("dve", t_[:, 0:n], t_[:, 0:n], lb.rstd[:, 0:n], ALU.mult, [tk, "rstd"], [tk])
                finish_fc(bi, n0, n, fc, t_[:, 0:n], [tk], do_h, lb.xh1)

        def ret_pass(l, gp, WR_, WO_, wk):
            ar = Arena()
            rotS = ar.f32([128, 4, 512])
            decmS = ar.f32([128, 2, 128])
            qdecS = ar.f32([128, 2, 512])
            t1 = ar.f32([128, 512])
            t2 = ar.f32([128, 512])
            t3 = t1
            t4 = t2
            qp = ar.bf16([128, 512])
            qpp = ar.bf16([128, 512])
            kf = ar.bf16([128, 512])
            ka = ar.bf16([128, 512])
            kb = ar.bf16([128, 512])
            gs = ar.bf16([128, 512])
            retT = ar.bf16([128, 512])
            vpad = [ar.bf16([128, 2, 128]), ar.bf16([128, 2, 128])]
            sT = [ar.bf16([128, 2, 128]), ar.bf16([128, 2, 128])]
            ktok = [ar.bf16([128, 128]), ar.bf16([128, 128])]
            of32 = [ar.f32([128, 128]), ar.f32([128, 128])]
            ob = [ar.bf16([128, 2, 128]), ar.bf16([128, 2, 128])]
            hn1 = [ar.f32([128, 128]), ar.f32([128, 128])]
            hn2 = [ar.f32([128, 128]), ar.f32([128, 128])]
            Ss = ar.f32([128, 16, 64])
            Ssblk = ar.bf16([128, 16, 2, 64])
            st1 = ar.f32([128, 4, 64])
            vblk = ar.bf16([128, 4, 128])
            acct = [ar.f32([128, 128]), ar.f32([128, 512]), ar.f32([128, 512])]
            ld("sp", rT, decmS[:, :, :], decm[:, gp * 2:gp * 2 + 2, :], ["decm"])
            ld("sp", rT, qdecS[:, :, :], qdec[:, gp, :, :], ["qdec"])
            for hb in range(2):
                h = gp * 2 + hb
                P.dma("sp", rT, Ss[hb * 64:(hb + 1) * 64, :, :], state_ret[l, :, h, :, :].rearrange("s d e -> d s e"), w=["Ss"])
            ms("pool", ka[:, :], 0.0, ["ka"])
            ms("pool", kb[:, :], 0.0, ["kb"])
            ms("pool", vpad[0][:, :, :], 0.0, [("vpad", 0)])
            ms("pool", vpad[1][:, :, :], 0.0, [("vpad", 1)])
            ms("dve", Sp[:, gp, :], 0.0, [("Sp", gp)])
            ms("dve", Sblk[:, gp, :, :], 0.0, [("Sblk", gp)])
            tt("dve", Ssblk[:, :, :, :], _ins(Ss[:, :, :], 2, 2),
               _ins(_ins(hmS[:, :], 1, 16), 3, 64), ALU.mult, ["Ss", "hm", "Ssblk"], ["Ssblk"])
            for bi, (n0, n) in enumerate(BLOCKS):
                drain(1)
                smp = bi == 4
                di = 1 if smp else 0
                hk = [("HT", kc, bi) for kc in range(8)]
                nch = n // 128
                P.dma("sp", rT, rotS[:, :, :], rot[bi], w=["rot"])

                def proj(col0):
                    bk = psum()
                    for kc in range(8):
                        mm(PS(bk)[:, 0:n], WR_[:, kc, col0:col0 + 128], HT[:, kc, n0:n0 + n], kc == 0, kc == 7,
                           [wk, hk[kc]], [("ps", bk)])
                    return bk
                bq = proj(0)
                bqs = proj(128)
                tt("dve", t1[:, 0:n], PS(bq)[:, 0:n], rotS[:, 0, 0:n], ALU.mult, [("ps", bq), "rot", "t1"], ["t1"])
                tt("dve", t2[:, 0:n], PS(bqs)[:, 0:n], rotS[:, 1, 0:n], ALU.mult, [("ps", bqs), "rot", "t2"], ["t2"])
                tt("pool", t1[:, 0:n], t1[:, 0:n], t2[:, 0:n], ALU.add, ["t1", "t2"], ["t1"])
                cp("act", qp[:, 0:n], t1[:, 0:n], ["t1", "qp"], ["qp"])
                tt("pool", qpp[:, 0:n], t1[:, 0:n], qdecS[:, di, 0:n], ALU.mult, ["t1", "qdec", "qpp"], ["qpp"])
                bk_ = proj(256)
                bks_ = proj(384)
                tt("dve", t3[:, 0:n], PS(bk_)[:, 0:n], rotS[:, 2, 0:n], ALU.mult, [("ps", bk_), "rot", "t1"], ["t1"])
                tt("dve", t4[:, 0:n], PS(bks_)[:, 0:n], rotS[:, 3, 0:n], ALU.mult, [("ps", bks_), "rot", "t2"], ["t2"])
                tt("pool", kf[:, 0:n], t3[:, 0:n], t4[:, 0:n], ALU.add, ["t1", "t2", "kf"], ["kf"])
                cp("act", ka[0:64, 0:n], kf[0:64, 0:n], ["kf", "ka"], ["ka"])
                cp("act", kb[64:128, 0:n], kf[64:128, 0:n], ["kf", "kb"], ["kb"])
                bg = proj(640)
                act(gs[:, 0:n], PS(bg)[:, 0:n], AF.Silu, [("ps", bg), "gs"], ["gs"])

                def stageA(ci):
                    cidx = bi * 4 + ci
                    par = cidx % 2
                    c0 = n0 + ci * 128
                    sl = slice(ci * 128, (ci + 1) * 128)
                    bv = psum()
                    if smp:
                        reserved.add(bv)
                    for kc in range(8):
                        mm(PS(bv)[:, 0:128], HT[:, kc, c0:c0 + 128], WR_[:, kc, 512:640], kc == 0, kc == 7, [wk, hk[kc]], [("ps", bv)])
                    vp = vpad[par]
                    vk = ("vpad", par)
                    vo = AP(vp.ap.tensor, vp.ap.offset, [list(vp.ap.ap[0]), [192, 2], [1, 64]])
                    cp("act", vo, PS(bv)[:, 0:128].rearrange("p (b e) -> p b e", b=2), [("ps", bv), vk], [vk])
                    bs = psum()
                    mm(PS(bs)[:, 0:128], ka[:, sl], qp[:, sl], True, True, ["ka", "qp"], [("ps", bs)])
                    mm(PS(bs)[:, 128:256], kb[:, sl], qp[:, sl], True, True, ["kb", "qp"], [("ps", bs)])
                    st_ = sT[par]
                    sk_ = ("sT", par)
                    tt("dve", st_[:, :, :], PS(bs)[:, 0:256].rearrange("p (a i) -> p a i", a=2),
                       decmS[:, :, :], ALU.mult, [("ps", bs), "decm", sk_], [sk_])
                    if smp:
                        tt("pool", st_[:, :, :], st_[:, :, :], _ins(blk[:, :], 1, 2), ALU.mult, [sk_, "blk"], [sk_])
                    bo = psum()
                    reserved.add(bo)
                    mm(PS(bo)[:, 0:128], vp[:, 0, :], st_[:, 0, :], True, False, [vk, sk_], [("ps", bo)])
                    mm(PS(bo)[:, 0:128], vp[:, 1, :], st_[:, 1, :], False, False, [vk, sk_], [("ps", bo)])
                    if not smp:
                        mm(PS(bo)[:, 0:128], Sblk[:, gp, :, :].rearrange("p a e -> p (a e)"), qpp[:, sl], False, True,
                           [("Sblk", gp), "qpp"], [("ps", bo)])
                    else:
                        for s_ in range(16):
                            mm(PS(bo)[:, s_ * 8:(s_ + 1) * 8], Ssblk[:, s_, :, :].rearrange("p a e -> p (a e)"),
                               qpp[:, ci * 128 + s_ * 8:ci * 128 + (s_ + 1) * 8], False, s_ == 15,
                               ["Ssblk", "qpp"], [("ps", bo)])
                    bt = psum()
                    tr(PSb(bt)[:, 0:128], kf[:, sl], ["kf"], [("ps", bt)])
                    kt = ktok[par]
                    kk = ("ktok", par)
                    tt("dve", kt[:, :].rearrange("p (a d) -> p a d", a=2), PSb(bt)[:, 0:128].rearrange("p (a d) -> p a d", a=2),
                       _ins(kdecS[:, di, gp * 2:gp * 2 + 2], 2, 64), ALU.mult, [("ps", bt), "kdec", kk], [kk])
                    if not smp:
                        bd = psum()
                        mm(PS(bd)[:, 0:128], kt[:, :], vp[:, 0, :], True, False, [kk, vk], [("ps", bd)])
                        mm(PS(bd)[:, 0:128], kt[:, :], vp[:, 1, :], False, True, [kk, vk], [("ps", bd)])
                        ts("dve", st1[:, 0, :], PS(bd)[:, 0:64], cdecS[:, 0, gp, 1:2], None, ALU.mult, None,
                           [("ps", bd), "cdec", "st1"], ["st1"])
                        stt("dve", st1[:, 0, :], PS(bd)[:, 64:128], cdecS[:, 0, gp, 2:3], st1[:, 0, :], ALU.mult, ALU.add,
                            [("ps", bd), "cdec", "st1"], ["st1"])
                        stt("dve", Sp[:, gp, :], Sp[:, gp, :], cdecS[:, 0, gp, 0:1], st1[:, 0, :], ALU.mult, ALU.add,
                            ["st1", "cdec", ("Sp", gp)], [("Sp", gp)])
                        tt("pool", Sblk[:, gp, :, :], _ins(Sp[:, gp, :], 1, 2), _ins(hmS[:, :], 2, 64),
                           ALU.mult, [("Sp", gp), "hm", ("Sblk", gp)], [("Sblk", gp)])
                        if cidx == 15:
                            P.dma("sp", rO, o_ret_p[l, gp], Sp[:, gp, :], r=[("Sp", gp)], w=[("o_ret_p", l, gp)], is_out=True)
                    else:
                        for q4 in range(4):
                            tt("dve", vblk[:, :, :], _ins(PS(bv)[:, 0:128], 1, 4),
                               _ins(seqS[:, q4 * 4:(q4 + 1) * 4], 2, 128), ALU.mult, [("ps", bv), "seqm", "vblk"], ["vblk"])
                            bd = psum()
                            mm(PS(bd)[:, 0:512], kt[:, :], vblk[:, :, :].rearrange("p s e -> p (s e)"), True, True,
                               [kk, "vblk"], [("ps", bd)])
                            pv = PS(bd)[:, 0:512].rearrange("p (s a e) -> p s a e", s=4, a=2)
                            ts("dve", st1[:, :, :], pv[:, :, 0, :], cdecS[:, 1, gp, 1:2], None, ALU.mult, None,
                               [("ps", bd), "cdec", "st1"], ["st1"])
                            stt("dve", st1[:, :, :], pv[:, :, 1, :], cdecS[:, 1, gp, 2:3], st1[:, :, :],
                                ALU.mult, ALU.add, [("ps", bd), "cdec", "st1"], ["st1"])
                            stt("dve", Ss[:, q4 * 4:(q4 + 1) * 4, :], Ss[:, q4 * 4:(q4 + 1) * 4, :], cdecS[:, 1, gp, 0:1], st1[:, :, :],
                                ALU.mult, ALU.add, ["st1", "cdec", "Ss", "Ssblk"], ["Ss"])
                        reserved.discard(bv)
                        P.dma("sp", rO, o_ret_s[l, gp], Ss[:, :, :], r=["Ss"], w=[("o_ret_s", l, gp)], is_out=True)
                    return bo

                def stageB(ci, bo):
                    cidx = bi * 4 + ci
                    par = cidx % 2
                    sl = slice(ci * 128, (ci + 1) * 128)
                    o_, ob_, h1, h2 = of32[par], ob[par], hn1[par], hn2[par]
                    ko, kob, k1, k2 = ("of32", par), ("ob", par), ("hn1", par), ("hn2", par)
                    cp("act", o_[:, :], PS(bo)[:, 0:128], [("ps", bo), ko], [ko])
                    cp("act", ob_[:, 0, :], PS(bo)[:, 0:128], [("ps", bo), kob], [kob])
                    act(ob_[:, 1, :], PS(bo)[:, 0:128], AF.Square, [("ps", bo), kob], [kob])
                    reserved.discard(bo)
                    bh = psum()
                    mm(PS(bh)[:, 0:256], hselb[:, :], ob_[:, :, :].rearrange("p a i -> p (a i)"), True, True, ["hselb", kob], [("ps", bh)])
                    cp("act", h2[:, :], PS(bh)[:, 0:128], [("ps", bh), k2], [k2])
                    tt("dve", h1[:, :], h2[:, :], h2[:, :], ALU.mult, [k2, k1], [k1])
                    tt("dve", h1[:, :], PS(bh)[:, 128:256], h1[:, :], ALU.subtract, [("ps", bh), k1], [k1])
                    act(h1[:, :], h1[:, :], AF.Ln, [k1, "epsT"], [k1], bias=epsT[:, 0:1])
                    act(h1[:, :], h1[:, :], AF.Exp, [k1], [k1], scale=-0.5)
                    tt("pool", o_[:, :], o_[:, :], h2[:, :], ALU.subtract, [ko, k2], [ko])
                    tt("pool", o_[:, :], o_[:, :], h1[:, :], ALU.mult, [ko, k1], [ko])
                    tt("dve", retT[:, sl], o_[:, :], gs[:, sl], ALU.mult, [ko, "gs", "retT"], ["retT"])

                prev = None
                for ci in range(nch):
                    bo = stageA(ci)
                    if prev is not None:
                        stageB(*prev)
                    prev = (ci, bo)
                stageB(*prev)
                for fo in range(8):
                    bk = psum()
                    mm(PS(bk)[:, 0:n], WO_[:, fo * 128:(fo + 1) * 128], retT[:, 0:n], True, True, [wk, "retT"], [("ps", bk)])
                    accum(bi, fo, bk, 0, acct)

        def swa_pass(l, WQ_, WO_, wk):
            ar = Arena()
            biasP = ar.bf16([128, 8, 256])
            qa = ar.bf16([128, 4, 512])
            qb = ar.bf16([128, 4, 512])
            skT = ar.bf16([128, NT])
            svt = ar.bf16([128, 17, 128])
            pP = [ar.bf16([128, 256]) for _ in range(8)]
            pT = [ar.bf16([128, 4, 2, 128]) for _ in range(2)]
            attn = ar.bf16([128, 512])
            swaT = ar.bf16([128, 4, 512])
            kcr = ar.bf16([128, 4, 128])
            kTc = ar.bf16([128, 8, 128])
            vc = ar.bf16([128, 8, 128])
            pS = [ar.bf16([32, 256]) for _ in range(4)]
            pTs = [ar.bf16([128, 2, 32]) for _ in range(4)]
            attS = ar.bf16([32, 4, 128])
            acct = [ar.f32([128, 128]), ar.f32([128, 512]), ar.f32([128, 512])]
            for cbi in range(8):
                c_, g_ = cbi // 2, cbi % 2
                h_ = g_ * 4 + c_
                src = AP(ext_d.tensor, h_ * 65536 + 127, [[511, 128], [1, 256]])
                P.dma("pool", rP, biasP[:, cbi, :], src, r=["ext_d"], w=["biasP"])
            ms("pool", qa[:, :, :], 0.0, [("qa", c) for c in range(4)])
            ms("pool", qb[:, :, :], 0.0, [("qb", c) for c in range(4)])
            for _k in range(4):
                ms("pool", pS[_k][:, :], 0.0, [("pS", _k)])
            def load_cache_half(hf):
                P.dma("pool", rP, vc[:, :, :], cache_v[l, hf * 8:(hf + 1) * 8].rearrange("s k e -> k s e"), w=["vc"])
                for q4 in range(2):
                    P.dma("pool", rP, kcr[:, :, :], cache_k[l, hf * 8 + q4 * 4:hf * 8 + (q4 + 1) * 4].rearrange("s k e -> k s e"), w=["kcr"])
                    bt = psum()
                    for j in range(4):
                        tr(PSb(bt)[:, j * 128:(j + 1) * 128], kcr[:, j, :], ["kcr"], [("ps", bt)])
                    cp("act", kTc[:, q4 * 4:q4 * 4 + 4, :].rearrange("p s k -> p (s k)"), PSb(bt)[:, 0:512],
                       [("ps", bt), "kTc"], ["kTc"])
            P.dma("sp", rO, o_k_s[l, :, 0:120, :], cache_k[l, :, 8:128, :], w=[("o_k_s", l, 0)], is_out=True)
            P.dma("sp", rO, o_v_s[l, :, 0:120, :], cache_v[l, :, 8:128, :], w=[("o_v_s", l, 0)], is_out=True)
            for bi, (n0, n) in enumerate(BLOCKS):
                drain(1)
                smp = bi == 4
                hk = [("HT", kc, bi) for kc in range(8)]
                for c in range(4):
                    bk = psum()
                    for kc in range(8):
                        mm(PS(bk)[:, 0:n], WQ_[:, kc, c * 128:(c + 1) * 128], HT[:, kc, n0:n0 + n], kc == 0, kc == 7, [wk, hk[kc]], [("ps", bk)])
                    act(qa[0:64, c, 0:n], PS(bk)[0:64, 0:n], AF.Identity, [("ps", bk), ("qa", c)], [("qa", c)], scale=0.125)
                    act(qb[64:128, c, 0:n], PS(bk)[64:128, 0:n], AF.Identity, [("ps", bk), ("qb", c)], [("qb", c)], scale=0.125)
                bk = psum()
                for kc in range(8):
                    mm(PS(bk)[:, 0:n], WQ_[:, kc, 512:640], HT[:, kc, n0:n0 + n], kc == 0, kc == 7, [wk, hk[kc]], [("ps", bk)])
                cp("act", skT[:, n0:n0 + n], PS(bk)[:, 0:n], [("ps", bk), ("skT", bi)], [("skT", bi)])
                if smp:
                    pa = pP[0].ap
                    qS = [attn.ap.rearrange("p (s c i) -> p s c i", s=16, c=4),
                          AP(pa.tensor, pa.offset, [list(pa.ap[0]), [1, 512]]).rearrange("p (s c i) -> p s c i", s=16, c=4)]
                    qSk = ["attn", ("pP", 0), ("pP", 1)]
                    cp("dve", qS[0], qa[:, :, 0:128].rearrange("p c (s i) -> p s c i", i=8), [("qa", c) for c in range(4)] + qSk, qSk)
                    cp("dve", qS[1], qb[:, :, 0:128].rearrange("p c (s i) -> p s c i", i=8), [("qb", c) for c in range(4)] + qSk, qSk)
                for ci in range(n // 128):
                    cidx = bi * 4 + ci
                    c0 = n0 + ci * 128
                    sl = slice(ci * 128, (ci + 1) * 128)
                    bv = psum()
                    for kc in range(8):
                        mm(PS(bv)[:, 0:128], HT[:, kc, c0:c0 + 128], WQ_[:, kc, 640:768], kc == 0, kc == 7, [wk, hk[kc]], [("ps", bv)])
                    cp("act", svt[:, cidx, :], PS(bv)[:, 0:128], [("ps", bv), ("svt", cidx)], [("svt", cidx)])
                    if cidx >= 15:
                        bk2 = psum()
                        for kc in range(8):
                            mm(PS(bk2)[:, 0:128], HT[:, kc, c0:c0 + 128], WQ_[:, kc, 512:640], kc == 0, kc == 7, [wk, hk[kc]], [("ps", bk2)])
                        cp("dve", kvf[:, 0, :], PS(bk2)[:, 0:128], [("ps", bk2), "kvf"], ["kvf"])
                        cp("dve", kvf[:, 1, :], PS(bv)[:, 0:128], [("ps", bv), "kvf"], ["kvf"])
                        if cidx == 15:
                            P.dma("sp", rO, o_k_p[l], kvf[:, 0, :], r=["kvf"], w=[("o_k_p", l)], is_out=True)
                            P.dma("sp", rO, o_v_p[l], kvf[:, 1, :], r=["kvf"], w=[("o_v_p", l)], is_out=True)
                        else:
                            for s_ in range(16):
                                P.dma("sp", rO, o_k_s[l, s_, 120:128, :], kvf[s_ * 8:(s_ + 1) * 8, 0, :], r=["kvf"], w=[("o_k_s", l, 1 + s_)], is_out=True)
                                P.dma("sp", rO, o_v_s[l, s_, 120:128, :], kvf[s_ * 8:(s_ + 1) * 8, 1, :], r=["kvf"], w=[("o_v_s", l, 1 + s_)], is_out=True)
                    if not smp:
                        ba = psum()
                        reserved.add(ba)
                        if cidx == 0:
                            k0, nk, boff = c0, 128, 128
                        else:
                            k0, nk, boff = c0 - 128, 256, 0
                        nj = nk // 128
                        lbk = []
                        for hp in range(4):
                            bl = psum()
                            reserved.add(bl)
                            lbk.append(bl)
                            for hh in range(2):
                                cbi = hp * 2 + hh
                                c_, g_ = cbi // 2, cbi % 2
                                qsrc = qa if g_ == 0 else qb
                                qk = ("qa", c_) if g_ == 0 else ("qb", c_)
                                mm(PS(bl)[:, hh * 256:hh * 256 + nk], qsrc[:, c_, sl], skT[:, k0:k0 + nk], True, False,
                                   [qk, ("skT", bi), ("skT", max(bi - 1, 0))], [("ps", bl)])
                                mm(PS(bl)[:, hh * 256:hh * 256 + nk], identb[:, :], biasP[:, cbi, boff:boff + nk], False, True,
                                   ["identb", "biasP"], [("ps", bl)])
                        for hp in range(4):
                            bl = lbk[hp]
                            P.op("dve", lambda e, bl=bl, nk=nk, hp=hp: e.reduce_max(
                                out=stat[:, 0, hp * 2:hp * 2 + 2], in_=PS(bl)[:, 0:512].rearrange("p (a k) -> p a k", a=2)[:, :, 0:nk], axis=AX.X),
                                r=[("ps", bl), "stat"], w=["stat"])
                        tt("dve", stat[:, 1, 0:8], stat[:, 0, 0:8], sinkp[:, l, 0:8], ALU.max, ["stat", "sinkp"], ["stat"])
                        ts("dve", stat[:, 1, 0:8], stat[:, 1, 0:8], -1.0, None, ALU.mult, None, ["stat"], ["stat"])
                        tt("dve", stat2[:, 0, 8:16], sinkp[:, l, 0:8], stat[:, 1, 0:8], ALU.add, ["stat", "sinkp", "stat2"], ["stat2"])
                        act(stat[:, 3, 0:8], stat2[:, 0, 8:16], AF.Exp, ["stat2", "stat"], ["stat"])
                        ms("dve", stat[:, 2, 0:8], 0.0, ["stat"])
                        for cbi in range(8):
                            bl = lbk[cbi // 2]
                            pk = ("pP", cbi)
                            act(pP[cbi][:, 0:nk], PS(bl)[:, (cbi % 2) * 256:(cbi % 2) * 256 + nk], AF.Exp, [("ps", bl), "stat", pk], [pk, "stat"],
                                bias=stat[:, 1, cbi:cbi + 1], accum=stat[:, 2, cbi:cbi + 1])
                        for bl in lbk:
                            reserved.discard(bl)
                        bts = [psum(), psum()]
                        for q_ in range(2):
                            reserved.add(bts[q_])
                        for cbi in range(8):
                            for j in range(nj):
                                tr(PSb(bts[cbi // 4])[:, ((cbi % 4) * 2 + j) * 128:((cbi % 4) * 2 + j + 1) * 128], pP[cbi][:, j * 128:(j + 1) * 128],
                                   [("pP", cbi)], [("ps", bts[cbi // 4])])
                        for q_ in range(2):
                            tk = ("pT", q_)
                            srcv = PSb(bts[q_])[:, 0:1024].rearrange("p (h j i) -> p h j i", h=4, j=2)[:, :, 0:nj, :]
                            dstv = pT[q_][:, :, 0:nj, :]
                            cp("act" if q_ == 0 else "dve", dstv, srcv, [("ps", bts[q_]), tk], [tk])
                            reserved.discard(bts[q_])
                        for cbi in range(8):
                            c_, g_ = cbi // 2, cbi % 2
                            col = c_ * 128 + g_ * 64
                            for j in range(nj):
                                kci = cidx - (nj - 1) + j
                                mm(PS(ba)[:, col:col + 64], pT[cbi // 4][:, cbi % 4, j, :], svt[:, kci, g_ * 64:(g_ + 1) * 64], j == 0, j == nj - 1,
                                   [("pT", cbi // 4), ("svt", kci)], [("ps", ba)])
                        tt("dve", stat2[:, 0, 0:8], stat[:, 2, 0:8], stat[:, 3, 0:8], ALU.add, ["stat", "stat2"], ["stat2"])
                        P.op("dve", lambda e: e.reciprocal(out=stat2[:, 1, 0:8], in_=stat2[:, 0, 0:8]), r=["stat2"], w=["stat2"])
                        tt("dve", attn[:, :].rearrange("p (h e) -> p h e", e=64), PS(ba)[:, 0:512].rearrange("p (h e) -> p h e", e=64),
                           _ins(stat2[:, 1, 0:8], 2, 64), ALU.mult, [("ps", ba), "stat2", "attn"], ["attn"])
                        reserved.discard(ba)
                        bt = psum()
                        for c in range(4):
                            tr(PSb(bt)[:, c * 128:(c + 1) * 128], attn[:, c * 128:(c + 1) * 128], ["attn"], [("ps", bt)])
                        cp("act", swaT[:, :, sl], PSb(bt)[:, 0:512].rearrange("p (c i) -> p c i", c=4), [("ps", bt)] + [("swaT", c) for c in range(4)],
                           [("swaT", c) for c in range(4)])
                    else:
                        for s4 in range(4):
                            if s4 % 2 == 0:
                                load_cache_half(s4 // 2)
                            ba = psum()
                            reserved.add(ba)
                            ms("dve", stat[:, 2:4, :], 0.0, ["stat"])
                            for sg in range(8):
                                s_ = s4 * 4 + sg // 2
                                g_ = sg % 2
                                qks = qSk
                                bl = psum()
                                lq = qS[g_][:, s_, :, :].rearrange("p c i -> p (c i)")
                                mm(PS(bl)[0:32, 0:128], lq, kTc[:, s_ % 8, :], True, False, qks + ["kTc"], [("ps", bl)])
                                mm(PS(bl)[0:32, 0:128], identb[0:32, 0:32], biasS[:, g_, 0:128], False, True, ["identb", "biasS"], [("ps", bl)])
                                mm(PS(bl)[0:32, 128:136], lq, skT[:, n0 + s_ * 8:n0 + (s_ + 1) * 8], True, False, qks + [("skT", bi)], [("ps", bl)])
                                mm(PS(bl)[0:32, 128:136], identb[0:32, 0:32], biasS[:, g_, 128:136], False, True, ["identb", "biasS"], [("ps", bl)])
                                col = sg
                                P.op("dve", lambda e, bl=bl, col=col: e.reduce_max(out=stat[0:32, 0, col:col + 1], in_=PS(bl)[0:32, 0:136], axis=AX.X),
                                     r=[("ps", bl), "stat"], w=["stat"])
                                ts("dve", stat[0:32, 1, col:col + 1], stat[0:32, 0, col:col + 1], sinks[:, l, g_:g_ + 1], -1.0, ALU.max, ALU.mult,
                                   ["stat", "sinks"], ["stat"])
                                pp = pS[sg % 4]
                                pk = ("pS", sg % 4)
                                ms("pool", pp[:, 128:256], 0.0, [pk])
                                act(pp[:, 0:128], PS(bl)[0:32, 0:128], AF.Exp, [("ps", bl), "stat", pk], [pk, "stat"],
                                    bias=stat[0:32, 1, col:col + 1], accum=stat[0:32, 2, col:col + 1])
                                act(pp[:, 128 + s_ * 8:128 + (s_ + 1) * 8], PS(bl)[0:32, 128:136], AF.Exp, [("ps", bl), "stat", pk], [pk, "stat"],
                                    bias=stat[0:32, 1, col:col + 1], accum=stat[0:32, 3, col:col + 1])
                                act(stat2[0:32, 0, col:col + 1], sinks[:, l, g_:g_ + 1], AF.Exp, ["sinks", "stat", "stat2"], ["stat2"],
                                    bias=stat[0:32, 1, col:col + 1])
                                bt = psum()
                                tr(PSb(bt)[:, 0:32], pp[:, 0:128], [pk], [("ps", bt)])
                                tr(PSb(bt)[:, 32:64], pp[:, 128:256], [pk], [("ps", bt)])
                                pt = pTs[sg % 4]
                                tk = ("pTs", sg % 4)
                                cp("act", pt[:, :, :].rearrange("p a i -> p (a i)"), PSb(bt)[:, 0:64], [("ps", bt), tk], [tk])
                                oc = (sg // 2) * 128 + g_ * 64
                                mm(PS(ba)[0:32, oc:oc + 64], pt[:, 0, :], vc[:, s_ % 8, g_ * 64:(g_ + 1) * 64], True, False, [tk, "vc"], [("ps", ba)])
                                mm(PS(ba)[0:32, oc:oc + 64], pt[:, 1, :], svt[:, 16, g_ * 64:(g_ + 1) * 64], False, True, [tk, ("svt", 16)], [("ps", ba)])
                            tt("dve", stat2[0:32, 1, 0:8], stat[0:32, 2, 0:8], stat[0:32, 3, 0:8], ALU.add, ["stat", "stat2"], ["stat2"])
                            tt("dve", stat2[0:32, 1, 0:8], stat2[0:32, 1, 0:8], stat2[0:32, 0, 0:8], ALU.add, ["stat2"], ["stat2"])
                            P.op("dve", lambda e: e.reciprocal(out=stat2[0:32, 1, 8:16], in_=stat2[0:32, 1, 0:8]), r=["stat2"], w=["stat2"])
                            tt("dve", attS[:, :, :].rearrange("p s (g e) -> p (s g) e", g=2), PS(ba)[0:32, 0:512].rearrange("p (h e) -> p h e", e=64),
                               _ins(stat2[0:32, 1, 8:16], 2, 64), ALU.mult, [("ps", ba), "stat2", "attS"], ["attS"])
                            reserved.discard(ba)
                            bt = psum()
                            for j in range(4):
                                tr(PSb(bt)[:, j * 32:(j + 1) * 32], attS[:, j, :], ["attS"], [("ps", bt)])
                            sw_ap = swaT.ap
                            dst = AP(sw_ap.tensor, sw_ap.offset + s4 * 32, [list(sw_ap.ap[0]), [512, 4], [8, 4], [1, 8]])
                            src = PSb(bt)[:, 0:128].rearrange("p (s c i) -> p c s i", s=4, c=4)
                            cp("act", dst, src, [("ps", bt)] + [("swaT", c) for c in range(4)], [("swaT", c) for c in range(4)])
                if DBG and l == 0 and bi == 3:
                    for c in range(4):
                        P.dma("pool", rO, yT[512 + c * 128:512 + (c + 1) * 128, 0:512], swaT[:, c, 0:512], r=[("swaT", c)], w=[("dbg", 4 + c)], is_out=True)
                if DBG and l == 0 and bi == 4:
                    for c in range(4):
                        P.dma("pool", rO, yT[512 + c * 128:512 + (c + 1) * 128, 512:640], swaT[:, c, 0:128], r=[("swaT", c)], w=[("dbgs", 4 + c)], is_out=True)
                for fo in range(8):
                    bk = psum()
                    for c in range(4):
                        mm(PS(bk)[:, 0:n], WO_[:, c, fo * 128:(fo + 1) * 128], swaT[:, c, 0:n], c == 0, c == 3,
                           [wk, ("swaT", c)], [("ps", bk)])
                    accum(bi, fo, bk, 0, acct)
            make_tables(True, 1, lnp[:, l, 0, :], lnp[:, l, 1, :], False)

        def ffn_pass(l, gi, last_layer, WU_, WD_, wk):
            ar = Arena()
            ubuf = [ar.f32([128, 514]) for _ in range(4)]
            ca = [ar.f32([128, 512]), ar.f32([128, 512])]
            cb = [ar.f32([128, 512]), ar.f32([128, 512])]
            gT = [ar.bf16([128, 3, 512]), ar.bf16([128, 3, 512])]
            halos = ar.f32([128, 6, 16, 2])
            scv = [ar.f32([32, 128]) for _ in range(6)]
            acct = [ar.f32([128, 128]), ar.f32([128, 512]), ar.f32([128, 512])]
            lb = LNBuf(ar)
            u0, nu = FGROUPS[gi]
            firstg = gi == 0
            lastg = gi == len(FGROUPS) - 1
            if firstg:
                ms("pool", halo[:], 0.0, ["halo"])
            for ab in range(2):
                for ui in range(nu):
                    j = ab * 22 + u0 + ui
                    P.dma("sp", rT, scv[ab * 3 + ui][:, :], state_conv[l, :, j * 128:(j + 1) * 128], w=[("scv", ab * 3 + ui)])

            def load_halos():
                for ab in range(2):
                    for ui in range(nu):
                        k = ab * 3 + ui
                        bt = psum()
                        P.op("pe", lambda e, bt=bt, k=k: e.transpose(PS(bt)[:, 0:32], scv[k][:, :], identF[0:32, 0:32]),
                             r=[("scv", k), "identF"], w=[("ps", bt)])
                        cp("act", halos[:, k, :, :].rearrange("p s r -> p (s r)"), PS(bt)[:, 0:32], [("ps", bt), "halos"], ["halos"])
            if lastg:
                if last_layer:
                    make_tables(False, 0, lnp[:, l, 2, :], lnp[:, l, 3, :], True)
                else:
                    make_tables(True, 0, lnp[:, l, 2, :], lnp[:, l, 3, :], False)

            def stage1(bi):
                n0, n = BLOCKS[bi]
                smp = bi == 4
                g_ = gT[bi % 2]
                if firstg:
                    ln_finish_block(bi, lb, True)
                hk = [("HT", kc, bi) for kc in range(8)]

                def front(ui):
                    u = u0 + ui
                    cres = []
                    for ab in range(2):
                        j = ab * 22 + u
                        bk = psum()
                        for kc in range(8):
                            mm(PS(bk)[:, 0:n], WU_[:, ui, kc, ab * 128:(ab + 1) * 128], HT[:, kc, n0:n0 + n], kc == 0, kc == 7, [wk, hk[kc]], [("ps", bk)])
                        ub = ubuf[(ui % 2) * 2 + ab]
                        uk = ("ubuf", (ui % 2) * 2 + ab)
                        cc = (ca if ab == 0 else cb)[ui % 2]
                        ck = ("c", ab, ui % 2)
                        w0 = convp[:, l, j, 0:1]
                        w1 = convp[:, l, j, 1:2]
                        w2 = convp[:, l, j, 2:3]
                        bb = convp[:, l, j, 3:4]
                        if not smp:
                            cp("act", ub[:, 2:2 + n], PS(bk)[:, 0:n], [("ps", bk), uk], [uk])
                            cp("pool", ub[:, 0:2], halo[:, j, :], ["halo", uk], [uk])
                            cp("pool", halo[:, j, :], ub[:, n:n + 2], [uk, "halo"], ["halo"])
                            ts("pool", cc[:, 0:n], ub[:, 2:2 + n], w2, bb, ALU.mult, ALU.add, [uk, "convp", ck], [ck])
                            stt("dve", cc[:, 0:n], ub[:, 1:1 + n], w1, cc[:, 0:n], ALU.mult, ALU.add, [uk, "convp", ck], [ck])
                            stt("dve", cc[:, 0:n], ub[:, 0:n], w0, cc[:, 0:n], ALU.mult, ALU.add, [uk, "convp", ck], [ck])
                        else:
                            u3 = ub[:, 0:160].rearrange("p (s t) -> p s t", t=10)
                            hs = halos[:, ab * 3 + ui, :, :]
                            cp("act", u3[:, :, 2:10], PS(bk)[:, 0:128].rearrange("p (s i) -> p s i", i=8), [("ps", bk), uk], [uk])
                            cp("pool", u3[:, :, 0:2], hs, ["halos", uk], [uk])
                            cp("pool", hs, u3[:, :, 8:10], [uk, "halos"], ["halos"])
                            P.dma("sp", rO, o_conv_s[l, :, j, :, :], hs, r=["halos"], w=[("o_conv_s", l, j)], is_out=True)
                            c3 = cc[:, 0:128].rearrange("p (s i) -> p s i", i=8)
                            ts("pool", c3, u3[:, :, 2:10], w2, bb, ALU.mult, ALU.add, [uk, "convp", ck], [ck])
                            stt("dve", c3, u3[:, :, 1:9], w1, c3, ALU.mult, ALU.add, [uk, "convp", ck], [ck])
                            stt("dve", c3, u3[:, :, 0:8], w0, c3, ALU.mult, ALU.add, [uk, "convp", ck], [ck])
                        cres.append((cc, ck))
                    return cres

                def back(ui, cres):
                    (cca, cka), (ccb, ckb) = cres
                    act(cca[:, 0:n], cca[:, 0:n], AF.Silu, [cka], [cka])
                    tt("dve", g_[:, ui, 0:n], cca[:, 0:n], ccb[:, 0:n], ALU.mult, [cka, ckb, ("gT", bi % 2, ui)], [("gT", bi % 2, ui)])

                prev = None
                for ui in range(nu):
                    cr = front(ui)
                    if prev is not None:
                        back(*prev)
                    prev = (ui, cr)
                back(*prev)

            def stage2(bi):
                drain(1)
                n0, n = BLOCKS[bi]
                g_ = gT[bi % 2]
                for fo in range(8):
                    bk = psum()
                    for ui in range(nu):
                        mm(PS(bk)[:, 0:n], WD_[:, ui, fo * 128:(fo + 1) * 128], g_[:, ui, 0:n], ui == 0, ui == nu - 1,
                           [wk, ("gT", bi % 2, ui)], [("ps", bk)])
                    accum(bi, fo, bk, 1, acct)
                if lastg:
                    ln_finish_block(bi, lb, not last_layer)
                    if last_layer and not DBG:
                        for fc in range(8):
                            P.dma("sp", rO, yT[fc * 128:(fc + 1) * 128, n0:n0 + n], X[:, fc, n0:n0 + n], r=[("X", fc, bi)], w=[("yT", fc, bi)], is_out=True)

            nb = len(BLOCKS)
            stage1(0)
            for bi in range(1, nb):
                if bi == nb - 1:
                    load_halos()
                stage1(bi)
                stage2(bi - 1)
            stage2(nb - 1)
            if lastg:
                P.dma("sp", rO, o_conv_p[l], halo[:], r=["halo"], w=[("o_conv_p", l)], is_out=True)

        passes = []
        for l in range(L):
            for gp in range(4):
                passes.append(("R", l, gp))
            passes.append(("S", l, 0))
            for gi in range(len(FGROUPS)):
                passes.append(("F", l, gi))

        compute_mod(0)
        make_tables(True, 0, None, None, False)
        make_gates()
        ar0 = Arena()
        xh1_0 = ar0.f32([128, 128])
        for bi, (n0b, nb) in enumerate(BLOCKS):
            for sub in range(nb // 128):
                n0 = n0b + sub * 128
                finish(bi, n0, 128, lambda fc, n0=n0: X[:, fc, n0:n0 + 128], [], True, xh1_0)
        views = {}
        views[0] = load_pass_weights(passes[0][0], passes[0][1], passes[0][2], 0)
        drain()
        marks = []
        for pi, (kind, l, idx) in enumerate(passes):
            marks.append((kind, l, idx, P.cnt["pe"]))
            if pi == 0 or passes[pi - 1][0] != kind or not os.environ.get('KNOBAR'):
                P.barrier()
            s = pi % 2
            if pi + 1 < len(passes):
                nk, nl, nidx = passes[pi + 1]
                views[pi + 1] = load_pass_weights(nk, nl, nidx, (pi + 1) % 2)
                drain(1)
            v1, v2 = views.pop(pi)
            wk = ("W", s)
            if kind == "R":
                ret_pass(l, idx, v1, v2, wk)
                drain()
            elif kind == "S":
                swa_pass(l, v1, v2, wk)
                drain()
                if l + 1 < L:
                    P.barrier()
                    compute_mod(l + 1)
                    make_gates_next = True
            else:
                if idx == 0 and l + 1 <= L:
                    pass
                ffn_pass(l, idx, l == L - 1, v1, v2, wk)
                drain()
                if idx == len(FGROUPS) - 1 and l + 1 < L:
                    make_gates()

        marks.append(("END", 0, 0, P.cnt["pe"]))
        if os.environ.get("KMARKS"):
            import json
            json.dump(marks, open(os.environ["KMARKS"], "w"))
        with nc.Block() as block:
            P.emit(block)
    return nc


def _consts():
    f32 = np.float32
    h = np.arange(8)
    gam = 1.0 - 2.0 ** (-5.0 - h.astype(np.float64))
    inv = (1.0 / (np.float32(10000.0) ** (np.arange(32, dtype=f32) / np.float32(32)))).astype(f32)
    rot = np.zeros((17, 128, 4, 128), f32)
    for c in range(17):
        if c < 16:
            pos = (c * 128 + np.arange(128)).astype(f32)
        else:
            pos = (PAST + (np.arange(128) % 8)).astype(f32)
        ang = pos[:, None] * inv[None, :]
        cs, sn = np.cos(ang).astype(f32), np.sin(ang).astype(f32)
        for p in range(128):
            d = p % 64
            f = d % 32
            sg = -1.0 if d < 32 else 1.0
            rot[c, p, 0] = cs[:, f]
            rot[c, p, 1] = sg * sn[:, f]
            rot[c, p, 2] = cs[:, f] * 0.125
            rot[c, p, 3] = sg * sn[:, f] * 0.125
    i = np.arange(128)
    decm = np.zeros((128, 8, 128), f32)
    for hh in range(8):
        diff = i[None, :] - i[:, None]
        decm[:, hh, :] = np.where(diff >= 0, gam[hh] ** np.maximum(diff, 0), 0.0)
    blkind = (i[:, None] // 8 == i[None, :] // 8).astype(f32)
    qdec = np.zeros((128, 4, 2, 128), f32)
    cdec = np.zeros((128, 2, 4, 3), f32)
    for p in range(128):
        hb = p // 64
        for pair in range(4):
            hh = pair * 2 + hb
            qdec[p, pair, 0] = gam[hh] ** (i + 1)
            qdec[p, pair, 1] = gam[hh] ** ((i % 8) + 1)
            for k, C in enumerate((128, 8)):
                dec = gam[hh] ** C
                cdec[p, k, pair] = (dec, dec if hb == 0 else 0.0, dec if hb == 1 else 0.0)
    cdec[:, :, :, 1] = (np.arange(128)[:, None, None] // 64 == 0)
    cdec[:, :, :, 2] = (np.arange(128)[:, None, None] // 64 == 1)
    kdec = np.zeros((128, 2, 8), f32)
    for hh in range(8):
        kdec[:, 0, hh] = gam[hh] ** (127 - i)
        kdec[:, 1, hh] = gam[hh] ** (7 - (i % 8))
    hmask = np.zeros((128, 2), f32)
    hmask[:64, 0] = 1
    hmask[64:, 1] = 1
    seqm = (i[:, None] // 8 == np.arange(16)[None, :]).astype(f32)
    ident = np.eye(128, dtype=f32)
    hsel = np.zeros((128, 128), f32)
    hsel[:64, :64] = 1.0 / 64
    hsel[64:, 64:] = 1.0 / 64
    dist = 127 - np.arange(128)
    n = np.maximum(dist, 0)
    nf = np.maximum(n, 16).astype(f32)
    large = 16 + (np.log(nf / np.float32(16)) / np.float32(np.log(128 / 16)) * np.float32(16)).astype(np.int32)
    large = np.minimum(large, 31)
    bucket = np.where(n < 16, n, large)
    ohr = (np.arange(32)[:, None] == bucket[None, :]).astype(f32)
    rot_b = np.zeros((5, 128, 4, 512), f32)
    for c in range(17):
        rot_b[c // 4, :, :, (c % 4) * 128:(c % 4 + 1) * 128] = rot[c]
    rot = rot_b
    qdec = np.ascontiguousarray(np.tile(qdec, (1, 1, 1, 4)))
    return dict(rot=rot, decm=decm, blkind=blkind, qdec=qdec, kdec=kdec, cdec=cdec, hmask=hmask,
                seqm=seqm, identf=ident, hsel=hsel, ohr=ohr)


_NC_CACHE = {}


def kernel(x_prompt, x_sample, c_prompt, c_sample, state_ret, cache_swa_k, cache_swa_v, state_conv,
           rel_bias, w_ada, b_ada, w_in, swa_sinks, w_out, ln1_g, ln1_b,
           w_up, conv_w, conv_b, w_down, ln2_g, ln2_b):
    f32 = np.float32
    A = lambda a: np.ascontiguousarray(np.asarray(a, dtype=f32))
    x_prompt, x_sample, c_prompt, c_sample = A(x_prompt), A(x_sample), A(c_prompt), A(c_sample)
    state_ret, cache_swa_k, cache_swa_v, state_conv = A(state_ret), A(cache_swa_k), A(cache_swa_v), A(state_conv)
    rel_bias, w_ada, b_ada, w_in, swa_sinks, w_out = A(rel_bias), A(w_ada), A(b_ada), A(w_in), A(swa_sinks), A(w_out)
    ln1_g, ln1_b, w_up, conv_w, conv_b, w_down, ln2_g, ln2_b = A(ln1_g), A(ln1_b), A(w_up), A(conv_w), A(conv_b), A(w_down), A(ln2_g), A(ln2_b)

    sw = np.concatenate([np.arange(32, 64), np.arange(0, 32)])
    w_r = np.zeros((L, 4, D, 768), f32)
    w_or = np.zeros((L, 4, 128, D), f32)
    for gp in range(4):
        hs = np.arange(gp * 2, gp * 2 + 2)
        cols = (hs[:, None] * 64 + np.arange(64)[None, :]).reshape(-1)
        cols_sw = (hs[:, None] * 64 + sw[None, :]).reshape(-1)
        w_r[:, gp, :, 0:128] = w_in[:, :, 0 + cols]
        w_r[:, gp, :, 128:256] = w_in[:, :, 0 + cols_sw]
        w_r[:, gp, :, 256:384] = w_in[:, :, 512 + cols]
        w_r[:, gp, :, 384:512] = w_in[:, :, 512 + cols_sw]
        w_r[:, gp, :, 512:640] = w_in[:, :, 1024 + cols]
        w_r[:, gp, :, 640:768] = w_in[:, :, 1536 + cols]
        w_or[:, gp] = w_out[:, cols, :]
    perm = np.concatenate([np.concatenate([np.arange(c * 64, c * 64 + 64), np.arange((4 + c) * 64, (4 + c) * 64 + 64)]) for c in range(4)])
    w_s = np.concatenate([w_in[:, :, 2048 + perm], w_in[:, :, 2560:2816]], axis=2)
    w_os = w_out[:, 512 + perm, :]
    w_r = np.ascontiguousarray(w_r.reshape(L, 4, 8, 128, 768).transpose(0, 1, 3, 2, 4))
    w_s = np.ascontiguousarray(w_s.reshape(L, 8, 128, 768).transpose(0, 2, 1, 3))
    w_os = np.ascontiguousarray(w_os.reshape(L, 4, 128, D).transpose(0, 2, 1, 3))
    wu = np.concatenate([w_up[:, :, :DFF].reshape(L, 8, 128, NU, 128), w_up[:, :, DFF:].reshape(L, 8, 128, NU, 128)], axis=4)
    w_up = np.ascontiguousarray(wu.transpose(0, 3, 2, 1, 4))
    w_down = np.ascontiguousarray(w_down.reshape(L, NU, 128, D).transpose(0, 2, 1, 3))
    bT_ada = np.ascontiguousarray(b_ada.reshape(L, 48, 128).transpose(2, 0, 1))
    convT = np.ascontiguousarray(np.concatenate([conv_w, conv_b[:, None, :]], axis=1).reshape(L, 4, 44, 128).transpose(3, 0, 2, 1))
    lnT = np.ascontiguousarray(np.stack([ln1_g, ln1_b, ln2_g, ln2_b], axis=1).reshape(L, 4, 8, 128).transpose(3, 0, 1, 2))
    cb_heads = np.array([(cbi % 2) * 4 + cbi // 2 for cbi in range(8)])
    sinkP = np.ascontiguousarray(np.broadcast_to(swa_sinks[:, cb_heads][None], (128, L, 8)))
    sinkS = np.zeros((32, L, 2), f32)
    for c in range(4):
        for g in range(2):
            sinkS[c * 8:(c + 1) * 8, :, g] = swa_sinks[None, :, g * 4 + c]
    cst = _consts()

    if "nc" not in _NC_CACHE:
        _NC_CACHE["nc"] = build_program()
    nc = _NC_CACHE["nc"]

    in_maps = []
    for b in range(8):
        xall = np.concatenate([x_prompt[b], x_sample[16 * b:16 * b + 16].reshape(128, D)], axis=0)
        call = np.concatenate([c_prompt[b:b + 1], c_sample[16 * b:16 * b + 16]], axis=0)
        m = dict(
            xT=np.ascontiguousarray(xall.T),
            cT=np.ascontiguousarray(call.T.reshape(8, 128, 17).transpose(1, 0, 2)),
            w_ada=w_ada, bT_ada=bT_ada, w_r=w_r, w_s=w_s, w_or=w_or, w_os=w_os, w_up=w_up, w_down=w_down,
            convT=convT, lnT=lnT, sinkP=sinkP, sinkS=sinkS, rel_bias=rel_bias,
            state_ret=np.ascontiguousarray(state_ret[:, 16 * b:16 * b + 16]),
            cache_k=np.ascontiguousarray(cache_swa_k[:, 16 * b:16 * b + 16].reshape(L, 16, 128, 128)),
            cache_v=np.ascontiguousarray(cache_swa_v[:, 16 * b:16 * b + 16].reshape(L, 16, 128, 128)),
            state_conv=np.ascontiguousarray(state_conv[:, 16 * b:16 * b + 16].reshape(L, 32, 2 * DFF)),
        )
        m.update(cst)
        in_maps.append(m)
    res = run_bass_kernel_spmd(nc, in_maps, core_ids=list(range(8)))
    R = res.results

    y_p = np.zeros((8, NP, D), f32)
    y_s = np.zeros((128, 8, D), f32)
    ret_p = np.zeros((L, 8, 8, 64, 64), f32)
    ret_s = np.zeros((L, 128, 8, 64, 64), f32)
    k_p = np.zeros((L, 8, 128, 2, 64), f32)
    v_p = np.zeros((L, 8, 128, 2, 64), f32)
    k_s = np.zeros((L, 128, 128, 2, 64), f32)
    v_s = np.zeros((L, 128, 128, 2, 64), f32)
    conv_p = np.zeros((L, 8, 2, 2 * DFF), f32)
    conv_s = np.zeros((L, 128, 2, 2 * DFF), f32)
    for b in range(8):
        r = R[b]
        yt = r["yT"].T
        y_p[b] = yt[:NP]
        y_s[16 * b:16 * b + 16] = yt[NP:].reshape(16, 8, D)
        ret_p[:, b] = r["o_ret_p"].reshape(L, 4, 2, 64, 64).reshape(L, 8, 64, 64)
        ret_s[:, 16 * b:16 * b + 16] = r["o_ret_s"].reshape(L, 4, 2, 64, 16, 64).transpose(0, 4, 1, 2, 3, 5).reshape(L, 16, 8, 64, 64)
        k_p[:, b] = r["o_k_p"].reshape(L, 128, 2, 64)
        v_p[:, b] = r["o_v_p"].reshape(L, 128, 2, 64)
        k_s[:, 16 * b:16 * b + 16] = r["o_k_s"].reshape(L, 16, 128, 2, 64)
        v_s[:, 16 * b:16 * b + 16] = r["o_v_s"].reshape(L, 16, 128, 2, 64)
        conv_p[:, b] = r["o_conv_p"].transpose(0, 3, 2, 1).reshape(L, 2, 2 * DFF)
        conv_s[:, 16 * b:16 * b + 16] = r["o_conv_s"].transpose(0, 3, 4, 2, 1).reshape(L, 16, 2, 2 * DFF)
    return (y_p, y_s, ret_p, k_p, v_p, conv_p, ret_s, k_s, v_s, conv_s)
```

```python
import numpy as np
import ml_dtypes
import concourse.bass as bass
import concourse.mybir as mybir
from concourse.ap import AP
from concourse.bass_utils import run_bass_kernel_spmd

F32 = mybir.dt.float32
BF16 = mybir.dt.bfloat16
ALU = mybir.AluOpType
AF = mybir.ActivationFunctionType
AX = mybir.AxisListType

D = 1024
L = 4
NT = 2176
NP = 2048
DFF = 2816
NU = 22
ALPHA = (2.0 * L) ** 0.25
EPS = 1e-5
PAST = 8192
BLOCKS = [(0, 512), (512, 512), (1024, 512), (1536, 512), (2048, 128)]
FGROUPS = [(0, 3), (3, 3), (6, 3), (9, 3), (12, 3), (15, 3), (18, 2), (20, 2)]
SLOT_ELEMS = 10240
ARENA_F32 = 12050
import os
DBG = int(os.environ.get('KDBG', '0'))


class Ring:
    def __init__(self, sems):
        self.sems = sems
        self.vals = [0] * len(sems)
        self.i = 0


class Prog:
    ENG = ("pe", "act", "dve", "pool", "sp")

    def __init__(self, nc, esem):
        self.nc = nc
        self.esem = esem
        self.q = {e: [] for e in self.ENG}
        self.cnt = {e: 0 for e in self.ENG}
        self.seen = {e: {} for e in self.ENG}
        self.res = {}
        self.out_toks = []
        self.recent = []

    def _need(self, eng, tok):
        sem, val, teng, kind = tok
        sid = id(sem)
        if self.seen[eng].get(sid, 0) >= val:
            return
        self.seen[eng][sid] = val
        self.q[eng].append(("wait", sem, val))

    def _deps(self, eng, r, w, is_dma):
        toks = []
        for k in r:
            st = self.res.get(k)
            if st and st["w"] is not None:
                toks.append(st["w"])
        for k in w:
            st = self.res.get(k)
            if not st:
                continue
            if st["w"] is not None:
                t = st["w"]
                if not (eng == "pe" and t[3] == "op" and t[2] == "pe"):
                    toks.append(t)
            for t in st["r"]:
                if not (eng == "pe" and t[3] == "op" and t[2] == "pe"):
                    toks.append(t)
        best = {}
        for t in toks:
            sid = id(t[0])
            if sid not in best or best[sid][1] < t[1]:
                best[sid] = t
        for t in best.values():
            self._need(eng, t)

    def _record(self, tok, r, w):
        for k in r:
            st = self.res.setdefault(k, {"w": None, "r": []})
            st["r"].append(tok)
        for k in w:
            self.res[k] = {"w": tok, "r": []}

    def op(self, eng, fn, r=(), w=()):
        self._deps(eng, r, w, False)
        self.cnt[eng] += 1
        tok = (self.esem[eng], self.cnt[eng], eng, "op")
        self.q[eng].append(("op", fn))
        self._record(tok, r, w)

    def dma(self, q, ring, out, in_, r=(), w=(), is_out=False):
        self._deps(q, r, w, True)
        s = ring.i
        ring.i = (ring.i + 1) % len(ring.sems)
        sem = ring.sems[s]
        if ring.vals[s] > 0:
            self._need(q, (sem, ring.vals[s], None, "dma"))
        ring.vals[s] += 16
        tok = (sem, ring.vals[s], None, "dma")
        self.q[q].append(("dma", out, in_, sem))
        self.recent.append(tok)
        self._record(tok, r, w)
        if is_out:
            self.out_toks.append(tok)

    def barrier(self):
        for e in self.ENG:
            for t in self.recent:
                self._need(e, t)
            for o in self.ENG:
                if o != e and self.cnt[o] > 0:
                    self._need(e, (self.esem[o], self.cnt[o], o, "op"))
        self.recent = []

    def emit(self, block):
        nc = self.nc
        engs = {"pe": nc.tensor, "act": nc.scalar, "dve": nc.vector, "pool": nc.gpsimd, "sp": nc.sync}
        for t in self.out_toks:
            self._need("sp", t)

        def run(name):
            e = engs[name]
            for it in self.q[name]:
                if it[0] == "wait":
                    e.wait_ge(it[1], it[2])
                elif it[0] == "op":
                    it[1](e).then_inc(self.esem[name], 1)
                else:
                    e.dma_start(out=it[1], in_=it[2]).then_inc(it[3], 16)

        @block.tensor
        def _(t):
            run("pe")

        @block.scalar
        def _(t):
            run("act")

        @block.vector
        def _(t):
            run("dve")

        @block.gpsimd
        def _(t):
            run("pool")

        @block.sync
        def _(t):
            run("sp")


def _ins(ap, pos, cnt):
    l = [list(x) for x in ap.ap]
    l.insert(pos, [0, cnt])
    return AP(ap.tensor, ap.offset, l)


def build_program():
    nc = bass.Bass("TRN2", target_bir_lowering=False)

    def din(name, shape, dt=F32):
        return nc.dram_tensor(name, list(shape), dt, kind="ExternalInput").ap()

    def dout(name, shape):
        return nc.dram_tensor(name, list(shape), F32, kind="ExternalOutput").ap()

    xT = din("xT", [D, NT])
    cT = din("cT", [128, 8, 17])
    w_ada = din("w_ada", [L, D, 6 * D])
    bT_ada = din("bT_ada", [128, L, 48])
    w_r = din("w_r", [L, 4, 128, 8, 768])
    w_s = din("w_s", [L, 128, 8, 768])
    w_or = din("w_or", [L, 4, 128, D])
    w_os = din("w_os", [L, 128, 4, D])
    w_up = din("w_up", [L, NU, 128, 8, 256])
    w_down = din("w_down", [L, 128, NU, D])
    convT = din("convT", [128, L, 44, 4])
    lnT = din("lnT", [128, L, 4, 8])
    sinkP = din("sinkP", [128, L, 8])
    sinkS = din("sinkS", [32, L, 2])
    rel_bias = din("rel_bias", [32, 8])
    ohr = din("ohr", [32, 128])
    rot = din("rot", [5, 128, 2, 512])
    decm = din("decm", [128, 8, 128])
    blkind = din("blkind", [128, 128])
    qdec = din("qdec", [128, 4, 2, 512])
    kdec = din("kdec", [128, 2, 8])
    cdec = din("cdec", [128, 2, 4, 3])
    hmask = din("hmask", [128, 2])
    seqm = din("seqm", [128, 16])
    identf = din("identf", [128, 128])
    hsel = din("hsel", [128, 128])
    state_ret = din("state_ret", [L, 4, 128, 16, 64])
    cache_k = din("cache_k", [L, 16, 128, 128])
    cache_v = din("cache_v", [L, 16, 128, 128])
    state_conv = din("state_conv", [L, 32, 2 * DFF])

    yT = dout("yT", [D, NT])
    o_ret_p = dout("o_ret_p", [L, 4, 128, 64])
    o_ret_s = dout("o_ret_s", [L, 4, 128, 16, 64])
    o_k_p = dout("o_k_p", [L, 128, 128])
    o_v_p = dout("o_v_p", [L, 128, 128])
    o_k_s = dout("o_k_s", [L, 16, 128, 128])
    o_v_s = dout("o_v_s", [L, 16, 128, 128])
    o_conv_p = dout("o_conv_p", [L, 128, 44, 2])
    o_conv_s = dout("o_conv_s", [L, 128, 44, 16, 2])
    ext_d = nc.dram_tensor("ext_d", [8, 128, 512], F32, kind="Internal").ap()

    from contextlib import ExitStack
    es = ExitStack()

    def sb(name, shape, dt=F32):
        return es.enter_context(nc.sbuf_tensor(name, list(shape), dt))

    with es:
        X = sb("X", [128, 8, NT])
        HT = sb("HT", [128, 8, NT], BF16)
        WS = [sb("WS0", [128, SLOT_ELEMS], BF16), sb("WS1", [128, SLOT_ELEMS], BF16)]
        ARF = sb("ARENA", [128, ARENA_F32])
        ARB = ARF.bitcast(BF16)
        modT = sb("modT", [128, 48, 17])
        scT = sb("scT", [128, 8, 17], BF16)
        c_sb = sb("c_sb", [128, 8, 17])
        bada = sb("bada", [128, L, 48])
        convp = sb("convp", [128, L, 44, 4])
        lnp = sb("lnp", [128, L, 4, 8])
        sinkp = sb("sinkp", [128, L, 8])
        sinks = sb("sinks", [32, L, 2])
        identb = sb("identb", [128, 128], BF16)
        identF = sb("identF", [128, 128])
        hselb = sb("hselb", [128, 128], BF16)
        onesb = sb("onesb", [128, 128], BF16)
        blk = sb("blk", [128, 128])
        kdecS = sb("kdecS", [128, 2, 8])
        cdecS = sb("cdecS", [128, 2, 4, 3])
        hmS = sb("hmS", [128, 2])
        seqS = sb("seqS", [128, 16])
        epsT = sb("epsT", [128, 1])
        biasS = sb("biasS", [32, 2, 136], BF16)
        tabA = sb("tabA", [128, 4, 8, 17])
        gate = sb("gate", [128, 2, 8, 17])
        Sp = sb("Sp", [128, 4, 64])
        Sblk = sb("Sblk", [128, 4, 2, 64], BF16)
        kvf = sb("kvf", [128, 2, 128])
        stat = sb("stat", [128, 4, 32])
        stat2 = sb("stat2", [128, 2, 32])
        halo = sb("halo", [128, 44, 2])

        PSB = [es.enter_context(nc.psum_tensor("ps%d" % i, [128, 512], F32)) for i in range(8)]

        def sem(name):
            return es.enter_context(nc.semaphore(name))

        esem = {e: sem("e_" + e) for e in Prog.ENG}
        P = Prog(nc, esem)
        rW = Ring([sem("rw%d" % i) for i in range(6)])
        rC = Ring([sem("rc%d" % i) for i in range(6)])
        rO = Ring([sem("ro%d" % i) for i in range(8)])
        rT = Ring([sem("rt%d" % i) for i in range(6)])
        rP = Ring([sem("rp%d" % i) for i in range(6)])

        psi = [0]

        reserved = set()

        def psum():
            while True:
                i = psi[0]
                psi[0] = (i + 1) % 8
                if i not in reserved:
                    return i

        def PS(i):
            return PSB[i]

        def PSb(i):
            return PSB[i].bitcast(BF16)

        class Arena:
            def __init__(self):
                self.off = 0

            def f32(self, shape):
                n = int(np.prod(shape[1:]))
                v = ARF[0:shape[0], self.off:self.off + n]
                self.off += n
                assert self.off <= ARENA_F32, ("arena overflow", self.off)
                return _View(v, shape)

            def bf16(self, shape):
                n = int(np.prod(shape[1:]))
                nw = (n + 1) // 2
                v = ARB[0:shape[0], 2 * self.off:2 * self.off + n]
                self.off += nw
                assert self.off <= ARENA_F32, ("arena overflow", self.off)
                return _View(v, shape)

        class _View:
            def __init__(self, ap, shape):
                self.ap = ap
                self.shape = list(shape)
                if len(shape) == 2:
                    self.v = ap
                else:
                    names = "abcdefg"[:len(shape) - 1]
                    pat = "p (" + " ".join(names) + ") -> p " + " ".join(names)
                    kw = {names[i]: shape[i + 1] for i in range(len(names))}
                    self.v = ap.rearrange(pat, **kw)

            def __getitem__(self, k):
                return self.v[k]

        def mm(out, lhsT, rhs, start, stop, r, w):
            P.op("pe", lambda e: e.matmul(out, lhsT=lhsT, rhs=rhs, start=start, stop=stop), r=r, w=w)

        def tr(out, in_, r, w):
            k = in_.shape[0]
            P.op("pe", lambda e: e.transpose(out, in_, identb[0:k, 0:k]), r=list(r) + ["identb"], w=w)

        def act(out, in_, func, r, w, bias=None, scale=None, accum=None):
            kw = {}
            if bias is not None:
                kw["bias"] = bias
            if scale is not None:
                kw["scale"] = scale
            if accum is not None:
                kw["accum_out"] = accum
            P.op("act", lambda e: e.activation(out=out, in_=in_, func=func, **kw), r=r, w=w)

        def tt(eng, out, in0, in1, op, r, w):
            P.op(eng, lambda e: e.tensor_tensor(out=out, in0=in0, in1=in1, op=op), r=r, w=w)

        def ts(eng, out, in0, s1, s2, op0, op1, r, w):
            if op1 is None:
                P.op(eng, lambda e: e.tensor_scalar(out=out, in0=in0, scalar1=s1, scalar2=None, op0=op0), r=r, w=w)
            else:
                P.op(eng, lambda e: e.tensor_scalar(out=out, in0=in0, scalar1=s1, scalar2=s2, op0=op0, op1=op1), r=r, w=w)

        def stt(eng, out, in0, scalar, in1, op0, op1, r, w):
            P.op(eng, lambda e: e.scalar_tensor_tensor(out=out, in0=in0, scalar=scalar, in1=in1, op0=op0, op1=op1), r=r, w=w)

        def cp(eng, out, in_, r, w):
            if eng == "act":
                P.op("act", lambda e: e.copy(out=out, in_=in_), r=r, w=w)
            else:
                P.op(eng, lambda e: e.tensor_copy(out=out, in_=in_), r=r, w=w)

        def ms(eng, ap, val, w):
            P.op(eng, lambda e: e.memset(ap, val), r=(), w=w)

        def ld(q, ring, out, in_, w, r=()):
            P.dma(q, ring, out, in_, r=r, w=w)

        arS = Arena()
        arS.off = 9000
        ext = arS.f32([8, 512])
        rb = arS.f32([32, 8])
        ohrS = arS.f32([32, 128])
        ld("sp", rC, c_sb[:], cT, ["c_sb"])
        ld("sp", rC, bada[:], bT_ada, ["bada"])
        ld("sp", rC, convp[:], convT, ["convp"])
        ld("sp", rC, lnp[:], lnT, ["lnp"])
        ld("sp", rC, sinkp[:], sinkP, ["sinkp"])
        ld("sp", rC, sinks[:], sinkS, ["sinks"])
        ld("sp", rC, blk[:], blkind, ["blk"])
        ld("sp", rC, kdecS[:], kdec, ["kdec"])
        ld("sp", rC, cdecS[:], cdec, ["cdec"])
        ld("sp", rC, hmS[:], hmask, ["hm"])
        ld("sp", rC, seqS[:], seqm, ["seqm"])
        ld("sp", rC, identF[:], identf, ["identF"])
        ld("sp", rC, rb[:, :], rel_bias, ["rb"])
        ld("sp", rC, ohrS[:, :], ohr, ["ohrS"])
        ld("pool", rP, identb[:], identf, ["identb"])
        ld("pool", rP, hselb[:], hsel, ["hselb"])
        ms("dve", epsT[:], EPS, ["epsT"])
        ms("dve", onesb[:], 1.0 / 1024.0, ["ones"])
        ms("dve", ext[:, :], -1e30, ["ext"])
        for fc in range(8):
            ld("sp", rC, X[:, fc, :], xT[fc * 128:(fc + 1) * 128, :], [("X", fc, b) for b in range(5)])

        b0 = psum()
        mm(PS(b0)[0:8, 0:128], rb[:, :], ohrS[:, :], True, True, ["rb", "ohrS"], [("ps", b0)])
        cp("dve", ext[:, 128:256], PS(b0)[0:8, 0:128], [("ps", b0), "ext"], ["ext"])
        P.dma("sp", rC, ext_d, _ins(ext[:, :], 1, 128), r=["ext"], w=["ext_d"])
        for cbi in range(8):
            c_, g_ = cbi // 2, cbi % 2
            h_ = g_ * 4 + c_
            src2 = AP(ext_d.tensor, h_ * 65536 + 127, [[511, 8], [1, 136]])
            P.dma("pool", rP, biasS[c_ * 8:(c_ + 1) * 8, g_, :], src2, r=["ext_d"], w=["biasS"])
        act(scT[:], c_sb[:], AF.Silu, ["c_sb"], ["scT"])

        deferred = []

        def drain(k=None):
            n_ = len(deferred) if k is None else min(k, len(deferred))
            for _ in range(n_):
                deferred.pop(0)()

        def load_pass_weights(kind, l, idx, s):
            W = WS[s]
            key = ("W", s)

            def q(out, in_):
                deferred.append(lambda: P.dma("pool", rW, out, in_, w=[key]))
            if kind == "R":
                v1 = W[:, 0:8 * 768].rearrange("p (k n) -> p k n", k=8)
                q(v1[:, 0:4, :], w_r[l, idx, :, 0:4, :])
                q(v1[:, 4:8, :], w_r[l, idx, :, 4:8, :])
                v2 = W[:, 6144:6144 + 1024]
                q(v2, w_or[l, idx])
                return v1, v2
            if kind == "S":
                v1 = W[:, 0:8 * 768].rearrange("p (k n) -> p k n", k=8)
                q(v1[:, 0:4, :], w_s[l, :, 0:4, :])
                q(v1[:, 4:8, :], w_s[l, :, 4:8, :])
                v2 = W[:, 6144:6144 + 4096].rearrange("p (k n) -> p k n", k=4)
                q(v2, w_os[l])
                return v1, v2
            u0, nu = FGROUPS[idx]
            vu = W[:, 0:nu * 2048].rearrange("p (u k n) -> p u k n", u=nu, k=8)
            vd = W[:, 6144:6144 + nu * 1024].rearrange("p (u n) -> p u n", u=nu)
            for ui in range(nu):
                q(vu[:, ui, :, :], w_up[l, u0 + ui])
            q(vd, w_down[l, :, u0:u0 + nu, :])
            return vu, vd

        def compute_mod(l):
            ar = Arena()
            WA = [ar.bf16([128, 8, 256]), ar.bf16([128, 8, 256])]
            mk = "modT"
            bks = [psum(), psum()]
            for jj in range(24):
                wa = WA[jj % 2]
                wk = ("WA", jj % 2)
                P.dma("pool", rW, wa[:], w_ada[l, :, jj * 256:(jj + 1) * 256].rearrange("(k p) n -> p k n", p=128), w=[wk])
                for sub in range(2):
                    j = jj * 2 + sub
                    bk = bks[j // 24]
                    for kc in range(8):
                        mm(PS(bk)[:, (j % 24) * 17:(j % 24) * 17 + 17], wa[:, kc, sub * 128:(sub + 1) * 128], scT[:, kc, :],
                           kc == 0, kc == 7, [wk, "scT"], [("ps", bk)])
            for hf in range(2):
                tt("dve", modT[:, hf * 24:(hf + 1) * 24, :],
                   PS(bks[hf])[:, 0:408].rearrange("p (j r) -> p j r", r=17),
                   _ins(bada[:, l, hf * 24:(hf + 1) * 24], 2, 17), ALU.add,
                   [("ps", bks[hf]), "bada"], [mk])

        def make_tables(use_mod, which_mod, g_ap, b_ap, last):
            mk = "modT"
            if use_mod:
                sh = modT[:, which_mod * 24 + 0:which_mod * 24 + 8, :]
                sc = modT[:, which_mod * 24 + 8:which_mod * 24 + 16, :]
                onep = tabA[:, 1, :, :]
                ts("pool", onep, sc, 1.0, None, ALU.add, None, [mk, "tabA"], ["tabA"])
                if g_ap is None:
                    cp("pool", tabA[:, 0, :, :], onep, ["tabA"], ["tabA"])
                    cp("pool", tabA[:, 1, :, :], sh, [mk, "tabA"], ["tabA"])
                else:
                    tt("pool", tabA[:, 0, :, :], onep, _ins(g_ap, 2, 17), ALU.mult, ["tabA", "lnp"], ["tabA"])
                    tt("pool", tabA[:, 1, :, :], onep, _ins(b_ap, 2, 17), ALU.mult, ["tabA", "lnp"], ["tabA"])
                    tt("pool", tabA[:, 1, :, :], tabA[:, 1, :, :], sh, ALU.add, ["tabA", mk], ["tabA"])
            a = 1.0 if last else ALPHA
            if g_ap is None:
                ms("pool", tabA[:, 2, :, 0:1], a, ["tabA"])
                ms("pool", tabA[:, 3, :, 0:1], 0.0, ["tabA"])
            else:
                ts("pool", tabA[:, 2, :, 0:1], _ins(g_ap, 2, 1), a, None, ALU.mult, None, ["lnp", "tabA"], ["tabA"])
                ts("pool", tabA[:, 3, :, 0:1], _ins(b_ap, 2, 1), a, None, ALU.mult, None, ["lnp", "tabA"], ["tabA"])

        def make_gates():
            cp("pool", gate[:, 0, :, :], modT[:, 16:24, :], ["modT", "gate"], ["gate"])
            cp("pool", gate[:, 1, :, :], modT[:, 40:48, :], ["modT", "gate"], ["gate"])

        def finish_fc(bi, n0, n, fc, src, src_keys, do_h, xh1):
            smp = bi == 4
            src_keys = list(src_keys) + [("X", fc, bi)]
            if do_h:
                if not smp:
                    act(HT[:, fc, n0:n0 + n], src, AF.Identity, list(src_keys) + ["tabA"], [("HT", fc, bi)],
                        bias=tabA[:, 1, fc, 0:1], scale=tabA[:, 0, fc, 0:1])
                else:
                    s3 = src.rearrange("p (s i) -> p s i", i=8)
                    x3 = xh1[:, 0:128].rearrange("p (s i) -> p s i", i=8)
                    tt("dve", x3, s3, _ins(tabA[:, 0, fc, 1:17], 2, 8), ALU.mult, list(src_keys) + ["tabA", "xh1"], ["xh1"])
                    tt("dve", HT[:, fc, n0:n0 + n].rearrange("p (s i) -> p s i", i=8), x3,
                       _ins(tabA[:, 1, fc, 1:17], 2, 8), ALU.add, ["xh1", "tabA"], [("HT", fc, bi)])
            ts("dve" if fc % 2 == 0 else "pool", X[:, fc, n0:n0 + n], src, tabA[:, 2, fc, 0:1], tabA[:, 3, fc, 0:1],
               ALU.mult, ALU.add, list(src_keys) + ["tabA", ("X", fc, bi)], [("X", fc, bi)])

        def finish(bi, n0, n, src_fn, src_keys, do_h, xh1):
            for fc in range(8):
                finish_fc(bi, n0, n, fc, src_fn(fc), src_keys, do_h, xh1)

        def accum(bi, fc, bk, gi, tmp):
            n0, n = BLOCKS[bi]
            if bi < 4 and fc % 2 == 1 and len(tmp) > 1:
                tb = tmp[1 + (fc // 2) % 2]
                tkk = ("acct2", (fc // 2) % 2)
                act(tb[:, 0:n], PS(bk)[:, 0:n], AF.Identity, [("ps", bk), "gate", tkk], [tkk], scale=gate[:, gi, fc, 0:1])
                tt("pool", X[:, fc, n0:n0 + n], X[:, fc, n0:n0 + n], tb[:, 0:n], ALU.add, [tkk, ("X", fc, bi)], [("X", fc, bi)])
            elif bi < 4:
                stt("dve", X[:, fc, n0:n0 + n], PS(bk)[:, 0:n], gate[:, gi, fc, 0:1], X[:, fc, n0:n0 + n], ALU.mult, ALU.add,
                    [("ps", bk), "gate", ("X", fc, bi)], [("X", fc, bi)])
            else:
                tt("dve", tmp[0][:, 0:128].rearrange("p (s i) -> p s i", i=8), PS(bk)[:, 0:128].rearrange("p (s i) -> p s i", i=8),
                   _ins(gate[:, gi, fc, 1:17], 2, 8), ALU.mult, [("ps", bk), "gate", "acctmp"], ["acctmp"])
                tt("pool", X[:, fc, n0:n0 + n], X[:, fc, n0:n0 + n], tmp[0][:, 0:128], ALU.add, ["acctmp", ("X", fc, bi)], [("X", fc, bi)])

        class LNBuf:
            def __init__(self, ar):
                self.vb = [ar.bf16([128, 512]), ar.bf16([128, 512])]
                self.vsq = [ar.bf16([128, 512]), ar.bf16([128, 512])]
                self.tmp = [ar.f32([128, 512]) for _ in range(3)]
                self.mean = ar.f32([128, 512])
                self.rstd = ar.f32([128, 512])
                self.xh1 = ar.f32([128, 128])

        def ln_finish_block(bi, lb, do_h):
            n0, n = BLOCKS[bi]
            bm, bq = psum(), psum()
            reserved.add(bm)
            reserved.add(bq)
            for fc in range(8):
                vbk, vsk = ("vb", fc % 2), ("vsq", fc % 2)
                cp("dve", lb.vb[fc % 2][:, 0:n], X[:, fc, n0:n0 + n], [("X", fc, bi), vbk], [vbk])
                act(lb.vsq[fc % 2][:, 0:n], X[:, fc, n0:n0 + n], AF.Square, [("X", fc, bi), vsk], [vsk])
                mm(PS(bm)[:, 0:n], onesb[:, :], lb.vb[fc % 2][:, 0:n], fc == 0, fc == 7, ["ones", vbk], [("ps", bm)])
                mm(PS(bq)[:, 0:n], onesb[:, :], lb.vsq[fc % 2][:, 0:n], fc == 0, fc == 7, ["ones", vsk], [("ps", bq)])
            cp("act", lb.mean[:, 0:n], PS(bm)[:, 0:n], [("ps", bm), "mean"], ["mean"])
            tt("dve", lb.rstd[:, 0:n], lb.mean[:, 0:n], lb.mean[:, 0:n], ALU.mult, ["mean", "rstd"], ["rstd"])
            tt("dve", lb.rstd[:, 0:n], PS(bq)[:, 0:n], lb.rstd[:, 0:n], ALU.subtract, [("ps", bq), "rstd"], ["rstd"])
            reserved.discard(bm)
            reserved.discard(bq)
            act(lb.rstd[:, 0:n], lb.rstd[:, 0:n], AF.Ln, ["rstd", "epsT"], ["rstd"], bias=epsT[:, 0:1])
            act(lb.rstd[:, 0:n], lb.rstd[:, 0:n], AF.Exp, ["rstd"], ["rstd"], scale=-0.5)
            for fc in range(8):
                tk = ("lntmp", fc % 3)
                t_ = lb.tmp[fc % 3]
                tt("pool", t_[:, 0:n], X[:, fc, n0:n0 + n], lb.mean[:, 0:n], ALU.subtract, [("X", fc, bi), "mean", tk], [tk])
                tt("dve", t_[:, 0:n], t_[:, 0:n], lb.rstd[:, 0:n], ALU.mult, [tk, "rstd"], [tk])
                finish_fc(bi, n0, n, fc, t_[:, 0:n], [tk], do_h, lb.xh1)

        def ret_pass(l, gp, WR_, WO_, wk):
            ar = Arena()
            rotS = [ar.f32([128, 2, 512]), ar.f32([128, 2, 512])]
            decmS = ar.f32([128, 2, 128])
            qdecS = ar.f32([128, 2, 512])
            t1 = ar.f32([128, 512])
            t2 = ar.f32([128, 512])
            t3 = t1
            t4 = t2
            qp = ar.bf16([128, 512])
            qpp = ar.bf16([128, 512])
            kf = ar.bf16([128, 512])
            ka = ar.bf16([128, 512])
            kb = ar.bf16([128, 512])
            gs = ar.bf16([128, 512])
            retT = ar.bf16([128, 512])
            vpad = [ar.bf16([128, 2, 128]), ar.bf16([128, 2, 128])]
            sT = [ar.bf16([128, 2, 128]), ar.bf16([128, 2, 128])]
            ktok = [ar.bf16([128, 128]), ar.bf16([128, 128])]
            of32 = [ar.f32([128, 128]), ar.f32([128, 128])]
            ob = [ar.bf16([128, 2, 128]), ar.bf16([128, 2, 128])]
            hn1 = [ar.f32([128, 128]), ar.f32([128, 128])]
            hn2 = [ar.f32([128, 128]), ar.f32([128, 128])]
            Ss = ar.f32([128, 16, 64])
            Ssblk = ar.bf16([128, 16, 2, 64])
            st1 = ar.f32([128, 4, 64])
            vblk = ar.bf16([128, 4, 128])
            acct = [ar.f32([128, 128]), ar.f32([128, 512]), ar.f32([128, 512])]
            ld("sp", rT, decmS[:, :, :], decm[:, gp * 2:gp * 2 + 2, :], ["decm"])
            ld("sp", rT, qdecS[:, :, :], qdec[:, gp, :, :], ["qdec"])
            P.dma("sp", rT, Ss[:, :, :], state_ret[l, gp], w=["Ss"])
            ms("pool", ka[:, :], 0.0, ["ka"])
            ms("pool", kb[:, :], 0.0, ["kb"])
            ms("pool", vpad[0][:, :, :], 0.0, [("vpad", 0)])
            ms("pool", vpad[1][:, :, :], 0.0, [("vpad", 1)])
            ms("dve", Sp[:, gp, :], 0.0, [("Sp", gp)])
            ms("dve", Sblk[:, gp, :, :], 0.0, [("Sblk", gp)])
            tt("dve", Ssblk[:, :, :, :], _ins(Ss[:, :, :], 2, 2),
               _ins(_ins(hmS[:, :], 1, 16), 3, 64), ALU.mult, ["Ss", "hm", "Ssblk"], ["Ssblk"])
            for bi, (n0, n) in enumerate(BLOCKS):
                drain(1)
                smp = bi == 4
                di = 1 if smp else 0
                hk = [("HT", kc, bi) for kc in range(8)]
                nch = n // 128
                if bi == 0:
                    P.dma("sp", rT, rotS[0][:, :, :], rot[0], w=[("rot", 0)])
                if bi + 1 < len(BLOCKS):
                    P.dma("sp", rT, rotS[(bi + 1) % 2][:, :, :], rot[bi + 1], w=[("rot", (bi + 1) % 2)])
                rS = rotS[bi % 2]
                rk_ = ("rot", bi % 2)

                def proj(col0):
                    bk = psum()
                    for kc in range(8):
                        mm(PS(bk)[:, 0:n], WR_[:, kc, col0:col0 + 128], HT[:, kc, n0:n0 + n], kc == 0, kc == 7,
                           [wk, hk[kc]], [("ps", bk)])
                    return bk
                bq = proj(0)
                bqs = proj(128)
                tt("dve", t1[:, 0:n], PS(bq)[:, 0:n], rS[:, 0, 0:n], ALU.mult, [("ps", bq), rk_, "t1"], ["t1"])
                tt("dve", t2[:, 0:n], PS(bqs)[:, 0:n], rS[:, 1, 0:n], ALU.mult, [("ps", bqs), rk_, "t2"], ["t2"])
                tt("pool", t1[:, 0:n], t1[:, 0:n], t2[:, 0:n], ALU.add, ["t1", "t2"], ["t1"])
                cp("act", qp[:, 0:n], t1[:, 0:n], ["t1", "qp"], ["qp"])
                tt("pool", qpp[:, 0:n], t1[:, 0:n], qdecS[:, di, 0:n], ALU.mult, ["t1", "qdec", "qpp"], ["qpp"])
                bk_ = proj(256)
                bks_ = proj(384)
                tt("dve", t3[:, 0:n], PS(bk_)[:, 0:n], rS[:, 0, 0:n], ALU.mult, [("ps", bk_), rk_, "t1"], ["t1"])
                tt("dve", t4[:, 0:n], PS(bks_)[:, 0:n], rS[:, 1, 0:n], ALU.mult, [("ps", bks_), rk_, "t2"], ["t2"])
                tt("pool", kf[:, 0:n], t3[:, 0:n], t4[:, 0:n], ALU.add, ["t1", "t2", "kf"], ["kf"])
                cp("act", ka[0:64, 0:n], kf[0:64, 0:n], ["kf", "ka"], ["ka"])
                cp("act", kb[64:128, 0:n], kf[64:128, 0:n], ["kf", "kb"], ["kb"])
                bg = proj(640)
                act(gs[:, 0:n], PS(bg)[:, 0:n], AF.Silu, [("ps", bg), "gs"], ["gs"])

                def stageA(ci):
                    cidx = bi * 4 + ci
                    par = cidx % 2
                    c0 = n0 + ci * 128
                    sl = slice(ci * 128, (ci + 1) * 128)
                    bv = psum()
                    if smp:
                        reserved.add(bv)
                    for kc in range(8):
                        mm(PS(bv)[:, 0:128], HT[:, kc, c0:c0 + 128], WR_[:, kc, 512:640], kc == 0, kc == 7, [wk, hk[kc]], [("ps", bv)])
                    vp = vpad[par]
                    vk = ("vpad", par)
                    vo = AP(vp.ap.tensor, vp.ap.offset, [list(vp.ap.ap[0]), [192, 2], [1, 64]])
                    cp("act", vo, PS(bv)[:, 0:128].rearrange("p (b e) -> p b e", b=2), [("ps", bv), vk], [vk])
                    bs = psum()
                    mm(PS(bs)[:, 0:128], ka[:, sl], qp[:, sl], True, True, ["ka", "qp"], [("ps", bs)])
                    mm(PS(bs)[:, 128:256], kb[:, sl], qp[:, sl], True, True, ["kb", "qp"], [("ps", bs)])
                    st_ = sT[par]
                    sk_ = ("sT", par)
                    tt("dve", st_[:, :, :], PS(bs)[:, 0:256].rearrange("p (a i) -> p a i", a=2),
                       decmS[:, :, :], ALU.mult, [("ps", bs), "decm", sk_], [sk_])
                    if smp:
                        tt("pool", st_[:, :, :], st_[:, :, :], _ins(blk[:, :], 1, 2), ALU.mult, [sk_, "blk"], [sk_])
                    bo = psum()
                    reserved.add(bo)
                    mm(PS(bo)[:, 0:128], vp[:, 0, :], st_[:, 0, :], True, False, [vk, sk_], [("ps", bo)])
                    mm(PS(bo)[:, 0:128], vp[:, 1, :], st_[:, 1, :], False, False, [vk, sk_], [("ps", bo)])
                    if not smp:
                        mm(PS(bo)[:, 0:128], Sblk[:, gp, :, :].rearrange("p a e -> p (a e)"), qpp[:, sl], False, True,
                           [("Sblk", gp), "qpp"], [("ps", bo)])
                    else:
                        for s_ in range(16):
                            mm(PS(bo)[:, s_ * 8:(s_ + 1) * 8], Ssblk[:, s_, :, :].rearrange("p a e -> p (a e)"),
                               qpp[:, ci * 128 + s_ * 8:ci * 128 + (s_ + 1) * 8], False, s_ == 15,
                               ["Ssblk", "qpp"], [("ps", bo)])
                    bt = psum()
                    tr(PSb(bt)[:, 0:128], kf[:, sl], ["kf"], [("ps", bt)])
                    kt = ktok[par]
                    kk = ("ktok", par)
                    tt("dve", kt[:, :].rearrange("p (a d) -> p a d", a=2), PSb(bt)[:, 0:128].rearrange("p (a d) -> p a d", a=2),
                       _ins(kdecS[:, di, gp * 2:gp * 2 + 2], 2, 64), ALU.mult, [("ps", bt), "kdec", kk], [kk])
                    if not smp:
                        bd = psum()
                        mm(PS(bd)[:, 0:128], kt[:, :], vp[:, 0, :], True, False, [kk, vk], [("ps", bd)])
                        mm(PS(bd)[:, 0:128], kt[:, :], vp[:, 1, :], False, True, [kk, vk], [("ps", bd)])
                        ts("dve", st1[:, 0, :], PS(bd)[:, 0:64], cdecS[:, 0, gp, 1:2], None, ALU.mult, None,
                           [("ps", bd), "cdec", "st1"], ["st1"])
                        stt("dve", st1[:, 0, :], PS(bd)[:, 64:128], cdecS[:, 0, gp, 2:3], st1[:, 0, :], ALU.mult, ALU.add,
                            [("ps", bd), "cdec", "st1"], ["st1"])
                        stt("dve", Sp[:, gp, :], Sp[:, gp, :], cdecS[:, 0, gp, 0:1], st1[:, 0, :], ALU.mult, ALU.add,
                            ["st1", "cdec", ("Sp", gp)], [("Sp", gp)])
                        tt("pool", Sblk[:, gp, :, :], _ins(Sp[:, gp, :], 1, 2), _ins(hmS[:, :], 2, 64),
                           ALU.mult, [("Sp", gp), "hm", ("Sblk", gp)], [("Sblk", gp)])
                        if cidx == 15:
                            P.dma("sp", rO, o_ret_p[l, gp], Sp[:, gp, :], r=[("Sp", gp)], w=[("o_ret_p", l, gp)], is_out=True)
                    else:
                        for q4 in range(4):
                            tt("dve", vblk[:, :, :], _ins(PS(bv)[:, 0:128], 1, 4),
                               _ins(seqS[:, q4 * 4:(q4 + 1) * 4], 2, 128), ALU.mult, [("ps", bv), "seqm", "vblk"], ["vblk"])
                            bd = psum()
                            mm(PS(bd)[:, 0:512], kt[:, :], vblk[:, :, :].rearrange("p s e -> p (s e)"), True, True,
                               [kk, "vblk"], [("ps", bd)])
                            pv = PS(bd)[:, 0:512].rearrange("p (s a e) -> p s a e", s=4, a=2)
                            ts("dve", st1[:, :, :], pv[:, :, 0, :], cdecS[:, 1, gp, 1:2], None, ALU.mult, None,
                               [("ps", bd), "cdec", "st1"], ["st1"])
                            stt("dve", st1[:, :, :], pv[:, :, 1, :], cdecS[:, 1, gp, 2:3], st1[:, :, :],
                                ALU.mult, ALU.add, [("ps", bd), "cdec", "st1"], ["st1"])
                            stt("dve", Ss[:, q4 * 4:(q4 + 1) * 4, :], Ss[:, q4 * 4:(q4 + 1) * 4, :], cdecS[:, 1, gp, 0:1], st1[:, :, :],
                                ALU.mult, ALU.add, ["st1", "cdec", "Ss", "Ssblk"], ["Ss"])
                        reserved.discard(bv)
                        P.dma("sp", rO, o_ret_s[l, gp], Ss[:, :, :], r=["Ss"], w=[("o_ret_s", l, gp)], is_out=True)
                    return bo

                def stageB(ci, bo):
                    cidx = bi * 4 + ci
                    par = cidx % 2
                    sl = slice(ci * 128, (ci + 1) * 128)
                    o_, ob_, h1, h2 = of32[par], ob[par], hn1[par], hn2[par]
                    ko, kob, k1, k2 = ("of32", par), ("ob", par), ("hn1", par), ("hn2", par)
                    cp("act", o_[:, :], PS(bo)[:, 0:128], [("ps", bo), ko], [ko])
                    cp("act", ob_[:, 0, :], PS(bo)[:, 0:128], [("ps", bo), kob], [kob])
                    act(ob_[:, 1, :], PS(bo)[:, 0:128], AF.Square, [("ps", bo), kob], [kob])
                    reserved.discard(bo)
                    bh = psum()
                    mm(PS(bh)[:, 0:256], hselb[:, :], ob_[:, :, :].rearrange("p a i -> p (a i)"), True, True, ["hselb", kob], [("ps", bh)])
                    cp("act", h2[:, :], PS(bh)[:, 0:128], [("ps", bh), k2], [k2])
                    tt("dve", h1[:, :], h2[:, :], h2[:, :], ALU.mult, [k2, k1], [k1])
                    tt("dve", h1[:, :], PS(bh)[:, 128:256], h1[:, :], ALU.subtract, [("ps", bh), k1], [k1])
                    act(h1[:, :], h1[:, :], AF.Ln, [k1, "epsT"], [k1], bias=epsT[:, 0:1])
                    act(h1[:, :], h1[:, :], AF.Exp, [k1], [k1], scale=-0.5)
                    tt("pool", o_[:, :], o_[:, :], h2[:, :], ALU.subtract, [ko, k2], [ko])
                    tt("pool", o_[:, :], o_[:, :], h1[:, :], ALU.mult, [ko, k1], [ko])
                    tt("dve", retT[:, sl], o_[:, :], gs[:, sl], ALU.mult, [ko, "gs", "retT"], ["retT"])

                prev = None
                for ci in range(nch):
                    bo = stageA(ci)
                    if prev is not None:
                        stageB(*prev)
                    prev = (ci, bo)
                stageB(*prev)
                for fo in range(8):
                    bk = psum()
                    mm(PS(bk)[:, 0:n], WO_[:, fo * 128:(fo + 1) * 128], retT[:, 0:n], True, True, [wk, "retT"], [("ps", bk)])
                    accum(bi, fo, bk, 0, acct)

        def swa_pass(l, WQ_, WO_, wk):
            ar = Arena()
            biasP = ar.bf16([128, 8, 256])
            qa = ar.bf16([128, 4, 512])
            qb = ar.bf16([128, 4, 512])
            skT = ar.bf16([128, NT])
            svt = ar.bf16([128, 17, 128])
            pP = [ar.bf16([128, 256]) for _ in range(8)]
            pT = [ar.bf16([128, 4, 2, 128]) for _ in range(2)]
            attn = ar.bf16([128, 512])
            swaT = ar.bf16([128, 4, 512])
            kcr = ar.bf16([128, 4, 128])
            kTc = ar.bf16([128, 8, 128])
            vc = ar.bf16([128, 8, 128])
            pS = [ar.bf16([32, 256]) for _ in range(4)]
            pTs = [ar.bf16([128, 2, 32]) for _ in range(4)]
            attS = ar.bf16([32, 4, 128])
            acct = [ar.f32([128, 128]), ar.f32([128, 512]), ar.f32([128, 512])]
            for cbi in range(8):
                c_, g_ = cbi // 2, cbi % 2
                h_ = g_ * 4 + c_
                src = AP(ext_d.tensor, h_ * 65536 + 127, [[511, 128], [1, 256]])
                P.dma("pool", rP, biasP[:, cbi, :], src, r=["ext_d"], w=["biasP"])
            ms("pool", qa[:, :, :], 0.0, [("qa", c) for c in range(4)])
            ms("pool", qb[:, :, :], 0.0, [("qb", c) for c in range(4)])
            for _k in range(4):
                ms("pool", pS[_k][:, :], 0.0, [("pS", _k)])
            def load_cache_half(hf):
                P.dma("pool", rP, vc[:, :, :], cache_v[l, hf * 8:(hf + 1) * 8].rearrange("s k e -> k s e"), w=["vc"])
                for q4 in range(2):
                    P.dma("pool", rP, kcr[:, :, :], cache_k[l, hf * 8 + q4 * 4:hf * 8 + (q4 + 1) * 4].rearrange("s k e -> k s e"), w=["kcr"])
                    bt = psum()
                    for j in range(4):
                        tr(PSb(bt)[:, j * 128:(j + 1) * 128], kcr[:, j, :], ["kcr"], [("ps", bt)])
                    cp("act", kTc[:, q4 * 4:q4 * 4 + 4, :].rearrange("p s k -> p (s k)"), PSb(bt)[:, 0:512],
                       [("ps", bt), "kTc"], ["kTc"])
            P.dma("sp", rO, o_k_s[l, :, 0:120, :], cache_k[l, :, 8:128, :], w=[("o_k_s", l, 0)], is_out=True)
            P.dma("sp", rO, o_v_s[l, :, 0:120, :], cache_v[l, :, 8:128, :], w=[("o_v_s", l, 0)], is_out=True)
            for bi, (n0, n) in enumerate(BLOCKS):
                drain(1)
                smp = bi == 4
                hk = [("HT", kc, bi) for kc in range(8)]
                for c in range(4):
                    bk = psum()
                    for kc in range(8):
                        mm(PS(bk)[:, 0:n], WQ_[:, kc, c * 128:(c + 1) * 128], HT[:, kc, n0:n0 + n], kc == 0, kc == 7, [wk, hk[kc]], [("ps", bk)])
                    act(qa[0:64, c, 0:n], PS(bk)[0:64, 0:n], AF.Identity, [("ps", bk), ("qa", c)], [("qa", c)], scale=0.125)
                    act(qb[64:128, c, 0:n], PS(bk)[64:128, 0:n], AF.Identity, [("ps", bk), ("qb", c)], [("qb", c)], scale=0.125)
                bk = psum()
                for kc in range(8):
                    mm(PS(bk)[:, 0:n], WQ_[:, kc, 512:640], HT[:, kc, n0:n0 + n], kc == 0, kc == 7, [wk, hk[kc]], [("ps", bk)])
                cp("act", skT[:, n0:n0 + n], PS(bk)[:, 0:n], [("ps", bk), ("skT", bi)], [("skT", bi)])
                if smp:
                    pa = pP[0].ap
                    qS = [attn.ap.rearrange("p (s c i) -> p s c i", s=16, c=4),
                          AP(pa.tensor, pa.offset, [list(pa.ap[0]), [1, 512]]).rearrange("p (s c i) -> p s c i", s=16, c=4)]
                    qSk = ["attn", ("pP", 0), ("pP", 1)]
                    cp("dve", qS[0], qa[:, :, 0:128].rearrange("p c (s i) -> p s c i", i=8), [("qa", c) for c in range(4)] + qSk, qSk)
                    cp("dve", qS[1], qb[:, :, 0:128].rearrange("p c (s i) -> p s c i", i=8), [("qb", c) for c in range(4)] + qSk, qSk)
                for ci in range(n // 128):
                    cidx = bi * 4 + ci
                    c0 = n0 + ci * 128
                    sl = slice(ci * 128, (ci + 1) * 128)
                    bv = psum()
                    for kc in range(8):
                        mm(PS(bv)[:, 0:128], HT[:, kc, c0:c0 + 128], WQ_[:, kc, 640:768], kc == 0, kc == 7, [wk, hk[kc]], [("ps", bv)])
                    cp("act", svt[:, cidx, :], PS(bv)[:, 0:128], [("ps", bv), ("svt", cidx)], [("svt", cidx)])
                    if cidx >= 15:
                        bk2 = psum()
                        for kc in range(8):
                            mm(PS(bk2)[:, 0:128], HT[:, kc, c0:c0 + 128], WQ_[:, kc, 512:640], kc == 0, kc == 7, [wk, hk[kc]], [("ps", bk2)])
                        cp("dve", kvf[:, 0, :], PS(bk2)[:, 0:128], [("ps", bk2), "kvf"], ["kvf"])
                        cp("dve", kvf[:, 1, :], PS(bv)[:, 0:128], [("ps", bv), "kvf"], ["kvf"])
                        if cidx == 15:
                            P.dma("sp", rO, o_k_p[l], kvf[:, 0, :], r=["kvf"], w=[("o_k_p", l)], is_out=True)
                            P.dma("sp", rO, o_v_p[l], kvf[:, 1, :], r=["kvf"], w=[("o_v_p", l)], is_out=True)
                        else:
                            for s_ in range(16):
                                P.dma("sp", rO, o_k_s[l, s_, 120:128, :], kvf[s_ * 8:(s_ + 1) * 8, 0, :], r=["kvf"], w=[("o_k_s", l, 1 + s_)], is_out=True)
                                P.dma("sp", rO, o_v_s[l, s_, 120:128, :], kvf[s_ * 8:(s_ + 1) * 8, 1, :], r=["kvf"], w=[("o_v_s", l, 1 + s_)], is_out=True)
                    if not smp:
                        ba = psum()
                        reserved.add(ba)
                        if cidx == 0:
                            k0, nk, boff = c0, 128, 128
                        else:
                            k0, nk, boff = c0 - 128, 256, 0
                        nj = nk // 128
                        lbk = []
                        for hp in range(4):
                            bl = psum()
                            reserved.add(bl)
                            lbk.append(bl)
                            for hh in range(2):
                                cbi = hp * 2 + hh
                                c_, g_ = cbi // 2, cbi % 2
                                qsrc = qa if g_ == 0 else qb
                                qk = ("qa", c_) if g_ == 0 else ("qb", c_)
                                mm(PS(bl)[:, hh * 256:hh * 256 + nk], qsrc[:, c_, sl], skT[:, k0:k0 + nk], True, False,
                                   [qk, ("skT", bi), ("skT", max(bi - 1, 0))], [("ps", bl)])
                                mm(PS(bl)[:, hh * 256:hh * 256 + nk], identb[:, :], biasP[:, cbi, boff:boff + nk], False, True,
                                   ["identb", "biasP"], [("ps", bl)])
                        for hp in range(4):
                            bl = lbk[hp]
                            P.op("dve", lambda e, bl=bl, nk=nk, hp=hp: e.reduce_max(
                                out=stat[:, 0, hp * 2:hp * 2 + 2], in_=PS(bl)[:, 0:512].rearrange("p (a k) -> p a k", a=2)[:, :, 0:nk], axis=AX.X),
                                r=[("ps", bl), "stat"], w=["stat"])
                        tt("dve", stat[:, 1, 0:8], stat[:, 0, 0:8], sinkp[:, l, 0:8], ALU.max, ["stat", "sinkp"], ["stat"])
                        ts("dve", stat[:, 1, 0:8], stat[:, 1, 0:8], -1.0, None, ALU.mult, None, ["stat"], ["stat"])
                        tt("dve", stat2[:, 0, 8:16], sinkp[:, l, 0:8], stat[:, 1, 0:8], ALU.add, ["stat", "sinkp", "stat2"], ["stat2"])
                        act(stat[:, 3, 0:8], stat2[:, 0, 8:16], AF.Exp, ["stat2", "stat"], ["stat"])
                        ms("dve", stat[:, 2, 0:8], 0.0, ["stat"])
                        for cbi in range(8):
                            bl = lbk[cbi // 2]
                            pk = ("pP", cbi)
                            act(pP[cbi][:, 0:nk], PS(bl)[:, (cbi % 2) * 256:(cbi % 2) * 256 + nk], AF.Exp, [("ps", bl), "stat", pk], [pk, "stat"],
                                bias=stat[:, 1, cbi:cbi + 1], accum=stat[:, 2, cbi:cbi + 1])
                        for bl in lbk:
                            reserved.discard(bl)
                        bts = [psum(), psum()]
                        for q_ in range(2):
                            reserved.add(bts[q_])
                        for cbi in range(8):
                            for j in range(nj):
                                tr(PSb(bts[cbi // 4])[:, ((cbi % 4) * 2 + j) * 128:((cbi % 4) * 2 + j + 1) * 128], pP[cbi][:, j * 128:(j + 1) * 128],
                                   [("pP", cbi)], [("ps", bts[cbi // 4])])
                        for q_ in range(2):
                            tk = ("pT", q_)
                            srcv = PSb(bts[q_])[:, 0:1024].rearrange("p (h j i) -> p h j i", h=4, j=2)[:, :, 0:nj, :]
                            dstv = pT[q_][:, :, 0:nj, :]
                            cp("act" if q_ == 0 else "dve", dstv, srcv, [("ps", bts[q_]), tk], [tk])
                            reserved.discard(bts[q_])
                        for cbi in range(8):
                            c_, g_ = cbi // 2, cbi % 2
                            col = c_ * 128 + g_ * 64
                            for j in range(nj):
                                kci = cidx - (nj - 1) + j
                                mm(PS(ba)[:, col:col + 64], pT[cbi // 4][:, cbi % 4, j, :], svt[:, kci, g_ * 64:(g_ + 1) * 64], j == 0, j == nj - 1,
                                   [("pT", cbi // 4), ("svt", kci)], [("ps", ba)])
                        tt("dve", stat2[:, 0, 0:8], stat[:, 2, 0:8], stat[:, 3, 0:8], ALU.add, ["stat", "stat2"], ["stat2"])
                        P.op("dve", lambda e: e.reciprocal(out=stat2[:, 1, 0:8], in_=stat2[:, 0, 0:8]), r=["stat2"], w=["stat2"])
                        tt("dve", attn[:, :].rearrange("p (h e) -> p h e", e=64), PS(ba)[:, 0:512].rearrange("p (h e) -> p h e", e=64),
                           _ins(stat2[:, 1, 0:8], 2, 64), ALU.mult, [("ps", ba), "stat2", "attn"], ["attn"])
                        reserved.discard(ba)
                        bt = psum()
                        for c in range(4):
                            tr(PSb(bt)[:, c * 128:(c + 1) * 128], attn[:, c * 128:(c + 1) * 128], ["attn"], [("ps", bt)])
                        cp("act", swaT[:, :, sl], PSb(bt)[:, 0:512].rearrange("p (c i) -> p c i", c=4), [("ps", bt)] + [("swaT", c) for c in range(4)],
                           [("swaT", c) for c in range(4)])
                    else:
                        for s4 in range(4):
                            if s4 % 2 == 0:
                                load_cache_half(s4 // 2)
                            ba = psum()
                            reserved.add(ba)
                            ms("dve", stat[:, 2:4, :], 0.0, ["stat"])
                            for sg in range(8):
                                s_ = s4 * 4 + sg // 2
                                g_ = sg % 2
                                qks = qSk
                                bl = psum()
                                lq = qS[g_][:, s_, :, :].rearrange("p c i -> p (c i)")
                                mm(PS(bl)[0:32, 0:128], lq, kTc[:, s_ % 8, :], True, False, qks + ["kTc"], [("ps", bl)])
                                mm(PS(bl)[0:32, 0:128], identb[0:32, 0:32], biasS[:, g_, 0:128], False, True, ["identb", "biasS"], [("ps", bl)])
                                mm(PS(bl)[0:32, 128:136], lq, skT[:, n0 + s_ * 8:n0 + (s_ + 1) * 8], True, False, qks + [("skT", bi)], [("ps", bl)])
                                mm(PS(bl)[0:32, 128:136], identb[0:32, 0:32], biasS[:, g_, 128:136], False, True, ["identb", "biasS"], [("ps", bl)])
                                col = sg
                                P.op("dve", lambda e, bl=bl, col=col: e.reduce_max(out=stat[0:32, 0, col:col + 1], in_=PS(bl)[0:32, 0:136], axis=AX.X),
                                     r=[("ps", bl), "stat"], w=["stat"])
                                ts("dve", stat[0:32, 1, col:col + 1], stat[0:32, 0, col:col + 1], sinks[:, l, g_:g_ + 1], -1.0, ALU.max, ALU.mult,
                                   ["stat", "sinks"], ["stat"])
                                pp = pS[sg % 4]
                                pk = ("pS", sg % 4)
                                ms("pool", pp[:, 128:256], 0.0, [pk])
                                act(pp[:, 0:128], PS(bl)[0:32, 0:128], AF.Exp, [("ps", bl), "stat", pk], [pk, "stat"],
                                    bias=stat[0:32, 1, col:col + 1], accum=stat[0:32, 2, col:col + 1])
                                act(pp[:, 128 + s_ * 8:128 + (s_ + 1) * 8], PS(bl)[0:32, 128:136], AF.Exp, [("ps", bl), "stat", pk], [pk, "stat"],
                                    bias=stat[0:32, 1, col:col + 1], accum=stat[0:32, 3, col:col + 1])
                                act(stat2[0:32, 0, col:col + 1], sinks[:, l, g_:g_ + 1], AF.Exp, ["sinks", "stat", "stat2"], ["stat2"],
                                    bias=stat[0:32, 1, col:col + 1])
                                bt = psum()
                                tr(PSb(bt)[:, 0:32], pp[:, 0:128], [pk], [("ps", bt)])
                                tr(PSb(bt)[:, 32:64], pp[:, 128:256], [pk], [("ps", bt)])
                                pt = pTs[sg % 4]
                                tk = ("pTs", sg % 4)
                                cp("act", pt[:, :, :].rearrange("p a i -> p (a i)"), PSb(bt)[:, 0:64], [("ps", bt), tk], [tk])
                                oc = (sg // 2) * 128 + g_ * 64
                                mm(PS(ba)[0:32, oc:oc + 64], pt[:, 0, :], vc[:, s_ % 8, g_ * 64:(g_ + 1) * 64], True, False, [tk, "vc"], [("ps", ba)])
                                mm(PS(ba)[0:32, oc:oc + 64], pt[:, 1, :], svt[:, 16, g_ * 64:(g_ + 1) * 64], False, True, [tk, ("svt", 16)], [("ps", ba)])
                            tt("dve", stat2[0:32, 1, 0:8], stat[0:32, 2, 0:8], stat[0:32, 3, 0:8], ALU.add, ["stat", "stat2"], ["stat2"])
                            tt("dve", stat2[0:32, 1, 0:8], stat2[0:32, 1, 0:8], stat2[0:32, 0, 0:8], ALU.add, ["stat2"], ["stat2"])
                            P.op("dve", lambda e: e.reciprocal(out=stat2[0:32, 1, 8:16], in_=stat2[0:32, 1, 0:8]), r=["stat2"], w=["stat2"])
                            tt("dve", attS[:, :, :].rearrange("p s (g e) -> p (s g) e", g=2), PS(ba)[0:32, 0:512].rearrange("p (h e) -> p h e", e=64),
                               _ins(stat2[0:32, 1, 8:16], 2, 64), ALU.mult, [("ps", ba), "stat2", "attS"], ["attS"])
                            reserved.discard(ba)
                            bt = psum()
                            for j in range(4):
                                tr(PSb(bt)[:, j * 32:(j + 1) * 32], attS[:, j, :], ["attS"], [("ps", bt)])
                            sw_ap = swaT.ap
                            dst = AP(sw_ap.tensor, sw_ap.offset + s4 * 32, [list(sw_ap.ap[0]), [512, 4], [8, 4], [1, 8]])
                            src = PSb(bt)[:, 0:128].rearrange("p (s c i) -> p c s i", s=4, c=4)
                            cp("act", dst, src, [("ps", bt)] + [("swaT", c) for c in range(4)], [("swaT", c) for c in range(4)])
                if DBG and l == 0 and bi == 3:
                    for c in range(4):
                        P.dma("pool", rO, yT[512 + c * 128:512 + (c + 1) * 128, 0:512], swaT[:, c, 0:512], r=[("swaT", c)], w=[("dbg", 4 + c)], is_out=True)
                if DBG and l == 0 and bi == 4:
                    for c in range(4):
                        P.dma("pool", rO, yT[512 + c * 128:512 + (c + 1) * 128, 512:640], swaT[:, c, 0:128], r=[("swaT", c)], w=[("dbgs", 4 + c)], is_out=True)
                for fo in range(8):
                    bk = psum()
                    for c in range(4):
                        mm(PS(bk)[:, 0:n], WO_[:, c, fo * 128:(fo + 1) * 128], swaT[:, c, 0:n], c == 0, c == 3,
                           [wk, ("swaT", c)], [("ps", bk)])
                    accum(bi, fo, bk, 0, acct)
            make_tables(True, 1, lnp[:, l, 0, :], lnp[:, l, 1, :], False)

        def ffn_pass(l, gi, last_layer, WU_, WD_, wk):
            ar = Arena()
            ubuf = [ar.f32([128, 514]) for _ in range(4)]
            ca = [ar.f32([128, 512]), ar.f32([128, 512])]
            cb = [ar.f32([128, 512]), ar.f32([128, 512])]
            gT = [ar.bf16([128, 3, 512]), ar.bf16([128, 3, 512])]
            halos = ar.f32([128, 6, 16, 2])
            scv = [ar.f32([32, 128]) for _ in range(6)]
            acct = [ar.f32([128, 128]), ar.f32([128, 512]), ar.f32([128, 512])]
            lb = LNBuf(ar)
            u0, nu = FGROUPS[gi]
            firstg = gi == 0
            lastg = gi == len(FGROUPS) - 1
            if firstg:
                ms("pool", halo[:], 0.0, ["halo"])
            for ab in range(2):
                for ui in range(nu):
                    j = ab * 22 + u0 + ui
                    P.dma("sp", rT, scv[ab * 3 + ui][:, :], state_conv[l, :, j * 128:(j + 1) * 128], w=[("scv", ab * 3 + ui)])

            def load_halos():
                for ab in range(2):
                    for ui in range(nu):
                        k = ab * 3 + ui
                        bt = psum()
                        P.op("pe", lambda e, bt=bt, k=k: e.transpose(PS(bt)[:, 0:32], scv[k][:, :], identF[0:32, 0:32]),
                             r=[("scv", k), "identF"], w=[("ps", bt)])
                        cp("act", halos[:, k, :, :].rearrange("p s r -> p (s r)"), PS(bt)[:, 0:32], [("ps", bt), "halos"], ["halos"])
            if lastg:
                if last_layer:
                    make_tables(False, 0, lnp[:, l, 2, :], lnp[:, l, 3, :], True)
                else:
                    make_tables(True, 0, lnp[:, l, 2, :], lnp[:, l, 3, :], False)

            def stage1(bi):
                n0, n = BLOCKS[bi]
                smp = bi == 4
                g_ = gT[bi % 2]
                if firstg:
                    ln_finish_block(bi, lb, True)
                hk = [("HT", kc, bi) for kc in range(8)]

                def front(ui):
                    u = u0 + ui
                    cres = []
                    for ab in range(2):
                        j = ab * 22 + u
                        bk = psum()
                        for kc in range(8):
                            mm(PS(bk)[:, 0:n], WU_[:, ui, kc, ab * 128:(ab + 1) * 128], HT[:, kc, n0:n0 + n], kc == 0, kc == 7, [wk, hk[kc]], [("ps", bk)])
                        ub = ubuf[(ui % 2) * 2 + ab]
                        uk = ("ubuf", (ui % 2) * 2 + ab)
                        cc = (ca if ab == 0 else cb)[ui % 2]
                        ck = ("c", ab, ui % 2)
                        w0 = convp[:, l, j, 0:1]
                        w1 = convp[:, l, j, 1:2]
                        w2 = convp[:, l, j, 2:3]
                        bb = convp[:, l, j, 3:4]
                        if not smp:
                            cp("act", ub[:, 2:2 + n], PS(bk)[:, 0:n], [("ps", bk), uk], [uk])
                            cp("pool", ub[:, 0:2], halo[:, j, :], ["halo", uk], [uk])
                            cp("pool", halo[:, j, :], ub[:, n:n + 2], [uk, "halo"], ["halo"])
                            ts("pool", cc[:, 0:n], ub[:, 2:2 + n], w2, bb, ALU.mult, ALU.add, [uk, "convp", ck], [ck])
                            stt("dve", cc[:, 0:n], ub[:, 1:1 + n], w1, cc[:, 0:n], ALU.mult, ALU.add, [uk, "convp", ck], [ck])
                            stt("dve", cc[:, 0:n], ub[:, 0:n], w0, cc[:, 0:n], ALU.mult, ALU.add, [uk, "convp", ck], [ck])
                        else:
                            u3 = ub[:, 0:160].rearrange("p (s t) -> p s t", t=10)
                            hs = halos[:, ab * 3 + ui, :, :]
                            cp("act", u3[:, :, 2:10], PS(bk)[:, 0:128].rearrange("p (s i) -> p s i", i=8), [("ps", bk), uk], [uk])
                            cp("pool", u3[:, :, 0:2], hs, ["halos", uk], [uk])
                            cp("pool", hs, u3[:, :, 8:10], [uk, "halos"], ["halos"])
                            P.dma("sp", rO, o_conv_s[l, :, j, :, :], hs, r=["halos"], w=[("o_conv_s", l, j)], is_out=True)
                            c3 = cc[:, 0:128].rearrange("p (s i) -> p s i", i=8)
                            ts("pool", c3, u3[:, :, 2:10], w2, bb, ALU.mult, ALU.add, [uk, "convp", ck], [ck])
                            stt("dve", c3, u3[:, :, 1:9], w1, c3, ALU.mult, ALU.add, [uk, "convp", ck], [ck])
                            stt("dve", c3, u3[:, :, 0:8], w0, c3, ALU.mult, ALU.add, [uk, "convp", ck], [ck])
                        cres.append((cc, ck))
                    return cres

                def back(ui, cres):
                    (cca, cka), (ccb, ckb) = cres
                    act(cca[:, 0:n], cca[:, 0:n], AF.Silu, [cka], [cka])
                    tt("dve", g_[:, ui, 0:n], cca[:, 0:n], ccb[:, 0:n], ALU.mult, [cka, ckb, ("gT", bi % 2, ui)], [("gT", bi % 2, ui)])

                prev = None
                for ui in range(nu):
                    cr = front(ui)
                    if prev is not None:
                        back(*prev)
                    prev = (ui, cr)
                back(*prev)

            def stage2(bi):
                drain(1)
                n0, n = BLOCKS[bi]
                g_ = gT[bi % 2]
                for fo in range(8):
                    bk = psum()
                    for ui in range(nu):
                        mm(PS(bk)[:, 0:n], WD_[:, ui, fo * 128:(fo + 1) * 128], g_[:, ui, 0:n], ui == 0, ui == nu - 1,
                           [wk, ("gT", bi % 2, ui)], [("ps", bk)])
                    accum(bi, fo, bk, 1, acct)
                if lastg:
                    ln_finish_block(bi, lb, not last_layer)
                    if last_layer and not DBG:
                        for fc in range(8):
                            P.dma("sp", rO, yT[fc * 128:(fc + 1) * 128, n0:n0 + n], X[:, fc, n0:n0 + n], r=[("X", fc, bi)], w=[("yT", fc, bi)], is_out=True)

            nb = len(BLOCKS)
            stage1(0)
            for bi in range(1, nb):
                if bi == nb - 1:
                    load_halos()
                stage1(bi)
                stage2(bi - 1)
            stage2(nb - 1)
            if lastg:
                P.dma("sp", rO, o_conv_p[l], halo[:], r=["halo"], w=[("o_conv_p", l)], is_out=True)

        passes = []
        for l in range(L):
            for gp in range(4):
                passes.append(("R", l, gp))
            passes.append(("S", l, 0))
            for gi in range(len(FGROUPS)):
                passes.append(("F", l, gi))

        compute_mod(0)
        make_tables(True, 0, None, None, False)
        make_gates()
        ar0 = Arena()
        xh1_0 = ar0.f32([128, 128])
        for bi, (n0b, nb) in enumerate(BLOCKS):
            for sub in range(nb // 128):
                n0 = n0b + sub * 128
                finish(bi, n0, 128, lambda fc, n0=n0: X[:, fc, n0:n0 + 128], [], True, xh1_0)
        views = {}
        views[0] = load_pass_weights(passes[0][0], passes[0][1], passes[0][2], 0)
        drain()
        marks = []
        for pi, (kind, l, idx) in enumerate(passes):
            marks.append((kind, l, idx, P.cnt["pe"]))
            if pi == 0 or passes[pi - 1][0] != kind or not os.environ.get('KNOBAR'):
                P.barrier()
            s = pi % 2
            if pi + 1 < len(passes):
                nk, nl, nidx = passes[pi + 1]
                views[pi + 1] = load_pass_weights(nk, nl, nidx, (pi + 1) % 2)
                drain(1)
            v1, v2 = views.pop(pi)
            wk = ("W", s)
            if kind == "R":
                ret_pass(l, idx, v1, v2, wk)
                drain()
            elif kind == "S":
                swa_pass(l, v1, v2, wk)
                drain()
                if l + 1 < L:
                    P.barrier()
                    compute_mod(l + 1)
                    make_gates_next = True
            else:
                if idx == 0 and l + 1 <= L:
                    pass
                ffn_pass(l, idx, l == L - 1, v1, v2, wk)
                drain()
                if idx == len(FGROUPS) - 1 and l + 1 < L:
                    make_gates()

        marks.append(("END", 0, 0, P.cnt["pe"]))
        if os.environ.get("KMARKS"):
            import json
            json.dump(marks, open(os.environ["KMARKS"], "w"))
        with nc.Block() as block:
            P.emit(block)
    return nc


def _consts():
    f32 = np.float32
    h = np.arange(8)
    gam = 1.0 - 2.0 ** (-5.0 - h.astype(np.float64))
    inv = (1.0 / (np.float32(10000.0) ** (np.arange(32, dtype=f32) / np.float32(32)))).astype(f32)
    rot = np.zeros((17, 128, 4, 128), f32)
    for c in range(17):
        if c < 16:
            pos = (c * 128 + np.arange(128)).astype(f32)
        else:
            pos = (PAST + (np.arange(128) % 8)).astype(f32)
        ang = pos[:, None] * inv[None, :]
        cs, sn = np.cos(ang).astype(f32), np.sin(ang).astype(f32)
        for p in range(128):
            d = p % 64
            f = d % 32
            sg = -1.0 if d < 32 else 1.0
            rot[c, p, 0] = cs[:, f]
            rot[c, p, 1] = sg * sn[:, f]
            rot[c, p, 2] = cs[:, f] * 0.125
            rot[c, p, 3] = sg * sn[:, f] * 0.125
    i = np.arange(128)
    decm = np.zeros((128, 8, 128), f32)
    for hh in range(8):
        diff = i[None, :] - i[:, None]
        decm[:, hh, :] = np.where(diff >= 0, gam[hh] ** np.maximum(diff, 0), 0.0)
    decm = decm * np.float32(0.125)
    blkind = (i[:, None] // 8 == i[None, :] // 8).astype(f32)
    qdec = np.zeros((128, 4, 2, 128), f32)
    cdec = np.zeros((128, 2, 4, 3), f32)
    for p in range(128):
        hb = p // 64
        for pair in range(4):
            hh = pair * 2 + hb
            qdec[p, pair, 0] = gam[hh] ** (i + 1)
            qdec[p, pair, 1] = gam[hh] ** ((i % 8) + 1)
            for k, C in enumerate((128, 8)):
                dec = gam[hh] ** C
                cdec[p, k, pair] = (dec, dec if hb == 0 else 0.0, dec if hb == 1 else 0.0)
    cdec[:, :, :, 1] = (np.arange(128)[:, None, None] // 64 == 0)
    cdec[:, :, :, 2] = (np.arange(128)[:, None, None] // 64 == 1)
    kdec = np.zeros((128, 2, 8), f32)
    for hh in range(8):
        kdec[:, 0, hh] = gam[hh] ** (127 - i) * 0.125
        kdec[:, 1, hh] = gam[hh] ** (7 - (i % 8)) * 0.125
    hmask = np.zeros((128, 2), f32)
    hmask[:64, 0] = 1
    hmask[64:, 1] = 1
    seqm = (i[:, None] // 8 == np.arange(16)[None, :]).astype(f32)
    ident = np.eye(128, dtype=f32)
    hsel = np.zeros((128, 128), f32)
    hsel[:64, :64] = 1.0 / 64
    hsel[64:, 64:] = 1.0 / 64
    dist = 127 - np.arange(128)
    n = np.maximum(dist, 0)
    nf = np.maximum(n, 16).astype(f32)
    large = 16 + (np.log(nf / np.float32(16)) / np.float32(np.log(128 / 16)) * np.float32(16)).astype(np.int32)
    large = np.minimum(large, 31)
    bucket = np.where(n < 16, n, large)
    ohr = (np.arange(32)[:, None] == bucket[None, :]).astype(f32)
    rot_b = np.zeros((5, 128, 2, 512), f32)
    for c in range(17):
        rot_b[c // 4, :, :, (c % 4) * 128:(c % 4 + 1) * 128] = rot[c][:, 0:2]
    rot = rot_b
    qdec = np.ascontiguousarray(np.tile(qdec, (1, 1, 1, 4)))
    return dict(rot=rot, decm=decm, blkind=blkind, qdec=qdec, kdec=kdec, cdec=cdec, hmask=hmask,
                seqm=seqm, identf=ident, hsel=hsel, ohr=ohr)


_NC_CACHE = {}


def kernel(x_prompt, x_sample, c_prompt, c_sample, state_ret, cache_swa_k, cache_swa_v, state_conv,
           rel_bias, w_ada, b_ada, w_in, swa_sinks, w_out, ln1_g, ln1_b,
           w_up, conv_w, conv_b, w_down, ln2_g, ln2_b):
    f32 = np.float32
    A = lambda a: np.ascontiguousarray(np.asarray(a, dtype=f32))
    x_prompt, x_sample, c_prompt, c_sample = A(x_prompt), A(x_sample), A(c_prompt), A(c_sample)
    state_ret, cache_swa_k, cache_swa_v, state_conv = A(state_ret), A(cache_swa_k), A(cache_swa_v), A(state_conv)
    rel_bias, w_ada, b_ada, w_in, swa_sinks, w_out = A(rel_bias), A(w_ada), A(b_ada), A(w_in), A(swa_sinks), A(w_out)
    ln1_g, ln1_b, w_up, conv_w, conv_b, w_down, ln2_g, ln2_b = A(ln1_g), A(ln1_b), A(w_up), A(conv_w), A(conv_b), A(w_down), A(ln2_g), A(ln2_b)

    sw = np.concatenate([np.arange(32, 64), np.arange(0, 32)])
    w_r = np.zeros((L, 4, D, 768), f32)
    w_or = np.zeros((L, 4, 128, D), f32)
    for gp in range(4):
        hs = np.arange(gp * 2, gp * 2 + 2)
        cols = (hs[:, None] * 64 + np.arange(64)[None, :]).reshape(-1)
        cols_sw = (hs[:, None] * 64 + sw[None, :]).reshape(-1)
        w_r[:, gp, :, 0:128] = w_in[:, :, 0 + cols]
        w_r[:, gp, :, 128:256] = w_in[:, :, 0 + cols_sw]
        w_r[:, gp, :, 256:384] = w_in[:, :, 512 + cols]
        w_r[:, gp, :, 384:512] = w_in[:, :, 512 + cols_sw]
        w_r[:, gp, :, 512:640] = w_in[:, :, 1024 + cols]
        w_r[:, gp, :, 640:768] = w_in[:, :, 1536 + cols]
        w_or[:, gp] = w_out[:, cols, :]
    perm = np.concatenate([np.concatenate([np.arange(c * 64, c * 64 + 64), np.arange((4 + c) * 64, (4 + c) * 64 + 64)]) for c in range(4)])
    w_s = np.concatenate([w_in[:, :, 2048 + perm], w_in[:, :, 2560:2816]], axis=2)
    w_os = w_out[:, 512 + perm, :]
    w_r = np.ascontiguousarray(w_r.reshape(L, 4, 8, 128, 768).transpose(0, 1, 3, 2, 4))
    w_s = np.ascontiguousarray(w_s.reshape(L, 8, 128, 768).transpose(0, 2, 1, 3))
    w_os = np.ascontiguousarray(w_os.reshape(L, 4, 128, D).transpose(0, 2, 1, 3))
    wu = np.concatenate([w_up[:, :, :DFF].reshape(L, 8, 128, NU, 128), w_up[:, :, DFF:].reshape(L, 8, 128, NU, 128)], axis=4)
    w_up = np.ascontiguousarray(wu.transpose(0, 3, 2, 1, 4))
    w_down = np.ascontiguousarray(w_down.reshape(L, NU, 128, D).transpose(0, 2, 1, 3))
    bT_ada = np.ascontiguousarray(b_ada.reshape(L, 48, 128).transpose(2, 0, 1))
    convT = np.ascontiguousarray(np.concatenate([conv_w, conv_b[:, None, :]], axis=1).reshape(L, 4, 44, 128).transpose(3, 0, 2, 1))
    lnT = np.ascontiguousarray(np.stack([ln1_g, ln1_b, ln2_g, ln2_b], axis=1).reshape(L, 4, 8, 128).transpose(3, 0, 1, 2))
    cb_heads = np.array([(cbi % 2) * 4 + cbi // 2 for cbi in range(8)])
    sinkP = np.ascontiguousarray(np.broadcast_to(swa_sinks[:, cb_heads][None], (128, L, 8)))
    sinkS = np.zeros((32, L, 2), f32)
    for c in range(4):
        for g in range(2):
            sinkS[c * 8:(c + 1) * 8, :, g] = swa_sinks[None, :, g * 4 + c]
    cst = _consts()

    if "nc" not in _NC_CACHE:
        _NC_CACHE["nc"] = build_program()
    nc = _NC_CACHE["nc"]

    in_maps = []
    for b in range(8):
        xall = np.concatenate([x_prompt[b], x_sample[16 * b:16 * b + 16].reshape(128, D)], axis=0)
        call = np.concatenate([c_prompt[b:b + 1], c_sample[16 * b:16 * b + 16]], axis=0)
        m = dict(
            xT=np.ascontiguousarray(xall.T),
            cT=np.ascontiguousarray(call.T.reshape(8, 128, 17).transpose(1, 0, 2)),
            w_ada=w_ada, bT_ada=bT_ada, w_r=w_r, w_s=w_s, w_or=w_or, w_os=w_os, w_up=w_up, w_down=w_down,
            convT=convT, lnT=lnT, sinkP=sinkP, sinkS=sinkS, rel_bias=rel_bias,
            state_ret=np.ascontiguousarray(state_ret[:, 16 * b:16 * b + 16].reshape(L, 16, 4, 128, 64).transpose(0, 2, 3, 1, 4)),
            cache_k=np.ascontiguousarray(cache_swa_k[:, 16 * b:16 * b + 16].reshape(L, 16, 128, 128)),
            cache_v=np.ascontiguousarray(cache_swa_v[:, 16 * b:16 * b + 16].reshape(L, 16, 128, 128)),
            state_conv=np.ascontiguousarray(state_conv[:, 16 * b:16 * b + 16].reshape(L, 32, 2 * DFF)),
        )
        m.update(cst)
        in_maps.append(m)
    res = run_bass_kernel_spmd(nc, in_maps, core_ids=list(range(8)))
    R = res.results

    y_p = np.zeros((8, NP, D), f32)
    y_s = np.zeros((128, 8, D), f32)
    ret_p = np.zeros((L, 8, 8, 64, 64), f32)
    ret_s = np.zeros((L, 128, 8, 64, 64), f32)
    k_p = np.zeros((L, 8, 128, 2, 64), f32)
    v_p = np.zeros((L, 8, 128, 2, 64), f32)
    k_s = np.zeros((L, 128, 128, 2, 64), f32)
    v_s = np.zeros((L, 128, 128, 2, 64), f32)
    conv_p = np.zeros((L, 8, 2, 2 * DFF), f32)
    conv_s = np.zeros((L, 128, 2, 2 * DFF), f32)
    for b in range(8):
        r = R[b]
        yt = r["yT"].T
        y_p[b] = yt[:NP]
        y_s[16 * b:16 * b + 16] = yt[NP:].reshape(16, 8, D)
        ret_p[:, b] = r["o_ret_p"].reshape(L, 4, 2, 64, 64).reshape(L, 8, 64, 64)
        ret_s[:, 16 * b:16 * b + 16] = r["o_ret_s"].reshape(L, 4, 2, 64, 16, 64).transpose(0, 4, 1, 2, 3, 5).reshape(L, 16, 8, 64, 64)
        k_p[:, b] = r["o_k_p"].reshape(L, 128, 2, 64)
        v_p[:, b] = r["o_v_p"].reshape(L, 128, 2, 64)
        k_s[:, 16 * b:16 * b + 16] = r["o_k_s"].reshape(L, 16, 128, 2, 64)
        v_s[:, 16 * b:16 * b + 16] = r["o_v_s"].reshape(L, 16, 128, 2, 64)
        conv_p[:, b] = r["o_conv_p"].transpose(0, 3, 2, 1).reshape(L, 2, 2 * DFF)
        conv_s[:, 16 * b:16 * b + 16] = r["o_conv_s"].transpose(0, 3, 4, 2, 1).reshape(L, 16, 2, 2 * DFF)
    return (y_p, y_s, ret_p, k_p, v_p, conv_p, ret_s, k_s, v_s, conv_s)
```

```python
import numpy as np
import ml_dtypes
import concourse.bass as bass
import concourse.mybir as mybir
from concourse.ap import AP
from concourse.bass_utils import run_bass_kernel_spmd

F32 = mybir.dt.float32
BF16 = mybir.dt.bfloat16
ALU = mybir.AluOpType
AF = mybir.ActivationFunctionType
AX = mybir.AxisListType

D = 1024
L = 4
NT = 2176
NP = 2048
DFF = 2816
NU = 22
ALPHA = (2.0 * L) ** 0.25
EPS = 1e-5
PAST = 8192
BLOCKS = [(0, 512), (512, 512), (1024, 512), (1536, 512), (2048, 128)]
FGROUPS = [(0, 3), (3, 3), (6, 3), (9, 3), (12, 3), (15, 3), (18, 2), (20, 2)]
SLOT_ELEMS = 10240
ARENA_F32 = 12050
import os
DBG = int(os.environ.get('KDBG', '0'))


class Ring:
    def __init__(self, sems):
        self.sems = sems
        self.vals = [0] * len(sems)
        self.i = 0


class Prog:
    ENG = ("pe", "act", "dve", "pool", "sp")

    def __init__(self, nc, esem):
        self.nc = nc
        self.esem = esem
        self.q = {e: [] for e in self.ENG}
        self.cnt = {e: 0 for e in self.ENG}
        self.seen = {e: {} for e in self.ENG}
        self.res = {}
        self.out_toks = []
        self.recent = []

    def _need(self, eng, tok):
        sem, val, teng, kind = tok
        sid = id(sem)
        if self.seen[eng].get(sid, 0) >= val:
            return
        self.seen[eng][sid] = val
        self.q[eng].append(("wait", sem, val))

    def _deps(self, eng, r, w, is_dma):
        toks = []
        for k in r:
            st = self.res.get(k)
            if st and st["w"] is not None:
                toks.append(st["w"])
        for k in w:
            st = self.res.get(k)
            if not st:
                continue
            if st["w"] is not None:
                t = st["w"]
                if not (eng == "pe" and t[3] == "op" and t[2] == "pe"):
                    toks.append(t)
            for t in st["r"]:
                if not (eng == "pe" and t[3] == "op" and t[2] == "pe"):
                    toks.append(t)
        best = {}
        for t in toks:
            sid = id(t[0])
            if sid not in best or best[sid][1] < t[1]:
                best[sid] = t
        for t in best.values():
            self._need(eng, t)

    def _record(self, tok, r, w):
        for k in r:
            st = self.res.setdefault(k, {"w": None, "r": []})
            st["r"].append(tok)
        for k in w:
            self.res[k] = {"w": tok, "r": []}

    def op(self, eng, fn, r=(), w=()):
        self._deps(eng, r, w, False)
        self.cnt[eng] += 1
        tok = (self.esem[eng], self.cnt[eng], eng, "op")
        self.q[eng].append(("op", fn))
        self._record(tok, r, w)

    def dma(self, q, ring, out, in_, r=(), w=(), is_out=False):
        self._deps(q, r, w, True)
        s = ring.i
        ring.i = (ring.i + 1) % len(ring.sems)
        sem = ring.sems[s]
        if ring.vals[s] > 0:
            self._need(q, (sem, ring.vals[s], None, "dma"))
        ring.vals[s] += 16
        tok = (sem, ring.vals[s], None, "dma")
        self.q[q].append(("dma", out, in_, sem))
        self.recent.append(tok)
        self._record(tok, r, w)
        if is_out:
            self.out_toks.append(tok)

    def barrier(self):
        for e in self.ENG:
            for t in self.recent:
                self._need(e, t)
            for o in self.ENG:
                if o != e and self.cnt[o] > 0:
                    self._need(e, (self.esem[o], self.cnt[o], o, "op"))
        self.recent = []

    def emit(self, block):
        nc = self.nc
        engs = {"pe": nc.tensor, "act": nc.scalar, "dve": nc.vector, "pool": nc.gpsimd, "sp": nc.sync}
        for t in self.out_toks:
            self._need("sp", t)

        def run(name):
            e = engs[name]
            for it in self.q[name]:
                if it[0] == "wait":
                    e.wait_ge(it[1], it[2])
                elif it[0] == "op":
                    it[1](e).then_inc(self.esem[name], 1)
                else:
                    e.dma_start(out=it[1], in_=it[2]).then_inc(it[3], 16)

        @block.tensor
        def _(t):
            run("pe")

        @block.scalar
        def _(t):
            run("act")

        @block.vector
        def _(t):
            run("dve")

        @block.gpsimd
        def _(t):
            run("pool")

        @block.sync
        def _(t):
            run("sp")


def _ins(ap, pos, cnt):
    l = [list(x) for x in ap.ap]
    l.insert(pos, [0, cnt])
    return AP(ap.tensor, ap.offset, l)


def build_program():
    nc = bass.Bass("TRN2", target_bir_lowering=False)

    def din(name, shape, dt=F32):
        return nc.dram_tensor(name, list(shape), dt, kind="ExternalInput").ap()

    def dout(name, shape):
        return nc.dram_tensor(name, list(shape), F32, kind="ExternalOutput").ap()

    xT = din("xT", [D, NT])
    cT = din("cT", [128, 8, 17])
    w_ada = din("w_ada", [L, D, 6 * D])
    bT_ada = din("bT_ada", [128, L, 48])
    w_r = din("w_r", [L, 4, 128, 8, 768])
    w_s = din("w_s", [L, 128, 8, 768])
    w_or = din("w_or", [L, 4, 128, D])
    w_os = din("w_os", [L, 128, 4, D])
    w_up = din("w_up", [L, NU, 128, 8, 256])
    w_down = din("w_down", [L, 128, NU, D])
    convT = din("convT", [128, L, 44, 4])
    lnT = din("lnT", [128, L, 4, 8])
    sinkP = din("sinkP", [128, L, 8])
    sinkS = din("sinkS", [32, L, 2])
    rel_bias = din("rel_bias", [32, 8])
    ohr = din("ohr", [32, 128])
    rot = din("rot", [5, 128, 2, 512])
    decm = din("decm", [128, 8, 128])
    blkind = din("blkind", [128, 128])
    qdec = din("qdec", [128, 4, 2, 512])
    kdec = din("kdec", [128, 2, 8])
    cdec = din("cdec", [128, 2, 4, 3])
    hmask = din("hmask", [128, 2])
    seqm = din("seqm", [128, 16])
    identf = din("identf", [128, 128])
    hsel = din("hsel", [128, 128])
    state_ret = din("state_ret", [L, 4, 128, 16, 64])
    cache_k = din("cache_k", [L, 16, 128, 128])
    cache_v = din("cache_v", [L, 16, 128, 128])
    state_conv = din("state_conv", [L, 32, 2 * DFF])

    yT = dout("yT", [D, NT])
    o_ret_p = dout("o_ret_p", [L, 4, 128, 64])
    o_ret_s = dout("o_ret_s", [L, 4, 128, 16, 64])
    o_k_p = dout("o_k_p", [L, 128, 128])
    o_v_p = dout("o_v_p", [L, 128, 128])
    o_k_s = dout("o_k_s", [L, 16, 128, 128])
    o_v_s = dout("o_v_s", [L, 16, 128, 128])
    o_conv_p = dout("o_conv_p", [L, 128, 44, 2])
    o_conv_s = dout("o_conv_s", [L, 128, 44, 16, 2])
    ext_d = nc.dram_tensor("ext_d", [8, 128, 512], F32, kind="Internal").ap()

    from contextlib import ExitStack
    es = ExitStack()

    def sb(name, shape, dt=F32):
        return es.enter_context(nc.sbuf_tensor(name, list(shape), dt))

    with es:
        X = sb("X", [128, 8, NT])
        HT = sb("HT", [128, 8, NT], BF16)
        WS = [sb("WS0", [128, SLOT_ELEMS], BF16), sb("WS1", [128, SLOT_ELEMS], BF16)]
        ARF = sb("ARENA", [128, ARENA_F32])
        ARB = ARF.bitcast(BF16)
        modT = sb("modT", [128, 48, 17])
        scT = sb("scT", [128, 8, 17], BF16)
        c_sb = sb("c_sb", [128, 8, 17])
        bada = sb("bada", [128, L, 48])
        convp = sb("convp", [128, L, 44, 4])
        lnp = sb("lnp", [128, L, 4, 8])
        sinkp = sb("sinkp", [128, L, 8])
        sinks = sb("sinks", [32, L, 2])
        identb = sb("identb", [128, 128], BF16)
        identF = sb("identF", [128, 128])
        hselb = sb("hselb", [128, 128], BF16)
        onesb = sb("onesb", [128, 128], BF16)
        blk = sb("blk", [128, 128])
        kdecS = sb("kdecS", [128, 2, 8])
        cdecS = sb("cdecS", [128, 2, 4, 3])
        hmS = sb("hmS", [128, 2])
        seqS = sb("seqS", [128, 16])
        epsT = sb("epsT", [128, 1])
        biasS = sb("biasS", [32, 2, 136], BF16)
        tabA = sb("tabA", [128, 4, 8, 17])
        gate = sb("gate", [128, 2, 8, 17])
        Sp = sb("Sp", [128, 4, 64])
        Sblk = sb("Sblk", [128, 4, 2, 64], BF16)
        kvf = sb("kvf", [128, 2, 128])
        stat = sb("stat", [128, 4, 32])
        stat2 = sb("stat2", [128, 2, 32])
        halo = sb("halo", [128, 44, 2])

        PSB = [es.enter_context(nc.psum_tensor("ps%d" % i, [128, 512], F32)) for i in range(8)]

        def sem(name):
            return es.enter_context(nc.semaphore(name))

        esem = {e: sem("e_" + e) for e in Prog.ENG}
        P = Prog(nc, esem)
        rW = Ring([sem("rw%d" % i) for i in range(6)])
        rC = Ring([sem("rc%d" % i) for i in range(6)])
        rO = Ring([sem("ro%d" % i) for i in range(8)])
        rT = Ring([sem("rt%d" % i) for i in range(6)])
        rP = Ring([sem("rp%d" % i) for i in range(6)])

        psi = [0]

        reserved = set()

        def psum():
            while True:
                i = psi[0]
                psi[0] = (i + 1) % 8
                if i not in reserved:
                    return i

        def PS(i):
            return PSB[i]

        def PSb(i):
            return PSB[i].bitcast(BF16)

        class Arena:
            def __init__(self):
                self.off = 0

            def f32(self, shape):
                n = int(np.prod(shape[1:]))
                v = ARF[0:shape[0], self.off:self.off + n]
                self.off += n
                assert self.off <= ARENA_F32, ("arena overflow", self.off)
                return _View(v, shape)

            def bf16(self, shape):
                n = int(np.prod(shape[1:]))
                nw = (n + 1) // 2
                v = ARB[0:shape[0], 2 * self.off:2 * self.off + n]
                self.off += nw
                assert self.off <= ARENA_F32, ("arena overflow", self.off)
                return _View(v, shape)

        class _View:
            def __init__(self, ap, shape):
                self.ap = ap
                self.shape = list(shape)
                if len(shape) == 2:
                    self.v = ap
                else:
                    names = "abcdefg"[:len(shape) - 1]
                    pat = "p (" + " ".join(names) + ") -> p " + " ".join(names)
                    kw = {names[i]: shape[i + 1] for i in range(len(names))}
                    self.v = ap.rearrange(pat, **kw)

            def __getitem__(self, k):
                return self.v[k]

        def mm(out, lhsT, rhs, start, stop, r, w):
            P.op("pe", lambda e: e.matmul(out, lhsT=lhsT, rhs=rhs, start=start, stop=stop), r=r, w=w)

        def tr(out, in_, r, w):
            k = in_.shape[0]
            P.op("pe", lambda e: e.transpose(out, in_, identb[0:k, 0:k]), r=list(r) + ["identb"], w=w)

        def act(out, in_, func, r, w, bias=None, scale=None, accum=None):
            kw = {}
            if bias is not None:
                kw["bias"] = bias
            if scale is not None:
                kw["scale"] = scale
            if accum is not None:
                kw["accum_out"] = accum
            P.op("act", lambda e: e.activation(out=out, in_=in_, func=func, **kw), r=r, w=w)

        def tt(eng, out, in0, in1, op, r, w):
            P.op(eng, lambda e: e.tensor_tensor(out=out, in0=in0, in1=in1, op=op), r=r, w=w)

        def ts(eng, out, in0, s1, s2, op0, op1, r, w):
            if op1 is None:
                P.op(eng, lambda e: e.tensor_scalar(out=out, in0=in0, scalar1=s1, scalar2=None, op0=op0), r=r, w=w)
            else:
                P.op(eng, lambda e: e.tensor_scalar(out=out, in0=in0, scalar1=s1, scalar2=s2, op0=op0, op1=op1), r=r, w=w)

        def stt(eng, out, in0, scalar, in1, op0, op1, r, w):
            P.op(eng, lambda e: e.scalar_tensor_tensor(out=out, in0=in0, scalar=scalar, in1=in1, op0=op0, op1=op1), r=r, w=w)

        def cp(eng, out, in_, r, w):
            if eng == "act":
                P.op("act", lambda e: e.copy(out=out, in_=in_), r=r, w=w)
            else:
                P.op(eng, lambda e: e.tensor_copy(out=out, in_=in_), r=r, w=w)

        def ms(eng, ap, val, w):
            P.op(eng, lambda e: e.memset(ap, val), r=(), w=w)

        def ld(q, ring, out, in_, w, r=()):
            P.dma(q, ring, out, in_, r=r, w=w)

        arS = Arena()
        arS.off = 9000
        ext = arS.f32([8, 512])
        rb = arS.f32([32, 8])
        ohrS = arS.f32([32, 128])
        ld("sp", rC, c_sb[:], cT, ["c_sb"])
        ld("sp", rC, bada[:], bT_ada, ["bada"])
        ld("sp", rC, convp[:], convT, ["convp"])
        ld("sp", rC, lnp[:], lnT, ["lnp"])
        ld("sp", rC, sinkp[:], sinkP, ["sinkp"])
        ld("sp", rC, sinks[:], sinkS, ["sinks"])
        ld("sp", rC, blk[:], blkind, ["blk"])
        ld("sp", rC, kdecS[:], kdec, ["kdec"])
        ld("sp", rC, cdecS[:], cdec, ["cdec"])
        ld("sp", rC, hmS[:], hmask, ["hm"])
        ld("sp", rC, seqS[:], seqm, ["seqm"])
        ld("sp", rC, identF[:], identf, ["identF"])
        ld("sp", rC, rb[:, :], rel_bias, ["rb"])
        ld("sp", rC, ohrS[:, :], ohr, ["ohrS"])
        ld("pool", rP, identb[:], identf, ["identb"])
        ld("pool", rP, hselb[:], hsel, ["hselb"])
        ms("dve", epsT[:], EPS, ["epsT"])
        ms("dve", onesb[:], 1.0 / 1024.0, ["ones"])
        ms("dve", ext[:, :], -1e30, ["ext"])
        for fc in range(8):
            ld("sp", rC, X[:, fc, :], xT[fc * 128:(fc + 1) * 128, :], [("X", fc, b) for b in range(5)])

        b0 = psum()
        mm(PS(b0)[0:8, 0:128], rb[:, :], ohrS[:, :], True, True, ["rb", "ohrS"], [("ps", b0)])
        cp("dve", ext[:, 128:256], PS(b0)[0:8, 0:128], [("ps", b0), "ext"], ["ext"])
        P.dma("sp", rC, ext_d, _ins(ext[:, :], 1, 128), r=["ext"], w=["ext_d"])
        for cbi in range(8):
            c_, g_ = cbi // 2, cbi % 2
            h_ = g_ * 4 + c_
            src2 = AP(ext_d.tensor, h_ * 65536 + 127, [[511, 8], [1, 136]])
            P.dma("pool", rP, biasS[c_ * 8:(c_ + 1) * 8, g_, :], src2, r=["ext_d"], w=["biasS"])
        act(scT[:], c_sb[:], AF.Silu, ["c_sb"], ["scT"])

        deferred = []

        def drain(k=None):
            n_ = len(deferred) if k is None else min(k, len(deferred))
            for _ in range(n_):
                deferred.pop(0)()

        def load_pass_weights(kind, l, idx, s):
            W = WS[s]
            key = ("W", s)

            def q(out, in_):
                deferred.append(lambda: P.dma("pool", rW, out, in_, w=[key]))
            if kind == "R":
                v1 = W[:, 0:8 * 768].rearrange("p (k n) -> p k n", k=8)
                q(v1[:, 0:4, :], w_r[l, idx, :, 0:4, :])
                q(v1[:, 4:8, :], w_r[l, idx, :, 4:8, :])
                v2 = W[:, 6144:6144 + 1024]
                q(v2, w_or[l, idx])
                return v1, v2
            if kind == "S":
                v1 = W[:, 0:8 * 768].rearrange("p (k n) -> p k n", k=8)
                q(v1[:, 0:4, :], w_s[l, :, 0:4, :])
                q(v1[:, 4:8, :], w_s[l, :, 4:8, :])
                v2 = W[:, 6144:6144 + 4096].rearrange("p (k n) -> p k n", k=4)
                q(v2, w_os[l])
                return v1, v2
            u0, nu = FGROUPS[idx]
            vu = W[:, 0:nu * 2048].rearrange("p (u k n) -> p u k n", u=nu, k=8)
            vd = W[:, 6144:6144 + nu * 1024].rearrange("p (u n) -> p u n", u=nu)
            for ui in range(nu):
                q(vu[:, ui, :, :], w_up[l, u0 + ui])
            q(vd, w_down[l, :, u0:u0 + nu, :])
            return vu, vd

        def compute_mod(l):
            ar = Arena()
            WA = [ar.bf16([128, 8, 256]), ar.bf16([128, 8, 256])]
            mk = "modT"
            bks = [psum(), psum()]
            for jj in range(24):
                wa = WA[jj % 2]
                wk = ("WA", jj % 2)
                P.dma("pool", rW, wa[:], w_ada[l, :, jj * 256:(jj + 1) * 256].rearrange("(k p) n -> p k n", p=128), w=[wk])
                for sub in range(2):
                    j = jj * 2 + sub
                    bk = bks[j // 24]
                    for kc in range(8):
                        mm(PS(bk)[:, (j % 24) * 17:(j % 24) * 17 + 17], wa[:, kc, sub * 128:(sub + 1) * 128], scT[:, kc, :],
                           kc == 0, kc == 7, [wk, "scT"], [("ps", bk)])
            for hf in range(2):
                tt("dve", modT[:, hf * 24:(hf + 1) * 24, :],
                   PS(bks[hf])[:, 0:408].rearrange("p (j r) -> p j r", r=17),
                   _ins(bada[:, l, hf * 24:(hf + 1) * 24], 2, 17), ALU.add,
                   [("ps", bks[hf]), "bada"], [mk])

        def make_tables(use_mod, which_mod, g_ap, b_ap, last):
            mk = "modT"
            if use_mod:
                sh = modT[:, which_mod * 24 + 0:which_mod * 24 + 8, :]
                sc = modT[:, which_mod * 24 + 8:which_mod * 24 + 16, :]
                onep = tabA[:, 1, :, :]
                ts("pool", onep, sc, 1.0, None, ALU.add, None, [mk, "tabA"], ["tabA"])
                if g_ap is None:
                    cp("pool", tabA[:, 0, :, :], onep, ["tabA"], ["tabA"])
                    cp("pool", tabA[:, 1, :, :], sh, [mk, "tabA"], ["tabA"])
                else:
                    tt("pool", tabA[:, 0, :, :], onep, _ins(g_ap, 2, 17), ALU.mult, ["tabA", "lnp"], ["tabA"])
                    tt("pool", tabA[:, 1, :, :], onep, _ins(b_ap, 2, 17), ALU.mult, ["tabA", "lnp"], ["tabA"])
                    tt("pool", tabA[:, 1, :, :], tabA[:, 1, :, :], sh, ALU.add, ["tabA", mk], ["tabA"])
            a = 1.0 if last else ALPHA
            if g_ap is None:
                ms("pool", tabA[:, 2, :, 0:1], a, ["tabA"])
                ms("pool", tabA[:, 3, :, 0:1], 0.0, ["tabA"])
            else:
                ts("pool", tabA[:, 2, :, 0:1], _ins(g_ap, 2, 1), a, None, ALU.mult, None, ["lnp", "tabA"], ["tabA"])
                ts("pool", tabA[:, 3, :, 0:1], _ins(b_ap, 2, 1), a, None, ALU.mult, None, ["lnp", "tabA"], ["tabA"])

        def make_gates():
            cp("pool", gate[:, 0, :, :], modT[:, 16:24, :], ["modT", "gate"], ["gate"])
            cp("pool", gate[:, 1, :, :], modT[:, 40:48, :], ["modT", "gate"], ["gate"])

        def finish_fc(bi, n0, n, fc, src, src_keys, do_h, xh1):
            smp = bi == 4
            src_keys = list(src_keys) + [("X", fc, bi)]
            if do_h:
                if not smp:
                    act(HT[:, fc, n0:n0 + n], src, AF.Identity, list(src_keys) + ["tabA"], [("HT", fc, bi)],
                        bias=tabA[:, 1, fc, 0:1], scale=tabA[:, 0, fc, 0:1])
                else:
                    s3 = src.rearrange("p (s i) -> p s i", i=8)
                    x3 = xh1[:, 0:128].rearrange("p (s i) -> p s i", i=8)
                    tt("dve", x3, s3, _ins(tabA[:, 0, fc, 1:17], 2, 8), ALU.mult, list(src_keys) + ["tabA", "xh1"], ["xh1"])
                    tt("dve", HT[:, fc, n0:n0 + n].rearrange("p (s i) -> p s i", i=8), x3,
                       _ins(tabA[:, 1, fc, 1:17], 2, 8), ALU.add, ["xh1", "tabA"], [("HT", fc, bi)])
            ts("dve" if fc % 2 == 0 else "pool", X[:, fc, n0:n0 + n], src, tabA[:, 2, fc, 0:1], tabA[:, 3, fc, 0:1],
               ALU.mult, ALU.add, list(src_keys) + ["tabA", ("X", fc, bi)], [("X", fc, bi)])

        def finish(bi, n0, n, src_fn, src_keys, do_h, xh1):
            for fc in range(8):
                finish_fc(bi, n0, n, fc, src_fn(fc), src_keys, do_h, xh1)

        def accum(bi, fc, bk, gi, tmp):
            n0, n = BLOCKS[bi]
            if bi < 4 and fc % 2 == 1 and len(tmp) > 1:
                tb = tmp[1 + (fc // 2) % 2]
                tkk = ("acct2", (fc // 2) % 2)
                act(tb[:, 0:n], PS(bk)[:, 0:n], AF.Identity, [("ps", bk), "gate", tkk], [tkk], scale=gate[:, gi, fc, 0:1])
                tt("pool", X[:, fc, n0:n0 + n], X[:, fc, n0:n0 + n], tb[:, 0:n], ALU.add, [tkk, ("X", fc, bi)], [("X", fc, bi)])
            elif bi < 4:
                stt("dve", X[:, fc, n0:n0 + n], PS(bk)[:, 0:n], gate[:, gi, fc, 0:1], X[:, fc, n0:n0 + n], ALU.mult, ALU.add,
                    [("ps", bk), "gate", ("X", fc, bi)], [("X", fc, bi)])
            else:
                tt("dve", tmp[0][:, 0:128].rearrange("p (s i) -> p s i", i=8), PS(bk)[:, 0:128].rearrange("p (s i) -> p s i", i=8),
                   _ins(gate[:, gi, fc, 1:17], 2, 8), ALU.mult, [("ps", bk), "gate", "acctmp"], ["acctmp"])
                tt("pool", X[:, fc, n0:n0 + n], X[:, fc, n0:n0 + n], tmp[0][:, 0:128], ALU.add, ["acctmp", ("X", fc, bi)], [("X", fc, bi)])

        class LNBuf:
            def __init__(self, ar):
                self.vb = [ar.bf16([128, 512]), ar.bf16([128, 512])]
                self.vsq = [ar.bf16([128, 512]), ar.bf16([128, 512])]
                self.tmp = [ar.f32([128, 512]) for _ in range(3)]
                self.mean = ar.f32([128, 512])
                self.rstd = ar.f32([128, 512])
                self.xh1 = ar.f32([128, 128])

        def ln_finish_block(bi, lb, do_h):
            n0, n = BLOCKS[bi]
            bm, bq = psum(), psum()
            reserved.add(bm)
            reserved.add(bq)
            for fc in range(8):
                vbk, vsk = ("vb", fc % 2), ("vsq", fc % 2)
                cp("dve", lb.vb[fc % 2][:, 0:n], X[:, fc, n0:n0 + n], [("X", fc, bi), vbk], [vbk])
                act(lb.vsq[fc % 2][:, 0:n], X[:, fc, n0:n0 + n], AF.Square, [("X", fc, bi), vsk], [vsk])
                mm(PS(bm)[:, 0:n], onesb[:, :], lb.vb[fc % 2][:, 0:n], fc == 0, fc == 7, ["ones", vbk], [("ps", bm)])
                mm(PS(bq)[:, 0:n], onesb[:, :], lb.vsq[fc % 2][:, 0:n], fc == 0, fc == 7, ["ones", vsk], [("ps", bq)])
            cp("act", lb.mean[:, 0:n], PS(bm)[:, 0:n], [("ps", bm), "mean"], ["mean"])
            tt("dve", lb.rstd[:, 0:n], lb.mean[:, 0:n], lb.mean[:, 0:n], ALU.mult, ["mean", "rstd"], ["rstd"])
            tt("dve", lb.rstd[:, 0:n], PS(bq)[:, 0:n], lb.rstd[:, 0:n], ALU.subtract, [("ps", bq), "rstd"], ["rstd"])
            reserved.discard(bm)
            reserved.discard(bq)
            act(lb.rstd[:, 0:n], lb.rstd[:, 0:n], AF.Ln, ["rstd", "epsT"], ["rstd"], bias=epsT[:, 0:1])
            act(lb.rstd[:, 0:n], lb.rstd[:, 0:n], AF.Exp, ["rstd"], ["rstd"], scale=-0.5)
            for fc in range(8):
                tk = ("lntmp", fc % 3)
                t_ = lb.tmp[fc % 3]
                tt("pool", t_[:, 0:n], X[:, fc, n0:n0 + n], lb.mean[:, 0:n], ALU.subtract, [("X", fc, bi), "mean", tk], [tk])
                tt("dve", t_[:, 0:n], t_[:, 0:n], lb.rstd[:, 0:n], ALU.mult, [tk, "rstd"], [tk])
                finish_fc(bi, n0, n, fc, t_[:, 0:n], [tk], do_h, lb.xh1)

        def ret_pass(l, gp, WR_, WO_, wk):
            ar = Arena()
            rotS = [ar.f32([128, 2, 512]), ar.f32([128, 2, 512])]
            decmS = ar.f32([128, 2, 128])
            qdecS = ar.f32([128, 2, 512])
            t1 = ar.f32([128, 512])
            t2 = ar.f32([128, 512])
            t3 = t1
            t4 = t2
            qp = ar.bf16([128, 512])
            qpp = ar.bf16([128, 512])
            kf = ar.bf16([128, 512])
            ka = ar.bf16([128, 512])
            kb = ar.bf16([128, 512])
            gs = ar.bf16([128, 512])
            retT = ar.bf16([128, 512])
            vpad = [ar.bf16([128, 2, 128]), ar.bf16([128, 2, 128])]
            sT = [ar.bf16([128, 2, 128]), ar.bf16([128, 2, 128])]
            ktok = [ar.bf16([128, 128]), ar.bf16([128, 128])]
            of32 = [ar.f32([128, 128]), ar.f32([128, 128])]
            ob = [ar.bf16([128, 2, 128]), ar.bf16([128, 2, 128])]
            hn1 = [ar.f32([128, 128]), ar.f32([128, 128])]
            hn2 = [ar.f32([128, 128]), ar.f32([128, 128])]
            Ss = ar.f32([128, 16, 64])
            Ssblk = ar.bf16([128, 16, 2, 64])
            st1 = ar.f32([128, 4, 64])
            vblk = ar.bf16([128, 4, 128])
            acct = [ar.f32([128, 128]), ar.f32([128, 512]), ar.f32([128, 512])]
            ld("sp", rT, decmS[:, :, :], decm[:, gp * 2:gp * 2 + 2, :], ["decm"])
            ld("sp", rT, qdecS[:, :, :], qdec[:, gp, :, :], ["qdec"])
            P.dma("sp", rT, Ss[:, :, :], state_ret[l, gp], w=["Ss"])
            ms("pool", ka[:, :], 0.0, ["ka"])
            ms("pool", kb[:, :], 0.0, ["kb"])
            ms("pool", vpad[0][:, :, :], 0.0, [("vpad", 0)])
            ms("pool", vpad[1][:, :, :], 0.0, [("vpad", 1)])
            ms("dve", Sp[:, gp, :], 0.0, [("Sp", gp)])
            ms("dve", Sblk[:, gp, :, :], 0.0, [("Sblk", gp)])
            tt("dve", Ssblk[:, :, :, :], _ins(Ss[:, :, :], 2, 2),
               _ins(_ins(hmS[:, :], 1, 16), 3, 64), ALU.mult, ["Ss", "hm", "Ssblk"], ["Ssblk"])
            for bi, (n0, n) in enumerate(BLOCKS):
                drain(1)
                smp = bi == 4
                di = 1 if smp else 0
                hk = [("HT", kc, bi) for kc in range(8)]
                nch = n // 128
                if bi == 0:
                    P.dma("sp", rT, rotS[0][:, :, :], rot[0], w=[("rot", 0)])
                if bi + 1 < len(BLOCKS):
                    P.dma("sp", rT, rotS[(bi + 1) % 2][:, :, :], rot[bi + 1], w=[("rot", (bi + 1) % 2)])
                rS = rotS[bi % 2]
                rk_ = ("rot", bi % 2)

                def proj(col0):
                    bk = psum()
                    for kc in range(8):
                        mm(PS(bk)[:, 0:n], WR_[:, kc, col0:col0 + 128], HT[:, kc, n0:n0 + n], kc == 0, kc == 7,
                           [wk, hk[kc]], [("ps", bk)])
                    return bk
                bq = proj(0)
                bqs = proj(128)
                tt("dve", t1[:, 0:n], PS(bq)[:, 0:n], rS[:, 0, 0:n], ALU.mult, [("ps", bq), rk_, "t1"], ["t1"])
                tt("dve", t2[:, 0:n], PS(bqs)[:, 0:n], rS[:, 1, 0:n], ALU.mult, [("ps", bqs), rk_, "t2"], ["t2"])
                tt("pool", t1[:, 0:n], t1[:, 0:n], t2[:, 0:n], ALU.add, ["t1", "t2"], ["t1"])
                cp("act", qp[:, 0:n], t1[:, 0:n], ["t1", "qp"], ["qp"])
                tt("pool", qpp[:, 0:n], t1[:, 0:n], qdecS[:, di, 0:n], ALU.mult, ["t1", "qdec", "qpp"], ["qpp"])
                bk_ = proj(256)
                bks_ = proj(384)
                tt("dve", t3[:, 0:n], PS(bk_)[:, 0:n], rS[:, 0, 0:n], ALU.mult, [("ps", bk_), rk_, "t1"], ["t1"])
                tt("dve", t4[:, 0:n], PS(bks_)[:, 0:n], rS[:, 1, 0:n], ALU.mult, [("ps", bks_), rk_, "t2"], ["t2"])
                tt("pool", kf[:, 0:n], t3[:, 0:n], t4[:, 0:n], ALU.add, ["t1", "t2", "kf"], ["kf"])
                cp("act", ka[0:64, 0:n], kf[0:64, 0:n], ["kf", "ka"], ["ka"])
                cp("act", kb[64:128, 0:n], kf[64:128, 0:n], ["kf", "kb"], ["kb"])
                bg = proj(640)
                act(gs[:, 0:n], PS(bg)[:, 0:n], AF.Silu, [("ps", bg), "gs"], ["gs"])

                def stageA(ci):
                    cidx = bi * 4 + ci
                    par = cidx % 2
                    c0 = n0 + ci * 128
                    sl = slice(ci * 128, (ci + 1) * 128)
                    bv = psum()
                    if smp:
                        reserved.add(bv)
                    for kc in range(8):
                        mm(PS(bv)[:, 0:128], HT[:, kc, c0:c0 + 128], WR_[:, kc, 512:640], kc == 0, kc == 7, [wk, hk[kc]], [("ps", bv)])
                    vp = vpad[par]
                    vk = ("vpad", par)
                    vo = AP(vp.ap.tensor, vp.ap.offset, [list(vp.ap.ap[0]), [192, 2], [1, 64]])
                    cp("act", vo, PS(bv)[:, 0:128].rearrange("p (b e) -> p b e", b=2), [("ps", bv), vk], [vk])
                    bs = psum()
                    mm(PS(bs)[:, 0:128], ka[:, sl], qp[:, sl], True, True, ["ka", "qp"], [("ps", bs)])
                    mm(PS(bs)[:, 128:256], kb[:, sl], qp[:, sl], True, True, ["kb", "qp"], [("ps", bs)])
                    st_ = sT[par]
                    sk_ = ("sT", par)
                    tt("dve", st_[:, :, :], PS(bs)[:, 0:256].rearrange("p (a i) -> p a i", a=2),
                       decmS[:, :, :], ALU.mult, [("ps", bs), "decm", sk_], [sk_])
                    if smp:
                        tt("pool", st_[:, :, :], st_[:, :, :], _ins(blk[:, :], 1, 2), ALU.mult, [sk_, "blk"], [sk_])
                    bo = psum()
                    reserved.add(bo)
                    mm(PS(bo)[:, 0:128], vp[:, 0, :], st_[:, 0, :], True, False, [vk, sk_], [("ps", bo)])
                    mm(PS(bo)[:, 0:128], vp[:, 1, :], st_[:, 1, :], False, False, [vk, sk_], [("ps", bo)])
                    if not smp:
                        mm(PS(bo)[:, 0:128], Sblk[:, gp, :, :].rearrange("p a e -> p (a e)"), qpp[:, sl], False, True,
                           [("Sblk", gp), "qpp"], [("ps", bo)])
                    else:
                        for s_ in range(16):
                            mm(PS(bo)[:, s_ * 8:(s_ + 1) * 8], Ssblk[:, s_, :, :].rearrange("p a e -> p (a e)"),
                               qpp[:, ci * 128 + s_ * 8:ci * 128 + (s_ + 1) * 8], False, s_ == 15,
                               ["Ssblk", "qpp"], [("ps", bo)])
                    bt = psum()
                    tr(PSb(bt)[:, 0:128], kf[:, sl], ["kf"], [("ps", bt)])
                    kt = ktok[par]
                    kk = ("ktok", par)
                    tt("dve", kt[:, :].rearrange("p (a d) -> p a d", a=2), PSb(bt)[:, 0:128].rearrange("p (a d) -> p a d", a=2),
                       _ins(kdecS[:, di, gp * 2:gp * 2 + 2], 2, 64), ALU.mult, [("ps", bt), "kdec", kk], [kk])
                    if not smp:
                        bd = psum()
                        mm(PS(bd)[:, 0:128], kt[:, :], vp[:, 0, :], True, False, [kk, vk], [("ps", bd)])
                        mm(PS(bd)[:, 0:128], kt[:, :], vp[:, 1, :], False, True, [kk, vk], [("ps", bd)])
                        ts("dve", st1[:, 0, :], PS(bd)[:, 0:64], cdecS[:, 0, gp, 1:2], None, ALU.mult, None,
                           [("ps", bd), "cdec", "st1"], ["st1"])
                        stt("dve", st1[:, 0, :], PS(bd)[:, 64:128], cdecS[:, 0, gp, 2:3], st1[:, 0, :], ALU.mult, ALU.add,
                            [("ps", bd), "cdec", "st1"], ["st1"])
                        stt("dve", Sp[:, gp, :], Sp[:, gp, :], cdecS[:, 0, gp, 0:1], st1[:, 0, :], ALU.mult, ALU.add,
                            ["st1", "cdec", ("Sp", gp)], [("Sp", gp)])
                        tt("pool", Sblk[:, gp, :, :], _ins(Sp[:, gp, :], 1, 2), _ins(hmS[:, :], 2, 64),
                           ALU.mult, [("Sp", gp), "hm", ("Sblk", gp)], [("Sblk", gp)])
                        if cidx == 15:
                            P.dma("sp", rO, o_ret_p[l, gp], Sp[:, gp, :], r=[("Sp", gp)], w=[("o_ret_p", l, gp)], is_out=True)
                    else:
                        for q4 in range(4):
                            tt("dve", vblk[:, :, :], _ins(PS(bv)[:, 0:128], 1, 4),
                               _ins(seqS[:, q4 * 4:(q4 + 1) * 4], 2, 128), ALU.mult, [("ps", bv), "seqm", "vblk"], ["vblk"])
                            bd = psum()
                            mm(PS(bd)[:, 0:512], kt[:, :], vblk[:, :, :].rearrange("p s e -> p (s e)"), True, True,
                               [kk, "vblk"], [("ps", bd)])
                            pv = PS(bd)[:, 0:512].rearrange("p (s a e) -> p s a e", s=4, a=2)
                            ts("dve", st1[:, :, :], pv[:, :, 0, :], cdecS[:, 1, gp, 1:2], None, ALU.mult, None,
                               [("ps", bd), "cdec", "st1"], ["st1"])
                            stt("dve", st1[:, :, :], pv[:, :, 1, :], cdecS[:, 1, gp, 2:3], st1[:, :, :],
                                ALU.mult, ALU.add, [("ps", bd), "cdec", "st1"], ["st1"])
                            stt("dve", Ss[:, q4 * 4:(q4 + 1) * 4, :], Ss[:, q4 * 4:(q4 + 1) * 4, :], cdecS[:, 1, gp, 0:1], st1[:, :, :],
                                ALU.mult, ALU.add, ["st1", "cdec", "Ss", "Ssblk"], ["Ss"])
                        reserved.discard(bv)
                        P.dma("sp", rO, o_ret_s[l, gp], Ss[:, :, :], r=["Ss"], w=[("o_ret_s", l, gp)], is_out=True)
                    return bo

                def stageB(ci, bo):
                    cidx = bi * 4 + ci
                    par = cidx % 2
                    sl = slice(ci * 128, (ci + 1) * 128)
                    o_, ob_, h1, h2 = of32[par], ob[par], hn1[par], hn2[par]
                    ko, kob, k1, k2 = ("of32", par), ("ob", par), ("hn1", par), ("hn2", par)
                    cp("act", o_[:, :], PS(bo)[:, 0:128], [("ps", bo), ko], [ko])
                    cp("act", ob_[:, 0, :], PS(bo)[:, 0:128], [("ps", bo), kob], [kob])
                    act(ob_[:, 1, :], PS(bo)[:, 0:128], AF.Square, [("ps", bo), kob], [kob])
                    reserved.discard(bo)
                    bh = psum()
                    mm(PS(bh)[:, 0:256], hselb[:, :], ob_[:, :, :].rearrange("p a i -> p (a i)"), True, True, ["hselb", kob], [("ps", bh)])
                    cp("act", h2[:, :], PS(bh)[:, 0:128], [("ps", bh), k2], [k2])
                    tt("dve", h1[:, :], h2[:, :], h2[:, :], ALU.mult, [k2, k1], [k1])
                    tt("dve", h1[:, :], PS(bh)[:, 128:256], h1[:, :], ALU.subtract, [("ps", bh), k1], [k1])
                    act(h1[:, :], h1[:, :], AF.Ln, [k1, "epsT"], [k1], bias=epsT[:, 0:1])
                    act(h1[:, :], h1[:, :], AF.Exp, [k1], [k1], scale=-0.5)
                    tt("pool", o_[:, :], o_[:, :], h2[:, :], ALU.subtract, [ko, k2], [ko])
                    tt("pool", o_[:, :], o_[:, :], h1[:, :], ALU.mult, [ko, k1], [ko])
                    tt("dve", retT[:, sl], o_[:, :], gs[:, sl], ALU.mult, [ko, "gs", "retT"], ["retT"])

                prev = None
                for ci in range(nch):
                    bo = stageA(ci)
                    if prev is not None:
                        stageB(*prev)
                    prev = (ci, bo)
                stageB(*prev)
                for fo in range(8):
                    bk = psum()
                    mm(PS(bk)[:, 0:n], WO_[:, fo * 128:(fo + 1) * 128], retT[:, 0:n], True, True, [wk, "retT"], [("ps", bk)])
                    accum(bi, fo, bk, 0, acct)

        def swa_pass(l, WQ_, WO_, wk):
            ar = Arena()
            biasP = ar.bf16([128, 8, 256])
            qa = ar.bf16([128, 4, 512])
            qb = ar.bf16([128, 4, 512])
            skT = ar.bf16([128, NT])
            svt = ar.bf16([128, 17, 128])
            pP = [ar.bf16([128, 256]) for _ in range(8)]
            pT = [ar.bf16([128, 4, 2, 128]) for _ in range(2)]
            attn = ar.bf16([128, 512])
            swaT = ar.bf16([128, 4, 512])
            kcr = ar.bf16([128, 4, 128])
            kTc = ar.bf16([128, 8, 128])
            vc = ar.bf16([128, 8, 128])
            pS = [ar.bf16([32, 256]) for _ in range(4)]
            pTs = [ar.bf16([128, 2, 32]) for _ in range(4)]
            attS = ar.bf16([32, 4, 128])
            acct = [ar.f32([128, 128]), ar.f32([128, 512]), ar.f32([128, 512])]
            for cbi in range(8):
                c_, g_ = cbi // 2, cbi % 2
                h_ = g_ * 4 + c_
                src = AP(ext_d.tensor, h_ * 65536 + 127, [[511, 128], [1, 256]])
                P.dma("pool", rP, biasP[:, cbi, :], src, r=["ext_d"], w=["biasP"])
            ms("pool", qa[:, :, :], 0.0, [("qa", c) for c in range(4)])
            ms("pool", qb[:, :, :], 0.0, [("qb", c) for c in range(4)])
            for _k in range(4):
                ms("pool", pS[_k][:, :], 0.0, [("pS", _k)])
            def load_cache_half(hf):
                P.dma("pool", rP, vc[:, :, :], cache_v[l, hf * 8:(hf + 1) * 8].rearrange("s k e -> k s e"), w=["vc"])
                for q4 in range(2):
                    P.dma("pool", rP, kcr[:, :, :], cache_k[l, hf * 8 + q4 * 4:hf * 8 + (q4 + 1) * 4].rearrange("s k e -> k s e"), w=["kcr"])
                    bt = psum()
                    for j in range(4):
                        tr(PSb(bt)[:, j * 128:(j + 1) * 128], kcr[:, j, :], ["kcr"], [("ps", bt)])
                    cp("act", kTc[:, q4 * 4:q4 * 4 + 4, :].rearrange("p s k -> p (s k)"), PSb(bt)[:, 0:512],
                       [("ps", bt), "kTc"], ["kTc"])
            P.dma("sp", rO, o_k_s[l, :, 0:120, :], cache_k[l, :, 8:128, :], w=[("o_k_s", l, 0)], is_out=True)
            P.dma("sp", rO, o_v_s[l, :, 0:120, :], cache_v[l, :, 8:128, :], w=[("o_v_s", l, 0)], is_out=True)
            for bi, (n0, n) in enumerate(BLOCKS):
                drain(1)
                smp = bi == 4
                hk = [("HT", kc, bi) for kc in range(8)]
                for c in range(4):
                    bk = psum()
                    for kc in range(8):
                        mm(PS(bk)[:, 0:n], WQ_[:, kc, c * 128:(c + 1) * 128], HT[:, kc, n0:n0 + n], kc == 0, kc == 7, [wk, hk[kc]], [("ps", bk)])
                    act(qa[0:64, c, 0:n], PS(bk)[0:64, 0:n], AF.Identity, [("ps", bk), ("qa", c)], [("qa", c)], scale=0.125)
                    act(qb[64:128, c, 0:n], PS(bk)[64:128, 0:n], AF.Identity, [("ps", bk), ("qb", c)], [("qb", c)], scale=0.125)
                bk = psum()
                for kc in range(8):
                    mm(PS(bk)[:, 0:n], WQ_[:, kc, 512:640], HT[:, kc, n0:n0 + n], kc == 0, kc == 7, [wk, hk[kc]], [("ps", bk)])
                cp("act", skT[:, n0:n0 + n], PS(bk)[:, 0:n], [("ps", bk), ("skT", bi)], [("skT", bi)])
                if smp:
                    pa = pP[0].ap
                    qS = [attn.ap.rearrange("p (s c i) -> p s c i", s=16, c=4),
                          AP(pa.tensor, pa.offset, [list(pa.ap[0]), [1, 512]]).rearrange("p (s c i) -> p s c i", s=16, c=4)]
                    qSk = ["attn", ("pP", 0), ("pP", 1)]
                    cp("dve", qS[0], qa[:, :, 0:128].rearrange("p c (s i) -> p s c i", i=8), [("qa", c) for c in range(4)] + qSk, qSk)
                    cp("dve", qS[1], qb[:, :, 0:128].rearrange("p c (s i) -> p s c i", i=8), [("qb", c) for c in range(4)] + qSk, qSk)
                for ci in range(n // 128):
                    cidx = bi * 4 + ci
                    c0 = n0 + ci * 128
                    sl = slice(ci * 128, (ci + 1) * 128)
                    bv = psum()
                    for kc in range(8):
                        mm(PS(bv)[:, 0:128], HT[:, kc, c0:c0 + 128], WQ_[:, kc, 640:768], kc == 0, kc == 7, [wk, hk[kc]], [("ps", bv)])
                    cp("act", svt[:, cidx, :], PS(bv)[:, 0:128], [("ps", bv), ("svt", cidx)], [("svt", cidx)])
                    if cidx >= 15:
                        bk2 = psum()
                        for kc in range(8):
                            mm(PS(bk2)[:, 0:128], HT[:, kc, c0:c0 + 128], WQ_[:, kc, 512:640], kc == 0, kc == 7, [wk, hk[kc]], [("ps", bk2)])
                        cp("dve", kvf[:, 0, :], PS(bk2)[:, 0:128], [("ps", bk2), "kvf"], ["kvf"])
                        cp("dve", kvf[:, 1, :], PS(bv)[:, 0:128], [("ps", bv), "kvf"], ["kvf"])
                        if cidx == 15:
                            P.dma("sp", rO, o_k_p[l], kvf[:, 0, :], r=["kvf"], w=[("o_k_p", l)], is_out=True)
                            P.dma("sp", rO, o_v_p[l], kvf[:, 1, :], r=["kvf"], w=[("o_v_p", l)], is_out=True)
                        else:
                            for s_ in range(16):
                                P.dma("sp", rO, o_k_s[l, s_, 120:128, :], kvf[s_ * 8:(s_ + 1) * 8, 0, :], r=["kvf"], w=[("o_k_s", l, 1 + s_)], is_out=True)
                                P.dma("sp", rO, o_v_s[l, s_, 120:128, :], kvf[s_ * 8:(s_ + 1) * 8, 1, :], r=["kvf"], w=[("o_v_s", l, 1 + s_)], is_out=True)
                    if not smp:
                        ba = psum()
                        reserved.add(ba)
                        if cidx == 0:
                            k0, nk, boff = c0, 128, 128
                        else:
                            k0, nk, boff = c0 - 128, 256, 0
                        nj = nk // 128
                        lbk = []
                        for hp in range(4):
                            bl = psum()
                            reserved.add(bl)
                            lbk.append(bl)
                            for hh in range(2):
                                cbi = hp * 2 + hh
                                c_, g_ = cbi // 2, cbi % 2
                                qsrc = qa if g_ == 0 else qb
                                qk = ("qa", c_) if g_ == 0 else ("qb", c_)
                                mm(PS(bl)[:, hh * 256:hh * 256 + nk], qsrc[:, c_, sl], skT[:, k0:k0 + nk], True, False,
                                   [qk, ("skT", bi), ("skT", max(bi - 1, 0))], [("ps", bl)])
                                mm(PS(bl)[:, hh * 256:hh * 256 + nk], identb[:, :], biasP[:, cbi, boff:boff + nk], False, True,
                                   ["identb", "biasP"], [("ps", bl)])
                        for hp in range(4):
                            bl = lbk[hp]
                            P.op("dve", lambda e, bl=bl, nk=nk, hp=hp: e.reduce_max(
                                out=stat[:, 0, hp * 2:hp * 2 + 2], in_=PS(bl)[:, 0:512].rearrange("p (a k) -> p a k", a=2)[:, :, 0:nk], axis=AX.X),
                                r=[("ps", bl), "stat"], w=["stat"])
                        tt("dve", stat[:, 1, 0:8], stat[:, 0, 0:8], sinkp[:, l, 0:8], ALU.max, ["stat", "sinkp"], ["stat"])
                        ts("dve", stat[:, 1, 0:8], stat[:, 1, 0:8], -1.0, None, ALU.mult, None, ["stat"], ["stat"])
                        tt("dve", stat2[:, 0, 8:16], sinkp[:, l, 0:8], stat[:, 1, 0:8], ALU.add, ["stat", "sinkp", "stat2"], ["stat2"])
                        act(stat[:, 3, 0:8], stat2[:, 0, 8:16], AF.Exp, ["stat2", "stat"], ["stat"])
                        ms("dve", stat[:, 2, 0:8], 0.0, ["stat"])
                        for cbi in range(8):
                            bl = lbk[cbi // 2]
                            pk = ("pP", cbi)
                            act(pP[cbi][:, 0:nk], PS(bl)[:, (cbi % 2) * 256:(cbi % 2) * 256 + nk], AF.Exp, [("ps", bl), "stat", pk], [pk, "stat"],
                                bias=stat[:, 1, cbi:cbi + 1], accum=stat[:, 2, cbi:cbi + 1])
                        for bl in lbk:
                            reserved.discard(bl)
                        bts = [psum(), psum()]
                        for q_ in range(2):
                            reserved.add(bts[q_])
                        for cbi in range(8):
                            for j in range(nj):
                                tr(PSb(bts[cbi // 4])[:, ((cbi % 4) * 2 + j) * 128:((cbi % 4) * 2 + j + 1) * 128], pP[cbi][:, j * 128:(j + 1) * 128],
                                   [("pP", cbi)], [("ps", bts[cbi // 4])])
                        for q_ in range(2):
                            tk = ("pT", q_)
                            srcv = PSb(bts[q_])[:, 0:1024].rearrange("p (h j i) -> p h j i", h=4, j=2)[:, :, 0:nj, :]
                            dstv = pT[q_][:, :, 0:nj, :]
                            cp("act" if q_ == 0 else "dve", dstv, srcv, [("ps", bts[q_]), tk], [tk])
                            reserved.discard(bts[q_])
                        for cbi in range(8):
                            c_, g_ = cbi // 2, cbi % 2
                            col = c_ * 128 + g_ * 64
                            for j in range(nj):
                                kci = cidx - (nj - 1) + j
                                mm(PS(ba)[:, col:col + 64], pT[cbi // 4][:, cbi % 4, j, :], svt[:, kci, g_ * 64:(g_ + 1) * 64], j == 0, j == nj - 1,
                                   [("pT", cbi // 4), ("svt", kci)], [("ps", ba)])
                        tt("dve", stat2[:, 0, 0:8], stat[:, 2, 0:8], stat[:, 3, 0:8], ALU.add, ["stat", "stat2"], ["stat2"])
                        P.op("dve", lambda e: e.reciprocal(out=stat2[:, 1, 0:8], in_=stat2[:, 0, 0:8]), r=["stat2"], w=["stat2"])
                        tt("dve", attn[:, :].rearrange("p (h e) -> p h e", e=64), PS(ba)[:, 0:512].rearrange("p (h e) -> p h e", e=64),
                           _ins(stat2[:, 1, 0:8], 2, 64), ALU.mult, [("ps", ba), "stat2", "attn"], ["attn"])
                        reserved.discard(ba)
                        bt = psum()
                        for c in range(4):
                            tr(PSb(bt)[:, c * 128:(c + 1) * 128], attn[:, c * 128:(c + 1) * 128], ["attn"], [("ps", bt)])
                        cp("act", swaT[:, :, sl], PSb(bt)[:, 0:512].rearrange("p (c i) -> p c i", c=4), [("ps", bt)] + [("swaT", c) for c in range(4)],
                           [("swaT", c) for c in range(4)])
                    else:
                        for s4 in range(4):
                            if s4 % 2 == 0:
                                load_cache_half(s4 // 2)
                            ba = psum()
                            reserved.add(ba)
                            ms("dve", stat[:, 2:4, :], 0.0, ["stat"])
                            for sg in range(8):
                                s_ = s4 * 4 + sg // 2
                                g_ = sg % 2
                                qks = qSk
                                bl = psum()
                                lq = qS[g_][:, s_, :, :].rearrange("p c i -> p (c i)")
                                mm(PS(bl)[0:32, 0:128], lq, kTc[:, s_ % 8, :], True, False, qks + ["kTc"], [("ps", bl)])
                                mm(PS(bl)[0:32, 0:128], identb[0:32, 0:32], biasS[:, g_, 0:128], False, True, ["identb", "biasS"], [("ps", bl)])
                                mm(PS(bl)[0:32, 128:136], lq, skT[:, n0 + s_ * 8:n0 + (s_ + 1) * 8], True, False, qks + [("skT", bi)], [("ps", bl)])
                                mm(PS(bl)[0:32, 128:136], identb[0:32, 0:32], biasS[:, g_, 128:136], False, True, ["identb", "biasS"], [("ps", bl)])
                                col = sg
                                P.op("dve", lambda e, bl=bl, col=col: e.reduce_max(out=stat[0:32, 0, col:col + 1], in_=PS(bl)[0:32, 0:136], axis=AX.X),
                                     r=[("ps", bl), "stat"], w=["stat"])
                                ts("dve", stat[0:32, 1, col:col + 1], stat[0:32, 0, col:col + 1], sinks[:, l, g_:g_ + 1], -1.0, ALU.max, ALU.mult,
                                   ["stat", "sinks"], ["stat"])
                                pp = pS[sg % 4]
                                pk = ("pS", sg % 4)
                                ms("pool", pp[:, 128:256], 0.0, [pk])
                                act(pp[:, 0:128], PS(bl)[0:32, 0:128], AF.Exp, [("ps", bl), "stat", pk], [pk, "stat"],
                                    bias=stat[0:32, 1, col:col + 1], accum=stat[0:32, 2, col:col + 1])
                                act(pp[:, 128 + s_ * 8:128 + (s_ + 1) * 8], PS(bl)[0:32, 128:136], AF.Exp, [("ps", bl), "stat", pk], [pk, "stat"],
                                    bias=stat[0:32, 1, col:col + 1], accum=stat[0:32, 3, col:col + 1])
                                act(stat2[0:32, 0, col:col + 1], sinks[:, l, g_:g_ + 1], AF.Exp, ["sinks", "stat", "stat2"], ["stat2"],
                                    bias=stat[0:32, 1, col:col + 1])
                                bt = psum()
                                tr(PSb(bt)[:, 0:32], pp[:, 0:128], [pk], [("ps", bt)])
                                tr(PSb(bt)[:, 32:64], pp[:, 128:256], [pk], [("ps", bt)])
                                pt = pTs[sg % 4]
                                tk = ("pTs", sg % 4)
                                cp("act", pt[:, :, :].rearrange("p a i -> p (a i)"), PSb(bt)[:, 0:64], [("ps", bt), tk], [tk])
                                oc = (sg // 2) * 128 + g_ * 64
                                mm(PS(ba)[0:32, oc:oc + 64], pt[:, 0, :], vc[:, s_ % 8, g_ * 64:(g_ + 1) * 64], True, False, [tk, "vc"], [("ps", ba)])
                                mm(PS(ba)[0:32, oc:oc + 64], pt[:, 1, :], svt[:, 16, g_ * 64:(g_ + 1) * 64], False, True, [tk, ("svt", 16)], [("ps", ba)])
                            tt("dve", stat2[0:32, 1, 0:8], stat[0:32, 2, 0:8], stat[0:32, 3, 0:8], ALU.add, ["stat", "stat2"], ["stat2"])
                            tt("dve", stat2[0:32, 1, 0:8], stat2[0:32, 1, 0:8], stat2[0:32, 0, 0:8], ALU.add, ["stat2"], ["stat2"])
                            P.op("dve", lambda e: e.reciprocal(out=stat2[0:32, 1, 8:16], in_=stat2[0:32, 1, 0:8]), r=["stat2"], w=["stat2"])
                            tt("dve", attS[:, :, :].rearrange("p s (g e) -> p (s g) e", g=2), PS(ba)[0:32, 0:512].rearrange("p (h e) -> p h e", e=64),
                               _ins(stat2[0:32, 1, 8:16], 2, 64), ALU.mult, [("ps", ba), "stat2", "attS"], ["attS"])
                            reserved.discard(ba)
                            bt = psum()
                            for j in range(4):
                                tr(PSb(bt)[:, j * 32:(j + 1) * 32], attS[:, j, :], ["attS"], [("ps", bt)])
                            sw_ap = swaT.ap
                            dst = AP(sw_ap.tensor, sw_ap.offset + s4 * 32, [list(sw_ap.ap[0]), [512, 4], [8, 4], [1, 8]])
                            src = PSb(bt)[:, 0:128].rearrange("p (s c i) -> p c s i", s=4, c=4)
                            cp("act", dst, src, [("ps", bt)] + [("swaT", c) for c in range(4)], [("swaT", c) for c in range(4)])
                if DBG and l == 0 and bi == 3:
                    for c in range(4):
                        P.dma("pool", rO, yT[512 + c * 128:512 + (c + 1) * 128, 0:512], swaT[:, c, 0:512], r=[("swaT", c)], w=[("dbg", 4 + c)], is_out=True)
                if DBG and l == 0 and bi == 4:
                    for c in range(4):
                        P.dma("pool", rO, yT[512 + c * 128:512 + (c + 1) * 128, 512:640], swaT[:, c, 0:128], r=[("swaT", c)], w=[("dbgs", 4 + c)], is_out=True)
                for fo in range(8):
                    bk = psum()
                    for c in range(4):
                        mm(PS(bk)[:, 0:n], WO_[:, c, fo * 128:(fo + 1) * 128], swaT[:, c, 0:n], c == 0, c == 3,
                           [wk, ("swaT", c)], [("ps", bk)])
                    accum(bi, fo, bk, 0, acct)
            make_tables(True, 1, lnp[:, l, 0, :], lnp[:, l, 1, :], False)

        def ffn_pass(l, gi, last_layer, WU_, WD_, wk):
            ar = Arena()
            ubuf = [ar.f32([128, 514]) for _ in range(4)]
            ca = [ar.f32([128, 512]), ar.f32([128, 512])]
            cb = [ar.f32([128, 512]), ar.f32([128, 512])]
            gT = [ar.bf16([128, 3, 512]), ar.bf16([128, 3, 512])]
            halos = ar.f32([128, 6, 16, 2])
            scv = [ar.f32([32, 128]) for _ in range(6)]
            acct = [ar.f32([128, 128]), ar.f32([128, 512]), ar.f32([128, 512])]
            lb = LNBuf(ar)
            u0, nu = FGROUPS[gi]
            firstg = gi == 0
            lastg = gi == len(FGROUPS) - 1
            if firstg:
                ms("pool", halo[:], 0.0, ["halo"])
            for ab in range(2):
                for ui in range(nu):
                    j = ab * 22 + u0 + ui
                    P.dma("sp", rT, scv[ab * 3 + ui][:, :], state_conv[l, :, j * 128:(j + 1) * 128], w=[("scv", ab * 3 + ui)])

            def load_halos():
                for ab in range(2):
                    for ui in range(nu):
                        k = ab * 3 + ui
                        bt = psum()
                        P.op("pe", lambda e, bt=bt, k=k: e.transpose(PS(bt)[:, 0:32], scv[k][:, :], identF[0:32, 0:32]),
                             r=[("scv", k), "identF"], w=[("ps", bt)])
                        cp("act", halos[:, k, :, :].rearrange("p s r -> p (s r)"), PS(bt)[:, 0:32], [("ps", bt), "halos"], ["halos"])
            if lastg:
                if last_layer:
                    make_tables(False, 0, lnp[:, l, 2, :], lnp[:, l, 3, :], True)
                else:
                    make_tables(True, 0, lnp[:, l, 2, :], lnp[:, l, 3, :], False)

            def stage1(bi):
                n0, n = BLOCKS[bi]
                smp = bi == 4
                g_ = gT[bi % 2]
                if firstg:
                    ln_finish_block(bi, lb, True)
                hk = [("HT", kc, bi) for kc in range(8)]

                def front(ui):
                    u = u0 + ui
                    cres = []
                    for ab in range(2):
                        j = ab * 22 + u
                        bk = psum()
                        for kc in range(8):
                            mm(PS(bk)[:, 0:n], WU_[:, ui, kc, ab * 128:(ab + 1) * 128], HT[:, kc, n0:n0 + n], kc == 0, kc == 7, [wk, hk[kc]], [("ps", bk)])
                        ub = ubuf[(ui % 2) * 2 + ab]
                        uk = ("ubuf", (ui % 2) * 2 + ab)
                        cc = (ca if ab == 0 else cb)[ui % 2]
                        ck = ("c", ab, ui % 2)
                        w0 = convp[:, l, j, 0:1]
                        w1 = convp[:, l, j, 1:2]
                        w2 = convp[:, l, j, 2:3]
                        bb = convp[:, l, j, 3:4]
                        if not smp:
                            cp("act", ub[:, 2:2 + n], PS(bk)[:, 0:n], [("ps", bk), uk], [uk])
                            cp("pool", ub[:, 0:2], halo[:, j, :], ["halo", uk], [uk])
                            cp("pool", halo[:, j, :], ub[:, n:n + 2], [uk, "halo"], ["halo"])
                            ts("pool", cc[:, 0:n], ub[:, 2:2 + n], w2, bb, ALU.mult, ALU.add, [uk, "convp", ck], [ck])
                            stt("dve", cc[:, 0:n], ub[:, 1:1 + n], w1, cc[:, 0:n], ALU.mult, ALU.add, [uk, "convp", ck], [ck])
                            stt("dve", cc[:, 0:n], ub[:, 0:n], w0, cc[:, 0:n], ALU.mult, ALU.add, [uk, "convp", ck], [ck])
                        else:
                            u3 = ub[:, 0:160].rearrange("p (s t) -> p s t", t=10)
                            hs = halos[:, ab * 3 + ui, :, :]
                            cp("act", u3[:, :, 2:10], PS(bk)[:, 0:128].rearrange("p (s i) -> p s i", i=8), [("ps", bk), uk], [uk])
                            cp("pool", u3[:, :, 0:2], hs, ["halos", uk], [uk])
                            cp("pool", hs, u3[:, :, 8:10], [uk, "halos"], ["halos"])
                            P.dma("sp", rO, o_conv_s[l, :, j, :, :], hs, r=["halos"], w=[("o_conv_s", l, j)], is_out=True)
                            c3 = cc[:, 0:128].rearrange("p (s i) -> p s i", i=8)
                            ts("pool", c3, u3[:, :, 2:10], w2, bb, ALU.mult, ALU.add, [uk, "convp", ck], [ck])
                            stt("dve", c3, u3[:, :, 1:9], w1, c3, ALU.mult, ALU.add, [uk, "convp", ck], [ck])
                            stt("dve", c3, u3[:, :, 0:8], w0, c3, ALU.mult, ALU.add, [uk, "convp", ck], [ck])
                        cres.append((cc, ck))
                    return cres

                def back(ui, cres):
                    (cca, cka), (ccb, ckb) = cres
                    act(cca[:, 0:n], cca[:, 0:n], AF.Silu, [cka], [cka])
                    tt("dve", g_[:, ui, 0:n], cca[:, 0:n], ccb[:, 0:n], ALU.mult, [cka, ckb, ("gT", bi % 2, ui)], [("gT", bi % 2, ui)])

                prev = None
                for ui in range(nu):
                    cr = front(ui)
                    if prev is not None:
                        back(*prev)
                    prev = (ui, cr)
                back(*prev)

            def stage2(bi):
                drain(1)
                n0, n = BLOCKS[bi]
                g_ = gT[bi % 2]
                for fo in range(8):
                    bk = psum()
                    for ui in range(nu):
                        mm(PS(bk)[:, 0:n], WD_[:, ui, fo * 128:(fo + 1) * 128], g_[:, ui, 0:n], ui == 0, ui == nu - 1,
                           [wk, ("gT", bi % 2, ui)], [("ps", bk)])
                    accum(bi, fo, bk, 1, acct)
                if lastg:
                    ln_finish_block(bi, lb, not last_layer)
                    if last_layer and not DBG:
                        for fc in range(8):
                            P.dma("sp", rO, yT[fc * 128:(fc + 1) * 128, n0:n0 + n], X[:, fc, n0:n0 + n], r=[("X", fc, bi)], w=[("yT", fc, bi)], is_out=True)

            nb = len(BLOCKS)
            stage1(0)
            for bi in range(1, nb):
                if bi == nb - 1:
                    load_halos()
                stage1(bi)
                stage2(bi - 1)
            stage2(nb - 1)
            if lastg:
                P.dma("sp", rO, o_conv_p[l], halo[:], r=["halo"], w=[("o_conv_p", l)], is_out=True)

        passes = []
        for l in range(L):
            for gp in range(4):
                passes.append(("R", l, gp))
            passes.append(("S", l, 0))
            for gi in range(len(FGROUPS)):
                passes.append(("F", l, gi))

        compute_mod(0)
        make_tables(True, 0, None, None, False)
        make_gates()
        ar0 = Arena()
        xh1_0 = ar0.f32([128, 128])
        for bi, (n0b, nb) in enumerate(BLOCKS):
            for sub in range(nb // 128):
                n0 = n0b + sub * 128
                finish(bi, n0, 128, lambda fc, n0=n0: X[:, fc, n0:n0 + 128], [], True, xh1_0)
        views = {}
        views[0] = load_pass_weights(passes[0][0], passes[0][1], passes[0][2], 0)
        drain()
        marks = []
        for pi, (kind, l, idx) in enumerate(passes):
            marks.append((kind, l, idx, P.cnt["pe"]))
            if pi == 0 or passes[pi - 1][0] != kind:
                P.barrier()
            s = pi % 2
            if pi + 1 < len(passes):
                nk, nl, nidx = passes[pi + 1]
                views[pi + 1] = load_pass_weights(nk, nl, nidx, (pi + 1) % 2)
                drain(1)
            v1, v2 = views.pop(pi)
            wk = ("W", s)
            if kind == "R":
                ret_pass(l, idx, v1, v2, wk)
                drain()
            elif kind == "S":
                swa_pass(l, v1, v2, wk)
                drain()
                if l + 1 < L:
                    P.barrier()
                    compute_mod(l + 1)
                    make_gates_next = True
            else:
                if idx == 0 and l + 1 <= L:
                    pass
                ffn_pass(l, idx, l == L - 1, v1, v2, wk)
                drain()
                if idx == len(FGROUPS) - 1 and l + 1 < L:
                    make_gates()

        marks.append(("END", 0, 0, P.cnt["pe"]))
        if os.environ.get("KMARKS"):
            import json
            json.dump(marks, open(os.environ["KMARKS"], "w"))
        with nc.Block() as block:
            P.emit(block)
    return nc


def _consts():
    f32 = np.float32
    h = np.arange(8)
    gam = 1.0 - 2.0 ** (-5.0 - h.astype(np.float64))
    inv = (1.0 / (np.float32(10000.0) ** (np.arange(32, dtype=f32) / np.float32(32)))).astype(f32)
    rot = np.zeros((17, 128, 4, 128), f32)
    for c in range(17):
        if c < 16:
            pos = (c * 128 + np.arange(128)).astype(f32)
        else:
            pos = (PAST + (np.arange(128) % 8)).astype(f32)
        ang = pos[:, None] * inv[None, :]
        cs, sn = np.cos(ang).astype(f32), np.sin(ang).astype(f32)
        for p in range(128):
            d = p % 64
            f = d % 32
            sg = -1.0 if d < 32 else 1.0
            rot[c, p, 0] = cs[:, f]
            rot[c, p, 1] = sg * sn[:, f]
            rot[c, p, 2] = cs[:, f] * 0.125
            rot[c, p, 3] = sg * sn[:, f] * 0.125
    i = np.arange(128)
    decm = np.zeros((128, 8, 128), f32)
    for hh in range(8):
        diff = i[None, :] - i[:, None]
        decm[:, hh, :] = np.where(diff >= 0, gam[hh] ** np.maximum(diff, 0), 0.0)
    decm = decm * np.float32(0.125)
    blkind = (i[:, None] // 8 == i[None, :] // 8).astype(f32)
    qdec = np.zeros((128, 4, 2, 128), f32)
    cdec = np.zeros((128, 2, 4, 3), f32)
    for p in range(128):
        hb = p // 64
        for pair in range(4):
            hh = pair * 2 + hb
            qdec[p, pair, 0] = gam[hh] ** (i + 1)
            qdec[p, pair, 1] = gam[hh] ** ((i % 8) + 1)
            for k, C in enumerate((128, 8)):
                dec = gam[hh] ** C
                cdec[p, k, pair] = (dec, dec if hb == 0 else 0.0, dec if hb == 1 else 0.0)
    cdec[:, :, :, 1] = (np.arange(128)[:, None, None] // 64 == 0)
    cdec[:, :, :, 2] = (np.arange(128)[:, None, None] // 64 == 1)
    kdec = np.zeros((128, 2, 8), f32)
    for hh in range(8):
        kdec[:, 0, hh] = gam[hh] ** (127 - i) * 0.125
        kdec[:, 1, hh] = gam[hh] ** (7 - (i % 8)) * 0.125
    hmask = np.zeros((128, 2), f32)
    hmask[:64, 0] = 1
    hmask[64:, 1] = 1
    seqm = (i[:, None] // 8 == np.arange(16)[None, :]).astype(f32)
    ident = np.eye(128, dtype=f32)
    hsel = np.zeros((128, 128), f32)
    hsel[:64, :64] = 1.0 / 64
    hsel[64:, 64:] = 1.0 / 64
    dist = 127 - np.arange(128)
    n = np.maximum(dist, 0)
    nf = np.maximum(n, 16).astype(f32)
    large = 16 + (np.log(nf / np.float32(16)) / np.float32(np.log(128 / 16)) * np.float32(16)).astype(np.int32)
    large = np.minimum(large, 31)
    bucket = np.where(n < 16, n, large)
    ohr = (np.arange(32)[:, None] == bucket[None, :]).astype(f32)
    rot_b = np.zeros((5, 128, 2, 512), f32)
    for c in range(17):
        rot_b[c // 4, :, :, (c % 4) * 128:(c % 4 + 1) * 128] = rot[c][:, 0:2]
    rot = rot_b
    qdec = np.ascontiguousarray(np.tile(qdec, (1, 1, 1, 4)))
    return dict(rot=rot, decm=decm, blkind=blkind, qdec=qdec, kdec=kdec, cdec=cdec, hmask=hmask,
                seqm=seqm, identf=ident, hsel=hsel, ohr=ohr)


_NC_CACHE = {}


def kernel(x_prompt, x_sample, c_prompt, c_sample, state_ret, cache_swa_k, cache_swa_v, state_conv,
           rel_bias, w_ada, b_ada, w_in, swa_sinks, w_out, ln1_g, ln1_b,
           w_up, conv_w, conv_b, w_down, ln2_g, ln2_b):
    f32 = np.float32
    A = lambda a: np.ascontiguousarray(np.asarray(a, dtype=f32))
    x_prompt, x_sample, c_prompt, c_sample = A(x_prompt), A(x_sample), A(c_prompt), A(c_sample)
    state_ret, cache_swa_k, cache_swa_v, state_conv = A(state_ret), A(cache_swa_k), A(cache_swa_v), A(state_conv)
    rel_bias, w_ada, b_ada, w_in, swa_sinks, w_out = A(rel_bias), A(w_ada), A(b_ada), A(w_in), A(swa_sinks), A(w_out)
    ln1_g, ln1_b, w_up, conv_w, conv_b, w_down, ln2_g, ln2_b = A(ln1_g), A(ln1_b), A(w_up), A(conv_w), A(conv_b), A(w_down), A(ln2_g), A(ln2_b)

    sw = np.concatenate([np.arange(32, 64), np.arange(0, 32)])
    w_r = np.zeros((L, 4, D, 768), f32)
    w_or = np.zeros((L, 4, 128, D), f32)
    for gp in range(4):
        hs = np.arange(gp * 2, gp * 2 + 2)
        cols = (hs[:, None] * 64 + np.arange(64)[None, :]).reshape(-1)
        cols_sw = (hs[:, None] * 64 + sw[None, :]).reshape(-1)
        w_r[:, gp, :, 0:128] = w_in[:, :, 0 + cols]
        w_r[:, gp, :, 128:256] = w_in[:, :, 0 + cols_sw]
        w_r[:, gp, :, 256:384] = w_in[:, :, 512 + cols]
        w_r[:, gp, :, 384:512] = w_in[:, :, 512 + cols_sw]
        w_r[:, gp, :, 512:640] = w_in[:, :, 1024 + cols]
        w_r[:, gp, :, 640:768] = w_in[:, :, 1536 + cols]
        w_or[:, gp] = w_out[:, cols, :]
    perm = np.concatenate([np.concatenate([np.arange(c * 64, c * 64 + 64), np.arange((4 + c) * 64, (4 + c) * 64 + 64)]) for c in range(4)])
    w_s = np.concatenate([w_in[:, :, 2048 + perm], w_in[:, :, 2560:2816]], axis=2)
    w_os = w_out[:, 512 + perm, :]
    w_r = np.ascontiguousarray(w_r.reshape(L, 4, 8, 128, 768).transpose(0, 1, 3, 2, 4))
    w_s = np.ascontiguousarray(w_s.reshape(L, 8, 128, 768).transpose(0, 2, 1, 3))
    w_os = np.ascontiguousarray(w_os.reshape(L, 4, 128, D).transpose(0, 2, 1, 3))
    wu = np.concatenate([w_up[:, :, :DFF].reshape(L, 8, 128, NU, 128), w_up[:, :, DFF:].reshape(L, 8, 128, NU, 128)], axis=4)
    w_up = np.ascontiguousarray(wu.transpose(0, 3, 2, 1, 4))
    w_down = np.ascontiguousarray(w_down.reshape(L, NU, 128, D).transpose(0, 2, 1, 3))
    bT_ada = np.ascontiguousarray(b_ada.reshape(L, 48, 128).transpose(2, 0, 1))
    convT = np.ascontiguousarray(np.concatenate([conv_w, conv_b[:, None, :]], axis=1).reshape(L, 4, 44, 128).transpose(3, 0, 2, 1))
    lnT = np.ascontiguousarray(np.stack([ln1_g, ln1_b, ln2_g, ln2_b], axis=1).reshape(L, 4, 8, 128).transpose(3, 0, 1, 2))
    cb_heads = np.array([(cbi % 2) * 4 + cbi // 2 for cbi in range(8)])
    sinkP = np.ascontiguousarray(np.broadcast_to(swa_sinks[:, cb_heads][None], (128, L, 8)))
    sinkS = np.zeros((32, L, 2), f32)
    for c in range(4):
        for g in range(2):
            sinkS[c * 8:(c + 1) * 8, :, g] = swa_sinks[None, :, g * 4 + c]
    cst = _consts()

    if "nc" not in _NC_CACHE:
        _NC_CACHE["nc"] = build_program()
    nc = _NC_CACHE["nc"]

    in_maps = []
    for b in range(8):
        xall = np.concatenate([x_prompt[b], x_sample[16 * b:16 * b + 16].reshape(128, D)], axis=0)
        call = np.concatenate([c_prompt[b:b + 1], c_sample[16 * b:16 * b + 16]], axis=0)
        m = dict(
            xT=np.ascontiguousarray(xall.T),
            cT=np.ascontiguousarray(call.T.reshape(8, 128, 17).transpose(1, 0, 2)),
            w_ada=w_ada, bT_ada=bT_ada, w_r=w_r, w_s=w_s, w_or=w_or, w_os=w_os, w_up=w_up, w_down=w_down,
            convT=convT, lnT=lnT, sinkP=sinkP, sinkS=sinkS, rel_bias=rel_bias,
            state_ret=np.ascontiguousarray(state_ret[:, 16 * b:16 * b + 16].reshape(L, 16, 4, 128, 64).transpose(0, 2, 3, 1, 4)),
            cache_k=np.ascontiguousarray(cache_swa_k[:, 16 * b:16 * b + 16].reshape(L, 16, 128, 128)),
            cache_v=np.ascontiguousarray(cache_swa_v[:, 16 * b:16 * b + 16].reshape(L, 16, 128, 128)),
            state_conv=np.ascontiguousarray(state_conv[:, 16 * b:16 * b + 16].reshape(L, 32, 2 * DFF)),
        )
        m.update(cst)
        in_maps.append(m)
    res = run_bass_kernel_spmd(nc, in_maps, core_ids=list(range(8)))
    R = res.results

    y_p = np.zeros((8, NP, D), f32)
    y_s = np.zeros((128, 8, D), f32)
    ret_p = np.zeros((L, 8, 8, 64, 64), f32)
    ret_s = np.zeros((L, 128, 8, 64, 64), f32)
    k_p = np.zeros((L, 8, 128, 2, 64), f32)
    v_p = np.zeros((L, 8, 128, 2, 64), f32)
    k_s = np.zeros((L, 128, 128, 2, 64), f32)
    v_s = np.zeros((L, 128, 128, 2, 64), f32)
    conv_p = np.zeros((L, 8, 2, 2 * DFF), f32)
    conv_s = np.zeros((L, 128, 2, 2 * DFF), f32)
    for b in range(8):
        r = R[b]
        yt = r["yT"].T
        y_p[b] = yt[:NP]
        y_s[16 * b:16 * b + 16] = yt[NP:].reshape(16, 8, D)
        ret_p[:, b] = r["o_ret_p"].reshape(L, 4, 2, 64, 64).reshape(L, 8, 64, 64)
        ret_s[:, 16 * b:16 * b + 16] = r["o_ret_s"].reshape(L, 4, 2, 64, 16, 64).transpose(0, 4, 1, 2, 3, 5).reshape(L, 16, 8, 64, 64)
        k_p[:, b] = r["o_k_p"].reshape(L, 128, 2, 64)
        v_p[:, b] = r["o_v_p"].reshape(L, 128, 2, 64)
        k_s[:, 16 * b:16 * b + 16] = r["o_k_s"].reshape(L, 16, 128, 2, 64)
        v_s[:, 16 * b:16 * b + 16] = r["o_v_s"].reshape(L, 16, 128, 2, 64)
        conv_p[:, b] = r["o_conv_p"].transpose(0, 3, 2, 1).reshape(L, 2, 2 * DFF)
        conv_s[:, 16 * b:16 * b + 16] = r["o_conv_s"].transpose(0, 3, 4, 2, 1).reshape(L, 16, 2, 2 * DFF)
    return (y_p, y_s, ret_p, k_p, v_p, conv_p, ret_s, k_s, v_s, conv_s)
```

```python
import numpy as np
import ml_dtypes
import concourse.bass as bass
import concourse.mybir as mybir
from concourse.ap import AP
from concourse.bass_utils import run_bass_kernel_spmd

F32 = mybir.dt.float32
BF16 = mybir.dt.bfloat16
ALU = mybir.AluOpType
AF = mybir.ActivationFunctionType
AX = mybir.AxisListType

D = 1024
L = 4
NT = 2176
NP = 2048
DFF = 2816
NU = 22
ALPHA = (2.0 * L) ** 0.25
EPS = 1e-5
PAST = 8192
BLOCKS = [(0, 512), (512, 512), (1024, 512), (1536, 512), (2048, 128)]
FGROUPS = [(0, 3), (3, 3), (6, 3), (9, 3), (12, 3), (15, 3), (18, 2), (20, 2)]
SLOT_ELEMS = 10240
ARENA_F32 = 12050
import os
DBG = int(os.environ.get('KDBG', '0'))


class Ring:
    def __init__(self, sems):
        self.sems = sems
        self.vals = [0] * len(sems)
        self.i = 0


class Prog:
    ENG = ("pe", "act", "dve", "pool", "sp")

    def __init__(self, nc, esem):
        self.nc = nc
        self.esem = esem
        self.q = {e: [] for e in self.ENG}
        self.cnt = {e: 0 for e in self.ENG}
        self.seen = {e: {} for e in self.ENG}
        self.res = {}
        self.out_toks = []
        self.recent = []

    def _need(self, eng, tok):
        sem, val, teng, kind = tok
        sid = id(sem)
        if self.seen[eng].get(sid, 0) >= val:
            return
        self.seen[eng][sid] = val
        self.q[eng].append(("wait", sem, val))

    def _deps(self, eng, r, w, is_dma):
        toks = []
        for k in r:
            st = self.res.get(k)
            if st and st["w"] is not None:
                toks.append(st["w"])
        for k in w:
            st = self.res.get(k)
            if not st:
                continue
            if st["w"] is not None:
                t = st["w"]
                if not (eng == "pe" and t[3] == "op" and t[2] == "pe"):
                    toks.append(t)
            for t in st["r"]:
                if not (eng == "pe" and t[3] == "op" and t[2] == "pe"):
                    toks.append(t)
        best = {}
        for t in toks:
            sid = id(t[0])
            if sid not in best or best[sid][1] < t[1]:
                best[sid] = t
        for t in best.values():
            self._need(eng, t)

    def _record(self, tok, r, w):
        for k in r:
            st = self.res.setdefault(k, {"w": None, "r": []})
            st["r"].append(tok)
        for k in w:
            self.res[k] = {"w": tok, "r": []}

    def op(self, eng, fn, r=(), w=()):
        self._deps(eng, r, w, False)
        self.cnt[eng] += 1
        tok = (self.esem[eng], self.cnt[eng], eng, "op")
        self.q[eng].append(("op", fn))
        self._record(tok, r, w)

    def dma(self, q, ring, out, in_, r=(), w=(), is_out=False):
        self._deps(q, r, w, True)
        s = ring.i
        ring.i = (ring.i + 1) % len(ring.sems)
        sem = ring.sems[s]
        if ring.vals[s] > 0:
            self._need(q, (sem, ring.vals[s], None, "dma"))
        ring.vals[s] += 16
        tok = (sem, ring.vals[s], None, "dma")
        self.q[q].append(("dma", out, in_, sem))
        self.recent.append(tok)
        self._record(tok, r, w)
        if is_out:
            self.out_toks.append(tok)

    def barrier(self):
        for e in self.ENG:
            for t in self.recent:
                self._need(e, t)
            for o in self.ENG:
                if o != e and self.cnt[o] > 0:
                    self._need(e, (self.esem[o], self.cnt[o], o, "op"))
        self.recent = []

    def emit(self, block):
        nc = self.nc
        engs = {"pe": nc.tensor, "act": nc.scalar, "dve": nc.vector, "pool": nc.gpsimd, "sp": nc.sync}
        for t in self.out_toks:
            self._need("sp", t)

        def run(name):
            e = engs[name]
            for it in self.q[name]:
                if it[0] == "wait":
                    e.wait_ge(it[1], it[2])
                elif it[0] == "op":
                    it[1](e).then_inc(self.esem[name], 1)
                else:
                    e.dma_start(out=it[1], in_=it[2]).then_inc(it[3], 16)

        @block.tensor
        def _(t):
            run("pe")

        @block.scalar
        def _(t):
            run("act")

        @block.vector
        def _(t):
            run("dve")

        @block.gpsimd
        def _(t):
            run("pool")

        @block.sync
        def _(t):
            run("sp")


def _ins(ap, pos, cnt):
    l = [list(x) for x in ap.ap]
    l.insert(pos, [0, cnt])
    return AP(ap.tensor, ap.offset, l)


def build_program():
    nc = bass.Bass("TRN2", target_bir_lowering=False)

    def din(name, shape, dt=F32):
        return nc.dram_tensor(name, list(shape), dt, kind="ExternalInput").ap()

    def dout(name, shape):
        return nc.dram_tensor(name, list(shape), F32, kind="ExternalOutput").ap()

    xT = din("xT", [D, NT])
    cT = din("cT", [128, 8, 17])
    w_ada = din("w_ada", [L, D, 6 * D])
    bT_ada = din("bT_ada", [128, L, 48])
    w_r = din("w_r", [L, 4, 128, 8, 768])
    w_s = din("w_s", [L, 128, 8, 768])
    w_or = din("w_or", [L, 4, 128, D])
    w_os = din("w_os", [L, 128, 4, D])
    w_up = din("w_up", [L, NU, 128, 8, 256])
    w_down = din("w_down", [L, 128, NU, D])
    convT = din("convT", [128, L, 44, 4])
    lnT = din("lnT", [128, L, 4, 8])
    sinkP = din("sinkP", [128, L, 8])
    sinkS = din("sinkS", [32, L, 2])
    rel_bias = din("rel_bias", [32, 8])
    ohr = din("ohr", [32, 128])
    rot = din("rot", [5, 128, 2, 512])
    decm = din("decm", [128, 8, 128])
    blkind = din("blkind", [128, 128])
    qdec = din("qdec", [128, 4, 2, 512])
    kdec = din("kdec", [128, 2, 8])
    cdec = din("cdec", [128, 2, 4, 3])
    hmask = din("hmask", [128, 2])
    seqm = din("seqm", [128, 16])
    identf = din("identf", [128, 128])
    hsel = din("hsel", [128, 128])
    state_ret = din("state_ret", [L, 4, 128, 16, 64])
    cache_k = din("cache_k", [L, 16, 128, 128])
    cache_v = din("cache_v", [L, 16, 128, 128])
    state_conv = din("state_conv", [L, 32, 2 * DFF])

    yT = dout("yT", [D, NT])
    o_ret_p = dout("o_ret_p", [L, 4, 128, 64])
    o_ret_s = dout("o_ret_s", [L, 4, 128, 16, 64])
    o_k_p = dout("o_k_p", [L, 128, 128])
    o_v_p = dout("o_v_p", [L, 128, 128])
    o_k_s = dout("o_k_s", [L, 16, 128, 128])
    o_v_s = dout("o_v_s", [L, 16, 128, 128])
    o_conv_p = dout("o_conv_p", [L, 128, 44, 2])
    o_conv_s = dout("o_conv_s", [L, 128, 44, 16, 2])
    ext_d = nc.dram_tensor("ext_d", [8, 128, 512], F32, kind="Internal").ap()

    from contextlib import ExitStack
    es = ExitStack()

    def sb(name, shape, dt=F32):
        return es.enter_context(nc.sbuf_tensor(name, list(shape), dt))

    with es:
        X = sb("X", [128, 8, NT])
        HT = sb("HT", [128, 8, NT], BF16)
        WS = [sb("WS0", [128, SLOT_ELEMS], BF16), sb("WS1", [128, SLOT_ELEMS], BF16)]
        ARF = sb("ARENA", [128, ARENA_F32])
        ARB = ARF.bitcast(BF16)
        modT = sb("modT", [128, 48, 17])
        scT = sb("scT", [128, 8, 17], BF16)
        c_sb = sb("c_sb", [128, 8, 17])
        bada = sb("bada", [128, L, 48])
        convp = sb("convp", [128, L, 44, 4])
        lnp = sb("lnp", [128, L, 4, 8])
        sinkp = sb("sinkp", [128, L, 8])
        sinks = sb("sinks", [32, L, 2])
        identb = sb("identb", [128, 128], BF16)
        identF = sb("identF", [128, 128])
        hselb = sb("hselb", [128, 128], BF16)
        onesb = sb("onesb", [128, 128], BF16)
        blk = sb("blk", [128, 128])
        kdecS = sb("kdecS", [128, 2, 8])
        cdecS = sb("cdecS", [128, 2, 4, 3])
        hmS = sb("hmS", [128, 2])
        seqS = sb("seqS", [128, 16])
        epsT = sb("epsT", [128, 1])
        biasS = sb("biasS", [32, 2, 136], BF16)
        tabA = sb("tabA", [128, 4, 8, 17])
        gate = sb("gate", [128, 2, 8, 17])
        Sp = sb("Sp", [128, 4, 64])
        Sblk = sb("Sblk", [128, 4, 2, 64], BF16)
        kvf = sb("kvf", [128, 2, 128])
        stat = sb("stat", [128, 4, 32])
        stat2 = sb("stat2", [128, 2, 32])
        halo = sb("halo", [128, 44, 2])

        PSB = [es.enter_context(nc.psum_tensor("ps%d" % i, [128, 512], F32)) for i in range(8)]

        def sem(name):
            return es.enter_context(nc.semaphore(name))

        esem = {e: sem("e_" + e) for e in Prog.ENG}
        P = Prog(nc, esem)
        rW = Ring([sem("rw%d" % i) for i in range(6)])
        rC = Ring([sem("rc%d" % i) for i in range(6)])
        rO = Ring([sem("ro%d" % i) for i in range(8)])
        rT = Ring([sem("rt%d" % i) for i in range(6)])
        rP = Ring([sem("rp%d" % i) for i in range(6)])

        psi = [0]

        reserved = set()

        def psum():
            while True:
                i = psi[0]
                psi[0] = (i + 1) % 8
                if i not in reserved:
                    return i

        def PS(i):
            return PSB[i]

        def PSb(i):
            return PSB[i].bitcast(BF16)

        class Arena:
            def __init__(self):
                self.off = 0

            def f32(self, shape):
                n = int(np.prod(shape[1:]))
                v = ARF[0:shape[0], self.off:self.off + n]
                self.off += n
                assert self.off <= ARENA_F32, ("arena overflow", self.off)
                return _View(v, shape)

            def bf16(self, shape):
                n = int(np.prod(shape[1:]))
                nw = (n + 1) // 2
                v = ARB[0:shape[0], 2 * self.off:2 * self.off + n]
                self.off += nw
                assert self.off <= ARENA_F32, ("arena overflow", self.off)
                return _View(v, shape)

        class _View:
            def __init__(self, ap, shape):
                self.ap = ap
                self.shape = list(shape)
                if len(shape) == 2:
                    self.v = ap
                else:
                    names = "abcdefg"[:len(shape) - 1]
                    pat = "p (" + " ".join(names) + ") -> p " + " ".join(names)
                    kw = {names[i]: shape[i + 1] for i in range(len(names))}
                    self.v = ap.rearrange(pat, **kw)

            def __getitem__(self, k):
                return self.v[k]

        def mm(out, lhsT, rhs, start, stop, r, w):
            P.op("pe", lambda e: e.matmul(out, lhsT=lhsT, rhs=rhs, start=start, stop=stop), r=r, w=w)

        def tr(out, in_, r, w):
            k = in_.shape[0]
            P.op("pe", lambda e: e.transpose(out, in_, identb[0:k, 0:k]), r=list(r) + ["identb"], w=w)

        def act(out, in_, func, r, w, bias=None, scale=None, accum=None):
            kw = {}
            if bias is not None:
                kw["bias"] = bias
            if scale is not None:
                kw["scale"] = scale
            if accum is not None:
                kw["accum_out"] = accum
            P.op("act", lambda e: e.activation(out=out, in_=in_, func=func, **kw), r=r, w=w)

        def tt(eng, out, in0, in1, op, r, w):
            P.op(eng, lambda e: e.tensor_tensor(out=out, in0=in0, in1=in1, op=op), r=r, w=w)

        def ts(eng, out, in0, s1, s2, op0, op1, r, w):
            if op1 is None:
                P.op(eng, lambda e: e.tensor_scalar(out=out, in0=in0, scalar1=s1, scalar2=None, op0=op0), r=r, w=w)
            else:
                P.op(eng, lambda e: e.tensor_scalar(out=out, in0=in0, scalar1=s1, scalar2=s2, op0=op0, op1=op1), r=r, w=w)

        def stt(eng, out, in0, scalar, in1, op0, op1, r, w):
            P.op(eng, lambda e: e.scalar_tensor_tensor(out=out, in0=in0, scalar=scalar, in1=in1, op0=op0, op1=op1), r=r, w=w)

        def cp(eng, out, in_, r, w):
            if eng == "act":
                P.op("act", lambda e: e.copy(out=out, in_=in_), r=r, w=w)
            else:
                P.op(eng, lambda e: e.tensor_copy(out=out, in_=in_), r=r, w=w)

        def ms(eng, ap, val, w):
            P.op(eng, lambda e: e.memset(ap, val), r=(), w=w)

        def ld(q, ring, out, in_, w, r=()):
            P.dma(q, ring, out, in_, r=r, w=w)

        arS = Arena()
        arS.off = 9000
        ext = arS.f32([8, 512])
        rb = arS.f32([32, 8])
        ohrS = arS.f32([32, 128])
        ld("sp", rC, c_sb[:], cT, ["c_sb"])
        ld("sp", rC, bada[:], bT_ada, ["bada"])
        ld("sp", rC, convp[:], convT, ["convp"])
        ld("sp", rC, lnp[:], lnT, ["lnp"])
        ld("sp", rC, sinkp[:], sinkP, ["sinkp"])
        ld("sp", rC, sinks[:], sinkS, ["sinks"])
        ld("sp", rC, blk[:], blkind, ["blk"])
        ld("sp", rC, kdecS[:], kdec, ["kdec"])
        ld("sp", rC, cdecS[:], cdec, ["cdec"])
        ld("sp", rC, hmS[:], hmask, ["hm"])
        ld("sp", rC, seqS[:], seqm, ["seqm"])
        ld("sp", rC, identF[:], identf, ["identF"])
        ld("sp", rC, rb[:, :], rel_bias, ["rb"])
        ld("sp", rC, ohrS[:, :], ohr, ["ohrS"])
        ld("pool", rP, identb[:], identf, ["identb"])
        ld("pool", rP, hselb[:], hsel, ["hselb"])
        ms("dve", epsT[:], EPS, ["epsT"])
        ms("dve", onesb[:], 1.0 / 1024.0, ["ones"])
        ms("dve", ext[:, :], -1e30, ["ext"])
        for fc in range(8):
            ld("sp", rC, X[:, fc, :], xT[fc * 128:(fc + 1) * 128, :], [("X", fc, b) for b in range(5)])

        b0 = psum()
        mm(PS(b0)[0:8, 0:128], rb[:, :], ohrS[:, :], True, True, ["rb", "ohrS"], [("ps", b0)])
        cp("dve", ext[:, 128:256], PS(b0)[0:8, 0:128], [("ps", b0), "ext"], ["ext"])
        P.dma("sp", rC, ext_d, _ins(ext[:, :], 1, 128), r=["ext"], w=["ext_d"])
        for cbi in range(8):
            c_, g_ = cbi // 2, cbi % 2
            h_ = g_ * 4 + c_
            src2 = AP(ext_d.tensor, h_ * 65536 + 127, [[511, 8], [1, 136]])
            P.dma("pool", rP, biasS[c_ * 8:(c_ + 1) * 8, g_, :], src2, r=["ext_d"], w=["biasS"])
        act(scT[:], c_sb[:], AF.Silu, ["c_sb"], ["scT"])

        deferred = []

        def drain(k=None):
            n_ = len(deferred) if k is None else min(k, len(deferred))
            for _ in range(n_):
                deferred.pop(0)()

        def load_pass_weights(kind, l, idx, s):
            W = WS[s]
            key = ("W", s)

            def q(out, in_):
                deferred.append(lambda: P.dma("pool", rW, out, in_, w=[key]))
            if kind == "R":
                v1 = W[:, 0:8 * 768].rearrange("p (k n) -> p k n", k=8)
                q(v1[:, 0:4, :], w_r[l, idx, :, 0:4, :])
                q(v1[:, 4:8, :], w_r[l, idx, :, 4:8, :])
                v2 = W[:, 6144:6144 + 1024]
                q(v2, w_or[l, idx])
                return v1, v2
            if kind == "S":
                v1 = W[:, 0:8 * 768].rearrange("p (k n) -> p k n", k=8)
                q(v1[:, 0:4, :], w_s[l, :, 0:4, :])
                q(v1[:, 4:8, :], w_s[l, :, 4:8, :])
                v2 = W[:, 6144:6144 + 4096].rearrange("p (k n) -> p k n", k=4)
                q(v2, w_os[l])
                return v1, v2
            u0, nu = FGROUPS[idx]
            vu = W[:, 0:nu * 2048].rearrange("p (u k n) -> p u k n", u=nu, k=8)
            vd = W[:, 6144:6144 + nu * 1024].rearrange("p (u n) -> p u n", u=nu)
            for ui in range(nu):
                q(vu[:, ui, :, :], w_up[l, u0 + ui])
            q(vd, w_down[l, :, u0:u0 + nu, :])
            return vu, vd

        def compute_mod(l):
            ar = Arena()
            WA = [ar.bf16([128, 8, 256]), ar.bf16([128, 8, 256])]
            mk = "modT"
            bks = [psum(), psum()]
            for jj in range(24):
                wa = WA[jj % 2]
                wk = ("WA", jj % 2)
                P.dma("pool", rW, wa[:], w_ada[l, :, jj * 256:(jj + 1) * 256].rearrange("(k p) n -> p k n", p=128), w=[wk])
                for sub in range(2):
                    j = jj * 2 + sub
                    bk = bks[j // 24]
                    for kc in range(8):
                        mm(PS(bk)[:, (j % 24) * 17:(j % 24) * 17 + 17], wa[:, kc, sub * 128:(sub + 1) * 128], scT[:, kc, :],
                           kc == 0, kc == 7, [wk, "scT"], [("ps", bk)])
            for hf in range(2):
                tt("dve", modT[:, hf * 24:(hf + 1) * 24, :],
                   PS(bks[hf])[:, 0:408].rearrange("p (j r) -> p j r", r=17),
                   _ins(bada[:, l, hf * 24:(hf + 1) * 24], 2, 17), ALU.add,
                   [("ps", bks[hf]), "bada"], [mk])

        def make_tables(use_mod, which_mod, g_ap, b_ap, last):
            mk = "modT"
            if use_mod:
                sh = modT[:, which_mod * 24 + 0:which_mod * 24 + 8, :]
                sc = modT[:, which_mod * 24 + 8:which_mod * 24 + 16, :]
                onep = tabA[:, 1, :, :]
                ts("pool", onep, sc, 1.0, None, ALU.add, None, [mk, "tabA"], ["tabA"])
                if g_ap is None:
                    cp("pool", tabA[:, 0, :, :], onep, ["tabA"], ["tabA"])
                    cp("pool", tabA[:, 1, :, :], sh, [mk, "tabA"], ["tabA"])
                else:
                    tt("pool", tabA[:, 0, :, :], onep, _ins(g_ap, 2, 17), ALU.mult, ["tabA", "lnp"], ["tabA"])
                    tt("pool", tabA[:, 1, :, :], onep, _ins(b_ap, 2, 17), ALU.mult, ["tabA", "lnp"], ["tabA"])
                    tt("pool", tabA[:, 1, :, :], tabA[:, 1, :, :], sh, ALU.add, ["tabA", mk], ["tabA"])
            a = 1.0 if last else ALPHA
            if g_ap is None:
                ms("pool", tabA[:, 2, :, 0:1], a, ["tabA"])
                ms("pool", tabA[:, 3, :, 0:1], 0.0, ["tabA"])
            else:
                ts("pool", tabA[:, 2, :, 0:1], _ins(g_ap, 2, 1), a, None, ALU.mult, None, ["lnp", "tabA"], ["tabA"])
                ts("pool", tabA[:, 3, :, 0:1], _ins(b_ap, 2, 1), a, None, ALU.mult, None, ["lnp", "tabA"], ["tabA"])

        def make_gates():
            cp("pool", gate[:, 0, :, :], modT[:, 16:24, :], ["modT", "gate"], ["gate"])
            cp("pool", gate[:, 1, :, :], modT[:, 40:48, :], ["modT", "gate"], ["gate"])

        def finish_fc(bi, n0, n, fc, src, src_keys, do_h, xh1):
            smp = bi == 4
            src_keys = list(src_keys) + [("X", fc, bi)]
            if do_h:
                if not smp:
                    act(HT[:, fc, n0:n0 + n], src, AF.Identity, list(src_keys) + ["tabA"], [("HT", fc, bi)],
                        bias=tabA[:, 1, fc, 0:1], scale=tabA[:, 0, fc, 0:1])
                else:
                    s3 = src.rearrange("p (s i) -> p s i", i=8)
                    x3 = xh1[:, 0:128].rearrange("p (s i) -> p s i", i=8)
                    tt("dve", x3, s3, _ins(tabA[:, 0, fc, 1:17], 2, 8), ALU.mult, list(src_keys) + ["tabA", "xh1"], ["xh1"])
                    tt("dve", HT[:, fc, n0:n0 + n].rearrange("p (s i) -> p s i", i=8), x3,
                       _ins(tabA[:, 1, fc, 1:17], 2, 8), ALU.add, ["xh1", "tabA"], [("HT", fc, bi)])
            ts("dve" if fc % 2 == 0 else "pool", X[:, fc, n0:n0 + n], src, tabA[:, 2, fc, 0:1], tabA[:, 3, fc, 0:1],
               ALU.mult, ALU.add, list(src_keys) + ["tabA", ("X", fc, bi)], [("X", fc, bi)])

        def finish(bi, n0, n, src_fn, src_keys, do_h, xh1):
            for fc in range(8):
                finish_fc(bi, n0, n, fc, src_fn(fc), src_keys, do_h, xh1)

        def accum(bi, fc, bk, gi, tmp):
            n0, n = BLOCKS[bi]
            if bi < 4 and fc % 2 == 1 and len(tmp) > 1:
                tb = tmp[1 + (fc // 2) % 2]
                tkk = ("acct2", (fc // 2) % 2)
                act(tb[:, 0:n], PS(bk)[:, 0:n], AF.Identity, [("ps", bk), "gate", tkk], [tkk], scale=gate[:, gi, fc, 0:1])
                tt("pool", X[:, fc, n0:n0 + n], X[:, fc, n0:n0 + n], tb[:, 0:n], ALU.add, [tkk, ("X", fc, bi)], [("X", fc, bi)])
            elif bi < 4:
                stt("dve", X[:, fc, n0:n0 + n], PS(bk)[:, 0:n], gate[:, gi, fc, 0:1], X[:, fc, n0:n0 + n], ALU.mult, ALU.add,
                    [("ps", bk), "gate", ("X", fc, bi)], [("X", fc, bi)])
            else:
                tt("dve", tmp[0][:, 0:128].rearrange("p (s i) -> p s i", i=8), PS(bk)[:, 0:128].rearrange("p (s i) -> p s i", i=8),
                   _ins(gate[:, gi, fc, 1:17], 2, 8), ALU.mult, [("ps", bk), "gate", "acctmp"], ["acctmp"])
                tt("pool", X[:, fc, n0:n0 + n], X[:, fc, n0:n0 + n], tmp[0][:, 0:128], ALU.add, ["acctmp", ("X", fc, bi)], [("X", fc, bi)])

        class LNBuf:
            def __init__(self, ar):
                self.vb = [ar.bf16([128, 512]), ar.bf16([128, 512])]
                self.vsq = [ar.bf16([128, 512]), ar.bf16([128, 512])]
                self.tmp = [ar.f32([128, 512]) for _ in range(3)]
                self.mean = ar.f32([128, 512])
                self.rstd = ar.f32([128, 512])
                self.xh1 = ar.f32([128, 128])

        def ln_finish_block(bi, lb, do_h):
            n0, n = BLOCKS[bi]
            bm, bq = psum(), psum()
            reserved.add(bm)
            reserved.add(bq)
            for fc in range(8):
                vbk, vsk = ("vb", fc % 2), ("vsq", fc % 2)
                cp("dve", lb.vb[fc % 2][:, 0:n], X[:, fc, n0:n0 + n], [("X", fc, bi), vbk], [vbk])
                act(lb.vsq[fc % 2][:, 0:n], X[:, fc, n0:n0 + n], AF.Square, [("X", fc, bi), vsk], [vsk])
                mm(PS(bm)[:, 0:n], onesb[:, :], lb.vb[fc % 2][:, 0:n], fc == 0, fc == 7, ["ones", vbk], [("ps", bm)])
                mm(PS(bq)[:, 0:n], onesb[:, :], lb.vsq[fc % 2][:, 0:n], fc == 0, fc == 7, ["ones", vsk], [("ps", bq)])
            cp("act", lb.mean[:, 0:n], PS(bm)[:, 0:n], [("ps", bm), "mean"], ["mean"])
            tt("dve", lb.rstd[:, 0:n], lb.mean[:, 0:n], lb.mean[:, 0:n], ALU.mult, ["mean", "rstd"], ["rstd"])
            tt("dve", lb.rstd[:, 0:n], PS(bq)[:, 0:n], lb.rstd[:, 0:n], ALU.subtract, [("ps", bq), "rstd"], ["rstd"])
            reserved.discard(bm)
            reserved.discard(bq)
            act(lb.rstd[:, 0:n], lb.rstd[:, 0:n], AF.Ln, ["rstd", "epsT"], ["rstd"], bias=epsT[:, 0:1])
            act(lb.rstd[:, 0:n], lb.rstd[:, 0:n], AF.Exp, ["rstd"], ["rstd"], scale=-0.5)
            for fc in range(8):
                tk = ("lntmp", fc % 3)
                t_ = lb.tmp[fc % 3]
                tt("pool", t_[:, 0:n], X[:, fc, n0:n0 + n], lb.mean[:, 0:n], ALU.subtract, [("X", fc, bi), "mean", tk], [tk])
                tt("dve", t_[:, 0:n], t_[:, 0:n], lb.rstd[:, 0:n], ALU.mult, [tk, "rstd"], [tk])
                finish_fc(bi, n0, n, fc, t_[:, 0:n], [tk], do_h, lb.xh1)

        def ret_pass(l, gp, WR_, WO_, wk):
            ar = Arena()
            rotS = [ar.f32([128, 2, 512]), ar.f32([128, 2, 512])]
            decmS = ar.f32([128, 2, 128])
            qdecS = ar.f32([128, 2, 512])
            t1 = ar.f32([128, 512])
            t2 = ar.f32([128, 512])
            t3 = t1
            t4 = t2
            qp = ar.bf16([128, 512])
            qpp = ar.bf16([128, 512])
            kf = ar.bf16([128, 512])
            ka = ar.bf16([128, 512])
            kb = ar.bf16([128, 512])
            gs = ar.bf16([128, 512])
            retT = ar.bf16([128, 512])
            vpad = [ar.bf16([128, 2, 128]), ar.bf16([128, 2, 128])]
            sT = [ar.bf16([128, 2, 128]), ar.bf16([128, 2, 128])]
            ktok = [ar.bf16([128, 128]), ar.bf16([128, 128])]
            of32 = [ar.f32([128, 128]), ar.f32([128, 128])]
            ob = [ar.bf16([128, 2, 128]), ar.bf16([128, 2, 128])]
            hn1 = [ar.f32([128, 128]), ar.f32([128, 128])]
            hn2 = [ar.f32([128, 128]), ar.f32([128, 128])]
            Ss = ar.f32([128, 16, 64])
            Ssblk = ar.bf16([128, 16, 2, 64])
            st1 = ar.f32([128, 4, 64])
            vblk = ar.bf16([128, 4, 128])
            acct = [ar.f32([128, 128]), ar.f32([128, 512]), ar.f32([128, 512])]
            ld("sp", rT, decmS[:, :, :], decm[:, gp * 2:gp * 2 + 2, :], ["decm"])
            ld("sp", rT, qdecS[:, :, :], qdec[:, gp, :, :], ["qdec"])
            P.dma("sp", rT, Ss[:, :, :], state_ret[l, gp], w=["Ss"])
            ms("pool", ka[:, :], 0.0, ["ka"])
            ms("pool", kb[:, :], 0.0, ["kb"])
            ms("pool", vpad[0][:, :, :], 0.0, [("vpad", 0)])
            ms("pool", vpad[1][:, :, :], 0.0, [("vpad", 1)])
            ms("dve", Sp[:, gp, :], 0.0, [("Sp", gp)])
            ms("dve", Sblk[:, gp, :, :], 0.0, [("Sblk", gp)])
            tt("dve", Ssblk[:, :, :, :], _ins(Ss[:, :, :], 2, 2),
               _ins(_ins(hmS[:, :], 1, 16), 3, 64), ALU.mult, ["Ss", "hm", "Ssblk"], ["Ssblk"])
            for bi, (n0, n) in enumerate(BLOCKS):
                drain(1)
                smp = bi == 4
                di = 1 if smp else 0
                hk = [("HT", kc, bi) for kc in range(8)]
                nch = n // 128
                if bi == 0:
                    P.dma("sp", rT, rotS[0][:, :, :], rot[0], w=[("rot", 0)])
                if bi + 1 < len(BLOCKS):
                    P.dma("sp", rT, rotS[(bi + 1) % 2][:, :, :], rot[bi + 1], w=[("rot", (bi + 1) % 2)])
                rS = rotS[bi % 2]
                rk_ = ("rot", bi % 2)

                def proj(col0):
                    bk = psum()
                    for kc in range(8):
                        mm(PS(bk)[:, 0:n], WR_[:, kc, col0:col0 + 128], HT[:, kc, n0:n0 + n], kc == 0, kc == 7,
                           [wk, hk[kc]], [("ps", bk)])
                    return bk
                bq = proj(0)
                bqs = proj(128)
                tt("dve", t1[:, 0:n], PS(bq)[:, 0:n], rS[:, 0, 0:n], ALU.mult, [("ps", bq), rk_, "t1"], ["t1"])
                tt("dve", t2[:, 0:n], PS(bqs)[:, 0:n], rS[:, 1, 0:n], ALU.mult, [("ps", bqs), rk_, "t2"], ["t2"])
                tt("pool", t1[:, 0:n], t1[:, 0:n], t2[:, 0:n], ALU.add, ["t1", "t2"], ["t1"])
                cp("act", qp[:, 0:n], t1[:, 0:n], ["t1", "qp"], ["qp"])
                tt("pool", qpp[:, 0:n], t1[:, 0:n], qdecS[:, di, 0:n], ALU.mult, ["t1", "qdec", "qpp"], ["qpp"])
                bk_ = proj(256)
                bks_ = proj(384)
                tt("dve", t3[:, 0:n], PS(bk_)[:, 0:n], rS[:, 0, 0:n], ALU.mult, [("ps", bk_), rk_, "t1"], ["t1"])
                tt("dve", t4[:, 0:n], PS(bks_)[:, 0:n], rS[:, 1, 0:n], ALU.mult, [("ps", bks_), rk_, "t2"], ["t2"])
                tt("pool", kf[:, 0:n], t3[:, 0:n], t4[:, 0:n], ALU.add, ["t1", "t2", "kf"], ["kf"])
                cp("act", ka[0:64, 0:n], kf[0:64, 0:n], ["kf", "ka"], ["ka"])
                cp("act", kb[64:128, 0:n], kf[64:128, 0:n], ["kf", "kb"], ["kb"])
                bg = proj(640)
                act(gs[:, 0:n], PS(bg)[:, 0:n], AF.Silu, [("ps", bg), "gs"], ["gs"])

                def stageA(ci):
                    cidx = bi * 4 + ci
                    par = cidx % 2
                    c0 = n0 + ci * 128
                    sl = slice(ci * 128, (ci + 1) * 128)
                    bv = psum()
                    if smp:
                        reserved.add(bv)
                    for kc in range(8):
                        mm(PS(bv)[:, 0:128], HT[:, kc, c0:c0 + 128], WR_[:, kc, 512:640], kc == 0, kc == 7, [wk, hk[kc]], [("ps", bv)])
                    vp = vpad[par]
                    vk = ("vpad", par)
                    vo = AP(vp.ap.tensor, vp.ap.offset, [list(vp.ap.ap[0]), [192, 2], [1, 64]])
                    cp("act", vo, PS(bv)[:, 0:128].rearrange("p (b e) -> p b e", b=2), [("ps", bv), vk], [vk])
                    bs = psum()
                    mm(PS(bs)[:, 0:128], ka[:, sl], qp[:, sl], True, True, ["ka", "qp"], [("ps", bs)])
                    mm(PS(bs)[:, 128:256], kb[:, sl], qp[:, sl], True, True, ["kb", "qp"], [("ps", bs)])
                    st_ = sT[par]
                    sk_ = ("sT", par)
                    tt("dve", st_[:, :, :], PS(bs)[:, 0:256].rearrange("p (a i) -> p a i", a=2),
                       decmS[:, :, :], ALU.mult, [("ps", bs), "decm", sk_], [sk_])
                    if smp:
                        tt("pool", st_[:, :, :], st_[:, :, :], _ins(blk[:, :], 1, 2), ALU.mult, [sk_, "blk"], [sk_])
                    bo = psum()
                    reserved.add(bo)
                    mm(PS(bo)[:, 0:128], vp[:, 0, :], st_[:, 0, :], True, False, [vk, sk_], [("ps", bo)])
                    mm(PS(bo)[:, 0:128], vp[:, 1, :], st_[:, 1, :], False, False, [vk, sk_], [("ps", bo)])
                    if not smp:
                        mm(PS(bo)[:, 0:128], Sblk[:, gp, :, :].rearrange("p a e -> p (a e)"), qpp[:, sl], False, True,
                           [("Sblk", gp), "qpp"], [("ps", bo)])
                    else:
                        for s_ in range(16):
                            mm(PS(bo)[:, s_ * 8:(s_ + 1) * 8], Ssblk[:, s_, :, :].rearrange("p a e -> p (a e)"),
                               qpp[:, ci * 128 + s_ * 8:ci * 128 + (s_ + 1) * 8], False, s_ == 15,
                               ["Ssblk", "qpp"], [("ps", bo)])
                    bt = psum()
                    tr(PSb(bt)[:, 0:128], kf[:, sl], ["kf"], [("ps", bt)])
                    kt = ktok[par]
                    kk = ("ktok", par)
                    tt("dve", kt[:, :].rearrange("p (a d) -> p a d", a=2), PSb(bt)[:, 0:128].rearrange("p (a d) -> p a d", a=2),
                       _ins(kdecS[:, di, gp * 2:gp * 2 + 2], 2, 64), ALU.mult, [("ps", bt), "kdec", kk], [kk])
                    if not smp:
                        bd = psum()
                        mm(PS(bd)[:, 0:128], kt[:, :], vp[:, 0, :], True, False, [kk, vk], [("ps", bd)])
                        mm(PS(bd)[:, 0:128], kt[:, :], vp[:, 1, :], False, True, [kk, vk], [("ps", bd)])
                        ts("dve", st1[:, 0, :], PS(bd)[:, 0:64], cdecS[:, 0, gp, 1:2], None, ALU.mult, None,
                           [("ps", bd), "cdec", "st1"], ["st1"])
                        stt("dve", st1[:, 0, :], PS(bd)[:, 64:128], cdecS[:, 0, gp, 2:3], st1[:, 0, :], ALU.mult, ALU.add,
                            [("ps", bd), "cdec", "st1"], ["st1"])
                        stt("dve", Sp[:, gp, :], Sp[:, gp, :], cdecS[:, 0, gp, 0:1], st1[:, 0, :], ALU.mult, ALU.add,
                            ["st1", "cdec", ("Sp", gp)], [("Sp", gp)])
                        tt("pool", Sblk[:, gp, :, :], _ins(Sp[:, gp, :], 1, 2), _ins(hmS[:, :], 2, 64),
                           ALU.mult, [("Sp", gp), "hm", ("Sblk", gp)], [("Sblk", gp)])
                        if cidx == 15:
                            P.dma("sp", rO, o_ret_p[l, gp], Sp[:, gp, :], r=[("Sp", gp)], w=[("o_ret_p", l, gp)], is_out=True)
                    else:
                        for q4 in range(4):
                            tt("dve", vblk[:, :, :], _ins(PS(bv)[:, 0:128], 1, 4),
                               _ins(seqS[:, q4 * 4:(q4 + 1) * 4], 2, 128), ALU.mult, [("ps", bv), "seqm", "vblk"], ["vblk"])
                            bd = psum()
                            mm(PS(bd)[:, 0:512], kt[:, :], vblk[:, :, :].rearrange("p s e -> p (s e)"), True, True,
                               [kk, "vblk"], [("ps", bd)])
                            pv = PS(bd)[:, 0:512].rearrange("p (s a e) -> p s a e", s=4, a=2)
                            ts("dve", st1[:, :, :], pv[:, :, 0, :], cdecS[:, 1, gp, 1:2], None, ALU.mult, None,
                               [("ps", bd), "cdec", "st1"], ["st1"])
                            stt("dve", st1[:, :, :], pv[:, :, 1, :], cdecS[:, 1, gp, 2:3], st1[:, :, :],
                                ALU.mult, ALU.add, [("ps", bd), "cdec", "st1"], ["st1"])
                            stt("dve", Ss[:, q4 * 4:(q4 + 1) * 4, :], Ss[:, q4 * 4:(q4 + 1) * 4, :], cdecS[:, 1, gp, 0:1], st1[:, :, :],
                                ALU.mult, ALU.add, ["st1", "cdec", "Ss", "Ssblk"], ["Ss"])
                        reserved.discard(bv)
                        P.dma("sp", rO, o_ret_s[l, gp], Ss[:, :, :], r=["Ss"], w=[("o_ret_s", l, gp)], is_out=True)
                    return bo

                def stageB(ci, bo):
                    cidx = bi * 4 + ci
                    par = cidx % 2
                    sl = slice(ci * 128, (ci + 1) * 128)
                    o_, ob_, h1, h2 = of32[par], ob[par], hn1[par], hn2[par]
                    ko, kob, k1, k2 = ("of32", par), ("ob", par), ("hn1", par), ("hn2", par)
                    cp("act", o_[:, :], PS(bo)[:, 0:128], [("ps", bo), ko], [ko])
                    cp("act", ob_[:, 0, :], PS(bo)[:, 0:128], [("ps", bo), kob], [kob])
                    act(ob_[:, 1, :], PS(bo)[:, 0:128], AF.Square, [("ps", bo), kob], [kob])
                    reserved.discard(bo)
                    bh = psum()
                    mm(PS(bh)[:, 0:256], hselb[:, :], ob_[:, :, :].rearrange("p a i -> p (a i)"), True, True, ["hselb", kob], [("ps", bh)])
                    cp("act", h2[:, :], PS(bh)[:, 0:128], [("ps", bh), k2], [k2])
                    tt("dve", h1[:, :], h2[:, :], h2[:, :], ALU.mult, [k2, k1], [k1])
                    tt("dve", h1[:, :], PS(bh)[:, 128:256], h1[:, :], ALU.subtract, [("ps", bh), k1], [k1])
                    act(h1[:, :], h1[:, :], AF.Ln, [k1, "epsT"], [k1], bias=epsT[:, 0:1])
                    act(h1[:, :], h1[:, :], AF.Exp, [k1], [k1], scale=-0.5)
                    tt("pool", o_[:, :], o_[:, :], h2[:, :], ALU.subtract, [ko, k2], [ko])
                    tt("pool", o_[:, :], o_[:, :], h1[:, :], ALU.mult, [ko, k1], [ko])
                    tt("dve", retT[:, sl], o_[:, :], gs[:, sl], ALU.mult, [ko, "gs", "retT"], ["retT"])

                prev = None
                for ci in range(nch):
                    bo = stageA(ci)
                    if prev is not None:
                        stageB(*prev)
                    prev = (ci, bo)
                stageB(*prev)
                for fo in range(8):
                    bk = psum()
                    mm(PS(bk)[:, 0:n], WO_[:, fo * 128:(fo + 1) * 128], retT[:, 0:n], True, True, [wk, "retT"], [("ps", bk)])
                    accum(bi, fo, bk, 0, acct)

        def swa_pass(l, WQ_, WO_, wk):
            ar = Arena()
            biasP = ar.bf16([128, 8, 256])
            qa = ar.bf16([128, 4, 512])
            qb = ar.bf16([128, 4, 512])
            skT = ar.bf16([128, NT])
            svt = ar.bf16([128, 17, 128])
            pP = [ar.bf16([128, 256]) for _ in range(8)]
            pT = [ar.bf16([128, 4, 2, 128]) for _ in range(2)]
            attn = ar.bf16([128, 512])
            swaT = ar.bf16([128, 4, 512])
            kcr = ar.bf16([128, 4, 128])
            kTc = ar.bf16([128, 8, 128])
            vc = ar.bf16([128, 8, 128])
            pS = [ar.bf16([32, 256]) for _ in range(4)]
            pTs = [ar.bf16([128, 2, 32]) for _ in range(4)]
            attS = ar.bf16([32, 4, 128])
            acct = [ar.f32([128, 128]), ar.f32([128, 512]), ar.f32([128, 512])]
            for cbi in range(8):
                c_, g_ = cbi // 2, cbi % 2
                h_ = g_ * 4 + c_
                src = AP(ext_d.tensor, h_ * 65536 + 127, [[511, 128], [1, 256]])
                P.dma("pool", rP, biasP[:, cbi, :], src, r=["ext_d"], w=["biasP"])
            ms("pool", qa[:, :, :], 0.0, [("qa", c) for c in range(4)])
            ms("pool", qb[:, :, :], 0.0, [("qb", c) for c in range(4)])
            for _k in range(4):
                ms("pool", pS[_k][:, :], 0.0, [("pS", _k)])
            def load_cache_half(hf):
                P.dma("pool", rP, vc[:, :, :], cache_v[l, hf * 8:(hf + 1) * 8].rearrange("s k e -> k s e"), w=["vc"])
                for q4 in range(2):
                    P.dma("pool", rP, kcr[:, :, :], cache_k[l, hf * 8 + q4 * 4:hf * 8 + (q4 + 1) * 4].rearrange("s k e -> k s e"), w=["kcr"])
                    bt = psum()
                    for j in range(4):
                        tr(PSb(bt)[:, j * 128:(j + 1) * 128], kcr[:, j, :], ["kcr"], [("ps", bt)])
                    cp("act", kTc[:, q4 * 4:q4 * 4 + 4, :].rearrange("p s k -> p (s k)"), PSb(bt)[:, 0:512],
                       [("ps", bt), "kTc"], ["kTc"])
            P.dma("sp", rO, o_k_s[l, :, 0:120, :], cache_k[l, :, 8:128, :], w=[("o_k_s", l, 0)], is_out=True)
            P.dma("sp", rO, o_v_s[l, :, 0:120, :], cache_v[l, :, 8:128, :], w=[("o_v_s", l, 0)], is_out=True)
            for bi, (n0, n) in enumerate(BLOCKS):
                drain(1)
                smp = bi == 4
                hk = [("HT", kc, bi) for kc in range(8)]
                for c in range(4):
                    bk = psum()
                    for kc in range(8):
                        mm(PS(bk)[:, 0:n], WQ_[:, kc, c * 128:(c + 1) * 128], HT[:, kc, n0:n0 + n], kc == 0, kc == 7, [wk, hk[kc]], [("ps", bk)])
                    act(qa[0:64, c, 0:n], PS(bk)[0:64, 0:n], AF.Identity, [("ps", bk), ("qa", c)], [("qa", c)], scale=0.125)
                    act(qb[64:128, c, 0:n], PS(bk)[64:128, 0:n], AF.Identity, [("ps", bk), ("qb", c)], [("qb", c)], scale=0.125)
                bk = psum()
                for kc in range(8):
                    mm(PS(bk)[:, 0:n], WQ_[:, kc, 512:640], HT[:, kc, n0:n0 + n], kc == 0, kc == 7, [wk, hk[kc]], [("ps", bk)])
                cp("act", skT[:, n0:n0 + n], PS(bk)[:, 0:n], [("ps", bk), ("skT", bi)], [("skT", bi)])
                if smp:
                    pa = pP[0].ap
                    qS = [attn.ap.rearrange("p (s c i) -> p s c i", s=16, c=4),
                          AP(pa.tensor, pa.offset, [list(pa.ap[0]), [1, 512]]).rearrange("p (s c i) -> p s c i", s=16, c=4)]
                    qSk = ["attn", ("pP", 0), ("pP", 1)]
                    cp("dve", qS[0], qa[:, :, 0:128].rearrange("p c (s i) -> p s c i", i=8), [("qa", c) for c in range(4)] + qSk, qSk)
                    cp("dve", qS[1], qb[:, :, 0:128].rearrange("p c (s i) -> p s c i", i=8), [("qb", c) for c in range(4)] + qSk, qSk)
                for ci in range(n // 128):
                    cidx = bi * 4 + ci
                    c0 = n0 + ci * 128
                    sl = slice(ci * 128, (ci + 1) * 128)
                    bv = psum()
                    for kc in range(8):
                        mm(PS(bv)[:, 0:128], HT[:, kc, c0:c0 + 128], WQ_[:, kc, 640:768], kc == 0, kc == 7, [wk, hk[kc]], [("ps", bv)])
                    cp("act", svt[:, cidx, :], PS(bv)[:, 0:128], [("ps", bv), ("svt", cidx)], [("svt", cidx)])
                    if cidx >= 15:
                        bk2 = psum()
                        for kc in range(8):
                            mm(PS(bk2)[:, 0:128], HT[:, kc, c0:c0 + 128], WQ_[:, kc, 512:640], kc == 0, kc == 7, [wk, hk[kc]], [("ps", bk2)])
                        cp("dve", kvf[:, 0, :], PS(bk2)[:, 0:128], [("ps", bk2), "kvf"], ["kvf"])
                        cp("dve", kvf[:, 1, :], PS(bv)[:, 0:128], [("ps", bv), "kvf"], ["kvf"])
                        if cidx == 15:
                            P.dma("sp", rO, o_k_p[l], kvf[:, 0, :], r=["kvf"], w=[("o_k_p", l)], is_out=True)
                            P.dma("sp", rO, o_v_p[l], kvf[:, 1, :], r=["kvf"], w=[("o_v_p", l)], is_out=True)
                        else:
                            for s_ in range(16):
                                P.dma("sp", rO, o_k_s[l, s_, 120:128, :], kvf[s_ * 8:(s_ + 1) * 8, 0, :], r=["kvf"], w=[("o_k_s", l, 1 + s_)], is_out=True)
                                P.dma("sp", rO, o_v_s[l, s_, 120:128, :], kvf[s_ * 8:(s_ + 1) * 8, 1, :], r=["kvf"], w=[("o_v_s", l, 1 + s_)], is_out=True)
                    if not smp:
                        ba = psum()
                        reserved.add(ba)
                        if cidx == 0:
                            k0, nk, boff = c0, 128, 128
                        else:
                            k0, nk, boff = c0 - 128, 256, 0
                        nj = nk // 128
                        lbk = []
                        for hp in range(4):
                            bl = psum()
                            reserved.add(bl)
                            lbk.append(bl)
                            for hh in range(2):
                                cbi = hp * 2 + hh
                                c_, g_ = cbi // 2, cbi % 2
                                qsrc = qa if g_ == 0 else qb
                                qk = ("qa", c_) if g_ == 0 else ("qb", c_)
                                mm(PS(bl)[:, hh * 256:hh * 256 + nk], qsrc[:, c_, sl], skT[:, k0:k0 + nk], True, False,
                                   [qk, ("skT", bi), ("skT", max(bi - 1, 0))], [("ps", bl)])
                                mm(PS(bl)[:, hh * 256:hh * 256 + nk], identb[:, :], biasP[:, cbi, boff:boff + nk], False, True,
                                   ["identb", "biasP"], [("ps", bl)])
                        for hp in range(4):
                            bl = lbk[hp]
                            P.op("dve", lambda e, bl=bl, nk=nk, hp=hp: e.reduce_max(
                                out=stat[:, 0, hp * 2:hp * 2 + 2], in_=PS(bl)[:, 0:512].rearrange("p (a k) -> p a k", a=2)[:, :, 0:nk], axis=AX.X),
                                r=[("ps", bl), "stat"], w=["stat"])
                        tt("dve", stat[:, 1, 0:8], stat[:, 0, 0:8], sinkp[:, l, 0:8], ALU.max, ["stat", "sinkp"], ["stat"])
                        ts("dve", stat[:, 1, 0:8], stat[:, 1, 0:8], -1.0, None, ALU.mult, None, ["stat"], ["stat"])
                        tt("dve", stat2[:, 0, 8:16], sinkp[:, l, 0:8], stat[:, 1, 0:8], ALU.add, ["stat", "sinkp", "stat2"], ["stat2"])
                        act(stat[:, 3, 0:8], stat2[:, 0, 8:16], AF.Exp, ["stat2", "stat"], ["stat"])
                        ms("dve", stat[:, 2, 0:8], 0.0, ["stat"])
                        for cbi in range(8):
                            bl = lbk[cbi // 2]
                            pk = ("pP", cbi)
                            act(pP[cbi][:, 0:nk], PS(bl)[:, (cbi % 2) * 256:(cbi % 2) * 256 + nk], AF.Exp, [("ps", bl), "stat", pk], [pk, "stat"],
                                bias=stat[:, 1, cbi:cbi + 1], accum=stat[:, 2, cbi:cbi + 1])
                        for bl in lbk:
                            reserved.discard(bl)
                        bts = [psum(), psum()]
                        for q_ in range(2):
                            reserved.add(bts[q_])
                        for cbi in range(8):
                            for j in range(nj):
                                tr(PSb(bts[cbi // 4])[:, ((cbi % 4) * 2 + j) * 128:((cbi % 4) * 2 + j + 1) * 128], pP[cbi][:, j * 128:(j + 1) * 128],
                                   [("pP", cbi)], [("ps", bts[cbi // 4])])
                        for q_ in range(2):
                            tk = ("pT", q_)
                            srcv = PSb(bts[q_])[:, 0:1024].rearrange("p (h j i) -> p h j i", h=4, j=2)[:, :, 0:nj, :]
                            dstv = pT[q_][:, :, 0:nj, :]
                            cp("act" if q_ == 0 else "dve", dstv, srcv, [("ps", bts[q_]), tk], [tk])
                            reserved.discard(bts[q_])
                        for cbi in range(8):
                            c_, g_ = cbi // 2, cbi % 2
                            col = c_ * 128 + g_ * 64
                            for j in range(nj):
                                kci = cidx - (nj - 1) + j
                                mm(PS(ba)[:, col:col + 64], pT[cbi // 4][:, cbi % 4, j, :], svt[:, kci, g_ * 64:(g_ + 1) * 64], j == 0, j == nj - 1,
                                   [("pT", cbi // 4), ("svt", kci)], [("ps", ba)])
                        tt("dve", stat2[:, 0, 0:8], stat[:, 2, 0:8], stat[:, 3, 0:8], ALU.add, ["stat", "stat2"], ["stat2"])
                        P.op("dve", lambda e: e.reciprocal(out=stat2[:, 1, 0:8], in_=stat2[:, 0, 0:8]), r=["stat2"], w=["stat2"])
                        tt("dve", attn[:, :].rearrange("p (h e) -> p h e", e=64), PS(ba)[:, 0:512].rearrange("p (h e) -> p h e", e=64),
                           _ins(stat2[:, 1, 0:8], 2, 64), ALU.mult, [("ps", ba), "stat2", "attn"], ["attn"])
                        reserved.discard(ba)
                        bt = psum()
                        for c in range(4):
                            tr(PSb(bt)[:, c * 128:(c + 1) * 128], attn[:, c * 128:(c + 1) * 128], ["attn"], [("ps", bt)])
                        cp("act", swaT[:, :, sl], PSb(bt)[:, 0:512].rearrange("p (c i) -> p c i", c=4), [("ps", bt)] + [("swaT", c) for c in range(4)],
                           [("swaT", c) for c in range(4)])
                    else:
                        for s4 in range(4):
                            if s4 % 2 == 0:
                                load_cache_half(s4 // 2)
                            ba = psum()
                            reserved.add(ba)
                            ms("dve", stat[:, 2:4, :], 0.0, ["stat"])
                            for sg in range(8):
                                s_ = s4 * 4 + sg // 2
                                g_ = sg % 2
                                qks = qSk
                                bl = psum()
                                lq = qS[g_][:, s_, :, :].rearrange("p c i -> p (c i)")
                                mm(PS(bl)[0:32, 0:128], lq, kTc[:, s_ % 8, :], True, False, qks + ["kTc"], [("ps", bl)])
                                mm(PS(bl)[0:32, 0:128], identb[0:32, 0:32], biasS[:, g_, 0:128], False, True, ["identb", "biasS"], [("ps", bl)])
                                mm(PS(bl)[0:32, 128:136], lq, skT[:, n0 + s_ * 8:n0 + (s_ + 1) * 8], True, False, qks + [("skT", bi)], [("ps", bl)])
                                mm(PS(bl)[0:32, 128:136], identb[0:32, 0:32], biasS[:, g_, 128:136], False, True, ["identb", "biasS"], [("ps", bl)])
                                col = sg
                                P.op("dve", lambda e, bl=bl, col=col: e.reduce_max(out=stat[0:32, 0, col:col + 1], in_=PS(bl)[0:32, 0:136], axis=AX.X),
                                     r=[("ps", bl), "stat"], w=["stat"])
                                ts("dve", stat[0:32, 1, col:col + 1], stat[0:32, 0, col:col + 1], sinks[:, l, g_:g_ + 1], -1.0, ALU.max, ALU.mult,
                                   ["stat", "sinks"], ["stat"])
                                pp = pS[sg % 4]
                                pk = ("pS", sg % 4)
                                ms("pool", pp[:, 128:256], 0.0, [pk])
                                act(pp[:, 0:128], PS(bl)[0:32, 0:128], AF.Exp, [("ps", bl), "stat", pk], [pk, "stat"],
                                    bias=stat[0:32, 1, col:col + 1], accum=stat[0:32, 2, col:col + 1])
                                act(pp[:, 128 + s_ * 8:128 + (s_ + 1) * 8], PS(bl)[0:32, 128:136], AF.Exp, [("ps", bl), "stat", pk], [pk, "stat"],
                                    bias=stat[0:32, 1, col:col + 1], accum=stat[0:32, 3, col:col + 1])
                                act(stat2[0:32, 0, col:col + 1], sinks[:, l, g_:g_ + 1], AF.Exp, ["sinks", "stat", "stat2"], ["stat2"],
                                    bias=stat[0:32, 1, col:col + 1])
                                bt = psum()
                                tr(PSb(bt)[:, 0:32], pp[:, 0:128], [pk], [("ps", bt)])
                                tr(PSb(bt)[:, 32:64], pp[:, 128:256], [pk], [("ps", bt)])
                                pt = pTs[sg % 4]
                                tk = ("pTs", sg % 4)
                                cp("act", pt[:, :, :].rearrange("p a i -> p (a i)"), PSb(bt)[:, 0:64], [("ps", bt), tk], [tk])
                                oc = (sg // 2) * 128 + g_ * 64
                                mm(PS(ba)[0:32, oc:oc + 64], pt[:, 0, :], vc[:, s_ % 8, g_ * 64:(g_ + 1) * 64], True, False, [tk, "vc"], [("ps", ba)])
                                mm(PS(ba)[0:32, oc:oc + 64], pt[:, 1, :], svt[:, 16, g_ * 64:(g_ + 1) * 64], False, True, [tk, ("svt", 16)], [("ps", ba)])
                            tt("dve", stat2[0:32, 1, 0:8], stat[0:32, 2, 0:8], stat[0:32, 3, 0:8], ALU.add, ["stat", "stat2"], ["stat2"])
                            tt("dve", stat2[0:32, 1, 0:8], stat2[0:32, 1, 0:8], stat2[0:32, 0, 0:8], ALU.add, ["stat2"], ["stat2"])
                            P.op("dve", lambda e: e.reciprocal(out=stat2[0:32, 1, 8:16], in_=stat2[0:32, 1, 0:8]), r=["stat2"], w=["stat2"])
                            tt("dve", attS[:, :, :].rearrange("p s (g e) -> p (s g) e", g=2), PS(ba)[0:32, 0:512].rearrange("p (h e) -> p h e", e=64),
                               _ins(stat2[0:32, 1, 8:16], 2, 64), ALU.mult, [("ps", ba), "stat2", "attS"], ["attS"])
                            reserved.discard(ba)
                            bt = psum()
                            for j in range(4):
                                tr(PSb(bt)[:, j * 32:(j + 1) * 32], attS[:, j, :], ["attS"], [("ps", bt)])
                            sw_ap = swaT.ap
                            dst = AP(sw_ap.tensor, sw_ap.offset + s4 * 32, [list(sw_ap.ap[0]), [512, 4], [8, 4], [1, 8]])
                            src = PSb(bt)[:, 0:128].rearrange("p (s c i) -> p c s i", s=4, c=4)
                            cp("act", dst, src, [("ps", bt)] + [("swaT", c) for c in range(4)], [("swaT", c) for c in range(4)])
                if DBG and l == 0 and bi == 3:
                    for c in range(4):
                        P.dma("pool", rO, yT[512 + c * 128:512 + (c + 1) * 128, 0:512], swaT[:, c, 0:512], r=[("swaT", c)], w=[("dbg", 4 + c)], is_out=True)
                if DBG and l == 0 and bi == 4:
                    for c in range(4):
                        P.dma("pool", rO, yT[512 + c * 128:512 + (c + 1) * 128, 512:640], swaT[:, c, 0:128], r=[("swaT", c)], w=[("dbgs", 4 + c)], is_out=True)
                for fo in range(8):
                    bk = psum()
                    for c in range(4):
                        mm(PS(bk)[:, 0:n], WO_[:, c, fo * 128:(fo + 1) * 128], swaT[:, c, 0:n], c == 0, c == 3,
                           [wk, ("swaT", c)], [("ps", bk)])
                    accum(bi, fo, bk, 0, acct)
            make_tables(True, 1, lnp[:, l, 0, :], lnp[:, l, 1, :], False)

        def ffn_pass(l, gi, last_layer, WU_, WD_, wk):
            ar = Arena()
            ubuf = [ar.f32([128, 514]) for _ in range(4)]
            ca = [ar.f32([128, 512]), ar.f32([128, 512])]
            cb = [ar.f32([128, 512]), ar.f32([128, 512])]
            gT = [ar.bf16([128, 3, 512]), ar.bf16([128, 3, 512])]
            halos = ar.f32([128, 6, 16, 2])
            scv = [ar.f32([32, 128]) for _ in range(6)]
            acct = [ar.f32([128, 128]), ar.f32([128, 512]), ar.f32([128, 512])]
            lb = LNBuf(ar)
            u0, nu = FGROUPS[gi]
            firstg = gi == 0
            lastg = gi == len(FGROUPS) - 1
            if firstg:
                ms("pool", halo[:], 0.0, ["halo"])
            for ab in range(2):
                for ui in range(nu):
                    j = ab * 22 + u0 + ui
                    P.dma("sp", rT, scv[ab * 3 + ui][:, :], state_conv[l, :, j * 128:(j + 1) * 128], w=[("scv", ab * 3 + ui)])

            def load_halos():
                for ab in range(2):
                    for ui in range(nu):
                        k = ab * 3 + ui
                        bt = psum()
                        P.op("pe", lambda e, bt=bt, k=k: e.transpose(PS(bt)[:, 0:32], scv[k][:, :], identF[0:32, 0:32]),
                             r=[("scv", k), "identF"], w=[("ps", bt)])
                        cp("act", halos[:, k, :, :].rearrange("p s r -> p (s r)"), PS(bt)[:, 0:32], [("ps", bt), "halos"], ["halos"])
            if lastg:
                if last_layer:
                    make_tables(False, 0, lnp[:, l, 2, :], lnp[:, l, 3, :], True)
                else:
                    make_tables(True, 0, lnp[:, l, 2, :], lnp[:, l, 3, :], False)

            def stage1(bi):
                n0, n = BLOCKS[bi]
                smp = bi == 4
                g_ = gT[bi % 2]
                if firstg:
                    ln_finish_block(bi, lb, True)
                hk = [("HT", kc, bi) for kc in range(8)]

                def front(ui):
                    u = u0 + ui
                    cres = []
                    for ab in range(2):
                        j = ab * 22 + u
                        bk = psum()
                        for kc in range(8):
                            mm(PS(bk)[:, 0:n], WU_[:, ui, kc, ab * 128:(ab + 1) * 128], HT[:, kc, n0:n0 + n], kc == 0, kc == 7, [wk, hk[kc]], [("ps", bk)])
                        ub = ubuf[(ui % 2) * 2 + ab]
                        uk = ("ubuf", (ui % 2) * 2 + ab)
                        cc = (ca if ab == 0 else cb)[ui % 2]
                        ck = ("c", ab, ui % 2)
                        w0 = convp[:, l, j, 0:1]
                        w1 = convp[:, l, j, 1:2]
                        w2 = convp[:, l, j, 2:3]
                        bb = convp[:, l, j, 3:4]
                        if not smp:
                            cp("act", ub[:, 2:2 + n], PS(bk)[:, 0:n], [("ps", bk), uk], [uk])
                            cp("pool", ub[:, 0:2], halo[:, j, :], ["halo", uk], [uk])
                            cp("pool", halo[:, j, :], ub[:, n:n + 2], [uk, "halo"], ["halo"])
                            ts("pool", cc[:, 0:n], ub[:, 2:2 + n], w2, bb, ALU.mult, ALU.add, [uk, "convp", ck], [ck])
                            stt("dve", cc[:, 0:n], ub[:, 1:1 + n], w1, cc[:, 0:n], ALU.mult, ALU.add, [uk, "convp", ck], [ck])
                            stt("dve", cc[:, 0:n], ub[:, 0:n], w0, cc[:, 0:n], ALU.mult, ALU.add, [uk, "convp", ck], [ck])
                        else:
                            u3 = ub[:, 0:160].rearrange("p (s t) -> p s t", t=10)
                            hs = halos[:, ab * 3 + ui, :, :]
                            cp("act", u3[:, :, 2:10], PS(bk)[:, 0:128].rearrange("p (s i) -> p s i", i=8), [("ps", bk), uk], [uk])
                            cp("pool", u3[:, :, 0:2], hs, ["halos", uk], [uk])
                            cp("pool", hs, u3[:, :, 8:10], [uk, "halos"], ["halos"])
                            P.dma("sp", rO, o_conv_s[l, :, j, :, :], hs, r=["halos"], w=[("o_conv_s", l, j)], is_out=True)
                            c3 = cc[:, 0:128].rearrange("p (s i) -> p s i", i=8)
                            ts("pool", c3, u3[:, :, 2:10], w2, bb, ALU.mult, ALU.add, [uk, "convp", ck], [ck])
                            stt("dve", c3, u3[:, :, 1:9], w1, c3, ALU.mult, ALU.add, [uk, "convp", ck], [ck])
                            stt("dve", c3, u3[:, :, 0:8], w0, c3, ALU.mult, ALU.add, [uk, "convp", ck], [ck])
                        cres.append((cc, ck))
                    return cres

                def back(ui, cres):
                    (cca, cka), (ccb, ckb) = cres
                    act(cca[:, 0:n], cca[:, 0:n], AF.Silu, [cka], [cka])
                    tt("dve", g_[:, ui, 0:n], cca[:, 0:n], ccb[:, 0:n], ALU.mult, [cka, ckb, ("gT", bi % 2, ui)], [("gT", bi % 2, ui)])

                prev = None
                for ui in range(nu):
                    cr = front(ui)
                    if prev is not None:
                        back(*prev)
                    prev = (ui, cr)
                back(*prev)

            def stage2(bi):
                drain(1)
                n0, n = BLOCKS[bi]
                g_ = gT[bi % 2]
                for fo in range(8):
                    bk = psum()
                    for ui in range(nu):
                        mm(PS(bk)[:, 0:n], WD_[:, ui, fo * 128:(fo + 1) * 128], g_[:, ui, 0:n], ui == 0, ui == nu - 1,
                           [wk, ("gT", bi % 2, ui)], [("ps", bk)])
                    accum(bi, fo, bk, 1, acct)

            def stage2b(bi):
                n0, n = BLOCKS[bi]
                if lastg:
                    ln_finish_block(bi, lb, not last_layer)
                    if last_layer and not DBG:
                        for fc in range(8):
                            P.dma("sp", rO, yT[fc * 128:(fc + 1) * 128, n0:n0 + n], X[:, fc, n0:n0 + n], r=[("X", fc, bi)], w=[("yT", fc, bi)], is_out=True)

            nb = len(BLOCKS)
            stage1(0)
            for bi in range(1, nb):
                if bi == nb - 1:
                    load_halos()
                stage1(bi)
                stage2(bi - 1)
                if bi >= 2:
                    stage2b(bi - 2)
            stage2(nb - 1)
            stage2b(nb - 2)
            stage2b(nb - 1)
            if lastg:
                P.dma("sp", rO, o_conv_p[l], halo[:], r=["halo"], w=[("o_conv_p", l)], is_out=True)

        passes = []
        for l in range(L):
            for gp in range(4):
                passes.append(("R", l, gp))
            passes.append(("S", l, 0))
            for gi in range(len(FGROUPS)):
                passes.append(("F", l, gi))

        compute_mod(0)
        make_tables(True, 0, None, None, False)
        make_gates()
        ar0 = Arena()
        xh1_0 = ar0.f32([128, 128])
        for bi, (n0b, nb) in enumerate(BLOCKS):
            for sub in range(nb // 128):
                n0 = n0b + sub * 128
                finish(bi, n0, 128, lambda fc, n0=n0: X[:, fc, n0:n0 + 128], [], True, xh1_0)
        views = {}
        views[0] = load_pass_weights(passes[0][0], passes[0][1], passes[0][2], 0)
        drain()
        marks = []
        for pi, (kind, l, idx) in enumerate(passes):
            marks.append((kind, l, idx, P.cnt["pe"]))
            if pi == 0 or passes[pi - 1][0] != kind:
                P.barrier()
            s = pi % 2
            if pi + 1 < len(passes):
                nk, nl, nidx = passes[pi + 1]
                views[pi + 1] = load_pass_weights(nk, nl, nidx, (pi + 1) % 2)
                drain(1)
            v1, v2 = views.pop(pi)
            wk = ("W", s)
            if kind == "R":
                ret_pass(l, idx, v1, v2, wk)
                drain()
            elif kind == "S":
                swa_pass(l, v1, v2, wk)
                drain()
                if l + 1 < L:
                    P.barrier()
                    compute_mod(l + 1)
                    make_gates_next = True
            else:
                if idx == 0 and l + 1 <= L:
                    pass
                ffn_pass(l, idx, l == L - 1, v1, v2, wk)
                drain()
                if idx == len(FGROUPS) - 1 and l + 1 < L:
                    make_gates()

        marks.append(("END", 0, 0, P.cnt["pe"]))
        if os.environ.get("KMARKS"):
            import json
            json.dump(marks, open(os.environ["KMARKS"], "w"))
        with nc.Block() as block:
            P.emit(block)
    return nc


def _consts():
    f32 = np.float32
    h = np.arange(8)
    gam = 1.0 - 2.0 ** (-5.0 - h.astype(np.float64))
    inv = (1.0 / (np.float32(10000.0) ** (np.arange(32, dtype=f32) / np.float32(32)))).astype(f32)
    rot = np.zeros((17, 128, 4, 128), f32)
    for c in range(17):
        if c < 16:
            pos = (c * 128 + np.arange(128)).astype(f32)
        else:
            pos = (PAST + (np.arange(128) % 8)).astype(f32)
        ang = pos[:, None] * inv[None, :]
        cs, sn = np.cos(ang).astype(f32), np.sin(ang).astype(f32)
        for p in range(128):
            d = p % 64
            f = d % 32
            sg = -1.0 if d < 32 else 1.0
            rot[c, p, 0] = cs[:, f]
            rot[c, p, 1] = sg * sn[:, f]
            rot[c, p, 2] = cs[:, f] * 0.125
            rot[c, p, 3] = sg * sn[:, f] * 0.125
    i = np.arange(128)
    decm = np.zeros((128, 8, 128), f32)
    for hh in range(8):
        diff = i[None, :] - i[:, None]
        decm[:, hh, :] = np.where(diff >= 0, gam[hh] ** np.maximum(diff, 0), 0.0)
    decm = decm * np.float32(0.125)
    blkind = (i[:, None] // 8 == i[None, :] // 8).astype(f32)
    qdec = np.zeros((128, 4, 2, 128), f32)
    cdec = np.zeros((128, 2, 4, 3), f32)
    for p in range(128):
        hb = p // 64
        for pair in range(4):
            hh = pair * 2 + hb
            qdec[p, pair, 0] = gam[hh] ** (i + 1)
            qdec[p, pair, 1] = gam[hh] ** ((i % 8) + 1)
            for k, C in enumerate((128, 8)):
                dec = gam[hh] ** C
                cdec[p, k, pair] = (dec, dec if hb == 0 else 0.0, dec if hb == 1 else 0.0)
    cdec[:, :, :, 1] = (np.arange(128)[:, None, None] // 64 == 0)
    cdec[:, :, :, 2] = (np.arange(128)[:, None, None] // 64 == 1)
    kdec = np.zeros((128, 2, 8), f32)
    for hh in range(8):
        kdec[:, 0, hh] = gam[hh] ** (127 - i) * 0.125
        kdec[:, 1, hh] = gam[hh] ** (7 - (i % 8)) * 0.125
    hmask = np.zeros((128, 2), f32)
    hmask[:64, 0] = 1
    hmask[64:, 1] = 1
    seqm = (i[:, None] // 8 == np.arange(16)[None, :]).astype(f32)
    ident = np.eye(128, dtype=f32)
    hsel = np.zeros((128, 128), f32)
    hsel[:64, :64] = 1.0 / 64
    hsel[64:, 64:] = 1.0 / 64
    dist = 127 - np.arange(128)
    n = np.maximum(dist, 0)
    nf = np.maximum(n, 16).astype(f32)
    large = 16 + (np.log(nf / np.float32(16)) / np.float32(np.log(128 / 16)) * np.float32(16)).astype(np.int32)
    large = np.minimum(large, 31)
    bucket = np.where(n < 16, n, large)
    ohr = (np.arange(32)[:, None] == bucket[None, :]).astype(f32)
    rot_b = np.zeros((5, 128, 2, 512), f32)
    for c in range(17):
        rot_b[c // 4, :, :, (c % 4) * 128:(c % 4 + 1) * 128] = rot[c][:, 0:2]
    rot = rot_b
    qdec = np.ascontiguousarray(np.tile(qdec, (1, 1, 1, 4)))
    return dict(rot=rot, decm=decm, blkind=blkind, qdec=qdec, kdec=kdec, cdec=cdec, hmask=hmask,
                seqm=seqm, identf=ident, hsel=hsel, ohr=ohr)


_NC_CACHE = {}


def kernel(x_prompt, x_sample, c_prompt, c_sample, state_ret, cache_swa_k, cache_swa_v, state_conv,
           rel_bias, w_ada, b_ada, w_in, swa_sinks, w_out, ln1_g, ln1_b,
           w_up, conv_w, conv_b, w_down, ln2_g, ln2_b):
    f32 = np.float32
    A = lambda a: np.ascontiguousarray(np.asarray(a, dtype=f32))
    x_prompt, x_sample, c_prompt, c_sample = A(x_prompt), A(x_sample), A(c_prompt), A(c_sample)
    state_ret, cache_swa_k, cache_swa_v, state_conv = A(state_ret), A(cache_swa_k), A(cache_swa_v), A(state_conv)
    rel_bias, w_ada, b_ada, w_in, swa_sinks, w_out = A(rel_bias), A(w_ada), A(b_ada), A(w_in), A(swa_sinks), A(w_out)
    ln1_g, ln1_b, w_up, conv_w, conv_b, w_down, ln2_g, ln2_b = A(ln1_g), A(ln1_b), A(w_up), A(conv_w), A(conv_b), A(w_down), A(ln2_g), A(ln2_b)

    sw = np.concatenate([np.arange(32, 64), np.arange(0, 32)])
    w_r = np.zeros((L, 4, D, 768), f32)
    w_or = np.zeros((L, 4, 128, D), f32)
    for gp in range(4):
        hs = np.arange(gp * 2, gp * 2 + 2)
        cols = (hs[:, None] * 64 + np.arange(64)[None, :]).reshape(-1)
        cols_sw = (hs[:, None] * 64 + sw[None, :]).reshape(-1)
        w_r[:, gp, :, 0:128] = w_in[:, :, 0 + cols]
        w_r[:, gp, :, 128:256] = w_in[:, :, 0 + cols_sw]
        w_r[:, gp, :, 256:384] = w_in[:, :, 512 + cols]
        w_r[:, gp, :, 384:512] = w_in[:, :, 512 + cols_sw]
        w_r[:, gp, :, 512:640] = w_in[:, :, 1024 + cols]
        w_r[:, gp, :, 640:768] = w_in[:, :, 1536 + cols]
        w_or[:, gp] = w_out[:, cols, :]
    perm = np.concatenate([np.concatenate([np.arange(c * 64, c * 64 + 64), np.arange((4 + c) * 64, (4 + c) * 64 + 64)]) for c in range(4)])
    w_s = np.concatenate([w_in[:, :, 2048 + perm], w_in[:, :, 2560:2816]], axis=2)
    w_os = w_out[:, 512 + perm, :]
    w_r = np.ascontiguousarray(w_r.reshape(L, 4, 8, 128, 768).transpose(0, 1, 3, 2, 4))
    w_s = np.ascontiguousarray(w_s.reshape(L, 8, 128, 768).transpose(0, 2, 1, 3))
    w_os = np.ascontiguousarray(w_os.reshape(L, 4, 128, D).transpose(0, 2, 1, 3))
    wu = np.concatenate([w_up[:, :, :DFF].reshape(L, 8, 128, NU, 128), w_up[:, :, DFF:].reshape(L, 8, 128, NU, 128)], axis=4)
    w_up = np.ascontiguousarray(wu.transpose(0, 3, 2, 1, 4))
    w_down = np.ascontiguousarray(w_down.reshape(L, NU, 128, D).transpose(0, 2, 1, 3))
    bT_ada = np.ascontiguousarray(b_ada.reshape(L, 48, 128).transpose(2, 0, 1))
    convT = np.ascontiguousarray(np.concatenate([conv_w, conv_b[:, None, :]], axis=1).reshape(L, 4, 44, 128).transpose(3, 0, 2, 1))
    lnT = np.ascontiguousarray(np.stack([ln1_g, ln1_b, ln2_g, ln2_b], axis=1).reshape(L, 4, 8, 128).transpose(3, 0, 1, 2))
    cb_heads = np.array([(cbi % 2) * 4 + cbi // 2 for cbi in range(8)])
    sinkP = np.ascontiguousarray(np.broadcast_to(swa_sinks[:, cb_heads][None], (128, L, 8)))
    sinkS = np.zeros((32, L, 2), f32)
    for c in range(4):
        for g in range(2):
            sinkS[c * 8:(c + 1) * 8, :, g] = swa_sinks[None, :, g * 4 + c]
    cst = _consts()

    if "nc" not in _NC_CACHE:
        _NC_CACHE["nc"] = build_program()
    nc = _NC_CACHE["nc"]

    in_maps = []
    for b in range(8):
        xall = np.concatenate([x_prompt[b], x_sample[16 * b:16 * b + 16].reshape(128, D)], axis=0)
        call = np.concatenate([c_prompt[b:b + 1], c_sample[16 * b:16 * b + 16]], axis=0)
        m = dict(
            xT=np.ascontiguousarray(xall.T),
            cT=np.ascontiguousarray(call.T.reshape(8, 128, 17).transpose(1, 0, 2)),
            w_ada=w_ada, bT_ada=bT_ada, w_r=w_r, w_s=w_s, w_or=w_or, w_os=w_os, w_up=w_up, w_down=w_down,
            convT=convT, lnT=lnT, sinkP=sinkP, sinkS=sinkS, rel_bias=rel_bias,
            state_ret=np.ascontiguousarray(state_ret[:, 16 * b:16 * b + 16].reshape(L, 16, 4, 128, 64).transpose(0, 2, 3, 1, 4)),
            cache_k=np.ascontiguousarray(cache_swa_k[:, 16 * b:16 * b + 16].reshape(L, 16, 128, 128)),
            cache_v=np.ascontiguousarray(cache_swa_v[:, 16 * b:16 * b + 16].reshape(L, 16, 128, 128)),
            state_conv=np.ascontiguousarray(state_conv[:, 16 * b:16 * b + 16].reshape(L, 32, 2 * DFF)),
        )
        m.update(cst)
        in_maps.append(m)
    res = run_bass_kernel_spmd(nc, in_maps, core_ids=list(range(8)))
    R = res.results

    y_p = np.zeros((8, NP, D), f32)
    y_s = np.zeros((128, 8, D), f32)
    ret_p = np.zeros((L, 8, 8, 64, 64), f32)
    ret_s = np.zeros((L, 128, 8, 64, 64), f32)
    k_p = np.zeros((L, 8, 128, 2, 64), f32)
    v_p = np.zeros((L, 8, 128, 2, 64), f32)
    k_s = np.zeros((L, 128, 128, 2, 64), f32)
    v_s = np.zeros((L, 128, 128, 2, 64), f32)
    conv_p = np.zeros((L, 8, 2, 2 * DFF), f32)
    conv_s = np.zeros((L, 128, 2, 2 * DFF), f32)
    for b in range(8):
        r = R[b]
        yt = r["yT"].T
        y_p[b] = yt[:NP]
        y_s[16 * b:16 * b + 16] = yt[NP:].reshape(16, 8, D)
        ret_p[:, b] = r["o_ret_p"].reshape(L, 4, 2, 64, 64).reshape(L, 8, 64, 64)
        ret_s[:, 16 * b:16 * b + 16] = r["o_ret_s"].reshape(L, 4, 2, 64, 16, 64).transpose(0, 4, 1, 2, 3, 5).reshape(L, 16, 8, 64, 64)
        k_p[:, b] = r["o_k_p"].reshape(L, 128, 2, 64)
        v_p[:, b] = r["o_v_p"].reshape(L, 128, 2, 64)
        k_s[:, 16 * b:16 * b + 16] = r["o_k_s"].reshape(L, 16, 128, 2, 64)
        v_s[:, 16 * b:16 * b + 16] = r["o_v_s"].reshape(L, 16, 128, 2, 64)
        conv_p[:, b] = r["o_conv_p"].transpose(0, 3, 2, 1).reshape(L, 2, 2 * DFF)
        conv_s[:, 16 * b:16 * b + 16] = r["o_conv_s"].transpose(0, 3, 4, 2, 1).reshape(L, 16, 2, 2 * DFF)
    return (y_p, y_s, ret_p, k_p, v_p, conv_p, ret_s, k_s, v_s, conv_s)
```
